# Optimizing a Trainium2 kernel written in Bass

```python
import jax, jax.numpy as jnp
from jax import lax
import numpy as np

D_MODEL = 1024
BATCH = 16
SEQ = 256
DEPTH = 4
DEC_BATCH = 2
DEC_SEQ = 1024
PAST_LEN = 256

GRID_W = 64
N_MIXERS = 3
N_FOURIER_LAYERS = (DEPTH + 2) // 3
N_DELTA_LAYERS = (DEPTH + 1) // 3
N_MLSTM_LAYERS = DEPTH // 3
N_DIR = 2
EPS = 1e-6

FNET_GROUPS = 4
FNET_GROUP_DIM = D_MODEL // FNET_GROUPS

DN_HEADS = 8
DN_DK = 128
DN_DV = 128
DN_CONV = 5
DN_CHUNK = 64
DN_QKV = DN_HEADS * (2 * DN_DK + DN_DV)
DN_PROJ = DN_QKV + DN_HEADS * DN_DV + 2 * N_DIR * DN_HEADS

ML_HEADS = 8
ML_DQK = 64
ML_DV = 128
ML_CHUNK = 64
ML_HQ = ML_HEADS * ML_DQK
ML_HV = ML_HEADS * ML_DV
ML_PROJ = 2 * ML_HQ + 2 * ML_HV + 2 * N_DIR * ML_HEADS

D_FF = 2816

kernel_name = 'hybrid_fnet_gdn_mlstm_diffusion_step'


def rmsnorm(x, g):
    xf = x.astype(jnp.float32)
    y = xf * lax.rsqrt(jnp.mean(xf * xf, axis=-1, keepdims=True) + EPS)
    return (y * g.astype(jnp.float32)).astype(x.dtype)


def l2norm(x):
    return x * lax.rsqrt(jnp.sum(x * x, axis=-1, keepdims=True) + EPS)


def adaln(cond, w, b):
    m = (jax.nn.silu(cond) @ w + b)[:, None, :]
    return jnp.split(m, 6, axis=-1)


def dwconv_seq(x, w):
    K, T = w.shape[0], x.shape[1]
    p = K // 2
    xp = jnp.pad(x, ((0, 0), (p, p), (0, 0)))
    return sum(xp[:, j:j + T, :] * w[j] for j in range(K))


def dwconv_grid(x, w):
    R, W = x.shape[1], x.shape[2]
    xp = jnp.pad(x, ((0, 0), (1, 1), (1, 1), (0, 0)))
    return sum(xp[:, i:i + R, j:j + W, :] * w[i, j] for i in range(3) for j in range(3))


def to_chunks(a, chunk):
    B, T, H = a.shape[:3]
    a = a.reshape((B, T // chunk, chunk, H) + a.shape[3:])
    return jnp.moveaxis(a, (1, 3), (0, 2))


def from_chunks(o):
    o = jnp.moveaxis(o, (0, 2), (1, 3))
    return o.reshape((o.shape[0], o.shape[1] * o.shape[2]) + o.shape[3:])


def conv_ffn(h, w_up, conv_w, conv_b, w_down, rows):
    B, T, _ = h.shape
    a, g = jnp.split(h @ w_up, 2, axis=-1)
    g = dwconv_grid(g.reshape(B, rows, T // rows, D_FF), conv_w).reshape(B, T, D_FF) + conv_b
    return (jax.nn.silu(g) * a) @ w_down


def fourier_mix(h, w, b):
    B, T, _ = h.shape
    hg = h.astype(jnp.float32).reshape(B, T, FNET_GROUPS, FNET_GROUP_DIM)
    f = jnp.real(jnp.fft.fft2(hg, axes=(1, 3), norm='ortho'))
    return f.reshape(B, T, D_MODEL).astype(h.dtype) @ w + b


def gdn_chunked(q, k, v, g, beta, s0):
    C = DN_CHUNK
    qc, kc, vc = to_chunks(q, C), to_chunks(k, C), to_chunks(v, C)
    gc = jnp.cumsum(to_chunks(g, C), axis=-1)
    bc = to_chunks(beta, C)
    tril = jnp.tril(jnp.ones((C, C), bool))
    strict = jnp.tril(jnp.ones((C, C), bool), -1)
    decay = jnp.exp(jnp.where(tril, gc[..., :, None] - gc[..., None, :], -jnp.inf))
    kb = kc * bc[..., None]
    a_mat = jnp.where(strict, jnp.einsum('nbhcd,nbhsd->nbhcs', kb, kc) * decay, 0.0) + jnp.eye(C, dtype=jnp.float32)
    u = lax.linalg.triangular_solve(a_mat, vc * bc[..., None], left_side=True, lower=True, unit_diagonal=True)
    w = lax.linalg.triangular_solve(a_mat, kb * jnp.exp(gc)[..., None], left_side=True, lower=True, unit_diagonal=True)
    qk = jnp.einsum('nbhcd,nbhsd->nbhcs', qc, kc) * decay
    q_dec = qc * jnp.exp(gc)[..., None]
    g_last = gc[..., -1]
    k_dec = kc * jnp.exp(g_last[..., None] - gc)[..., None]

    def step(s, xs):
        u_n, w_n, qk_n, qd_n, kd_n, gl_n = xs
        v_new = u_n - jnp.einsum('bhcd,bhde->bhce', w_n, s)
        o_n = jnp.einsum('bhcd,bhde->bhce', qd_n, s) + jnp.einsum('bhcs,bhse->bhce', qk_n, v_new)
        s = s * jnp.exp(gl_n)[..., None, None] + jnp.einsum('bhcd,bhce->bhde', kd_n, v_new)
        return s, o_n

    s_fin, o = lax.scan(step, s0, (u, w, qk, q_dec, k_dec, g_last))
    return from_chunks(o), s_fin


def gated_delta_mix(h, w_in, conv_w, a_log, dt_bias, norm_g, w_out, s0):
    B, T, _ = h.shape
    f32 = jnp.float32
    qkv, z, gates = jnp.split(h @ w_in, [DN_QKV, DN_QKV + DN_HEADS * DN_DV], axis=-1)
    qkv = jax.nn.silu(dwconv_seq(qkv, conv_w)).astype(f32)
    q, k, v = jnp.split(qkv, [DN_HEADS * DN_DK, 2 * DN_HEADS * DN_DK], axis=-1)
    q = l2norm(q.reshape(B, T, DN_HEADS, DN_DK)) * (DN_DK ** -0.5)
    k = l2norm(k.reshape(B, T, DN_HEADS, DN_DK))
    v = v.reshape(B, T, DN_HEADS, DN_DV)
    gates = gates.astype(f32).reshape(B, T, N_DIR, 2, DN_HEADS)
    g = -jnp.exp(a_log.astype(f32)) * jax.nn.softplus(gates[..., 0, :] + dt_bias)
    beta = jax.nn.sigmoid(gates[..., 1, :])
    s0 = s0.astype(f32)
    o_f, s_f = gdn_chunked(q, k, v, g[:, :, 0], beta[:, :, 0], s0[:, 0])
    o_b, s_b = gdn_chunked(q[:, ::-1], k[:, ::-1], v[:, ::-1], g[:, ::-1, 1], beta[:, ::-1, 1], s0[:, 1])
    o = o_f + o_b[:, ::-1]
    o = rmsnorm(o, norm_g) * jax.nn.silu(z.astype(f32).reshape(B, T, DN_HEADS, DN_DV))
    y = o.reshape(B, T, DN_HEADS * DN_DV).astype(h.dtype) @ w_out
    return y, jnp.stack([s_f, s_b], axis=1)


def mlstm_chunked(q, k, v, li, lf, c0, n0, m0):
    C = ML_CHUNK
    qc, kc, vc = to_chunks(q, C), to_chunks(k, C), to_chunks(v, C)
    bcum = jnp.cumsum(to_chunks(lf, C), axis=-1)
    lic = to_chunks(li, C)
    tril = jnp.tril(jnp.ones((C, C), bool))
    d_log = jnp.where(tril, bcum[..., :, None] - bcum[..., None, :] + lic[..., None, :], -jnp.inf)
    d_max = jnp.max(d_log, axis=-1)
    qk = jnp.einsum('nbhcd,nbhsd->nbhcs', qc, kc)
    b_last = bcum[..., -1]
    w_log = b_last[..., None] - bcum + lic
    w_max = jnp.max(w_log, axis=-1)

    def step(carry, xs):
        c, n, m = carry
        qk_n, dl_n, dm_n, b_n, q_n, k_n, v_n, bl_n, wl_n, wm_n = xs
        m_t = jnp.maximum(b_n + m[..., None], dm_n)
        inter = jnp.exp(b_n + m[..., None] - m_t)
        s = qk_n * jnp.exp(dl_n - m_t[..., None])
        num = inter[..., None] * jnp.einsum('bhcd,bhde->bhce', q_n, c) + jnp.einsum('bhcs,bhse->bhce', s, v_n)
        den = inter * jnp.einsum('bhcd,bhd->bhc', q_n, n) + jnp.sum(s, axis=-1)
        h = num / jnp.maximum(jnp.abs(den), jnp.exp(-m_t))[..., None]
        m_new = jnp.maximum(bl_n + m, wm_n)
        dec = jnp.exp(bl_n + m - m_new)
        kw = k_n * jnp.exp(wl_n - m_new[..., None])[..., None]
        c = dec[..., None, None] * c + jnp.einsum('bhcd,bhce->bhde', kw, v_n)
        n = dec[..., None] * n + jnp.sum(kw, axis=-2)
        return (c, n, m_new), h

    (c_f, n_f, m_f), h = lax.scan(step, (c0, n0, m0), (qk, d_log, d_max, bcum, qc, kc, vc, b_last, w_log, w_max))
    return from_chunks(h), c_f, n_f, m_f


def mlstm_mix(h, w_in, b_i, b_f, norm_g, w_out, c0, n0, m0):
    B, T, _ = h.shape
    f32 = jnp.float32
    q, k, v, o, gates = jnp.split((h @ w_in).astype(f32), [ML_HQ, 2 * ML_HQ, 2 * ML_HQ + ML_HV, 2 * ML_HQ + 2 * ML_HV], axis=-1)
    q = q.reshape(B, T, ML_HEADS, ML_DQK) * (ML_DQK ** -0.5)
    k = k.reshape(B, T, ML_HEADS, ML_DQK)
    v = v.reshape(B, T, ML_HEADS, ML_DV)
    gates = gates.reshape(B, T, N_DIR, 2, ML_HEADS)
    li = gates[..., 0, :] + b_i
    lf = jax.nn.log_sigmoid(gates[..., 1, :] + b_f)
    c0, n0, m0 = c0.astype(f32), n0.astype(f32), m0.astype(f32)
    h_f, cf, nf, mf = mlstm_chunked(q, k, v, li[:, :, 0], lf[:, :, 0], c0[:, 0], n0[:, 0], m0[:, 0])
    h_b, cb, nb, mb = mlstm_chunked(q[:, ::-1], k[:, ::-1], v[:, ::-1], li[:, ::-1, 1], lf[:, ::-1, 1], c0[:, 1], n0[:, 1], m0[:, 1])
    hs = h_f + h_b[:, ::-1]
    hs = rmsnorm(hs, norm_g) * jax.nn.sigmoid(o.reshape(B, T, ML_HEADS, ML_DV))
    y = hs.reshape(B, T, ML_HV).astype(h.dtype) @ w_out
    return y, jnp.stack([cf, cb], axis=1), jnp.stack([nf, nb], axis=1), jnp.stack([mf, mb], axis=1)


def trunk(x, cond, rows, st_d, st_c, st_n, st_m, params):
    (w_ada, b_ada, norm_mix, norm_ffn, norm_final, ffn_w_up, ffn_conv_w, ffn_conv_b, ffn_w_down,
     fnet_w, fnet_b, dn_w_in, dn_conv_w, dn_a_log, dn_dt_bias, dn_norm, dn_w_out,
     ml_w_in, ml_b_i, ml_b_f, ml_norm, ml_w_out) = params
    out_d, out_c, out_n, out_m = [], [], [], []
    for layer in range(DEPTH):
        sh1, sc1, g1, sh2, sc2, g2 = adaln(cond, w_ada[layer], b_ada[layer])
        h = rmsnorm(x, norm_mix[layer]) * (1 + sc1) + sh1
        kind, j = layer % N_MIXERS, layer // N_MIXERS
        if kind == 0:
            y = fourier_mix(h, fnet_w[j], fnet_b[j])
        elif kind == 1:
            y, sd = gated_delta_mix(h, dn_w_in[j], dn_conv_w[j], dn_a_log[j], dn_dt_bias[j], dn_norm[j], dn_w_out[j], st_d[:, j])
            out_d.append(sd)
        else:
            y, mc, mn, mm = mlstm_mix(h, ml_w_in[j], ml_b_i[j], ml_b_f[j], ml_norm[j], ml_w_out[j], st_c[:, j], st_n[:, j], st_m[:, j])
            out_c.append(mc)
            out_n.append(mn)
            out_m.append(mm)
        x = x + g1 * y
        h = rmsnorm(x, norm_ffn[layer]) * (1 + sc2) + sh2
        x = x + g2 * conv_ffn(h, ffn_w_up[layer], ffn_conv_w[layer], ffn_conv_b[layer], ffn_w_down[layer], rows)
    return (rmsnorm(x, norm_final), jnp.stack(out_d, axis=1), jnp.stack(out_c, axis=1),
            jnp.stack(out_n, axis=1), jnp.stack(out_m, axis=1))


def setup_inputs(seed: int = 0) -> dict:
    key = jax.random.key(seed)
    ks = jax.random.split(key, 32)
    f32 = jnp.float32

    def nrm(k, shape, s):
        return jax.random.normal(k, shape, f32) * s

    D = D_MODEL
    dt = jnp.exp(jax.random.uniform(ks[22], (N_DELTA_LAYERS, N_DIR, DN_HEADS), f32, np.log(1e-3), np.log(1e-1)))
    return {
        'x_prompt': nrm(ks[0], (BATCH, SEQ, D), 1.0),
        'x_sample': nrm(ks[1], (DEC_BATCH, DEC_SEQ, D), 1.0),
        'state_delta': nrm(ks[2], (DEC_BATCH, N_DELTA_LAYERS, N_DIR, DN_HEADS, DN_DK, DN_DV), 0.1),
        'state_mlstm_c': nrm(ks[3], (DEC_BATCH, N_MLSTM_LAYERS, N_DIR, ML_HEADS, ML_DQK, ML_DV), 0.5),
        'state_mlstm_n': nrm(ks[4], (DEC_BATCH, N_MLSTM_LAYERS, N_DIR, ML_HEADS, ML_DQK), 0.5),
        'state_mlstm_m': nrm(ks[5], (DEC_BATCH, N_MLSTM_LAYERS, N_DIR, ML_HEADS), 1.0),
        'c': nrm(ks[6], (DEC_BATCH, D), 1.0),
        'c_ctx': nrm(ks[7], (D,), 1.0),
        'w_ada': nrm(ks[8], (DEPTH, D, 6 * D), D ** -0.5),
        'b_ada': nrm(ks[9], (DEPTH, 6 * D), 0.02),
        'norm_mix': 1.0 + nrm(ks[10], (DEPTH, D), 0.02),
        'norm_ffn': 1.0 + nrm(ks[11], (DEPTH, D), 0.02),
        'norm_final': 1.0 + nrm(ks[12], (D,), 0.02),
        'ffn_w_up': nrm(ks[13], (DEPTH, D, 2 * D_FF), D ** -0.5),
        'ffn_conv_w': nrm(ks[14], (DEPTH, 3, 3, D_FF), 1.0 / 3.0),
        'ffn_conv_b': nrm(ks[15], (DEPTH, D_FF), 0.02),
        'ffn_w_down': nrm(ks[16], (DEPTH, D_FF, D), D_FF ** -0.5),
        'fnet_w': nrm(ks[17], (N_FOURIER_LAYERS, D, D), D ** -0.5),
        'fnet_b': nrm(ks[18], (N_FOURIER_LAYERS, D), 0.02),
        'dn_w_in': nrm(ks[19], (N_DELTA_LAYERS, D, DN_PROJ), D ** -0.5),
        'dn_conv_w': nrm(ks[20], (N_DELTA_LAYERS, DN_CONV, DN_QKV), DN_CONV ** -0.5),
        'dn_a_log': jnp.log(jax.random.uniform(ks[21], (N_DELTA_LAYERS, N_DIR, DN_HEADS), f32, 1.0, 16.0)),
        'dn_dt_bias': jnp.log(jnp.expm1(dt)),
        'dn_norm': 1.0 + nrm(ks[23], (N_DELTA_LAYERS, DN_DV), 0.02),
        'dn_w_out': nrm(ks[24], (N_DELTA_LAYERS, DN_HEADS * DN_DV, D), (DN_HEADS * DN_DV) ** -0.5),
        'ml_w_in': nrm(ks[25], (N_MLSTM_LAYERS, D, ML_PROJ), D ** -0.5),
        'ml_b_i': nrm(ks[26], (N_MLSTM_LAYERS, N_DIR, ML_HEADS), 0.1),
        'ml_b_f': jax.random.uniform(ks[27], (N_MLSTM_LAYERS, N_DIR, ML_HEADS), f32, 3.0, 6.0),
        'ml_norm': 1.0 + nrm(ks[28], (N_MLSTM_LAYERS, ML_DV), 0.02),
        'ml_w_out': nrm(ks[29], (N_MLSTM_LAYERS, ML_HV, D), ML_HV ** -0.5),
    }


def reference(x_prompt, x_sample, state_delta, state_mlstm_c, state_mlstm_n, state_mlstm_m, c,
              c_ctx, w_ada, b_ada, norm_mix, norm_ffn, norm_final,
              ffn_w_up, ffn_conv_w, ffn_conv_b, ffn_w_down,
              fnet_w, fnet_b,
              dn_w_in, dn_conv_w, dn_a_log, dn_dt_bias, dn_norm, dn_w_out,
              ml_w_in, ml_b_i, ml_b_f, ml_norm, ml_w_out):
    params = (w_ada, b_ada, norm_mix, norm_ffn, norm_final, ffn_w_up, ffn_conv_w, ffn_conv_b, ffn_w_down,
              fnet_w, fnet_b, dn_w_in, dn_conv_w, dn_a_log, dn_dt_bias, dn_norm, dn_w_out,
              ml_w_in, ml_b_i, ml_b_f, ml_norm, ml_w_out)
    b = x_prompt.shape[0]
    f32 = jnp.float32
    zd = jnp.zeros((b, N_DELTA_LAYERS, N_DIR, DN_HEADS, DN_DK, DN_DV), f32)
    zc = jnp.zeros((b, N_MLSTM_LAYERS, N_DIR, ML_HEADS, ML_DQK, ML_DV), f32)
    zn = jnp.zeros((b, N_MLSTM_LAYERS, N_DIR, ML_HEADS, ML_DQK), f32)
    zm = jnp.zeros((b, N_MLSTM_LAYERS, N_DIR, ML_HEADS), f32)
    y_prompt, new_d, new_c, new_n, new_m = trunk(x_prompt, c_ctx[None, :], 1, zd, zc, zn, zm, params)
    rows = x_sample.shape[1] // GRID_W
    y_sample, _, _, _, _ = trunk(x_sample, c, rows, state_delta, state_mlstm_c, state_mlstm_n, state_mlstm_m, params)
    return (y_prompt, y_sample, new_d, new_c, new_n, new_m)
```

```python
import numpy as np
from contextlib import ExitStack
import concourse.bass as bass
import concourse.mybir as mybir
from concourse.bass_utils import run_bass_kernel_spmd

F32 = mybir.dt.float32
BF16 = mybir.dt.bfloat16
AF = mybir.ActivationFunctionType
ALU = mybir.AluOpType

ENGS = ("pe", "act", "dve", "pool", "sp")
D = 1024
NT = 1536
DFF = 2816
NCH = 22
TILES = [(0, 512), (512, 1024), (1024, 1536)]
COND = [0, 1, 1]
EPS = 1e-6
N_CORES = 8


def _rect(ap):
    t = ap.tensor
    name = t.name
    pat = ap.ap
    off = ap.offset
    esz = mybir.dt.size(ap.dtype)
    if "dram" in str(type(t)).lower() or "DRam" in str(type(t)):
        ext = 1
        for st, cnt in pat:
            ext += (cnt - 1) * abs(st)
        return (name, 0, 1, off * esz, (off + ext) * esz)
    shape = list(t.shape)
    fsz = 1
    for s in shape[1:]:
        fsz *= s
    pcnt = pat[0][1]
    p_lo = off // fsz
    f_lo = off % fsz
    lo = 0
    hi = 0
    for st, cnt in pat[1:]:
        if st >= 0:
            hi += (cnt - 1) * st
        else:
            lo += (cnt - 1) * st
    return (name, p_lo, p_lo + pcnt, (f_lo + lo) * esz, (f_lo + hi + 1) * esz)


class Sched:
    def __init__(self, nc, n_dma_sems=8):
        self.nc = nc
        self.q = {e: [] for e in ENGS}
        self.cnt = {e: 0 for e in ENGS}
        self.waited = {e: {} for e in ENGS}
        self.recs = {}
        self.n_dma_sems = n_dma_sems
        self.dma_i = {e: 0 for e in ENGS}
        self.dma_cnt = {}
        self.n_ops = 0

    def _deps(self, eng, ap, is_write):
        r = _rect(ap)
        lst = self.recs.setdefault(r[0], [])
        is_psum = r[0].startswith("ps")
        deps = []
        keep = []
        for rec in lst:
            (_, pl, ph, fl, fh), tok, w, e = rec
            overlap = not (ph <= r[1] or r[2] <= pl or fh <= r[3] or r[4] <= fl)
            if is_psum and e != eng:
                deps.append(tok)
                continue
            if overlap:
                if is_write or w:
                    same = (e == eng) and tok[0] == e
                    if same and eng == "pe":
                        pass
                    else:
                        deps.append(tok)
                if is_write and pl >= r[1] and ph <= r[2] and fl >= r[3] and fh <= r[4]:
                    continue
            keep.append(rec)
        self.recs[r[0]] = keep
        return deps, r

    def _emit_waits(self, eng, deps):
        w = self.waited[eng]
        best = {}
        for k, v in deps:
            if w.get(k, 0) >= v:
                continue
            if best.get(k, 0) < v:
                best[k] = v
        for k, v in best.items():
            w[k] = v
            self.q[eng].append(("wait", k, v))

    def _record(self, r, tok, w, eng):
        lst = self.recs[r[0]]
        if not w:
            for rec in lst:
                if (not rec[2]) and rec[3] == eng and rec[0] == r and rec[1][0] == tok[0]:
                    rec[1] = tok
                    return
        lst.append([r, tok, w, eng])

    def op(self, eng, fn, reads=(), writes=()):
        deps = []
        rr = []
        for ap in reads:
            d, r = self._deps(eng, ap, False)
            deps += d
            rr.append((r, False))
        for ap in writes:
            d, r = self._deps(eng, ap, True)
            deps += d
            rr.append((r, True))
        self._emit_waits(eng, deps)
        self.cnt[eng] += 1
        tok = (eng, self.cnt[eng])
        self.q[eng].append(("op", fn, eng, 1))
        for r, w in rr:
            self._record(r, tok, w, eng)
        self.n_ops += 1
        return tok

    def dma(self, eng, out, in_, **kw):
        deps = []
        d, r_in = self._deps(eng, in_, False)
        deps += d
        d, r_out = self._deps(eng, out, True)
        deps += d
        i = self.dma_i[eng] % self.n_dma_sems
        self.dma_i[eng] += 1
        key = ("dma", eng, i)
        prev = self.dma_cnt.get(key, 0)
        if prev:
            deps.append((key, prev))
        self._emit_waits(eng, deps)
        val = prev + 16
        self.dma_cnt[key] = val
        tok = (key, val)
        self.q[eng].append(("op", lambda e: e.dma_start(out=out, in_=in_, **kw), key, 16))
        self._record(r_in, tok, False, eng)
        self._record(r_out, tok, True, eng)
        self.n_ops += 1
        return tok

    def wait_all(self, eng):
        deps = []
        for e in ENGS:
            if self.cnt[e] and e != eng:
                deps.append((e, self.cnt[e]))
        for k, v in self.dma_cnt.items():
            deps.append((k, v))
        self._emit_waits(eng, deps)

    def matmul(self, out, lhsT, rhs, start=True, stop=True):
        return self.op("pe", lambda e: e.matmul(out, lhsT, rhs, start=start, stop=stop), [lhsT, rhs], [out])

    def transpose(self, out, in_, ident):
        return self.op("pe", lambda e: e.transpose(out, in_, ident), [in_, ident], [out])

    def act(self, out, in_, func, bias=None, scale=None, accum_out=None):
        kw = {}
        rd = [in_]
        if bias is not None:
            kw["bias"] = bias
            if not isinstance(bias, (int, float)):
                rd.append(bias)
        if scale is not None:
            kw["scale"] = scale
            if not isinstance(scale, (int, float)):
                rd.append(scale)
        wr = [out]
        if accum_out is not None:
            kw["accum_out"] = accum_out
            wr.append(accum_out)
        return self.op("act", lambda e: e.activation(out, in_, func, **kw), rd, wr)

    def tt(self, eng, out, in0, in1, op):
        return self.op(eng, lambda e: e.tensor_tensor(out, in0, in1, op), [in0, in1], [out])

    def ts(self, eng, out, in0, s1, s2=None, op0=ALU.mult, op1=None):
        rd = [in0]
        for s in (s1, s2):
            if s is not None and not isinstance(s, (int, float)):
                rd.append(s)
        if op1 is None:
            return self.op(eng, lambda e: e.tensor_scalar(out, in0, s1, None, op0), rd, [out])
        return self.op(eng, lambda e: e.tensor_scalar(out, in0, s1, s2, op0, op1), rd, [out])

    def stt(self, out, in0, scalar, in1, op0, op1):
        rd = [in0, in1]
        if not isinstance(scalar, (int, float)):
            rd.append(scalar)
        return self.op("dve", lambda e: e.scalar_tensor_tensor(out, in0, scalar, in1, op0, op1), rd, [out])

    def copy(self, eng, out, in_):
        if eng == "act":
            return self.op(eng, lambda e: e.copy(out, in_), [in_], [out])
        return self.op(eng, lambda e: e.tensor_copy(out, in_), [in_], [out])

    def memset(self, eng, ap, val):
        return self.op(eng, lambda e: e.memset(ap, val), [], [ap])

    def recip(self, out, in_):
        return self.op("dve", lambda e: e.reciprocal(out, in_), [in_], [out])

    def emit(self):
        nc = self.nc
        keys = [e for e in ENGS if self.cnt[e]] + list(self.dma_cnt.keys())
        with ExitStack() as es:
            sems = {}
            for i, k in enumerate(keys):
                sems[k] = es.enter_context(nc.semaphore("s%d" % i))
            block = es.enter_context(nc.Block())
            q = self.q

            def run(engname, eng):
                for it in q[engname]:
                    if it[0] == "wait":
                        eng.wait_ge(sems[it[1]], it[2])
                    else:
                        it[1](eng).then_inc(sems[it[2]], it[3])

            if q["sp"]:
                @block.sync
                def _(e):
                    run("sp", e)
            if q["act"]:
                @block.scalar
                def _(e):
                    run("act", e)
            if q["dve"]:
                @block.vector
                def _(e):
                    run("dve", e)
            if q["pool"]:
                @block.gpsimd
                def _(e):
                    run("pool", e)
            if q["pe"]:
                @block.tensor
                def _(e):
                    run("pe", e)


def _const_tables():
    i = np.arange(128)
    r, c = np.meshgrid(i, i, indexing="ij")
    mats = [
        (r == c), np.ones((128, 128)), -np.ones((128, 128)),
        (r >= c), (r <= c), (r > c), (r < c), (r <= c), (r >= c),
    ]
    consts = np.concatenate([m.astype(np.float32) for m in mats], axis=1)
    m2 = [(r // 8 == c // 8)]
    for sz in (8, 16, 32, 64):
        m2.append((r // (2 * sz) == c // (2 * sz)) & (r // sz != c // sz))
    mask2 = np.concatenate([np.concatenate([m, m], axis=1).astype(np.float32) for m in m2], axis=1)
    k = np.arange(256)
    ang = 2.0 * np.pi * ((k[:, None] * k[None, :]) % 256) / 256.0
    cs3 = np.concatenate([np.cos(ang), np.sin(ang), -np.sin(ang)], axis=1).astype(np.float32)
    t = np.arange(1024)
    ang = 2.0 * np.pi * ((t[:, None] * t[None, :]) % 1024) / 1024.0
    ct = np.cos(ang).astype(np.float32)
    nst = (-np.sin(ang)).astype(np.float32)
    tab = np.zeros((4, 128, 2, 8, 256), np.float32)
    for q in range(4):
        for j, m in enumerate((ct, nst)):
            blk = m[:, q * 256:(q + 1) * 256].reshape(8, 128, 256)
            tab[q, :, j] = blk.transpose(1, 0, 2)
    return consts, cs3, tab, mask2


C_ID, C_ONE, C_NEG, C_LT, C_UT, C_SLT, C_SUT = range(7)


def _fm(v):
    v = np.asarray(v, np.float32)
    lead = v.shape[:-1]
    n = v.shape[-1] // 128
    return np.ascontiguousarray(np.moveaxis(v.reshape(lead + (n, 128)), -1, 0))


class Builder:
    def __init__(self, cfg):
        self.cfg = cfg
        self.nc = bass.Bass("TRN2", target_bir_lowering=False)
        self.S = Sched(self.nc)
        self.es = ExitStack()
        self.steps = []
        self.bank_ctr = {}

    def dram_in(self, name, shape):
        return self.nc.dram_tensor(name, list(shape), F32, kind="ExternalInput").ap()

    def dram_out(self, name, shape):
        return self.nc.dram_tensor(name, list(shape), F32, kind="ExternalOutput").ap()

    def sb(self, name, shape, dt=F32):
        return self.es.enter_context(self.nc.sbuf_tensor(name, list(shape), dt))

    def carve(self, off, shape, dt=F32):
        n = int(np.prod(shape))
        if dt == BF16:
            w = (n + 1) // 2
            v = self.scr[:, off:off + w].bitcast(BF16)[:, 0:n]
        else:
            w = n
            v = self.scr[:, off:off + w]
        assert off + w <= self.scr_words, (off, w, self.scr_words)
        if len(shape) == 2:
            v = v.rearrange("p (a b) -> p a b", a=shape[0])
        elif len(shape) == 3:
            v = v.rearrange("p (a b c) -> p a b c", a=shape[0], b=shape[1])
        elif len(shape) == 4:
            v = v.rearrange("p (a b c d) -> p a b c d", a=shape[0], b=shape[1], c=shape[2])
        return v, w

    def bank(self, role="x", pool=None):
        pool = pool or list(range(8))
        i = self.bank_ctr.get(role, 0)
        self.bank_ctr[role] = i + 1
        return self.ps[pool[i % len(pool)]]

    def step(self, loads, fn):
        self.steps.append((loads, fn))

    def run_steps(self):
        S = self.S
        R = len(self.slots)
        load_steps = [i for i, (l, f) in enumerate(self.steps) if l is not None]
        slot_of = {si: k % R for k, si in enumerate(load_steps)}
        issued = 0

        def issue(upto):
            nonlocal issued
            while issued < len(load_steps) and issued <= upto:
                si = load_steps[issued]
                slot = self.slots[slot_of[si]]
                for (dstf, src) in self.steps[si][0](slot):
                    S.dma("pool", dstf, src)
                issued += 1

        k = 0
        for i, (l, f) in enumerate(self.steps):
            issue(k + R - 1)
            if l is not None:
                f(self.slots[slot_of[i]])
                k += 1
            else:
                f(None)
        self.steps = []

    def norm_mod(self, A, Bv, out_h=True, out_y=None):
        S = self.S
        off = self.scr_tmp
        SQ, w = self.carve(off, [8, 512], BF16); off += w
        RS, w = self.carve(off, [512]); off += w
        TM, w = self.carve(off, [2, 512]); off += w
        for tt, (t0, t1) in enumerate(TILES):
            cd = COND[tt]
            S.act(SQ, self.X[:, :, t0:t1], AF.Square)
            ps = self.bank("n", [6, 7])
            for fc in range(8):
                S.matmul(ps[:], self.ones_bf[:, 0:128], SQ[:, fc, :], start=(fc == 0), stop=(fc == 7))
            S.act(RS, ps[:], AF.Sqrt, bias=self.eps_col[:, 0:1], scale=1.0 / D)
            S.recip(RS, RS)
            for fc in range(8):
                if out_y is not None:
                    S.stt(out_y[:, fc, t0:t1], self.X[:, fc, t0:t1], A[:, fc:fc + 1], RS, ALU.mult, ALU.mult)
                else:
                    tm = TM[:, fc % 2, :]
                    S.tt("dve", tm, self.X[:, fc, t0:t1], RS, ALU.mult)
                    S.act(self.H[:, fc, t0:t1], tm, AF.Identity, bias=Bv[:, fc, cd:cd + 1], scale=A[:, fc, cd:cd + 1])

    def adaln(self, l):
        S = self.S
        for q in range(12):
            def loads(slot, q=q):
                v = slot[:, 0:4096].rearrange("p (k n) -> p k n", k=8)
                return [(v, self.w_ada[l][:, q * 512:(q + 1) * 512].rearrange("(k p) n -> p k n", p=128))]

            def fn(slot, q=q):
                v = slot[:, 0:4096].rearrange("p (k n) -> p k n", k=8)
                ps = self.bank("n", [6, 7])
                for ocl in range(4):
                    for kc in range(8):
                        S.matmul(ps[:, ocl * 2:ocl * 2 + 2], v[:, kc, ocl * 128:(ocl + 1) * 128],
                                 self.SC[:, kc, :], start=(kc == 0), stop=(kc == 7))
                pv = ps[:, 0:8].rearrange("p (o c) -> p o c", c=2)
                for c in range(2):
                    S.tt("dve", self.MOD[:, l, q * 4:(q + 1) * 4, c], pv[:, :, c],
                         self.b_adaT[:, l * 48 + q * 4: l * 48 + (q + 1) * 4], ALU.add)
            self.step(loads, fn)

        def mkfin(sub):
            def fin(slot):
                for c in range(2):
                    S.stt(self.AMOD[:, l, sub, :, c], self.MOD[:, l, (1 + 3 * sub) * 8:(2 + 3 * sub) * 8, c], 1.0,
                          self.nmT[:, (sub * 4 + l) * 8:(sub * 4 + l + 1) * 8], ALU.add, ALU.mult)
            return fin
        st = self.steps
        self.steps = st[:-12] + st[-12:-8] + [(None, mkfin(0))] + st[-8:] + [(None, mkfin(1))]

    def mod(self, l, j):
        return self.MOD[:, l, j * 8:(j + 1) * 8, :]

    def tap(self, k):
        if self.dbg is not None and k < self.cfg.get("ntaps", 0):
            self.S.dma("sp", self.dbg[k].rearrange("(c p) t -> p c t", p=128), self.X[:])

    def ffn(self, l):
        S = self.S
        self.step(None, lambda slot: self.norm_mod(self.AMOD[:, l, 1], self.mod(l, 3)))
        off = self.scr_main
        P, w = self.carve(off, [4, NT], BF16); off += w
        GP = []
        for b in range(2):
            g, w = self.carve(off, [1720], BF16); off += w
            GP.append(g)
        SB = []
        for b in range(2):
            s, w = self.carve(off, [512]); off += w
            SB.append(s)
        DG = []
        for b in range(2):
            d, w = self.carve(off, [9, 128], BF16); off += w
            DG.append(d)
        assert off <= self.scr_tmp

        def zero(slot):
            for g in GP:
                S.memset("dve", g, 0.0)
        self.step(None, zero)

        def chunk(c, slot, j, pj):
            wv = slot[:, 0:4096].rearrange("p (k n) -> p k n", k=8)
            gp = GP[c % 2]
            gpP = gp[:, 0:516].rearrange("p (s t) -> p s t", s=2)
            gpS = gp[:, 516:516 + 18 * 66].rearrange("p (r c) -> p r c", r=18)
            dg = DG[c % 2]
            i0 = (l * NCH + c) * 9
            S.tt("dve", dg, self.ident_bf.unsqueeze(1).broadcast_to([128, 9, 128]),
                 self.cwT[:, i0:i0 + 9].unsqueeze(2).broadcast_to([128, 9, 128]), ALU.mult)
            psG = []
            for tt, (t0, t1) in enumerate(TILES):
                ps = self.bank("g", [0, 1, 2])
                for kc in range(8):
                    S.matmul(ps[:], wv[:, kc, 256 + j * 128:256 + (j + 1) * 128], self.H[:, kc, t0:t1],
                             start=(kc == 0), stop=(kc == 7))
                psG.append(ps)
            S.copy("act", gpP[:, :, 1:257], psG[0][:].rearrange("p (s t) -> p s t", s=2))
            for hf in range(2):
                S.copy("act", gpS[:, 1 + 8 * hf:9 + 8 * hf, 1:65], psG[1 + hf][:].rearrange("p (r c) -> p r c", r=8))
            psA = []
            for tt, (t0, t1) in enumerate(TILES):
                ps = self.bank("a", [3, 4, 5])
                for kc in range(8):
                    S.matmul(ps[:], wv[:, kc, j * 128:(j + 1) * 128], self.H[:, kc, t0:t1],
                             start=(kc == 0), stop=(kc == 7))
                psA.append(ps)
            for tt, (t0, t1) in enumerate(TILES):
                ps = self.bank("c", [6, 7])
                if tt == 0:
                    for dc in range(3):
                        S.matmul(ps[:].rearrange("p (s t) -> p s t", s=2), dg[:, 3 + dc, :], gpP[:, :, dc:dc + 256],
                                 start=(dc == 0), stop=(dc == 2))
                else:
                    hf = tt - 1
                    n = 0
                    for dr in range(3):
                        for dc in range(3):
                            S.matmul(ps[:].rearrange("p (r c) -> p r c", r=8), dg[:, dr * 3 + dc, :],
                                     gpS[:, 8 * hf + dr:8 * hf + dr + 8, dc:dc + 64], start=(n == 0), stop=(n == 8))
                            n += 1
                sb = SB[tt % 2]
                S.act(sb, ps[:], AF.Silu, bias=self.cbT[:, l * NCH + c:l * NCH + c + 1])
                S.tt("dve", P[:, pj, t0:t1], sb, psA[tt][:], ALU.mult)

        c0 = 0
        while c0 < NCH:
            G = min(4, NCH - c0)
            for pr in range(0, G, 2):
                c = c0 + pr

                def loads(slot, c=c):
                    wv = slot[:, 0:4096].rearrange("p (k n) -> p k n", k=8)
                    wu = self.w_up[l]
                    return [(wv[:, :, 0:256], wu[:, c * 128:(c + 2) * 128].rearrange("(k p) n -> p k n", p=128)),
                            (wv[:, :, 256:512], wu[:, DFF + c * 128:DFF + (c + 2) * 128].rearrange("(k p) n -> p k n", p=128))]

                def fn(slot, c=c, pr=pr):
                    chunk(c, slot, 0, pr)
                    chunk(c + 1, slot, 1, pr + 1)
                self.step(loads, fn)

            def dloads(slot, c0=c0, G=G):
                wv = slot[:, 0:G * 1024].rearrange("p (g n) -> p g n", g=G)
                return [(wv, self.w_down[l][c0 * 128:(c0 + G) * 128, :].rearrange("(g p) n -> p g n", p=128))]

            def dfn(slot, c0=c0, G=G):
                wv = slot[:, 0:G * 1024].rearrange("p (g n) -> p g n", g=G)
                g2 = self.mod(l, 5)
                for oc in range(8):
                    for tt, (t0, t1) in enumerate(TILES):
                        ps = self.bank("g", [0, 1, 2])
                        for j in range(G):
                            S.matmul(ps[:], wv[:, j, oc * 128:(oc + 1) * 128], P[:, j, t0:t1], start=(j == 0), stop=(j == G - 1))
                        cd = COND[tt]
                        S.stt(self.X[:, oc, t0:t1], ps[:], g2[:, oc, cd:cd + 1], self.X[:, oc, t0:t1], ALU.mult, ALU.add)
            self.step(dloads, dfn)
            c0 += G

    def fnet(self, l, jf):
        S = self.S
        self.step(None, lambda slot: self.norm_mod(self.AMOD[:, l, 0], self.mod(l, 0)))
        off = self.scr_main
        AB, w = self.carve(off, [12, 4, 512], BF16); off += w
        Fm, w = self.carve(off, [8, NT], BF16); off += w
        assert off <= self.scr_words
        CS3 = self.CS3

        def stage1(slot):
            S.dma("pool", self.fb_row[:], self.fnet_b_d[:, jf * D:(jf + 1) * D])
            n = 0
            for ti in range(12):
                for g in range(4):
                    ps = self.bank("x")
                    for j in range(2):
                        S.matmul(ps[:], self.H[:, 2 * g + j, ti * 128:(ti + 1) * 128], CS3[:, j, 0:512], start=(j == 0), stop=(j == 1))
                    S.copy("act" if n % 2 else "dve", AB[:, ti, g, :], ps[:])
                    n += 1
        self.step(None, stage1)

        def stage2p(slot):
            for cc in range(8):
                g, hf = cc // 2, cc % 2
                ps = self.bank("x")
                for s in range(2):
                    n = 0
                    for tk in range(2):
                        for part in range(2):
                            S.matmul(ps[:, s * 256:(s + 1) * 256], AB[:, 2 * s + tk, g, part * 256 + hf * 128:part * 256 + (hf + 1) * 128],
                                     CS3[:, tk, part * 512:part * 512 + 256], start=(n == 0), stop=(n == 3))
                            n += 1
                S.act(Fm[:, cc, 0:512], ps[:], AF.Copy, scale=1.0 / 256.0)
        self.step(None, stage2p)

        for q in range(4):
            def loads(slot, q=q):
                return [(slot[:, 0:4096].rearrange("p (a b) -> p a b", a=4),
                         self.tab[q].rearrange("p a k u -> p (a k u)").rearrange("p (a b) -> p a b", a=4))]

            def fn(slot, q=q):
                tv = slot[:, 0:4096].rearrange("p (a k u) -> p a k u", a=2, k=8)
                for cc in range(8):
                    g, hf = cc // 2, cc % 2
                    ps = self.bank("x")
                    n = 0
                    for tk in range(8):
                        for part in range(2):
                            S.matmul(ps[:, 0:256], AB[:, 4 + tk, g, part * 256 + hf * 128:part * 256 + (hf + 1) * 128],
                                     tv[:, part, tk, :], start=(n == 0), stop=(n == 15))
                            n += 1
                    S.act(Fm[:, cc, 512 + q * 256:512 + (q + 1) * 256], ps[:, 0:256], AF.Copy, scale=1.0 / 512.0)
            self.step(loads, fn)

        for half in range(2):
            def loads(slot, half=half):
                wv = slot[:, 0:4096].rearrange("p (k n) -> p k n", k=8)
                return [(wv, self.fnet_w[jf][:, half * 512:(half + 1) * 512].rearrange("(k p) n -> p k n", p=128))]

            def fn(slot, half=half):
                wv = slot[:, 0:4096].rearrange("p (k n) -> p k n", k=8)
                g1 = self.mod(l, 2)
                for ocl in range(4):
                    oc = half * 4 + ocl
                    for tt, (t0, t1) in enumerate(TILES):
                        ps = self.bank("x")
                        for kc in range(8):
                            S.matmul(ps[:], wv[:, kc, ocl * 128:(ocl + 1) * 128], Fm[:, kc, t0:t1], start=(kc == 0), stop=False)
                        S.matmul(ps[:], self.fb_row[0:1, oc * 128:(oc + 1) * 128], self.ones_bf[0:1, :],
                                 start=False, stop=True)
                        cd = COND[tt]
                        S.stt(self.X[:, oc, t0:t1], ps[:], g1[:, oc, cd:cd + 1], self.X[:, oc, t0:t1], ALU.mult, ALU.add)
            self.step(loads, fn)

    def gdn(self, l):
        S = self.S
        CF = self.consts
        cf = lambda k: CF[:, k * 128:(k + 1) * 128]
        ident_bf = self.ident_bf
        self.step(None, lambda slot: self.norm_mod(self.AMOD[:, l, 0], self.mod(l, 0)))
        off = 0
        def cv(shape, dt=F32):
            nonlocal off
            v, w = self.carve(off, shape, dt)
            off += w
            return v
        GA = cv([12, 16]); BA = cv([12, 16])
        Z = cv([NT], BF16)
        QKV = cv([3, NT], BF16)
        K_tm = cv([12, 128], BF16); V_tm = cv([12, 128], BF16)
        O_tm = cv([12, 128])
        ST_T = cv([8, 2, 128], BF16); ST_Q = cv([8, 2, 128], BF16); ST_QD = cv([8, 2, 128], BF16); ST_KD = cv([8, 2, 128], BF16)
        CCE = cv([8, 8])
        Sf = [cv([128]) for _ in range(4)]; Sb = [cv([128], BF16) for _ in range(4)]
        RP = [cv([128], BF16) for _ in range(4)]; VN = [cv([128], BF16) for _ in range(4)]
        SSQ = cv([12])
        base = off
        CP = cv([3, 1548], BF16); DG5 = cv([15, 128], BF16); SQ = cv([512], BF16); RS = cv([512])
        T0 = cv([12, 16]); EA = cv([12, 16])
        endA = off
        off = base
        NB = 5
        GM2 = []; EM = []; ER = []; T1 = []; CC = []; LNs = []; PQs = []; MBs = []
        for _ in range(NB):
            o1 = off
            GM2.append(cv([2, 128]))
            ln1, _w = self.carve(o1, [512], BF16)
            o2 = off
            EM.append(cv([512], BF16))
            ln2, _w = self.carve(o2, [512], BF16)
            ER.append(cv([256], BF16)); T1.append(cv([2, 128], BF16)); CC.append(cv([8]))
            LNs.append([cv([512], BF16), ln1, ln2])
            PQs.append(cv([512], BF16))
            MBs.append(cv([512], BF16))
        endB = off
        off = base
        SQ2 = cv([12, 128]); ON = cv([12, 128], BF16); OGh = cv([NT], BF16)
        TMPX = [cv([512]) for _ in range(2)]
        endC = off
        off = max(endA, endB, endC)
        assert off <= self.scr_words, off
        cnt = {"pre": 0, "sc": 0}
        one_col = CF[:, C_ONE * 128:C_ONE * 128 + 1]
        ones_f, neg_f = cf(C_ONE), cf(C_NEG)
        MdT2 = CF[:, 7 * 128:9 * 128].rearrange("p (d c) -> p d c", d=2)
        SM2 = CF[:, 5 * 128:7 * 128].rearrange("p (d c) -> p d c", d=2)
        bc2 = lambda col2: col2.unsqueeze(2).broadcast_to([128, 2, 128])
        v22 = lambda t: t.rearrange("p (a d c) -> p a d c", a=2, d=2)
        v2 = lambda t: t.rearrange("p (d c) -> p d c", d=2)
        MdT2b = self.consts_bf[:, 7 * 128:9 * 128].rearrange("p (d c) -> p d c", d=2)
        SM2b = self.consts_bf[:, 5 * 128:7 * 128].rearrange("p (d c) -> p d c", d=2)

        def gloads(slot):
            return [(slot[:, 0:256].rearrange("p (k n) -> p k n", k=8), self.dn_wg.rearrange("(k p) n -> p k n", p=128))]

        def gfn(slot):
            wg = slot[:, 0:256].rearrange("p (k n) -> p k n", k=8)
            if self.cfg.get("gcut", 9) < 0:
                return
            if self.cfg.get("gcut", 9) < 1:
                return
            ps = self.bank("x")
            for ti in range(12):
                for kc in range(8):
                    S.matmul(ps[:, ti * 32:(ti + 1) * 32], self.H[:, kc, ti * 128:(ti + 1) * 128], wg[:, kc, :],
                             start=(kc == 0), stop=(kc == 7))
            pv = ps[:, 0:384].rearrange("p (t d k h) -> p t d k h", t=12, d=2, k=2)
            gp = self.gpar[:].rearrange("p (a t n) -> p a t n", a=2, t=12)
            for d in range(2):
                S.tt("dve", T0[:, :, d * 8:(d + 1) * 8], pv[:, :, d, 0, :], gp[:, 1, :, d * 8:(d + 1) * 8], ALU.add)
                S.act(BA[:, :, d * 8:(d + 1) * 8], pv[:, :, d, 1, :], AF.Sigmoid)
            cut = self.cfg.get("gcut", 9)
            if cut < 2:
                return
            S.act(T0, T0, AF.Exp)
            if cut < 3:
                return
            S.act(T0, T0, AF.Ln, bias=one_col)
            if cut < 4:
                return
            S.act(EA, gp[:, 0], AF.Exp)
            S.stt(GA, T0, -1.0, EA, ALU.mult, ALU.mult)
        self.step(gloads, gfn)
        stage = self.cfg.get("gdn_stage", 9)
        nheads = self.cfg.get("gdn_heads", 8)

        def pre2(h, ti, n, b):
            LN0b, LN1b, LN2b = LNs[b]
            PQ = [PQs[b], PQs[b]]
            MB = [MBs[b], MBs[b]]
            T1b = T1[b]
            g2 = GA[:, ti, h:h + 9:8]
            b2 = BA[:, ti, h:h + 9:8]
            kT = QKV[:, 1, ti * 128:(ti + 1) * 128]
            qT = QKV[:, 0, ti * 128:(ti + 1) * 128]
            l4 = lambda t: t.rearrange("p (a c) -> p a c", a=4)
            def mask(lv):
                S.tt("pool", l4(MB[lv % 2]), l4(LN0b), self.mask2[:, lv, 0:128].unsqueeze(1).broadcast_to([128, 4, 128]), ALU.mult)
                return v22(MB[lv % 2])
            S.tt("pool", GM2[b], MdT2, bc2(g2), ALU.mult)
            GMf = GM2[b].rearrange("p d c -> p (d c)")
            pD = self.ps[b]
            S.matmul(pD[:, 0:256], neg_f, GMf, start=True, stop=False)
            for d in range(2):
                S.matmul(pD[:, d * 128:(d + 1) * 128], GM2[b][:, d, :], ones_f, start=False, stop=(d == 1))
            S.matmul(pD[:, 256:512], ones_f, GMf, start=True, stop=False)
            for d in range(2):
                S.matmul(pD[:, 256 + d * 128:256 + (d + 1) * 128], GM2[b][:, d, :], neg_f, start=False, stop=(d == 1))
            yield
            S.act(EM[b], pD[:, 0:512], AF.Relu, scale=-1.0)
            S.act(EM[b], EM[b], AF.Exp, scale=-1.0)
            pG = self.ps[b]
            S.matmul(pG[:, 0:256], ones_f, GMf)
            for d in range(2):
                S.matmul(pG[:, 256 + d:257 + d], GM2[b][:, d, :], ones_f[:, 0:1])
            S.matmul(pG[:, 258:260], ones_f, g2)
            yield
            S.act(ER[b], pG[:, 0:256], AF.Exp)
            S.copy("act", CC[b][:, 0:4], pG[:, 256:260])
            S.act(CCE[:, n, 0:4], CC[b][:, 0:4], AF.Exp)
            for d in range(2):
                S.act(CCE[:, n, 4 + d:5 + d], CC[b][:, d:d + 1], AF.Exp, bias=CC[b][:, 2 + d:3 + d], scale=-1.0)
            S.act(CCE[:, n, 6:8], CCE[:, n, 0:2], AF.Copy, scale=-1.0)
            S.tt("pool", T1b, SM2b, bc2(b2), ALU.mult)
            S.tt("pool", EM[b][:, 0:256].rearrange("p (d c) -> p d c", d=2), EM[b][:, 0:256].rearrange("p (d c) -> p d c", d=2), T1b, ALU.mult)
            S.tt("pool", EM[b][:, 256:512].rearrange("p (d c) -> p d c", d=2), EM[b][:, 256:512].rearrange("p (d c) -> p d c", d=2), MdT2b, ALU.mult)
            pB = self.ps[b]
            S.matmul(pB[:, 0:128], kT, kT)
            S.matmul(pB[:, 128:256], kT, qT)
            yield
            E2, ET2 = v2(EM[b][:, 0:256]), v2(EM[b][:, 256:512])
            LN0 = v22(LN0b)
            S.tt("dve", LN0[:, 0], pB[:, 0:128].unsqueeze(1).broadcast_to([128, 2, 128]), E2, ALU.mult)
            yield
            S.tt("dve", ST_Q[:, n], pB[:, 128:256].unsqueeze(1).broadcast_to([128, 2, 128]), ET2, ALU.mult)
            pT = self.ps[b][:].bitcast(BF16)
            for d in range(2):
                S.transpose(pT[:, d * 128:(d + 1) * 128], LN0[:, 0, d, :], ident_bf)
            S.copy("act", LN0b[:, 256:512], pT[:, 0:256])
            S.tt("pool", ST_QD[:, n], qT.unsqueeze(1).broadcast_to([128, 2, 128]), v2(ER[b]), ALU.mult)
            S.tt("pool", ST_KD[:, n], K_tm[:, ti, :].unsqueeze(1).broadcast_to([128, 2, 128]), bc2(CCE[:, n, 4:6]), ALU.mult)
            yield
            LM0 = mask(0)
            S.tt("pool", l4(PQ[0]), ident_bf.unsqueeze(1).broadcast_to([128, 4, 128]), l4(MB[0]), ALU.subtract)
            def blk(ps, a, d):
                return ps[:, (a * 2 + d) * 128:(a * 2 + d + 1) * 128]
            Lc, Nc = LM0[:, 0], LM0[:, 1]
            cur = 0
            for r in range(2):
                pR = self.ps[b]
                for d in range(2):
                    S.matmul(blk(pR, 0, d), Nc[:, d, :], Lc[:, d, :])
                    S.matmul(blk(pR, 1, d), Lc[:, d, :], Nc[:, d, :])
                LNr_b = LN1b if r == 0 else LN2b
                S.copy("act", LNr_b, pR[:, 0:512])
                if r == 1:
                    LMn = mask(1)
                yield
                LNr = v22(LNr_b)
                Lc, Nc = LNr[:, 0], LNr[:, 1]
                PQc = v22(PQ[cur])
                pP = self.ps[b]
                for d in range(2):
                    S.matmul(blk(pP, 0, d), Nc[:, d, :], PQc[:, 0, d, :])
                    S.matmul(blk(pP, 1, d), Lc[:, d, :], PQc[:, 1, d, :])
                S.tt("dve", PQ[1 - cur], pP[:, 0:512], PQ[cur], ALU.add)
                cur = 1 - cur
                yield
            for lv in range(1, 5):
                LMc = LMn
                PQc = v22(PQ[cur])
                YY = v22(LN1b)
                pY = self.ps[b]
                for d in range(2):
                    S.matmul(blk(pY, 1, d), LMc[:, 0, d, :], PQc[:, 1, d, :])
                    if lv < 4:
                        S.matmul(blk(pY, 0, d), LMc[:, 1, d, :], PQc[:, 0, d, :])
                if lv < 4:
                    S.copy("act", LN1b, pY[:, 0:512])
                    LMn = mask(lv + 1)
                else:
                    S.copy("act", LN1b[:, 256:512], pY[:, 256:512])
                yield
                pU = self.ps[b]
                for d in range(2):
                    S.matmul(blk(pU, 1, d), PQc[:, 0, d, :], YY[:, 1, d, :])
                    if lv < 4:
                        S.matmul(blk(pU, 0, d), PQc[:, 1, d, :], YY[:, 0, d, :])
                if lv < 4:
                    S.tt("dve", PQ[1 - cur], PQ[cur], pU[:, 0:512], ALU.subtract)
                else:
                    S.tt("dve", PQ[1 - cur][:, 256:512], PQ[cur][:, 256:512], pU[:, 256:512], ALU.subtract)
                cur = 1 - cur
                yield
            S.tt("pool", ST_T[:, n], v22(PQ[cur])[:, 1], bc2(b2), ALU.mult)

        def scan_step(ti, n, d, sb_i, first):
            b = sb_i
            kT = QKV[:, 1, ti * 128:(ti + 1) * 128]
            pA = self.bank("x")
            S.matmul(pA[:, 0:128], kT, Sb[sb_i])
            S.stt(RP[b], pA[:, 0:128], CCE[:, n, 6 + d:7 + d], V_tm[:, ti, :], ALU.mult, ALU.add)
            yield
            pB = self.bank("x")
            S.matmul(pB[:, 0:128], ST_T[:, n, d, :], RP[b])
            S.copy("act", VN[b], pB[:, 0:128])
            yield
            pC = self.bank("x")
            S.matmul(pC[:, 0:128], ST_QD[:, n, d, :], Sb[sb_i], start=True, stop=False)
            S.matmul(pC[:, 0:128], ST_Q[:, n, d, :], VN[b], start=False, stop=True)
            S.tt("dve", O_tm[:, ti, :], O_tm[:, ti, :], pC[:, 0:128], ALU.add)
            pE = self.bank("x")
            S.matmul(pE[:, 0:128], ST_KD[:, n, d, :], VN[b])
            S.stt(Sf[sb_i], Sf[sb_i], CCE[:, n, 2 + d:3 + d], pE[:, 0:128], ALU.mult, ALU.add)
            S.copy("act", Sb[sb_i], Sf[sb_i])
            yield

        for h in range(nheads if stage >= 2 else 0):
            def loadsA(slot, h=h):
                return [(slot[:, 0:4096].rearrange("p (k n) -> p k n", k=8), self.dn_wh[h].rearrange("(k p) n -> p k n", p=128))]

            def fnA(slot, h=h):
                wv = slot[:, 0:4096].rearrange("p (k n) -> p k n", k=8)
                S.memset("dve", CP, 0.0)
                S.tt("dve", DG5.rearrange("p (j t) c -> p j t c", j=3),
                     ident_bf.unsqueeze(1).unsqueeze(1).broadcast_to([128, 3, 5, 128]),
                     self.dcwT[:].rearrange("p (j h t) -> p j h t", j=3, h=8)[:, :, h, :].unsqueeze(3).broadcast_to([128, 3, 5, 128]),
                     ALU.mult)
                cpP = lambda j: CP[:, j, 0:520].rearrange("p (s t) -> p s t", s=2)
                for j in range(4):
                    for tt, (t0, t1) in enumerate(TILES):
                        ps = self.bank("x")
                        for kc in range(8):
                            S.matmul(ps[:], wv[:, kc, j * 128:(j + 1) * 128], self.H[:, kc, t0:t1], start=(kc == 0), stop=(kc == 7))
                        if j == 3:
                            S.act(Z[:, t0:t1], ps[:], AF.Silu)
                        elif tt == 0:
                            S.copy("act", cpP(j)[:, :, 2:258], ps[:].rearrange("p (s t) -> p s t", s=2))
                        else:
                            S.copy("act", CP[:, j, 522 + (tt - 1) * 512:522 + tt * 512], ps[:])
                for j in range(3):
                    for tt, (t0, t1) in enumerate(TILES):
                        ps = self.bank("x")
                        for tap in range(5):
                            if tt == 0:
                                S.matmul(ps[:].rearrange("p (s t) -> p s t", s=2), DG5[:, j * 5 + tap, :], cpP(j)[:, :, tap:tap + 256],
                                         start=(tap == 0), stop=(tap == 4))
                            else:
                                st = 520 + (tt - 1) * 512 + tap
                                S.matmul(ps[:], DG5[:, j * 5 + tap, :], CP[:, j, st:st + 512], start=(tap == 0), stop=(tap == 4))
                        S.act(QKV[:, j, t0:t1], ps[:], AF.Silu)
                for j in range(2):
                    for tt, (t0, t1) in enumerate(TILES):
                        S.act(SQ, QKV[:, j, t0:t1], AF.Square)
                        ps = self.bank("x")
                        S.matmul(ps[:], self.ones_bf[:, 0:128], SQ)
                        if j == 0:
                            S.act(RS, ps[:], AF.Sqrt, bias=self.eps_col[:, 1:2], scale=128.0)
                        else:
                            S.act(RS, ps[:], AF.Sqrt, bias=self.eps_col[:, 0:1], scale=1.0)
                        S.recip(RS, RS)
                        S.tt("dve", QKV[:, j, t0:t1], QKV[:, j, t0:t1], RS, ALU.mult)
                for (j, dst) in ((1, K_tm), (2, V_tm)):
                    for g4 in range(3):
                        pb = self.bank("x")[:].bitcast(BF16)
                        for i in range(4):
                            ti = g4 * 4 + i
                            S.transpose(pb[:, i * 128:(i + 1) * 128], QKV[:, j, ti * 128:(ti + 1) * 128], ident_bf)
                        S.copy("act", dst[:, g4 * 4:(g4 + 1) * 4, :], pb[:, 0:512].rearrange("p (a b) -> p a b", a=4))
            self.step(loadsA, fnA)

            def loadsB(slot, h=h):
                return [(slot[:, 0:1024], self.dn_wo[h * 128:(h + 1) * 128, :])]

            def fnB(slot, h=h):
                wo = slot[:, 0:1024]
                seqs = [([0, 1], 0), ([2, 3], 1), (list(range(4, 12)), 2)]
                sbi = 0
                def interleave(gens):
                    gens = list(gens)
                    while gens:
                        for g in list(gens):
                            try:
                                next(g)
                            except StopIteration:
                                gens.remove(g)

                def chainx(tiles, sidx, d, n0, sb_i):
                    order = tiles if d == 0 else tiles[::-1]
                    if sidx == 2:
                        S.dma("sp", Sf[sb_i], self.sd[d, h])
                        S.copy("act", Sb[sb_i], Sf[sb_i])
                    else:
                        S.memset("dve", Sf[sb_i], 0.0)
                        S.memset("dve", Sb[sb_i], 0.0)
                    for ti in order:
                        yield from scan_step(ti, n0 + tiles.index(ti), d, sb_i, first=False)
                    if sidx < 2:
                        S.dma("sp", self.nsd[sidx, d, h], Sf[sb_i])

                S.memset("dve", O_tm, 0.0)
                if stage >= 3:
                    for grp in ([0, 1, 2, 3],):
                        interleave([pre2(h, ti, ti, k) for k, ti in enumerate(grp)])
                    if stage >= 4:
                        interleave([chainx([0, 1], 0, d, 0, 2 * 0 + d) for d in range(2)] +
                                   [chainx([2, 3], 1, d, 2, 2 * 1 + d) for d in range(2)])
                    for grp in ([4, 5, 6, 7, 8], [9, 10, 11]):
                        interleave([pre2(h, ti, ti - 4, k) for k, ti in enumerate(grp)])
                    if stage >= 4:
                        interleave([chainx(list(range(4, 12)), 2, d, 0, d) for d in range(2)])
                if stage < 5:
                    return
                S.act(SQ2, O_tm, AF.Square)
                S.op("dve", lambda e: e.reduce_sum(SSQ, SQ2, mybir.AxisListType.X), [SQ2], [SSQ])
                S.act(SSQ, SSQ, AF.Sqrt, bias=self.eps_col[:, 0:1], scale=1.0 / 128.0)
                S.recip(SSQ, SSQ)
                S.tt("dve", ON, O_tm, SSQ.unsqueeze(2).broadcast_to([128, 12, 128]), ALU.mult)
                for g4 in range(3):
                    pb = self.bank("x")[:].bitcast(BF16)
                    for i in range(4):
                        ti = g4 * 4 + i
                        S.transpose(pb[:, i * 128:(i + 1) * 128], ON[:, ti, :], ident_bf)
                    S.stt(OGh[:, g4 * 512:(g4 + 1) * 512], pb[:, 0:512], self.dn_normT[:, 0:1], Z[:, g4 * 512:(g4 + 1) * 512],
                          ALU.mult, ALU.mult)
                g1 = self.mod(l, 2)
                for oc in range(8):
                    for tt, (t0, t1) in enumerate(TILES):
                        ps = self.bank("x")
                        S.matmul(ps[:], wo[:, oc * 128:(oc + 1) * 128], OGh[:, t0:t1])
                        cd = COND[tt]
                        if (oc * 3 + tt) % 2 == 0:
                            S.stt(self.X[:, oc, t0:t1], ps[:], g1[:, oc, cd:cd + 1], self.X[:, oc, t0:t1], ALU.mult, ALU.add)
                        else:
                            tx = TMPX[((oc * 3 + tt) // 2) % 2]
                            S.act(tx, ps[:], AF.Identity, scale=g1[:, oc, cd:cd + 1])
                            S.tt("pool", self.X[:, oc, t0:t1], self.X[:, oc, t0:t1], tx, ALU.add)
            self.step(loadsB, fnB)

    def mlstm(self, l):
        S = self.S
        CF = self.consts
        cf = lambda k: CF[:, k * 128:(k + 1) * 128]
        ident_bf = self.ident_bf
        self.step(None, lambda slot: self.norm_mod(self.AMOD[:, l, 0], self.mod(l, 0)))
        off = 0
        def cv(shape, dt=F32):
            nonlocal off
            v, w = self.carve(off, shape, dt)
            off += w
            return v
        LI = cv([12, 16]); LF = cv([12, 16]); T0 = cv([12, 16])
        LIr = cv([512]); LFr = cv([512]); SCN = cv([2, 2, 256]); NBF = cv([2])
        MFB = cv([2, 2]); EMF = cv([2, 2]); DGE = cv([2, 2, 16]); EMB = cv([2, 2, 16]); EM0 = cv([16])
        qTs = [cv([NT], BF16) for _ in range(2)]; kTs = [cv([NT], BF16) for _ in range(2)]
        vTs = [cv([NT], BF16) for _ in range(2)]; OGts = [cv([NT], BF16) for _ in range(2)]
        V_tms = [cv([12, 129], BF16) for _ in range(2)]; K_tms = [cv([12, 64], BF16) for _ in range(2)]
        Hs = cv([12, 128])
        off_st = off
        ST_S = cv([8, 2, 128], BF16); ST_QB = cv([8, 2, 128], BF16); ST_KW = cv([8, 2, 64], BF16); CCE = cv([8, 4])
        SQ2, _ = self.carve(off_st, [12, 128]); SSQ = cv([12])
        off_tmp = off
        ON, _w = self.carve(off_tmp, [12, 128], BF16); OGh, _w2 = self.carve(off_tmp + 768, [NT], BF16)
        TMPX = [self.carve(off_tmp + 1536 + 512 * i, [512])[0] for i in range(2)]
        NBm = 4
        FM2 = [cv([2, 128]) for _ in range(NBm)]
        EMn = [cv([256]) for _ in range(NBm)]
        ER = [cv([256], BF16) for _ in range(NBm)]; CC = [cv([8]) for _ in range(NBm)]
        MdT2 = CF[:, 7 * 128:9 * 128].rearrange("p (d c) -> p d c", d=2)
        bc2 = lambda col2: col2.unsqueeze(2).broadcast_to([128, 2, 128])
        v2 = lambda t: t.rearrange("p (d c) -> p d c", d=2)
        CA = [cv([129]) for _ in range(4)]; CAb = [cv([130], BF16) for _ in range(4)]; CAo = [cv([129]) for _ in range(4)]
        DN = [cv([2]) for _ in range(4)]
        assert off <= self.scr_words, off
        cnt = {"pre": 0, "sc": 0}
        one_col = CF[:, C_ONE * 128:C_ONE * 128 + 1]
        ones_f, neg_f = cf(C_ONE), cf(C_NEG)
        mp = self.mpar[:].rearrange("p (a t n) -> p a t n", a=2, t=12)

        def gloads(slot):
            return [(slot[:, 0:256].rearrange("p (k n) -> p k n", k=8), self.ml_wg.rearrange("(k p) n -> p k n", p=128)),
                    (slot[:, 256:512].rearrange("p (k n) -> p k n", k=8), self.ml_wgr.rearrange("(k p) n -> p k n", p=128))]

        def gfn(slot):
            wg = slot[:, 0:256].rearrange("p (k n) -> p k n", k=8)
            wr = slot[:, 256:512].rearrange("p (k n) -> p k n", k=8)
            for vv in V_tms:
                S.memset("dve", vv[:, :, 128:129], 1.0)
            ps = self.bank("x")
            for ti in range(12):
                for kc in range(8):
                    S.matmul(ps[:, ti * 32:(ti + 1) * 32], self.H[:, kc, ti * 128:(ti + 1) * 128], wg[:, kc, :],
                             start=(kc == 0), stop=(kc == 7))
            pv = ps[:, 0:384].rearrange("p (t d k h) -> p t d k h", t=12, d=2, k=2)
            for d in range(2):
                S.tt("dve", LI[:, :, d * 8:(d + 1) * 8], pv[:, :, d, 0, :], mp[:, 0, :, d * 8:(d + 1) * 8], ALU.add)
                S.tt("dve", T0[:, :, d * 8:(d + 1) * 8], pv[:, :, d, 1, :], mp[:, 1, :, d * 8:(d + 1) * 8], ALU.add)
            S.act(T0, T0, AF.Exp, scale=-1.0)
            S.act(T0, T0, AF.Ln, bias=one_col)
            S.ts("dve", LF, T0, -1.0)
            pr = self.bank("x")
            for kc in range(8):
                S.matmul(pr[0:16, 0:512], wr[:, kc, 0:16], self.H[:, kc, 0:512], start=(kc == 0), stop=(kc == 7))
            S.act(LIr[0:16, :], pr[0:16, 0:512], AF.Identity, bias=self.mparT[0:16, 0:1])
            pr2 = self.bank("x")
            for kc in range(8):
                S.matmul(pr2[0:16, 0:512], wr[:, kc, 16:32], self.H[:, kc, 0:512], start=(kc == 0), stop=(kc == 7))
            S.ts("dve", NBF[0:16, 0:1], self.mparT[0:16, 1:2], -1.0)
            S.act(LFr[0:16, :], pr2[0:16, 0:512], AF.Exp, bias=NBF[0:16, 0:1], scale=-1.0)
            S.act(LFr[0:16, :], LFr[0:16, :], AF.Ln, bias=one_col[0:16, :])
            S.ts("dve", LFr[0:16, :], LFr[0:16, :], -1.0)
            for s in range(2):
                for fb in range(2):
                    if fb == 0:
                        d0, d1 = LFr[0:16, s * 256:(s + 1) * 256], LIr[0:16, s * 256:(s + 1) * 256]
                    elif s == 0:
                        d0, d1 = LFr[0:16, 255::-1], LIr[0:16, 255::-1]
                    else:
                        d0, d1 = LFr[0:16, 511:255:-1], LIr[0:16, 511:255:-1]
                    o = SCN[0:16, s, fb, :]
                    S.op("dve", lambda e, o=o, d0=d0, d1=d1: e.tensor_tensor_scan(o, d0, d1, 0.0, ALU.add, ALU.max), [d0, d1], [o])
                    S.copy("dve", MFB[0:16, s, fb:fb + 1], SCN[0:16, s, fb, 255:256])
                S.dma("sp", self.nsm[s, 0, :], MFB[0:8, s, 0:1])
                S.dma("sp", self.nsm[s, 1, :], MFB[8:16, s, 1:2])
            S.act(EMF[0:16], MFB[0:16], AF.Exp, scale=-1.0)
            pe = self.bank("x")
            for s in range(2):
                for fb in range(2):
                    S.ts("dve", DGE[0:16, s, fb, :], CF[0:16, 0:16], EMF[0:16, s, fb:fb + 1])
                    c0 = (s * 2 + fb) * 16
                    S.matmul(pe[0:64, c0:c0 + 16], ones_f[0:16, 0:64], DGE[0:16, s, fb, :])
            S.copy("dve", EMB[0:64].rearrange("p a b c -> p (a b c)"), pe[0:64, 0:64])
            S.act(EM0[0:64], self.smm[0:64, :], AF.Exp)
        self.step(gloads, gfn)

        cur = {}

        def pre2(h, ti, n, b):
            qT, kT, K_tm = cur["qT"], cur["kT"], cur["K_tm"]
            lf2 = LF[:, ti, h:h + 9:8]
            li2 = LI[:, ti, h:h + 9:8]
            S.tt("pool", FM2[b], MdT2, bc2(lf2), ALU.mult)
            FMf = FM2[b].rearrange("p d c -> p (d c)")
            pD = self.ps[b]
            S.matmul(pD[:, 0:256], ones_f, FMf, start=True, stop=False)
            for d in range(2):
                S.matmul(pD[:, d * 128:(d + 1) * 128], FM2[b][:, d, :], neg_f, start=False, stop=(d == 1))
            S.matmul(pD[:, 256:512], ones_f, FMf)
            yield
            S.act(EMn[b], pD[:, 0:256], AF.Relu, scale=-1.0)
            for d in range(2):
                S.act(EMn[b][:, d * 128:(d + 1) * 128], EMn[b][:, d * 128:(d + 1) * 128], AF.Exp, bias=li2[:, d:d + 1], scale=-1.0)
            S.tt("pool", v2(EMn[b]), v2(EMn[b]), MdT2, ALU.mult)
            S.act(ER[b][0:64, :], pD[0:64, 256:512], AF.Exp)
            pG = self.ps[b]
            for d in range(2):
                S.matmul(pG[:, d:d + 1], FM2[b][:, d, :], ones_f[:, 0:1])
            S.matmul(pG[:, 2:4], ones_f, lf2)
            yield
            S.copy("act", CC[b][:, 0:4], pG[:, 0:4])
            S.tt("pool", CC[b][:, 4:6], CC[b][:, 2:4], li2, ALU.add)
            S.act(CCE[:, n, 0:2], CC[b][:, 2:4], AF.Exp)
            for d in range(2):
                S.act(CCE[:, n, 2 + d:3 + d], CC[b][:, d:d + 1], AF.Exp, bias=CC[b][:, 4 + d:5 + d], scale=-1.0)
            yield
            pB = self.ps[b]
            S.matmul(pB[:, 0:128], kT[0:64, ti * 128:(ti + 1) * 128], qT[0:64, ti * 128:(ti + 1) * 128])
            S.tt("dve", ST_S[:, n], pB[:, 0:128].unsqueeze(1).broadcast_to([128, 2, 128]), v2(EMn[b]), ALU.mult)
            S.tt("pool", ST_QB[0:64, n], qT[0:64, ti * 128:(ti + 1) * 128].unsqueeze(1).broadcast_to([64, 2, 128]),
                 v2(ER[b])[0:64], ALU.mult)
            S.tt("pool", ST_KW[:, n], K_tm[:, ti, :].unsqueeze(1).broadcast_to([128, 2, 64]),
                 CCE[:, n, 2:4].unsqueeze(2).broadcast_to([128, 2, 64]), ALU.mult)
            yield

        def scan_step(ti, n, d, ci, first):
            b = ci
            V_tm = cur["V_tm"]
            pN = self.bank("sc", [0, 1, 2, 3, 4, 5])
            S.matmul(pN[:, 0:129], ST_QB[0:64, n, d, :], CAb[ci][0:64, 0:129], start=True, stop=False)
            S.matmul(pN[:, 0:129], ST_S[:, n, d, :], V_tm[:, ti, :], start=False, stop=True)
            S.act(DN[b][:, 0:1], pN[:, 128:129], AF.Abs)
            S.ts("dve", DN[b][:, 0:1], DN[b][:, 0:1], 1.0, None, ALU.max)
            S.recip(DN[b][:, 0:1], DN[b][:, 0:1])
            S.stt(Hs[:, ti, :], pN[:, 0:128], DN[b][:, 0:1], Hs[:, ti, :], ALU.mult, ALU.add)
            yield
            pS = self.bank("sc", [0, 1, 2, 3, 4, 5])
            S.matmul(pS[0:64, 0:129], ST_KW[:, n, d, :], V_tm[:, ti, :])
            S.stt(CA[ci][0:64, :], CA[ci][0:64, :], CCE[0:64, n, d:d + 1], pS[0:64, 0:129], ALU.mult, ALU.add)
            S.copy("act", CAb[ci][0:64, 0:129], CA[ci][0:64, :])
            yield

        nheads = self.cfg.get("ml_heads", 8)

        def interleave_g(gens):
            gens = list(gens)
            while gens:
                for g in list(gens):
                    try:
                        next(g)
                    except StopIteration:
                        gens.remove(g)
                yield

        def genA(h, wv):
            p = h % 2
            qT, kT, vT, OGt, V_tm, K_tm = qTs[p], kTs[p], vTs[p], OGts[p], V_tms[p], K_tms[p]
            for j in range(4):
                lo, hi = [(0, 64), (64, 128), (128, 256), (256, 384)][j]
                M = hi - lo
                for tt, (t0, t1) in enumerate(TILES):
                    ps = self.bank("fa", [6, 7])
                    for kc in range(8):
                        S.matmul(ps[0:M, :], wv[:, kc, lo:hi], self.H[:, kc, t0:t1], start=(kc == 0), stop=(kc == 7))
                    if j == 0:
                        S.act(qT[0:64, t0:t1], ps[0:64, :], AF.Copy, scale=0.125)
                    elif j == 1:
                        S.copy("act", kT[0:64, t0:t1], ps[0:64, :])
                    elif j == 2:
                        S.copy("act", vT[:, t0:t1], ps[:])
                    else:
                        S.act(OGt[:, t0:t1], ps[:], AF.Sigmoid)
                    yield
            for g4 in range(3):
                pb = self.bank("fa", [6, 7])[:].bitcast(BF16)
                for i in range(4):
                    ti = g4 * 4 + i
                    S.transpose(pb[:, i * 128:(i + 1) * 128], vT[:, ti * 128:(ti + 1) * 128], ident_bf)
                S.copy("act", V_tm[:, g4 * 4:(g4 + 1) * 4, 0:128], pb[:, 0:512].rearrange("p (a b) -> p a b", a=4))
                yield
            for g4 in range(3):
                pb = self.bank("fa", [6, 7])[:].bitcast(BF16)
                for i in range(4):
                    ti = g4 * 4 + i
                    S.transpose(pb[:, i * 64:(i + 1) * 64], kT[0:64, ti * 128:(ti + 1) * 128], ident_bf[0:64, 0:64])
                S.copy("act", K_tm[:, g4 * 4:(g4 + 1) * 4, :], pb[:, 0:256].rearrange("p (a b) -> p a b", a=4))
                yield

        def genB(h, wo):
            p = h % 2
            cur.update(qT=qTs[p], kT=kTs[p], K_tm=K_tms[p], V_tm=V_tms[p])
            OGt = OGts[p]

            def chainx(tiles, sidx, d, n0, ci):
                order = tiles if d == 0 else tiles[::-1]
                if sidx == 2:
                    S.dma("sp", CA[ci][0:64, :], self.smca[d, h])
                    S.ts("dve", CA[ci][0:64, :], CA[ci][0:64, :], EM0[0:64, d * 8 + h:d * 8 + h + 1])
                    S.copy("act", CAb[ci][0:64, 0:129], CA[ci][0:64, :])
                else:
                    S.memset("dve", CA[ci][0:64, :], 0.0)
                    S.memset("dve", CAb[ci][0:64, :], 0.0)
                for ti in order:
                    yield from scan_step(ti, n0 + tiles.index(ti), d, ci, first=False)
                if sidx < 2:
                    S.ts("dve", CAo[ci][0:64, :], CA[ci][0:64, :], EMB[0:64, sidx, d, d * 8 + h:d * 8 + h + 1])
                    S.dma("sp", self.nsc[sidx, d, h], CAo[ci][0:64, 0:128])
                    S.dma("sp", self.nsn[sidx, d, h, :], CAo[ci][0:64, 128:129])

            S.memset("dve", Hs, 0.0)
            for grp in ([0, 1, 2, 3],):
                yield from interleave_g([pre2(h, ti, ti, k) for k, ti in enumerate(grp)])
            yield from interleave_g([chainx([0, 1], 0, d, 0, d) for d in range(2)] + [chainx([2, 3], 1, d, 2, 2 + d) for d in range(2)])
            for grp in ([4, 5, 6, 7], [8, 9, 10, 11]):
                yield from interleave_g([pre2(h, ti, ti - 4, k) for k, ti in enumerate(grp)])
            yield from interleave_g([chainx(list(range(4, 12)), 2, d, 0, d) for d in range(2)])
            S.act(SQ2, Hs, AF.Square)
            S.op("dve", lambda e: e.reduce_sum(SSQ, SQ2, mybir.AxisListType.X), [SQ2], [SSQ])
            S.act(SSQ, SSQ, AF.Sqrt, bias=self.eps_col[:, 0:1], scale=1.0 / 128.0)
            S.recip(SSQ, SSQ)
            yield
            S.tt("dve", ON, Hs, SSQ.unsqueeze(2).broadcast_to([128, 12, 128]), ALU.mult)
            yield
            for g4 in range(3):
                pb = self.bank("sc", [0, 1, 2, 3, 4, 5])[:].bitcast(BF16)
                for i in range(4):
                    ti = g4 * 4 + i
                    S.transpose(pb[:, i * 128:(i + 1) * 128], ON[:, ti, :], ident_bf)
                S.stt(OGh[:, g4 * 512:(g4 + 1) * 512], pb[:, 0:512], self.ml_normT[:, 0:1], OGt[:, g4 * 512:(g4 + 1) * 512],
                      ALU.mult, ALU.mult)
                yield
            g1 = self.mod(l, 2)
            for oc in range(8):
                for tt, (t0, t1) in enumerate(TILES):
                    ps = self.bank("sc", [0, 1, 2, 3, 4, 5])
                    S.matmul(ps[:], wo[:, oc * 128:(oc + 1) * 128], OGh[:, t0:t1])
                    cd = COND[tt]
                    if (oc * 3 + tt) % 2 == 0:
                        S.stt(self.X[:, oc, t0:t1], ps[:], g1[:, oc, cd:cd + 1], self.X[:, oc, t0:t1], ALU.mult, ALU.add)
                    else:
                        tx = TMPX[((oc * 3 + tt) // 2) % 2]
                        S.act(tx, ps[:], AF.Identity, scale=g1[:, oc, cd:cd + 1])
                        S.tt("pool", self.X[:, oc, t0:t1], self.X[:, oc, t0:t1], tx, ALU.add)
                yield

        def drain(g):
            for _ in g:
                pass

        def loads0(slot):
            return [(slot[:, 0:3072].rearrange("p (k n) -> p k n", k=8), self.ml_wh[0].rearrange("(k p) n -> p k n", p=128))]
        self.step(loads0, lambda slot: drain(genA(0, slot[:, 0:3072].rearrange("p (k n) -> p k n", k=8))))
        for h in range(nheads):
            def loadsH(slot, h=h):
                out = [(slot[:, 3072:4096], self.ml_wo[h * 128:(h + 1) * 128, :])]
                if h + 1 < nheads:
                    out.append((slot[:, 0:3072].rearrange("p (k n) -> p k n", k=8), self.ml_wh[h + 1].rearrange("(k p) n -> p k n", p=128)))
                return out

            def fnH(slot, h=h):
                gens = [genB(h, slot[:, 3072:4096])]
                if h + 1 < nheads:
                    gens.append(genA(h + 1, slot[:, 0:3072].rearrange("p (k n) -> p k n", k=8)))
                drain(interleave_g(gens))
            self.step(loadsH, fnH)

    def build(self):
        cfg = self.cfg
        nc = self.nc
        S = self.S
        self.xT = self.dram_in("xT", [D, NT])
        self.condT = self.dram_in("condT", [128, 16])
        self.w_ada = self.dram_in("w_ada", [4, D, 6 * D])
        b_adaT_d = self.dram_in("b_adaT", [128, 4 * 48])
        nmT_d = self.dram_in("nmT", [128, 72])
        self.w_up = self.dram_in("w_up", [4, D, 2 * DFF])
        cwT_d = self.dram_in("cwT", [128, 4 * NCH * 9])
        cbT_d = self.dram_in("cbT", [128, 4 * NCH])
        self.w_down = self.dram_in("w_down", [4, DFF, D])
        self.fnet_w = self.dram_in("fnet_w", [2, D, D])
        self.fnet_b_d = self.dram_in("fnet_b", [1, 2 * D])
        consts_d = self.dram_in("consts", [128, 1152])
        cs3_d = self.dram_in("cs3", [256, 768])
        self.tab = self.dram_in("tab", [4, 128, 2, 8, 256])
        self.dn_wh = self.dram_in("dn_wh", [8, D, 512])
        self.dn_wg = self.dram_in("dn_wg", [D, 32])
        dcwT_d = self.dram_in("dcwT", [128, 120])
        mask2_d = self.dram_in("mask2", [128, 1280])
        gpar_d = self.dram_in("gpar", [128, 2 * 12 * 16])
        dn_normT_d = self.dram_in("dn_normT", [128, 1])
        self.dn_wo = self.dram_in("dn_wo", [D, D])
        self.sd = self.dram_in("sd", [2, 8, 128, 128])
        self.nsd = self.dram_out("nsd", [2, 2, 8, 128, 128])
        self.ml_wh = self.dram_in("ml_wh", [8, D, 384])
        self.ml_wg = self.dram_in("ml_wg", [D, 32])
        self.ml_wgr = self.dram_in("ml_wgr", [D, 32])
        mpar_d = self.dram_in("mpar", [128, 2 * 12 * 16])
        mparT_d = self.dram_in("mparT", [16, 2])
        ml_normT_d = self.dram_in("ml_normT", [128, 1])
        self.ml_wo = self.dram_in("ml_wo", [D, D])
        self.smca = self.dram_in("smca", [2, 8, 64, 129])
        smm_d = self.dram_in("smm", [64, 16])
        self.nsc = self.dram_out("nsc", [2, 2, 8, 64, 128])
        self.nsn = self.dram_out("nsn", [2, 2, 8, 64])
        self.nsm = self.dram_out("nsm", [2, 2, 8])
        self.yT = self.dram_out("yT", [D, NT])
        ntaps = cfg.get("ntaps", 0)
        self.dbg = self.dram_out("dbg", [ntaps, D, NT]) if ntaps else None

        self.X = self.sb("X", [128, 8, NT])
        self.H = self.sb("H", [128, 8, NT], BF16)
        self.slots = [self.sb("slot%d" % i, [128, 4096], BF16) for i in range(4)]
        self.scr_words = cfg.get("scr_words", 19712)
        self.scr = self.sb("scr", [128, self.scr_words])
        self.scr_tmp = self.scr_words - 3584
        self.scr_main = 0
        self.consts = self.sb("consts_f", [128, 1152])
        self.consts_bf = self.sb("consts_b", [128, 1152], BF16)
        self.ones_bf = self.sb("ones_bf", [128, 512], BF16)
        self.CS3 = self.sb("CS3", [128, 2, 768], BF16)
        self.fb_row = self.sb("fb_row", [1, D], BF16)
        self.b_adaT = self.sb("b_adaT_s", [128, 4 * 48])
        self.nmT = self.sb("nmT_s", [128, 72])
        self.cwT = self.sb("cwT_s", [128, 4 * NCH * 9])
        self.cbT = self.sb("cbT_s", [128, 4 * NCH])
        self.condS = self.sb("condS", [128, 16])
        self.SC = self.sb("SC", [128, 8, 2], BF16)
        self.MOD = self.sb("MOD", [128, 4, 48, 2])
        self.AMOD = self.sb("AMOD", [128, 4, 2, 8, 2])
        self.eps_col = self.sb("eps_col", [128, 2])
        self.mpar = self.sb("mpar_s", [128, 2 * 12 * 16])
        self.mparT = self.sb("mparT_s", [16, 2])
        self.ml_normT = self.sb("ml_normT_s", [128, 1])
        self.smm = self.sb("smm_s", [64, 16])
        self.dcwT = self.sb("dcwT_s", [128, 120])
        self.mask2 = self.sb("mask2_s", [128, 5, 256], BF16)
        self.gpar = self.sb("gpar_s", [128, 2 * 12 * 16])
        self.dn_normT = self.sb("dn_normT_s", [128, 1])
        self.ps = [self.es.enter_context(nc.psum_tensor("ps%d" % i, [128, 512], F32)) for i in range(8)]
        self.ident_bf = self.consts_bf[:, C_ID * 128:(C_ID + 1) * 128]

        S.dma("sp", self.X[:], self.xT.rearrange("(c p) t -> p c t", p=128))
        S.dma("sp", self.condS[:], self.condT)
        S.dma("sp", self.consts[:], consts_d)
        S.dma("pool", self.consts_bf[:], consts_d)
        S.dma("pool", self.CS3[:], cs3_d.rearrange("(j p) n -> p j n", p=128))
        S.dma("sp", self.b_adaT[:], b_adaT_d)
        S.dma("sp", self.nmT[:], nmT_d)
        S.dma("sp", self.cwT[:], cwT_d)
        S.dma("sp", self.cbT[:], cbT_d)
        S.memset("dve", self.ones_bf[:], 1.0)
        S.memset("dve", self.eps_col[:, 0:1], EPS)
        S.memset("dve", self.eps_col[:, 1:2], 128.0 * EPS)
        S.dma("sp", self.mpar[:], mpar_d)
        S.dma("sp", self.mparT[:], mparT_d)
        S.dma("sp", self.ml_normT[:], ml_normT_d)
        S.dma("sp", self.smm[:], smm_d)
        S.dma("sp", self.dcwT[:], dcwT_d)
        S.dma("pool", self.mask2[:].rearrange("p a b -> p (a b)"), mask2_d)
        S.dma("sp", self.gpar[:], gpar_d)
        S.dma("sp", self.dn_normT[:], dn_normT_d)
        S.act(self.SC[:].rearrange("p k c -> p (k c)"), self.condS[:], AF.Silu)

        layers = cfg.get("layers", [0, 1, 2, 3])

        def collect(fn):
            keep = self.steps
            self.steps = []
            fn()
            out = self.steps
            self.steps = keep
            return out

        def merge(a, b):
            out = []
            ia = ib = 0
            while ia < len(a) or ib < len(b):
                if ia < len(a):
                    out.append(a[ia]); ia += 1
                want = (ia * len(b)) // max(1, len(a)) if ia < len(a) else len(b)
                while ib < want:
                    out.append(b[ib]); ib += 1
            return out

        k = 0
        ada0 = collect(lambda: self.adaln(layers[0]))
        self.steps += ada0[:5]
        pending = ada0[5:]
        for li, l in enumerate(layers):
            kind = l % 3
            mix = []
            if kind == 0 and cfg.get("fnet", True):
                mix = collect(lambda: self.fnet(l, l // 3))
            elif kind == 1 and cfg.get("gdn", True):
                mix = collect(lambda: self.gdn(l))
            elif kind == 2 and cfg.get("mlstm", True):
                mix = collect(lambda: self.mlstm(l))
            extra = pending
            if li + 1 < len(layers):
                extra = extra + collect(lambda: self.adaln(layers[li + 1]))
            pending = []
            self.steps += merge(mix, extra)
            self.step(None, lambda slot, k=k: self.tap(k))
            k += 1
            if cfg.get("ffn", True):
                self.ffn(l)
            self.step(None, lambda slot, k=k: self.tap(k))
            k += 1
        self.run_steps()

        Y, w = self.carve(0, [8, NT])
        self.norm_mod(self.nmT[:, 64:72], None, out_y=Y)
        S.dma("sp", self.yT.rearrange("(c p) t -> p c t", p=128), Y)
        S.wait_all("sp")
        S.emit()
        self.es.close()
        return nc


def _prep(inputs):
    consts, cs3, tab, mask2 = _const_tables()
    f = lambda k: np.ascontiguousarray(np.asarray(inputs[k], np.float32))
    shared = {
        "w_ada": f("w_ada"),
        "b_adaT": _fm(f("b_ada")).reshape(128, 4 * 48),
        "nmT": np.concatenate([_fm(f("norm_mix")).reshape(128, 32), _fm(f("norm_ffn")).reshape(128, 32),
                               _fm(f("norm_final")).reshape(128, 8)], axis=1),
        "w_up": f("ffn_w_up"),
        "cwT": np.ascontiguousarray(np.moveaxis(_fm(f("ffn_conv_w").reshape(4, 9, DFF)), 2, 3)).reshape(128, 4 * NCH * 9),
        "cbT": _fm(f("ffn_conv_b")).reshape(128, 4 * NCH),
        "w_down": f("ffn_w_down"),
        "fnet_w": f("fnet_w"),
        "fnet_b": f("fnet_b").reshape(1, 2 * D),
        "consts": consts, "cs3": cs3, "tab": tab, "mask2": mask2,
    }
    wi = f("dn_w_in")[0]
    shared["dn_wh"] = np.ascontiguousarray(np.stack(
        [np.concatenate([wi[:, j * 1024 + h * 128:j * 1024 + (h + 1) * 128] for j in range(4)], axis=1) for h in range(8)]))
    shared["dn_wg"] = np.ascontiguousarray(wi[:, 4096:4128])
    shared["dcwT"] = np.ascontiguousarray(np.moveaxis(_fm(f("dn_conv_w")[0]), 1, 2)).reshape(128, 120)
    gp = np.stack([f("dn_a_log")[0].reshape(16), f("dn_dt_bias")[0].reshape(16)])
    shared["gpar"] = np.ascontiguousarray(np.broadcast_to(gp[None, :, None, :], (128, 2, 12, 16))).reshape(128, 384)
    shared["dn_normT"] = np.ascontiguousarray(f("dn_norm")[0].reshape(128, 1))
    shared["dn_wo"] = f("dn_w_out")[0]
    sdel = f("state_delta")
    mw = f("ml_w_in")[0]
    shared["ml_wh"] = np.ascontiguousarray(np.stack(
        [np.concatenate([mw[:, h * 64:(h + 1) * 64], mw[:, 512 + h * 64:512 + (h + 1) * 64],
                         mw[:, 1024 + h * 128:1024 + (h + 1) * 128], mw[:, 2048 + h * 128:2048 + (h + 1) * 128]], axis=1)
         for h in range(8)]))
    mg = mw[:, 3072:3104]
    shared["ml_wg"] = np.ascontiguousarray(mg)
    mg4 = mg.reshape(1024, 2, 2, 8)
    shared["ml_wgr"] = np.ascontiguousarray(np.concatenate([mg4[:, :, 0, :].reshape(1024, 16), mg4[:, :, 1, :].reshape(1024, 16)], axis=1))
    bp = np.stack([f("ml_b_i")[0].reshape(16), f("ml_b_f")[0].reshape(16)])
    shared["mpar"] = np.ascontiguousarray(np.broadcast_to(bp[None, :, None, :], (128, 2, 12, 16))).reshape(128, 384)
    shared["mparT"] = np.ascontiguousarray(bp.T)
    shared["ml_normT"] = np.ascontiguousarray(f("ml_norm")[0].reshape(128, 1))
    shared["ml_wo"] = f("ml_w_out")[0]
    smc, smn, smmm = f("state_mlstm_c"), f("state_mlstm_n"), f("state_mlstm_m")
    xp = f("x_prompt")
    xs = f("x_sample")
    c = f("c")
    cctx = f("c_ctx")
    per_core = []
    for i in range(N_CORES):
        b = i // 4
        x = np.concatenate([xp[2 * i], xp[2 * i + 1], xs[b]], axis=0)
        cond = np.stack([cctx, c[b]], axis=-1)
        m = dict(shared)
        m["xT"] = np.ascontiguousarray(x.T)
        m["sd"] = np.ascontiguousarray(sdel[b, 0])
        m["smca"] = np.ascontiguousarray(np.concatenate([smc[b, 0], smn[b, 0][..., None]], axis=-1))
        m["smm"] = np.ascontiguousarray(np.broadcast_to(smmm[b, 0].reshape(1, 16), (64, 16)))
        m["condT"] = np.ascontiguousarray(cond.reshape(8, 128, 2).transpose(1, 0, 2)).reshape(128, 16)
        per_core.append(m)
    return per_core


def run(inputs, cfg, core_ids=None, trace=False):
    b = Builder(cfg)
    nc = b.build()
    maps = _prep(inputs)
    core_ids = core_ids or list(range(N_CORES))
    maps = [maps[i] for i in core_ids]
    res = run_bass_kernel_spmd(nc, maps, core_ids=list(range(len(core_ids))), trace=trace)
    return res, b


def kernel(**inputs):
    res, b = run(inputs, dict())
    R = res.results
    y_prompt = np.zeros((16, 256, D), np.float32)
    y_sample = np.zeros((2, 1024, D), np.float32)
    new_d = np.zeros((16, 1, 2, 8, 128, 128), np.float32)
    new_c = np.zeros((16, 1, 2, 8, 64, 128), np.float32)
    new_n = np.zeros((16, 1, 2, 8, 64), np.float32)
    new_m = np.zeros((16, 1, 2, 8), np.float32)
    for i in range(N_CORES):
        y = np.asarray(R[i]["yT"]).T
        y_prompt[2 * i] = y[0:256]
        y_prompt[2 * i + 1] = y[256:512]
        if i % 4 == 0:
            y_sample[i // 4] = y[512:]
        new_d[2 * i:2 * i + 2, 0] = np.asarray(R[i]["nsd"])
        new_c[2 * i:2 * i + 2, 0] = np.asarray(R[i]["nsc"])
        new_n[2 * i:2 * i + 2, 0] = np.asarray(R[i]["nsn"])
        new_m[2 * i:2 * i + 2, 0] = np.asarray(R[i]["nsm"])
    return (y_prompt, y_sample, new_d, new_c, new_n, new_m)
```

```python
import numpy as np
from contextlib import ExitStack
import concourse.bass as bass
import concourse.mybir as mybir
from concourse.bass_utils import run_bass_kernel_spmd

F32 = mybir.dt.float32
BF16 = mybir.dt.bfloat16
AF = mybir.ActivationFunctionType
ALU = mybir.AluOpType

ENGS = ("pe", "act", "dve", "pool", "sp")
D = 1024
NT = 1536
DFF = 2816
NCH = 22
TILES = [(0, 512), (512, 1024), (1024, 1536)]
COND = [0, 1, 1]
EPS = 1e-6
N_CORES = 8


def _rect(ap):
    t = ap.tensor
    name = t.name
    pat = ap.ap
    off = ap.offset
    esz = mybir.dt.size(ap.dtype)
    if "dram" in str(type(t)).lower() or "DRam" in str(type(t)):
        ext = 1
        for st, cnt in pat:
            ext += (cnt - 1) * abs(st)
        return (name, 0, 1, off * esz, (off + ext) * esz)
    shape = list(t.shape)
    fsz = 1
    for s in shape[1:]:
        fsz *= s
    pcnt = pat[0][1]
    p_lo = off // fsz
    f_lo = off % fsz
    lo = 0
    hi = 0
    for st, cnt in pat[1:]:
        if st >= 0:
            hi += (cnt - 1) * st
        else:
            lo += (cnt - 1) * st
    return (name, p_lo, p_lo + pcnt, (f_lo + lo) * esz, (f_lo + hi + 1) * esz)


class Sched:
    def __init__(self, nc, n_dma_sems=8):
        self.nc = nc
        self.q = {e: [] for e in ENGS}
        self.cnt = {e: 0 for e in ENGS}
        self.waited = {e: {} for e in ENGS}
        self.recs = {}
        self.n_dma_sems = n_dma_sems
        self.dma_i = {e: 0 for e in ENGS}
        self.dma_cnt = {}
        self.n_ops = 0

    def _deps(self, eng, ap, is_write):
        r = _rect(ap)
        lst = self.recs.setdefault(r[0], [])
        is_psum = r[0].startswith("ps")
        deps = []
        keep = []
        for rec in lst:
            (_, pl, ph, fl, fh), tok, w, e = rec
            overlap = not (ph <= r[1] or r[2] <= pl or fh <= r[3] or r[4] <= fl)
            if is_psum and e != eng:
                deps.append(tok)
                continue
            if overlap:
                if is_write or w:
                    same = (e == eng) and tok[0] == e
                    if same and eng == "pe":
                        pass
                    else:
                        deps.append(tok)
                if is_write and pl >= r[1] and ph <= r[2] and fl >= r[3] and fh <= r[4]:
                    continue
            keep.append(rec)
        self.recs[r[0]] = keep
        return deps, r

    def _emit_waits(self, eng, deps):
        w = self.waited[eng]
        best = {}
        for k, v in deps:
            if w.get(k, 0) >= v:
                continue
            if best.get(k, 0) < v:
                best[k] = v
        for k, v in best.items():
            w[k] = v
            self.q[eng].append(("wait", k, v))

    def _record(self, r, tok, w, eng):
        lst = self.recs[r[0]]
        if not w:
            for rec in lst:
                if (not rec[2]) and rec[3] == eng and rec[0] == r and rec[1][0] == tok[0]:
                    rec[1] = tok
                    return
        lst.append([r, tok, w, eng])

    def op(self, eng, fn, reads=(), writes=()):
        deps = []
        rr = []
        for ap in reads:
            d, r = self._deps(eng, ap, False)
            deps += d
            rr.append((r, False))
        for ap in writes:
            d, r = self._deps(eng, ap, True)
            deps += d
            rr.append((r, True))
        self._emit_waits(eng, deps)
        self.cnt[eng] += 1
        tok = (eng, self.cnt[eng])
        self.q[eng].append(("op", fn, eng, 1))
        for r, w in rr:
            self._record(r, tok, w, eng)
        self.n_ops += 1
        return tok

    def dma(self, eng, out, in_, **kw):
        deps = []
        d, r_in = self._deps(eng, in_, False)
        deps += d
        d, r_out = self._deps(eng, out, True)
        deps += d
        i = self.dma_i[eng] % self.n_dma_sems
        self.dma_i[eng] += 1
        key = ("dma", eng, i)
        prev = self.dma_cnt.get(key, 0)
        if prev:
            deps.append((key, prev))
        self._emit_waits(eng, deps)
        val = prev + 16
        self.dma_cnt[key] = val
        tok = (key, val)
        self.q[eng].append(("op", lambda e: e.dma_start(out=out, in_=in_, **kw), key, 16))
        self._record(r_in, tok, False, eng)
        self._record(r_out, tok, True, eng)
        self.n_ops += 1
        return tok

    def wait_all(self, eng):
        deps = []
        for e in ENGS:
            if self.cnt[e] and e != eng:
                deps.append((e, self.cnt[e]))
        for k, v in self.dma_cnt.items():
            deps.append((k, v))
        self._emit_waits(eng, deps)

    def matmul(self, out, lhsT, rhs, start=True, stop=True):
        return self.op("pe", lambda e: e.matmul(out, lhsT, rhs, start=start, stop=stop), [lhsT, rhs], [out])

    def transpose(self, out, in_, ident):
        return self.op("pe", lambda e: e.transpose(out, in_, ident), [in_, ident], [out])

    def act(self, out, in_, func, bias=None, scale=None, accum_out=None):
        kw = {}
        rd = [in_]
        if bias is not None:
            kw["bias"] = bias
            if not isinstance(bias, (int, float)):
                rd.append(bias)
        if scale is not None:
            kw["scale"] = scale
            if not isinstance(scale, (int, float)):
                rd.append(scale)
        wr = [out]
        if accum_out is not None:
            kw["accum_out"] = accum_out
            wr.append(accum_out)
        return self.op("act", lambda e: e.activation(out, in_, func, **kw), rd, wr)

    def tt(self, eng, out, in0, in1, op):
        return self.op(eng, lambda e: e.tensor_tensor(out, in0, in1, op), [in0, in1], [out])

    def ts(self, eng, out, in0, s1, s2=None, op0=ALU.mult, op1=None):
        rd = [in0]
        for s in (s1, s2):
            if s is not None and not isinstance(s, (int, float)):
                rd.append(s)
        if op1 is None:
            return self.op(eng, lambda e: e.tensor_scalar(out, in0, s1, None, op0), rd, [out])
        return self.op(eng, lambda e: e.tensor_scalar(out, in0, s1, s2, op0, op1), rd, [out])

    def stt(self, out, in0, scalar, in1, op0, op1):
        rd = [in0, in1]
        if not isinstance(scalar, (int, float)):
            rd.append(scalar)
        return self.op("dve", lambda e: e.scalar_tensor_tensor(out, in0, scalar, in1, op0, op1), rd, [out])

    def copy(self, eng, out, in_):
        if eng == "act":
            return self.op(eng, lambda e: e.copy(out, in_), [in_], [out])
        return self.op(eng, lambda e: e.tensor_copy(out, in_), [in_], [out])

    def memset(self, eng, ap, val):
        return self.op(eng, lambda e: e.memset(ap, val), [], [ap])

    def recip(self, out, in_):
        return self.op("dve", lambda e: e.reciprocal(out, in_), [in_], [out])

    def emit(self):
        nc = self.nc
        keys = [e for e in ENGS if self.cnt[e]] + list(self.dma_cnt.keys())
        with ExitStack() as es:
            sems = {}
            for i, k in enumerate(keys):
                sems[k] = es.enter_context(nc.semaphore("s%d" % i))
            block = es.enter_context(nc.Block())
            q = self.q

            def run(engname, eng):
                for it in q[engname]:
                    if it[0] == "wait":
                        eng.wait_ge(sems[it[1]], it[2])
                    else:
                        it[1](eng).then_inc(sems[it[2]], it[3])

            if q["sp"]:
                @block.sync
                def _(e):
                    run("sp", e)
            if q["act"]:
                @block.scalar
                def _(e):
                    run("act", e)
            if q["dve"]:
                @block.vector
                def _(e):
                    run("dve", e)
            if q["pool"]:
                @block.gpsimd
                def _(e):
                    run("pool", e)
            if q["pe"]:
                @block.tensor
                def _(e):
                    run("pe", e)


def _const_tables():
    i = np.arange(128)
    r, c = np.meshgrid(i, i, indexing="ij")
    mats = [
        (r == c), np.ones((128, 128)), -np.ones((128, 128)),
        (r >= c), (r <= c), (r > c), (r < c), (r <= c), (r >= c),
    ]
    consts = np.concatenate([m.astype(np.float32) for m in mats], axis=1)
    m2 = [(r // 8 == c // 8)]
    for sz in (8, 16, 32, 64):
        m2.append((r // (2 * sz) == c // (2 * sz)) & (r // sz != c // sz))
    mask2 = np.concatenate([np.concatenate([m, m], axis=1).astype(np.float32) for m in m2], axis=1)
    k = np.arange(256)
    ang = 2.0 * np.pi * ((k[:, None] * k[None, :]) % 256) / 256.0
    cs3 = np.concatenate([np.cos(ang), np.sin(ang), -np.sin(ang)], axis=1).astype(np.float32)
    t = np.arange(1024)
    ang = 2.0 * np.pi * ((t[:, None] * t[None, :]) % 1024) / 1024.0
    ct = np.cos(ang).astype(np.float32)
    nst = (-np.sin(ang)).astype(np.float32)
    tab = np.zeros((4, 128, 2, 8, 256), np.float32)
    for q in range(4):
        for j, m in enumerate((ct, nst)):
            blk = m[:, q * 256:(q + 1) * 256].reshape(8, 128, 256)
            tab[q, :, j] = blk.transpose(1, 0, 2)
    return consts, cs3, tab, mask2


C_ID, C_ONE, C_NEG, C_LT, C_UT, C_SLT, C_SUT = range(7)


def _fm(v):
    v = np.asarray(v, np.float32)
    lead = v.shape[:-1]
    n = v.shape[-1] // 128
    return np.ascontiguousarray(np.moveaxis(v.reshape(lead + (n, 128)), -1, 0))


class Builder:
    def __init__(self, cfg):
        self.cfg = cfg
        self.nc = bass.Bass("TRN2", target_bir_lowering=False)
        self.S = Sched(self.nc)
        self.es = ExitStack()
        self.steps = []
        self.bank_ctr = {}

    def dram_in(self, name, shape):
        return self.nc.dram_tensor(name, list(shape), F32, kind="ExternalInput").ap()

    def dram_out(self, name, shape):
        return self.nc.dram_tensor(name, list(shape), F32, kind="ExternalOutput").ap()

    def sb(self, name, shape, dt=F32):
        return self.es.enter_context(self.nc.sbuf_tensor(name, list(shape), dt))

    def carve(self, off, shape, dt=F32):
        n = int(np.prod(shape))
        if dt == BF16:
            w = (n + 1) // 2
            v = self.scr[:, off:off + w].bitcast(BF16)[:, 0:n]
        else:
            w = n
            v = self.scr[:, off:off + w]
        assert off + w <= self.scr_words, (off, w, self.scr_words)
        if len(shape) == 2:
            v = v.rearrange("p (a b) -> p a b", a=shape[0])
        elif len(shape) == 3:
            v = v.rearrange("p (a b c) -> p a b c", a=shape[0], b=shape[1])
        elif len(shape) == 4:
            v = v.rearrange("p (a b c d) -> p a b c d", a=shape[0], b=shape[1], c=shape[2])
        return v, w

    def bank(self, role="x", pool=None):
        pool = pool or list(range(8))
        i = self.bank_ctr.get(role, 0)
        self.bank_ctr[role] = i + 1
        return self.ps[pool[i % len(pool)]]

    def step(self, loads, fn):
        self.steps.append((loads, fn))

    def run_steps(self):
        S = self.S
        R = len(self.slots)
        load_steps = [i for i, (l, f) in enumerate(self.steps) if l is not None]
        slot_of = {si: k % R for k, si in enumerate(load_steps)}
        issued = 0

        def issue(upto):
            nonlocal issued
            while issued < len(load_steps) and issued <= upto:
                si = load_steps[issued]
                slot = self.slots[slot_of[si]]
                for (dstf, src) in self.steps[si][0](slot):
                    S.dma("pool", dstf, src)
                issued += 1

        k = 0
        for i, (l, f) in enumerate(self.steps):
            issue(k + R - 1)
            if l is not None:
                f(self.slots[slot_of[i]])
                k += 1
            else:
                f(None)
        self.steps = []

    def norm_mod(self, A, Bv, out_h=True, out_y=None):
        S = self.S
        off = self.scr_tmp
        SQ, w = self.carve(off, [8, 512], BF16); off += w
        RS, w = self.carve(off, [512]); off += w
        TM, w = self.carve(off, [2, 512]); off += w
        for tt, (t0, t1) in enumerate(TILES):
            cd = COND[tt]
            S.act(SQ, self.X[:, :, t0:t1], AF.Square)
            ps = self.bank("n", [6, 7])
            for fc in range(8):
                S.matmul(ps[:], self.ones_bf[:, 0:128], SQ[:, fc, :], start=(fc == 0), stop=(fc == 7))
            S.act(RS, ps[:], AF.Sqrt, bias=self.eps_col[:, 0:1], scale=1.0 / D)
            S.recip(RS, RS)
            for fc in range(8):
                if out_y is not None:
                    S.stt(out_y[:, fc, t0:t1], self.X[:, fc, t0:t1], A[:, fc:fc + 1], RS, ALU.mult, ALU.mult)
                else:
                    tm = TM[:, fc % 2, :]
                    S.tt("dve", tm, self.X[:, fc, t0:t1], RS, ALU.mult)
                    S.act(self.H[:, fc, t0:t1], tm, AF.Identity, bias=Bv[:, fc, cd:cd + 1], scale=A[:, fc, cd:cd + 1])

    def adaln(self, l):
        S = self.S
        for q in range(12):
            def loads(slot, q=q):
                v = slot[:, 0:4096].rearrange("p (k n) -> p k n", k=8)
                return [(v, self.w_ada[l][:, q * 512:(q + 1) * 512].rearrange("(k p) n -> p k n", p=128))]

            def fn(slot, q=q):
                v = slot[:, 0:4096].rearrange("p (k n) -> p k n", k=8)
                ps = self.bank("n", [6, 7])
                for ocl in range(4):
                    for kc in range(8):
                        S.matmul(ps[:, ocl * 2:ocl * 2 + 2], v[:, kc, ocl * 128:(ocl + 1) * 128],
                                 self.SC[:, kc, :], start=(kc == 0), stop=(kc == 7))
                pv = ps[:, 0:8].rearrange("p (o c) -> p o c", c=2)
                for c in range(2):
                    S.tt("dve", self.MOD[:, l, q * 4:(q + 1) * 4, c], pv[:, :, c],
                         self.b_adaT[:, l * 48 + q * 4: l * 48 + (q + 1) * 4], ALU.add)
            self.step(loads, fn)

        def mkfin(sub):
            def fin(slot):
                for c in range(2):
                    S.stt(self.AMOD[:, l, sub, :, c], self.MOD[:, l, (1 + 3 * sub) * 8:(2 + 3 * sub) * 8, c], 1.0,
                          self.nmT[:, (sub * 4 + l) * 8:(sub * 4 + l + 1) * 8], ALU.add, ALU.mult)
            return fin
        st = self.steps
        self.steps = st[:-12] + st[-12:-8] + [(None, mkfin(0))] + st[-8:] + [(None, mkfin(1))]

    def mod(self, l, j):
        return self.MOD[:, l, j * 8:(j + 1) * 8, :]

    def tap(self, k):
        if self.dbg is not None and k < self.cfg.get("ntaps", 0):
            self.S.dma("sp", self.dbg[k].rearrange("(c p) t -> p c t", p=128), self.X[:])

    def ffn(self, l):
        S = self.S
        self.step(None, lambda slot: self.norm_mod(self.AMOD[:, l, 1], self.mod(l, 3)))
        off = self.scr_main
        P, w = self.carve(off, [4, NT], BF16); off += w
        GP = []
        for b in range(2):
            g, w = self.carve(off, [1720], BF16); off += w
            GP.append(g)
        SB = []
        for b in range(2):
            s, w = self.carve(off, [512]); off += w
            SB.append(s)
        DG = []
        for b in range(2):
            d, w = self.carve(off, [9, 128], BF16); off += w
            DG.append(d)
        assert off <= self.scr_tmp

        def zero(slot):
            for g in GP:
                S.memset("dve", g, 0.0)
        self.step(None, zero)

        def chunk(c, slot, j, pj):
            wv = slot[:, 0:4096].rearrange("p (k n) -> p k n", k=8)
            gp = GP[c % 2]
            gpP = gp[:, 0:516].rearrange("p (s t) -> p s t", s=2)
            gpS = gp[:, 516:516 + 18 * 66].rearrange("p (r c) -> p r c", r=18)
            dg = DG[c % 2]
            i0 = (l * NCH + c) * 9
            S.tt("dve", dg, self.ident_bf.unsqueeze(1).broadcast_to([128, 9, 128]),
                 self.cwT[:, i0:i0 + 9].unsqueeze(2).broadcast_to([128, 9, 128]), ALU.mult)
            psG = []
            for tt, (t0, t1) in enumerate(TILES):
                ps = self.bank("g", [0, 1, 2])
                for kc in range(8):
                    S.matmul(ps[:], wv[:, kc, 256 + j * 128:256 + (j + 1) * 128], self.H[:, kc, t0:t1],
                             start=(kc == 0), stop=(kc == 7))
                psG.append(ps)
            S.copy("act", gpP[:, :, 1:257], psG[0][:].rearrange("p (s t) -> p s t", s=2))
            for hf in range(2):
                S.copy("act", gpS[:, 1 + 8 * hf:9 + 8 * hf, 1:65], psG[1 + hf][:].rearrange("p (r c) -> p r c", r=8))
            psA = []
            for tt, (t0, t1) in enumerate(TILES):
                ps = self.bank("a", [3, 4, 5])
                for kc in range(8):
                    S.matmul(ps[:], wv[:, kc, j * 128:(j + 1) * 128], self.H[:, kc, t0:t1],
                             start=(kc == 0), stop=(kc == 7))
                psA.append(ps)
            for tt, (t0, t1) in enumerate(TILES):
                ps = self.bank("c", [6, 7])
                if tt == 0:
                    for dc in range(3):
                        S.matmul(ps[:].rearrange("p (s t) -> p s t", s=2), dg[:, 3 + dc, :], gpP[:, :, dc:dc + 256],
                                 start=(dc == 0), stop=(dc == 2))
                else:
                    hf = tt - 1
                    n = 0
                    for dr in range(3):
                        for dc in range(3):
                            S.matmul(ps[:].rearrange("p (r c) -> p r c", r=8), dg[:, dr * 3 + dc, :],
                                     gpS[:, 8 * hf + dr:8 * hf + dr + 8, dc:dc + 64], start=(n == 0), stop=(n == 8))
                            n += 1
                sb = SB[tt % 2]
                S.act(sb, ps[:], AF.Silu, bias=self.cbT[:, l * NCH + c:l * NCH + c + 1])
                S.tt("dve", P[:, pj, t0:t1], sb, psA[tt][:], ALU.mult)

        c0 = 0
        while c0 < NCH:
            G = min(4, NCH - c0)
            for pr in range(0, G, 2):
                c = c0 + pr

                def loads(slot, c=c):
                    wv = slot[:, 0:4096].rearrange("p (k n) -> p k n", k=8)
                    wu = self.w_up[l]
                    return [(wv[:, :, 0:256], wu[:, c * 128:(c + 2) * 128].rearrange("(k p) n -> p k n", p=128)),
                            (wv[:, :, 256:512], wu[:, DFF + c * 128:DFF + (c + 2) * 128].rearrange("(k p) n -> p k n", p=128))]

                def fn(slot, c=c, pr=pr):
                    chunk(c, slot, 0, pr)
                    chunk(c + 1, slot, 1, pr + 1)
                self.step(loads, fn)

            def dloads(slot, c0=c0, G=G):
                wv = slot[:, 0:G * 1024].rearrange("p (g n) -> p g n", g=G)
                return [(wv, self.w_down[l][c0 * 128:(c0 + G) * 128, :].rearrange("(g p) n -> p g n", p=128))]

            def dfn(slot, c0=c0, G=G):
                wv = slot[:, 0:G * 1024].rearrange("p (g n) -> p g n", g=G)
                g2 = self.mod(l, 5)
                for oc in range(8):
                    for tt, (t0, t1) in enumerate(TILES):
                        ps = self.bank("g", [0, 1, 2])
                        for j in range(G):
                            S.matmul(ps[:], wv[:, j, oc * 128:(oc + 1) * 128], P[:, j, t0:t1], start=(j == 0), stop=(j == G - 1))
                        cd = COND[tt]
                        S.stt(self.X[:, oc, t0:t1], ps[:], g2[:, oc, cd:cd + 1], self.X[:, oc, t0:t1], ALU.mult, ALU.add)
            self.step(dloads, dfn)
            c0 += G

    def fnet(self, l, jf):
        S = self.S
        self.step(None, lambda slot: self.norm_mod(self.AMOD[:, l, 0], self.mod(l, 0)))
        off = self.scr_main
        AB, w = self.carve(off, [12, 4, 512], BF16); off += w
        Fm, w = self.carve(off, [8, NT], BF16); off += w
        assert off <= self.scr_words
        CS3 = self.CS3

        def stage1(slot):
            S.dma("pool", self.fb_row[:], self.fnet_b_d[:, jf * D:(jf + 1) * D])
            n = 0
            for ti in range(12):
                for g in range(4):
                    ps = self.bank("x")
                    for j in range(2):
                        S.matmul(ps[:], self.H[:, 2 * g + j, ti * 128:(ti + 1) * 128], CS3[:, j, 0:512], start=(j == 0), stop=(j == 1))
                    S.copy("act" if n % 2 else "dve", AB[:, ti, g, :], ps[:])
                    n += 1
        self.step(None, stage1)

        def stage2p(slot):
            for cc in range(8):
                g, hf = cc // 2, cc % 2
                ps = self.bank("x")
                for s in range(2):
                    n = 0
                    for tk in range(2):
                        for part in range(2):
                            S.matmul(ps[:, s * 256:(s + 1) * 256], AB[:, 2 * s + tk, g, part * 256 + hf * 128:part * 256 + (hf + 1) * 128],
                                     CS3[:, tk, part * 512:part * 512 + 256], start=(n == 0), stop=(n == 3))
                            n += 1
                S.act(Fm[:, cc, 0:512], ps[:], AF.Copy, scale=1.0 / 256.0)
        self.step(None, stage2p)

        for q in range(4):
            def loads(slot, q=q):
                return [(slot[:, 0:4096].rearrange("p (a b) -> p a b", a=4),
                         self.tab[q].rearrange("p a k u -> p (a k u)").rearrange("p (a b) -> p a b", a=4))]

            def fn(slot, q=q):
                tv = slot[:, 0:4096].rearrange("p (a k u) -> p a k u", a=2, k=8)
                for cc in range(8):
                    g, hf = cc // 2, cc % 2
                    ps = self.bank("x")
                    n = 0
                    for tk in range(8):
                        for part in range(2):
                            S.matmul(ps[:, 0:256], AB[:, 4 + tk, g, part * 256 + hf * 128:part * 256 + (hf + 1) * 128],
                                     tv[:, part, tk, :], start=(n == 0), stop=(n == 15))
                            n += 1
                    S.act(Fm[:, cc, 512 + q * 256:512 + (q + 1) * 256], ps[:, 0:256], AF.Copy, scale=1.0 / 512.0)
            self.step(loads, fn)

        for half in range(2):
            def loads(slot, half=half):
                wv = slot[:, 0:4096].rearrange("p (k n) -> p k n", k=8)
                return [(wv, self.fnet_w[jf][:, half * 512:(half + 1) * 512].rearrange("(k p) n -> p k n", p=128))]

            def fn(slot, half=half):
                wv = slot[:, 0:4096].rearrange("p (k n) -> p k n", k=8)
                g1 = self.mod(l, 2)
                for ocl in range(4):
                    oc = half * 4 + ocl
                    for tt, (t0, t1) in enumerate(TILES):
                        ps = self.bank("x")
                        for kc in range(8):
                            S.matmul(ps[:], wv[:, kc, ocl * 128:(ocl + 1) * 128], Fm[:, kc, t0:t1], start=(kc == 0), stop=False)
                        S.matmul(ps[:], self.fb_row[0:1, oc * 128:(oc + 1) * 128], self.ones_bf[0:1, :],
                                 start=False, stop=True)
                        cd = COND[tt]
                        S.stt(self.X[:, oc, t0:t1], ps[:], g1[:, oc, cd:cd + 1], self.X[:, oc, t0:t1], ALU.mult, ALU.add)
            self.step(loads, fn)

    def gdn(self, l):
        S = self.S
        CF = self.consts
        cf = lambda k: CF[:, k * 128:(k + 1) * 128]
        ident_bf = self.ident_bf
        self.step(None, lambda slot: self.norm_mod(self.AMOD[:, l, 0], self.mod(l, 0)))
        off = 0
        def cv(shape, dt=F32):
            nonlocal off
            v, w = self.carve(off, shape, dt)
            off += w
            return v
        GA = cv([12, 16]); BA = cv([12, 16])
        Z = cv([NT], BF16)
        QKV = cv([3, NT], BF16)
        K_tm = cv([12, 128], BF16); V_tm = cv([12, 128], BF16)
        O_tm = cv([12, 128])
        ST_T = cv([8, 2, 128], BF16); ST_Q = cv([8, 2, 128], BF16); ST_QD = cv([8, 2, 128], BF16); ST_KD = cv([8, 2, 128], BF16)
        CCE = cv([8, 8])
        Sf = [cv([128]) for _ in range(4)]; Sb = [cv([128], BF16) for _ in range(4)]
        RP = [cv([128], BF16) for _ in range(4)]; VN = [cv([128], BF16) for _ in range(4)]
        SSQ = cv([12])
        base = off
        CP = cv([3, 1548], BF16); DG5 = cv([15, 128], BF16); SQ = cv([512], BF16); RS = cv([512])
        T0 = cv([12, 16]); EA = cv([12, 16])
        endA = off
        off = base
        NB = 5
        GM2 = []; EM = []; ER = []; T1 = []; CC = []; LNs = []; PQs = []; MBs = []
        for _ in range(NB):
            o1 = off
            GM2.append(cv([2, 128]))
            ln1, _w = self.carve(o1, [512], BF16)
            o2 = off
            EM.append(cv([512], BF16))
            ln2, _w = self.carve(o2, [512], BF16)
            ER.append(cv([256], BF16)); T1.append(cv([2, 128], BF16)); CC.append(cv([8]))
            LNs.append([cv([512], BF16), ln1, ln2])
            PQs.append(cv([512], BF16))
            MBs.append(cv([512], BF16))
        endB = off
        off = base
        SQ2 = cv([12, 128]); ON = cv([12, 128], BF16); OGh = cv([NT], BF16)
        TMPX = [cv([512]) for _ in range(2)]
        endC = off
        off = max(endA, endB, endC)
        assert off <= self.scr_words, off
        cnt = {"pre": 0, "sc": 0}
        one_col = CF[:, C_ONE * 128:C_ONE * 128 + 1]
        ones_f, neg_f = cf(C_ONE), cf(C_NEG)
        MdT2 = CF[:, 7 * 128:9 * 128].rearrange("p (d c) -> p d c", d=2)
        SM2 = CF[:, 5 * 128:7 * 128].rearrange("p (d c) -> p d c", d=2)
        bc2 = lambda col2: col2.unsqueeze(2).broadcast_to([128, 2, 128])
        v22 = lambda t: t.rearrange("p (a d c) -> p a d c", a=2, d=2)
        v2 = lambda t: t.rearrange("p (d c) -> p d c", d=2)
        MdT2b = self.consts_bf[:, 7 * 128:9 * 128].rearrange("p (d c) -> p d c", d=2)
        SM2b = self.consts_bf[:, 5 * 128:7 * 128].rearrange("p (d c) -> p d c", d=2)

        def gloads(slot):
            return [(slot[:, 0:256].rearrange("p (k n) -> p k n", k=8), self.dn_wg.rearrange("(k p) n -> p k n", p=128))]

        def gfn(slot):
            wg = slot[:, 0:256].rearrange("p (k n) -> p k n", k=8)
            if self.cfg.get("gcut", 9) < 0:
                return
            if self.cfg.get("gcut", 9) < 1:
                return
            ps = self.bank("x")
            for ti in range(12):
                for kc in range(8):
                    S.matmul(ps[:, ti * 32:(ti + 1) * 32], self.H[:, kc, ti * 128:(ti + 1) * 128], wg[:, kc, :],
                             start=(kc == 0), stop=(kc == 7))
            pv = ps[:, 0:384].rearrange("p (t d k h) -> p t d k h", t=12, d=2, k=2)
            gp = self.gpar[:].rearrange("p (a t n) -> p a t n", a=2, t=12)
            for d in range(2):
                S.tt("dve", T0[:, :, d * 8:(d + 1) * 8], pv[:, :, d, 0, :], gp[:, 1, :, d * 8:(d + 1) * 8], ALU.add)
                S.act(BA[:, :, d * 8:(d + 1) * 8], pv[:, :, d, 1, :], AF.Sigmoid)
            cut = self.cfg.get("gcut", 9)
            if cut < 2:
                return
            S.act(T0, T0, AF.Exp)
            if cut < 3:
                return
            S.act(T0, T0, AF.Ln, bias=one_col)
            if cut < 4:
                return
            S.act(EA, gp[:, 0], AF.Exp)
            S.stt(GA, T0, -1.0, EA, ALU.mult, ALU.mult)
        self.step(gloads, gfn)
        stage = self.cfg.get("gdn_stage", 9)
        nheads = self.cfg.get("gdn_heads", 8)

        def pre2(h, ti, n, b):
            LN0b, LN1b, LN2b = LNs[b]
            PQ = [PQs[b], PQs[b]]
            MB = [MBs[b], MBs[b]]
            T1b = T1[b]
            g2 = GA[:, ti, h:h + 9:8]
            b2 = BA[:, ti, h:h + 9:8]
            kT = QKV[:, 1, ti * 128:(ti + 1) * 128]
            qT = QKV[:, 0, ti * 128:(ti + 1) * 128]
            l4 = lambda t: t.rearrange("p (a c) -> p a c", a=4)
            def mask(lv):
                S.tt("pool", l4(MB[lv % 2]), l4(LN0b), self.mask2[:, lv, 0:128].unsqueeze(1).broadcast_to([128, 4, 128]), ALU.mult)
                return v22(MB[lv % 2])
            S.tt("dve", GM2[b], MdT2, bc2(g2), ALU.mult)
            GMf = GM2[b].rearrange("p d c -> p (d c)")
            pD = self.ps[b]
            S.matmul(pD[:, 0:256], neg_f, GMf, start=True, stop=False)
            for d in range(2):
                S.matmul(pD[:, d * 128:(d + 1) * 128], GM2[b][:, d, :], ones_f, start=False, stop=(d == 1))
            S.matmul(pD[:, 256:512], ones_f, GMf, start=True, stop=False)
            for d in range(2):
                S.matmul(pD[:, 256 + d * 128:256 + (d + 1) * 128], GM2[b][:, d, :], neg_f, start=False, stop=(d == 1))
            yield
            S.act(EM[b], pD[:, 0:512], AF.Relu, scale=-1.0)
            S.act(EM[b], EM[b], AF.Exp, scale=-1.0)
            pG = self.ps[b]
            S.matmul(pG[:, 0:256], ones_f, GMf)
            for d in range(2):
                S.matmul(pG[:, 256 + d:257 + d], GM2[b][:, d, :], ones_f[:, 0:1])
            S.matmul(pG[:, 258:260], ones_f, g2)
            yield
            S.act(ER[b], pG[:, 0:256], AF.Exp)
            S.copy("act", CC[b][:, 0:4], pG[:, 256:260])
            S.act(CCE[:, n, 0:4], CC[b][:, 0:4], AF.Exp)
            for d in range(2):
                S.act(CCE[:, n, 4 + d:5 + d], CC[b][:, d:d + 1], AF.Exp, bias=CC[b][:, 2 + d:3 + d], scale=-1.0)
            S.act(CCE[:, n, 6:8], CCE[:, n, 0:2], AF.Copy, scale=-1.0)
            S.tt("pool", T1b, SM2b, bc2(b2), ALU.mult)
            S.tt("pool", EM[b][:, 0:256].rearrange("p (d c) -> p d c", d=2), EM[b][:, 0:256].rearrange("p (d c) -> p d c", d=2), T1b, ALU.mult)
            S.tt("pool", EM[b][:, 256:512].rearrange("p (d c) -> p d c", d=2), EM[b][:, 256:512].rearrange("p (d c) -> p d c", d=2), MdT2b, ALU.mult)
            pB = self.ps[b]
            S.matmul(pB[:, 0:128], kT, kT)
            S.matmul(pB[:, 128:256], kT, qT)
            yield
            E2, ET2 = v2(EM[b][:, 0:256]), v2(EM[b][:, 256:512])
            LN0 = v22(LN0b)
            S.tt("dve", LN0[:, 0], pB[:, 0:128].unsqueeze(1).broadcast_to([128, 2, 128]), E2, ALU.mult)
            yield
            S.tt("dve", ST_Q[:, n], pB[:, 128:256].unsqueeze(1).broadcast_to([128, 2, 128]), ET2, ALU.mult)
            pT = self.ps[b][:].bitcast(BF16)
            for d in range(2):
                S.transpose(pT[:, d * 128:(d + 1) * 128], LN0[:, 0, d, :], ident_bf)
            S.copy("act", LN0b[:, 256:512], pT[:, 0:256])
            S.tt("pool", ST_QD[:, n], qT.unsqueeze(1).broadcast_to([128, 2, 128]), v2(ER[b]), ALU.mult)
            S.tt("pool", ST_KD[:, n], K_tm[:, ti, :].unsqueeze(1).broadcast_to([128, 2, 128]), bc2(CCE[:, n, 4:6]), ALU.mult)
            yield
            LM0 = mask(0)
            S.tt("pool", l4(PQ[0]), ident_bf.unsqueeze(1).broadcast_to([128, 4, 128]), l4(MB[0]), ALU.subtract)
            def blk(ps, a, d):
                return ps[:, (a * 2 + d) * 128:(a * 2 + d + 1) * 128]
            Lc, Nc = LM0[:, 0], LM0[:, 1]
            cur = 0
            for r in range(2):
                pR = self.ps[b]
                for d in range(2):
                    S.matmul(blk(pR, 0, d), Nc[:, d, :], Lc[:, d, :])
                    S.matmul(blk(pR, 1, d), Lc[:, d, :], Nc[:, d, :])
                LNr_b = LN1b if r == 0 else LN2b
                S.copy("act", LNr_b, pR[:, 0:512])
                if r == 1:
                    LMn = mask(1)
                yield
                LNr = v22(LNr_b)
                Lc, Nc = LNr[:, 0], LNr[:, 1]
                PQc = v22(PQ[cur])
                pP = self.ps[b]
                for d in range(2):
                    S.matmul(blk(pP, 0, d), Nc[:, d, :], PQc[:, 0, d, :])
                    S.matmul(blk(pP, 1, d), Lc[:, d, :], PQc[:, 1, d, :])
                S.tt("dve", PQ[1 - cur], pP[:, 0:512], PQ[cur], ALU.add)
                cur = 1 - cur
                yield
            for lv in range(1, 5):
                LMc = LMn
                PQc = v22(PQ[cur])
                YY = v22(LN1b)
                pY = self.ps[b]
                for d in range(2):
                    S.matmul(blk(pY, 1, d), LMc[:, 0, d, :], PQc[:, 1, d, :])
                    if lv < 4:
                        S.matmul(blk(pY, 0, d), LMc[:, 1, d, :], PQc[:, 0, d, :])
                if lv < 4:
                    S.copy("act", LN1b, pY[:, 0:512])
                    LMn = mask(lv + 1)
                else:
                    S.copy("act", LN1b[:, 256:512], pY[:, 256:512])
                yield
                pU = self.ps[b]
                for d in range(2):
                    S.matmul(blk(pU, 1, d), PQc[:, 0, d, :], YY[:, 1, d, :])
                    if lv < 4:
                        S.matmul(blk(pU, 0, d), PQc[:, 1, d, :], YY[:, 0, d, :])
                if lv < 4:
                    S.tt("dve", PQ[1 - cur], PQ[cur], pU[:, 0:512], ALU.subtract)
                else:
                    S.tt("dve", PQ[1 - cur][:, 256:512], PQ[cur][:, 256:512], pU[:, 256:512], ALU.subtract)
                cur = 1 - cur
                yield
            S.tt("pool", ST_T[:, n], v22(PQ[cur])[:, 1], bc2(b2), ALU.mult)

        def scan_step(ti, n, d, sb_i, first):
            b = sb_i
            kT = QKV[:, 1, ti * 128:(ti + 1) * 128]
            pA = self.bank("x")
            S.matmul(pA[:, 0:128], kT, Sb[sb_i])
            S.stt(RP[b], pA[:, 0:128], CCE[:, n, 6 + d:7 + d], V_tm[:, ti, :], ALU.mult, ALU.add)
            yield
            pB = self.bank("x")
            S.matmul(pB[:, 0:128], ST_T[:, n, d, :], RP[b])
            S.copy("act", VN[b], pB[:, 0:128])
            yield
            pC = self.bank("x")
            S.matmul(pC[:, 0:128], ST_QD[:, n, d, :], Sb[sb_i], start=True, stop=False)
            S.matmul(pC[:, 0:128], ST_Q[:, n, d, :], VN[b], start=False, stop=True)
            S.tt("dve", O_tm[:, ti, :], O_tm[:, ti, :], pC[:, 0:128], ALU.add)
            pE = self.bank("x")
            S.matmul(pE[:, 0:128], ST_KD[:, n, d, :], VN[b])
            S.stt(Sf[sb_i], Sf[sb_i], CCE[:, n, 2 + d:3 + d], pE[:, 0:128], ALU.mult, ALU.add)
            S.copy("act", Sb[sb_i], Sf[sb_i])
            yield

        for h in range(nheads if stage >= 2 else 0):
            def loadsA(slot, h=h):
                return [(slot[:, 0:4096].rearrange("p (k n) -> p k n", k=8), self.dn_wh[h].rearrange("(k p) n -> p k n", p=128))]

            def fnA(slot, h=h):
                wv = slot[:, 0:4096].rearrange("p (k n) -> p k n", k=8)
                S.memset("dve", CP, 0.0)
                S.tt("dve", DG5.rearrange("p (j t) c -> p j t c", j=3),
                     ident_bf.unsqueeze(1).unsqueeze(1).broadcast_to([128, 3, 5, 128]),
                     self.dcwT[:].rearrange("p (j h t) -> p j h t", j=3, h=8)[:, :, h, :].unsqueeze(3).broadcast_to([128, 3, 5, 128]),
                     ALU.mult)
                cpP = lambda j: CP[:, j, 0:520].rearrange("p (s t) -> p s t", s=2)
                for j in range(4):
                    for tt, (t0, t1) in enumerate(TILES):
                        ps = self.bank("x")
                        for kc in range(8):
                            S.matmul(ps[:], wv[:, kc, j * 128:(j + 1) * 128], self.H[:, kc, t0:t1], start=(kc == 0), stop=(kc == 7))
                        if j == 3:
                            S.act(Z[:, t0:t1], ps[:], AF.Silu)
                        elif tt == 0:
                            S.copy("act", cpP(j)[:, :, 2:258], ps[:].rearrange("p (s t) -> p s t", s=2))
                        else:
                            S.copy("act", CP[:, j, 522 + (tt - 1) * 512:522 + tt * 512], ps[:])
                for j in range(3):
                    for tt, (t0, t1) in enumerate(TILES):
                        ps = self.bank("x")
                        for tap in range(5):
                            if tt == 0:
                                S.matmul(ps[:].rearrange("p (s t) -> p s t", s=2), DG5[:, j * 5 + tap, :], cpP(j)[:, :, tap:tap + 256],
                                         start=(tap == 0), stop=(tap == 4))
                            else:
                                st = 520 + (tt - 1) * 512 + tap
                                S.matmul(ps[:], DG5[:, j * 5 + tap, :], CP[:, j, st:st + 512], start=(tap == 0), stop=(tap == 4))
                        S.act(QKV[:, j, t0:t1], ps[:], AF.Silu)
                for j in range(2):
                    for tt, (t0, t1) in enumerate(TILES):
                        S.act(SQ, QKV[:, j, t0:t1], AF.Square)
                        ps = self.bank("x")
                        S.matmul(ps[:], self.ones_bf[:, 0:128], SQ)
                        if j == 0:
                            S.act(RS, ps[:], AF.Sqrt, bias=self.eps_col[:, 1:2], scale=128.0)
                        else:
                            S.act(RS, ps[:], AF.Sqrt, bias=self.eps_col[:, 0:1], scale=1.0)
                        S.recip(RS, RS)
                        S.tt("dve", QKV[:, j, t0:t1], QKV[:, j, t0:t1], RS, ALU.mult)
                for (j, dst) in ((1, K_tm), (2, V_tm)):
                    for g4 in range(3):
                        pb = self.bank("x")[:].bitcast(BF16)
                        for i in range(4):
                            ti = g4 * 4 + i
                            S.transpose(pb[:, i * 128:(i + 1) * 128], QKV[:, j, ti * 128:(ti + 1) * 128], ident_bf)
                        S.copy("act", dst[:, g4 * 4:(g4 + 1) * 4, :], pb[:, 0:512].rearrange("p (a b) -> p a b", a=4))
            self.step(loadsA, fnA)

            def loadsB(slot, h=h):
                return [(slot[:, 0:1024], self.dn_wo[h * 128:(h + 1) * 128, :])]

            def fnB(slot, h=h):
                wo = slot[:, 0:1024]
                seqs = [([0, 1], 0), ([2, 3], 1), (list(range(4, 12)), 2)]
                sbi = 0
                def interleave(gens):
                    gens = list(gens)
                    while gens:
                        for g in list(gens):
                            try:
                                next(g)
                            except StopIteration:
                                gens.remove(g)

                def chainx(tiles, sidx, d, n0, sb_i):
                    order = tiles if d == 0 else tiles[::-1]
                    if sidx == 2:
                        S.dma("sp", Sf[sb_i], self.sd[d, h])
                        S.copy("act", Sb[sb_i], Sf[sb_i])
                    else:
                        S.memset("dve", Sf[sb_i], 0.0)
                        S.memset("dve", Sb[sb_i], 0.0)
                    for ti in order:
                        yield from scan_step(ti, n0 + tiles.index(ti), d, sb_i, first=False)
                    if sidx < 2:
                        S.dma("sp", self.nsd[sidx, d, h], Sf[sb_i])

                S.memset("dve", O_tm, 0.0)
                if stage >= 3:
                    for grp in ([0, 1, 2, 3],):
                        interleave([pre2(h, ti, ti, k) for k, ti in enumerate(grp)])
                    if stage >= 4:
                        interleave([chainx([0, 1], 0, d, 0, 2 * 0 + d) for d in range(2)] +
                                   [chainx([2, 3], 1, d, 2, 2 * 1 + d) for d in range(2)])
                    for grp in ([4, 5, 6, 7, 8], [9, 10, 11]):
                        interleave([pre2(h, ti, ti - 4, k) for k, ti in enumerate(grp)])
                    if stage >= 4:
                        interleave([chainx(list(range(4, 12)), 2, d, 0, d) for d in range(2)])
                if stage < 5:
                    return
                S.act(SQ2, O_tm, AF.Square)
                S.op("dve", lambda e: e.reduce_sum(SSQ, SQ2, mybir.AxisListType.X), [SQ2], [SSQ])
                S.act(SSQ, SSQ, AF.Sqrt, bias=self.eps_col[:, 0:1], scale=1.0 / 128.0)
                S.recip(SSQ, SSQ)
                S.tt("dve", ON, O_tm, SSQ.unsqueeze(2).broadcast_to([128, 12, 128]), ALU.mult)
                for g4 in range(3):
                    pb = self.bank("x")[:].bitcast(BF16)
                    for i in range(4):
                        ti = g4 * 4 + i
                        S.transpose(pb[:, i * 128:(i + 1) * 128], ON[:, ti, :], ident_bf)
                    S.stt(OGh[:, g4 * 512:(g4 + 1) * 512], pb[:, 0:512], self.dn_normT[:, 0:1], Z[:, g4 * 512:(g4 + 1) * 512],
                          ALU.mult, ALU.mult)
                g1 = self.mod(l, 2)
                for oc in range(8):
                    for tt, (t0, t1) in enumerate(TILES):
                        ps = self.bank("x")
                        S.matmul(ps[:], wo[:, oc * 128:(oc + 1) * 128], OGh[:, t0:t1])
                        cd = COND[tt]
                        if (oc * 3 + tt) % 2 == 0:
                            S.stt(self.X[:, oc, t0:t1], ps[:], g1[:, oc, cd:cd + 1], self.X[:, oc, t0:t1], ALU.mult, ALU.add)
                        else:
                            tx = TMPX[((oc * 3 + tt) // 2) % 2]
                            S.act(tx, ps[:], AF.Identity, scale=g1[:, oc, cd:cd + 1])
                            S.tt("pool", self.X[:, oc, t0:t1], self.X[:, oc, t0:t1], tx, ALU.add)
            self.step(loadsB, fnB)

    def mlstm(self, l):
        S = self.S
        CF = self.consts
        cf = lambda k: CF[:, k * 128:(k + 1) * 128]
        ident_bf = self.ident_bf
        self.step(None, lambda slot: self.norm_mod(self.AMOD[:, l, 0], self.mod(l, 0)))
        off = 0
        def cv(shape, dt=F32):
            nonlocal off
            v, w = self.carve(off, shape, dt)
            off += w
            return v
        LI = cv([12, 16]); LF = cv([12, 16]); T0 = cv([12, 16])
        LIr = cv([512]); LFr = cv([512]); SCN = cv([2, 2, 256]); NBF = cv([2])
        MFB = cv([2, 2]); EMF = cv([2, 2]); DGE = cv([2, 2, 16]); EMB = cv([2, 2, 16]); EM0 = cv([16])
        qTs = [cv([NT], BF16) for _ in range(2)]; kTs = [cv([NT], BF16) for _ in range(2)]
        vTs = [cv([NT], BF16) for _ in range(2)]; OGts = [cv([NT], BF16) for _ in range(2)]
        V_tms = [cv([12, 129], BF16) for _ in range(2)]; K_tms = [cv([12, 64], BF16) for _ in range(2)]
        Hs = cv([12, 128])
        off_st = off
        ST_S = cv([8, 2, 128], BF16); ST_QB = cv([8, 2, 128], BF16); ST_KW = cv([8, 2, 64], BF16); CCE = cv([8, 4])
        SQ2, _ = self.carve(off_st, [12, 128]); SSQ = cv([12])
        off_tmp = off
        ON, _w = self.carve(off_tmp, [12, 128], BF16); OGh, _w2 = self.carve(off_tmp + 768, [NT], BF16)
        TMPX = [self.carve(off_tmp + 1536 + 512 * i, [512])[0] for i in range(2)]
        NBm = 4
        FM2 = [cv([2, 128]) for _ in range(NBm)]
        EMn = [cv([256]) for _ in range(NBm)]
        ER = [cv([256], BF16) for _ in range(NBm)]; CC = [cv([8]) for _ in range(NBm)]
        MdT2 = CF[:, 7 * 128:9 * 128].rearrange("p (d c) -> p d c", d=2)
        bc2 = lambda col2: col2.unsqueeze(2).broadcast_to([128, 2, 128])
        v2 = lambda t: t.rearrange("p (d c) -> p d c", d=2)
        CA = [cv([129]) for _ in range(4)]; CAb = [cv([130], BF16) for _ in range(4)]; CAo = [cv([129]) for _ in range(4)]
        DN = [cv([2]) for _ in range(4)]
        assert off <= self.scr_words, off
        cnt = {"pre": 0, "sc": 0}
        one_col = CF[:, C_ONE * 128:C_ONE * 128 + 1]
        ones_f, neg_f = cf(C_ONE), cf(C_NEG)
        mp = self.mpar[:].rearrange("p (a t n) -> p a t n", a=2, t=12)

        def gloads(slot):
            return [(slot[:, 0:256].rearrange("p (k n) -> p k n", k=8), self.ml_wg.rearrange("(k p) n -> p k n", p=128)),
                    (slot[:, 256:512].rearrange("p (k n) -> p k n", k=8), self.ml_wgr.rearrange("(k p) n -> p k n", p=128))]

        def gfn(slot):
            wg = slot[:, 0:256].rearrange("p (k n) -> p k n", k=8)
            wr = slot[:, 256:512].rearrange("p (k n) -> p k n", k=8)
            for vv in V_tms:
                S.memset("dve", vv[:, :, 128:129], 1.0)
            ps = self.bank("x")
            for ti in range(12):
                for kc in range(8):
                    S.matmul(ps[:, ti * 32:(ti + 1) * 32], self.H[:, kc, ti * 128:(ti + 1) * 128], wg[:, kc, :],
                             start=(kc == 0), stop=(kc == 7))
            pv = ps[:, 0:384].rearrange("p (t d k h) -> p t d k h", t=12, d=2, k=2)
            for d in range(2):
                S.tt("dve", LI[:, :, d * 8:(d + 1) * 8], pv[:, :, d, 0, :], mp[:, 0, :, d * 8:(d + 1) * 8], ALU.add)
                S.tt("dve", T0[:, :, d * 8:(d + 1) * 8], pv[:, :, d, 1, :], mp[:, 1, :, d * 8:(d + 1) * 8], ALU.add)
            S.act(T0, T0, AF.Exp, scale=-1.0)
            S.act(T0, T0, AF.Ln, bias=one_col)
            S.ts("dve", LF, T0, -1.0)
            pr = self.bank("x")
            for kc in range(8):
                S.matmul(pr[0:16, 0:512], wr[:, kc, 0:16], self.H[:, kc, 0:512], start=(kc == 0), stop=(kc == 7))
            S.act(LIr[0:16, :], pr[0:16, 0:512], AF.Identity, bias=self.mparT[0:16, 0:1])
            pr2 = self.bank("x")
            for kc in range(8):
                S.matmul(pr2[0:16, 0:512], wr[:, kc, 16:32], self.H[:, kc, 0:512], start=(kc == 0), stop=(kc == 7))
            S.ts("dve", NBF[0:16, 0:1], self.mparT[0:16, 1:2], -1.0)
            S.act(LFr[0:16, :], pr2[0:16, 0:512], AF.Exp, bias=NBF[0:16, 0:1], scale=-1.0)
            S.act(LFr[0:16, :], LFr[0:16, :], AF.Ln, bias=one_col[0:16, :])
            S.ts("dve", LFr[0:16, :], LFr[0:16, :], -1.0)
            for s in range(2):
                for fb in range(2):
                    if fb == 0:
                        d0, d1 = LFr[0:16, s * 256:(s + 1) * 256], LIr[0:16, s * 256:(s + 1) * 256]
                    elif s == 0:
                        d0, d1 = LFr[0:16, 255::-1], LIr[0:16, 255::-1]
                    else:
                        d0, d1 = LFr[0:16, 511:255:-1], LIr[0:16, 511:255:-1]
                    o = SCN[0:16, s, fb, :]
                    S.op("dve", lambda e, o=o, d0=d0, d1=d1: e.tensor_tensor_scan(o, d0, d1, 0.0, ALU.add, ALU.max), [d0, d1], [o])
                    S.copy("dve", MFB[0:16, s, fb:fb + 1], SCN[0:16, s, fb, 255:256])
                S.dma("sp", self.nsm[s, 0, :], MFB[0:8, s, 0:1])
                S.dma("sp", self.nsm[s, 1, :], MFB[8:16, s, 1:2])
            S.act(EMF[0:16], MFB[0:16], AF.Exp, scale=-1.0)
            pe = self.bank("x")
            for s in range(2):
                for fb in range(2):
                    S.ts("dve", DGE[0:16, s, fb, :], CF[0:16, 0:16], EMF[0:16, s, fb:fb + 1])
                    c0 = (s * 2 + fb) * 16
                    S.matmul(pe[0:64, c0:c0 + 16], ones_f[0:16, 0:64], DGE[0:16, s, fb, :])
            S.copy("dve", EMB[0:64].rearrange("p a b c -> p (a b c)"), pe[0:64, 0:64])
            S.act(EM0[0:64], self.smm[0:64, :], AF.Exp)
        self.step(gloads, gfn)

        cur = {}

        def pre2(h, ti, n, b):
            qT, kT, K_tm = cur["qT"], cur["kT"], cur["K_tm"]
            lf2 = LF[:, ti, h:h + 9:8]
            li2 = LI[:, ti, h:h + 9:8]
            S.tt("dve", FM2[b], MdT2, bc2(lf2), ALU.mult)
            FMf = FM2[b].rearrange("p d c -> p (d c)")
            pD = self.ps[b]
            S.matmul(pD[:, 0:256], ones_f, FMf, start=True, stop=False)
            for d in range(2):
                S.matmul(pD[:, d * 128:(d + 1) * 128], FM2[b][:, d, :], neg_f, start=False, stop=(d == 1))
            S.matmul(pD[:, 256:512], ones_f, FMf)
            yield
            S.act(EMn[b], pD[:, 0:256], AF.Relu, scale=-1.0)
            for d in range(2):
                S.act(EMn[b][:, d * 128:(d + 1) * 128], EMn[b][:, d * 128:(d + 1) * 128], AF.Exp, bias=li2[:, d:d + 1], scale=-1.0)
            S.tt("pool", v2(EMn[b]), v2(EMn[b]), MdT2, ALU.mult)
            S.act(ER[b][0:64, :], pD[0:64, 256:512], AF.Exp)
            pG = self.ps[b]
            for d in range(2):
                S.matmul(pG[:, d:d + 1], FM2[b][:, d, :], ones_f[:, 0:1])
            S.matmul(pG[:, 2:4], ones_f, lf2)
            yield
            S.copy("act", CC[b][:, 0:4], pG[:, 0:4])
            S.tt("pool", CC[b][:, 4:6], CC[b][:, 2:4], li2, ALU.add)
            S.act(CCE[:, n, 0:2], CC[b][:, 2:4], AF.Exp)
            for d in range(2):
                S.act(CCE[:, n, 2 + d:3 + d], CC[b][:, d:d + 1], AF.Exp, bias=CC[b][:, 4 + d:5 + d], scale=-1.0)
            yield
            pB = self.ps[b]
            S.matmul(pB[:, 0:128], kT[0:64, ti * 128:(ti + 1) * 128], qT[0:64, ti * 128:(ti + 1) * 128])
            S.tt("dve", ST_S[:, n], pB[:, 0:128].unsqueeze(1).broadcast_to([128, 2, 128]), v2(EMn[b]), ALU.mult)
            S.tt("pool", ST_QB[0:64, n], qT[0:64, ti * 128:(ti + 1) * 128].unsqueeze(1).broadcast_to([64, 2, 128]),
                 v2(ER[b])[0:64], ALU.mult)
            S.tt("pool", ST_KW[:, n], K_tm[:, ti, :].unsqueeze(1).broadcast_to([128, 2, 64]),
                 CCE[:, n, 2:4].unsqueeze(2).broadcast_to([128, 2, 64]), ALU.mult)
            yield

        def scan_step(ti, n, d, ci, first):
            b = ci
            V_tm = cur["V_tm"]
            pN = self.bank("sc", [0, 1, 2, 3, 4, 5])
            S.matmul(pN[:, 0:129], ST_QB[0:64, n, d, :], CAb[ci][0:64, 0:129], start=True, stop=False)
            S.matmul(pN[:, 0:129], ST_S[:, n, d, :], V_tm[:, ti, :], start=False, stop=True)
            S.act(DN[b][:, 0:1], pN[:, 128:129], AF.Abs)
            S.ts("dve", DN[b][:, 0:1], DN[b][:, 0:1], 1.0, None, ALU.max)
            S.recip(DN[b][:, 0:1], DN[b][:, 0:1])
            S.stt(Hs[:, ti, :], pN[:, 0:128], DN[b][:, 0:1], Hs[:, ti, :], ALU.mult, ALU.add)
            yield
            pS = self.bank("sc", [0, 1, 2, 3, 4, 5])
            S.matmul(pS[0:64, 0:129], ST_KW[:, n, d, :], V_tm[:, ti, :])
            S.stt(CA[ci][0:64, :], CA[ci][0:64, :], CCE[0:64, n, d:d + 1], pS[0:64, 0:129], ALU.mult, ALU.add)
            S.copy("act", CAb[ci][0:64, 0:129], CA[ci][0:64, :])
            yield

        nheads = self.cfg.get("ml_heads", 8)

        def interleave_g(gens):
            gens = list(gens)
            while gens:
                for g in list(gens):
                    try:
                        next(g)
                    except StopIteration:
                        gens.remove(g)
                yield

        def genA(h, wv):
            p = h % 2
            qT, kT, vT, OGt, V_tm, K_tm = qTs[p], kTs[p], vTs[p], OGts[p], V_tms[p], K_tms[p]
            for j in range(4):
                lo, hi = [(0, 64), (64, 128), (128, 256), (256, 384)][j]
                M = hi - lo
                for tt, (t0, t1) in enumerate(TILES):
                    ps = self.bank("fa", [6, 7])
                    for kc in range(8):
                        S.matmul(ps[0:M, :], wv[:, kc, lo:hi], self.H[:, kc, t0:t1], start=(kc == 0), stop=(kc == 7))
                    if j == 0:
                        S.act(qT[0:64, t0:t1], ps[0:64, :], AF.Copy, scale=0.125)
                    elif j == 1:
                        S.copy("act", kT[0:64, t0:t1], ps[0:64, :])
                    elif j == 2:
                        S.copy("act", vT[:, t0:t1], ps[:])
                    else:
                        S.act(OGt[:, t0:t1], ps[:], AF.Sigmoid)
                    yield
            for g4 in range(3):
                pb = self.bank("fa", [6, 7])[:].bitcast(BF16)
                for i in range(4):
                    ti = g4 * 4 + i
                    S.transpose(pb[:, i * 128:(i + 1) * 128], vT[:, ti * 128:(ti + 1) * 128], ident_bf)
                S.copy("act", V_tm[:, g4 * 4:(g4 + 1) * 4, 0:128], pb[:, 0:512].rearrange("p (a b) -> p a b", a=4))
                yield
            for g4 in range(3):
                pb = self.bank("fa", [6, 7])[:].bitcast(BF16)
                for i in range(4):
                    ti = g4 * 4 + i
                    S.transpose(pb[:, i * 64:(i + 1) * 64], kT[0:64, ti * 128:(ti + 1) * 128], ident_bf[0:64, 0:64])
                S.copy("act", K_tm[:, g4 * 4:(g4 + 1) * 4, :], pb[:, 0:256].rearrange("p (a b) -> p a b", a=4))
                yield

        def genB(h, wo):
            p = h % 2
            cur.update(qT=qTs[p], kT=kTs[p], K_tm=K_tms[p], V_tm=V_tms[p])
            OGt = OGts[p]

            def chainx(tiles, sidx, d, n0, ci):
                order = tiles if d == 0 else tiles[::-1]
                if sidx == 2:
                    S.dma("sp", CA[ci][0:64, :], self.smca[d, h])
                    S.ts("dve", CA[ci][0:64, :], CA[ci][0:64, :], EM0[0:64, d * 8 + h:d * 8 + h + 1])
                    S.copy("act", CAb[ci][0:64, 0:129], CA[ci][0:64, :])
                else:
                    S.memset("dve", CA[ci][0:64, :], 0.0)
                    S.memset("dve", CAb[ci][0:64, :], 0.0)
                for ti in order:
                    yield from scan_step(ti, n0 + tiles.index(ti), d, ci, first=False)
                if sidx < 2:
                    S.ts("dve", CAo[ci][0:64, :], CA[ci][0:64, :], EMB[0:64, sidx, d, d * 8 + h:d * 8 + h + 1])
                    S.dma("sp", self.nsc[sidx, d, h], CAo[ci][0:64, 0:128])
                    S.dma("sp", self.nsn[sidx, d, h, :], CAo[ci][0:64, 128:129])

            S.memset("dve", Hs, 0.0)
            for grp in ([0, 1, 2, 3],):
                yield from interleave_g([pre2(h, ti, ti, k) for k, ti in enumerate(grp)])
            yield from interleave_g([chainx([0, 1], 0, d, 0, d) for d in range(2)] + [chainx([2, 3], 1, d, 2, 2 + d) for d in range(2)])
            for grp in ([4, 5, 6, 7], [8, 9, 10, 11]):
                yield from interleave_g([pre2(h, ti, ti - 4, k) for k, ti in enumerate(grp)])
            yield from interleave_g([chainx(list(range(4, 12)), 2, d, 0, d) for d in range(2)])
            S.act(SQ2, Hs, AF.Square)
            S.op("dve", lambda e: e.reduce_sum(SSQ, SQ2, mybir.AxisListType.X), [SQ2], [SSQ])
            S.act(SSQ, SSQ, AF.Sqrt, bias=self.eps_col[:, 0:1], scale=1.0 / 128.0)
            S.recip(SSQ, SSQ)
            yield
            S.tt("dve", ON, Hs, SSQ.unsqueeze(2).broadcast_to([128, 12, 128]), ALU.mult)
            yield
            for g4 in range(3):
                pb = self.bank("sc", [0, 1, 2, 3, 4, 5])[:].bitcast(BF16)
                for i in range(4):
                    ti = g4 * 4 + i
                    S.transpose(pb[:, i * 128:(i + 1) * 128], ON[:, ti, :], ident_bf)
                S.stt(OGh[:, g4 * 512:(g4 + 1) * 512], pb[:, 0:512], self.ml_normT[:, 0:1], OGt[:, g4 * 512:(g4 + 1) * 512],
                      ALU.mult, ALU.mult)
                yield
            g1 = self.mod(l, 2)
            for oc in range(8):
                for tt, (t0, t1) in enumerate(TILES):
                    ps = self.bank("sc", [0, 1, 2, 3, 4, 5])
                    S.matmul(ps[:], wo[:, oc * 128:(oc + 1) * 128], OGh[:, t0:t1])
                    cd = COND[tt]
                    if (oc * 3 + tt) % 2 == 0:
                        S.stt(self.X[:, oc, t0:t1], ps[:], g1[:, oc, cd:cd + 1], self.X[:, oc, t0:t1], ALU.mult, ALU.add)
                    else:
                        tx = TMPX[((oc * 3 + tt) // 2) % 2]
                        S.act(tx, ps[:], AF.Identity, scale=g1[:, oc, cd:cd + 1])
                        S.tt("pool", self.X[:, oc, t0:t1], self.X[:, oc, t0:t1], tx, ALU.add)
                yield

        def drain(g):
            for _ in g:
                pass

        def loads0(slot):
            return [(slot[:, 0:3072].rearrange("p (k n) -> p k n", k=8), self.ml_wh[0].rearrange("(k p) n -> p k n", p=128))]
        self.step(loads0, lambda slot: drain(genA(0, slot[:, 0:3072].rearrange("p (k n) -> p k n", k=8))))
        for h in range(nheads):
            def loadsH(slot, h=h):
                out = [(slot[:, 3072:4096], self.ml_wo[h * 128:(h + 1) * 128, :])]
                if h + 1 < nheads:
                    out.append((slot[:, 0:3072].rearrange("p (k n) -> p k n", k=8), self.ml_wh[h + 1].rearrange("(k p) n -> p k n", p=128)))
                return out

            def fnH(slot, h=h):
                gens = [genB(h, slot[:, 3072:4096])]
                if h + 1 < nheads:
                    gens.append(genA(h + 1, slot[:, 0:3072].rearrange("p (k n) -> p k n", k=8)))
                drain(interleave_g(gens))
            self.step(loadsH, fnH)

    def build(self):
        cfg = self.cfg
        nc = self.nc
        S = self.S
        self.xT = self.dram_in("xT", [D, NT])
        self.condT = self.dram_in("condT", [128, 16])
        self.w_ada = self.dram_in("w_ada", [4, D, 6 * D])
        b_adaT_d = self.dram_in("b_adaT", [128, 4 * 48])
        nmT_d = self.dram_in("nmT", [128, 72])
        self.w_up = self.dram_in("w_up", [4, D, 2 * DFF])
        cwT_d = self.dram_in("cwT", [128, 4 * NCH * 9])
        cbT_d = self.dram_in("cbT", [128, 4 * NCH])
        self.w_down = self.dram_in("w_down", [4, DFF, D])
        self.fnet_w = self.dram_in("fnet_w", [2, D, D])
        self.fnet_b_d = self.dram_in("fnet_b", [1, 2 * D])
        consts_d = self.dram_in("consts", [128, 1152])
        cs3_d = self.dram_in("cs3", [256, 768])
        self.tab = self.dram_in("tab", [4, 128, 2, 8, 256])
        self.dn_wh = self.dram_in("dn_wh", [8, D, 512])
        self.dn_wg = self.dram_in("dn_wg", [D, 32])
        dcwT_d = self.dram_in("dcwT", [128, 120])
        mask2_d = self.dram_in("mask2", [128, 1280])
        gpar_d = self.dram_in("gpar", [128, 2 * 12 * 16])
        dn_normT_d = self.dram_in("dn_normT", [128, 1])
        self.dn_wo = self.dram_in("dn_wo", [D, D])
        self.sd = self.dram_in("sd", [2, 8, 128, 128])
        self.nsd = self.dram_out("nsd", [2, 2, 8, 128, 128])
        self.ml_wh = self.dram_in("ml_wh", [8, D, 384])
        self.ml_wg = self.dram_in("ml_wg", [D, 32])
        self.ml_wgr = self.dram_in("ml_wgr", [D, 32])
        mpar_d = self.dram_in("mpar", [128, 2 * 12 * 16])
        mparT_d = self.dram_in("mparT", [16, 2])
        ml_normT_d = self.dram_in("ml_normT", [128, 1])
        self.ml_wo = self.dram_in("ml_wo", [D, D])
        self.smca = self.dram_in("smca", [2, 8, 64, 129])
        smm_d = self.dram_in("smm", [64, 16])
        self.nsc = self.dram_out("nsc", [2, 2, 8, 64, 128])
        self.nsn = self.dram_out("nsn", [2, 2, 8, 64])
        self.nsm = self.dram_out("nsm", [2, 2, 8])
        self.yT = self.dram_out("yT", [D, NT])
        ntaps = cfg.get("ntaps", 0)
        self.dbg = self.dram_out("dbg", [ntaps, D, NT]) if ntaps else None

        self.X = self.sb("X", [128, 8, NT])
        self.H = self.sb("H", [128, 8, NT], BF16)
        self.slots = [self.sb("slot%d" % i, [128, 4096], BF16) for i in range(4)]
        self.scr_words = cfg.get("scr_words", 19712)
        self.scr = self.sb("scr", [128, self.scr_words])
        self.scr_tmp = self.scr_words - 3584
        self.scr_main = 0
        self.consts = self.sb("consts_f", [128, 1152])
        self.consts_bf = self.sb("consts_b", [128, 1152], BF16)
        self.ones_bf = self.sb("ones_bf", [128, 512], BF16)
        self.CS3 = self.sb("CS3", [128, 2, 768], BF16)
        self.fb_row = self.sb("fb_row", [1, D], BF16)
        self.b_adaT = self.sb("b_adaT_s", [128, 4 * 48])
        self.nmT = self.sb("nmT_s", [128, 72])
        self.cwT = self.sb("cwT_s", [128, 4 * NCH * 9])
        self.cbT = self.sb("cbT_s", [128, 4 * NCH])
        self.condS = self.sb("condS", [128, 16])
        self.SC = self.sb("SC", [128, 8, 2], BF16)
        self.MOD = self.sb("MOD", [128, 4, 48, 2])
        self.AMOD = self.sb("AMOD", [128, 4, 2, 8, 2])
        self.eps_col = self.sb("eps_col", [128, 2])
        self.mpar = self.sb("mpar_s", [128, 2 * 12 * 16])
        self.mparT = self.sb("mparT_s", [16, 2])
        self.ml_normT = self.sb("ml_normT_s", [128, 1])
        self.smm = self.sb("smm_s", [64, 16])
        self.dcwT = self.sb("dcwT_s", [128, 120])
        self.mask2 = self.sb("mask2_s", [128, 5, 256], BF16)
        self.gpar = self.sb("gpar_s", [128, 2 * 12 * 16])
        self.dn_normT = self.sb("dn_normT_s", [128, 1])
        self.ps = [self.es.enter_context(nc.psum_tensor("ps%d" % i, [128, 512], F32)) for i in range(8)]
        self.ident_bf = self.consts_bf[:, C_ID * 128:(C_ID + 1) * 128]

        S.dma("sp", self.X[:], self.xT.rearrange("(c p) t -> p c t", p=128))
        S.dma("sp", self.condS[:], self.condT)
        S.dma("sp", self.consts[:], consts_d)
        S.dma("pool", self.consts_bf[:], consts_d)
        S.dma("pool", self.CS3[:], cs3_d.rearrange("(j p) n -> p j n", p=128))
        S.dma("sp", self.b_adaT[:], b_adaT_d)
        S.dma("sp", self.nmT[:], nmT_d)
        S.dma("sp", self.cwT[:], cwT_d)
        S.dma("sp", self.cbT[:], cbT_d)
        S.memset("dve", self.ones_bf[:], 1.0)
        S.memset("dve", self.eps_col[:, 0:1], EPS)
        S.memset("dve", self.eps_col[:, 1:2], 128.0 * EPS)
        S.dma("sp", self.mpar[:], mpar_d)
        S.dma("sp", self.mparT[:], mparT_d)
        S.dma("sp", self.ml_normT[:], ml_normT_d)
        S.dma("sp", self.smm[:], smm_d)
        S.dma("sp", self.dcwT[:], dcwT_d)
        S.dma("pool", self.mask2[:].rearrange("p a b -> p (a b)"), mask2_d)
        S.dma("sp", self.gpar[:], gpar_d)
        S.dma("sp", self.dn_normT[:], dn_normT_d)
        S.act(self.SC[:].rearrange("p k c -> p (k c)"), self.condS[:], AF.Silu)

        layers = cfg.get("layers", [0, 1, 2, 3])

        def collect(fn):
            keep = self.steps
            self.steps = []
            fn()
            out = self.steps
            self.steps = keep
            return out

        def merge(a, b):
            out = []
            ia = ib = 0
            while ia < len(a) or ib < len(b):
                if ia < len(a):
                    out.append(a[ia]); ia += 1
                want = (ia * len(b)) // max(1, len(a)) if ia < len(a) else len(b)
                while ib < want:
                    out.append(b[ib]); ib += 1
            return out

        k = 0
        ada0 = collect(lambda: self.adaln(layers[0]))
        self.steps += ada0[:5]
        pending = ada0[5:]
        for li, l in enumerate(layers):
            kind = l % 3
            mix = []
            if kind == 0 and cfg.get("fnet", True):
                mix = collect(lambda: self.fnet(l, l // 3))
            elif kind == 1 and cfg.get("gdn", True):
                mix = collect(lambda: self.gdn(l))
            elif kind == 2 and cfg.get("mlstm", True):
                mix = collect(lambda: self.mlstm(l))
            extra = pending
            if li + 1 < len(layers):
                extra = extra + collect(lambda: self.adaln(layers[li + 1]))
            pending = []
            self.steps += merge(mix, extra)
            self.step(None, lambda slot, k=k: self.tap(k))
            k += 1
            if cfg.get("ffn", True):
                self.ffn(l)
            self.step(None, lambda slot, k=k: self.tap(k))
            k += 1
        self.run_steps()

        Y, w = self.carve(0, [8, NT])
        self.norm_mod(self.nmT[:, 64:72], None, out_y=Y)
        S.dma("sp", self.yT.rearrange("(c p) t -> p c t", p=128), Y)
        S.wait_all("sp")
        S.emit()
        self.es.close()
        return nc


def _prep(inputs):
    consts, cs3, tab, mask2 = _const_tables()
    f = lambda k: np.ascontiguousarray(np.asarray(inputs[k], np.float32))
    shared = {
        "w_ada": f("w_ada"),
        "b_adaT": _fm(f("b_ada")).reshape(128, 4 * 48),
        "nmT": np.concatenate([_fm(f("norm_mix")).reshape(128, 32), _fm(f("norm_ffn")).reshape(128, 32),
                               _fm(f("norm_final")).reshape(128, 8)], axis=1),
        "w_up": f("ffn_w_up"),
        "cwT": np.ascontiguousarray(np.moveaxis(_fm(f("ffn_conv_w").reshape(4, 9, DFF)), 2, 3)).reshape(128, 4 * NCH * 9),
        "cbT": _fm(f("ffn_conv_b")).reshape(128, 4 * NCH),
        "w_down": f("ffn_w_down"),
        "fnet_w": f("fnet_w"),
        "fnet_b": f("fnet_b").reshape(1, 2 * D),
        "consts": consts, "cs3": cs3, "tab": tab, "mask2": mask2,
    }
    wi = f("dn_w_in")[0]
    shared["dn_wh"] = np.ascontiguousarray(np.stack(
        [np.concatenate([wi[:, j * 1024 + h * 128:j * 1024 + (h + 1) * 128] for j in range(4)], axis=1) for h in range(8)]))
    shared["dn_wg"] = np.ascontiguousarray(wi[:, 4096:4128])
    shared["dcwT"] = np.ascontiguousarray(np.moveaxis(_fm(f("dn_conv_w")[0]), 1, 2)).reshape(128, 120)
    gp = np.stack([f("dn_a_log")[0].reshape(16), f("dn_dt_bias")[0].reshape(16)])
    shared["gpar"] = np.ascontiguousarray(np.broadcast_to(gp[None, :, None, :], (128, 2, 12, 16))).reshape(128, 384)
    shared["dn_normT"] = np.ascontiguousarray(f("dn_norm")[0].reshape(128, 1))
    shared["dn_wo"] = f("dn_w_out")[0]
    sdel = f("state_delta")
    mw = f("ml_w_in")[0]
    shared["ml_wh"] = np.ascontiguousarray(np.stack(
        [np.concatenate([mw[:, h * 64:(h + 1) * 64], mw[:, 512 + h * 64:512 + (h + 1) * 64],
                         mw[:, 1024 + h * 128:1024 + (h + 1) * 128], mw[:, 2048 + h * 128:2048 + (h + 1) * 128]], axis=1)
         for h in range(8)]))
    mg = mw[:, 3072:3104]
    shared["ml_wg"] = np.ascontiguousarray(mg)
    mg4 = mg.reshape(1024, 2, 2, 8)
    shared["ml_wgr"] = np.ascontiguousarray(np.concatenate([mg4[:, :, 0, :].reshape(1024, 16), mg4[:, :, 1, :].reshape(1024, 16)], axis=1))
    bp = np.stack([f("ml_b_i")[0].reshape(16), f("ml_b_f")[0].reshape(16)])
    shared["mpar"] = np.ascontiguousarray(np.broadcast_to(bp[None, :, None, :], (128, 2, 12, 16))).reshape(128, 384)
    shared["mparT"] = np.ascontiguousarray(bp.T)
    shared["ml_normT"] = np.ascontiguousarray(f("ml_norm")[0].reshape(128, 1))
    shared["ml_wo"] = f("ml_w_out")[0]
    smc, smn, smmm = f("state_mlstm_c"), f("state_mlstm_n"), f("state_mlstm_m")
    xp = f("x_prompt")
    xs = f("x_sample")
    c = f("c")
    cctx = f("c_ctx")
    per_core = []
    for i in range(N_CORES):
        b = i // 4
        x = np.concatenate([xp[2 * i], xp[2 * i + 1], xs[b]], axis=0)
        cond = np.stack([cctx, c[b]], axis=-1)
        m = dict(shared)
        m["xT"] = np.ascontiguousarray(x.T)
        m["sd"] = np.ascontiguousarray(sdel[b, 0])
        m["smca"] = np.ascontiguousarray(np.concatenate([smc[b, 0], smn[b, 0][..., None]], axis=-1))
        m["smm"] = np.ascontiguousarray(np.broadcast_to(smmm[b, 0].reshape(1, 16), (64, 16)))
        m["condT"] = np.ascontiguousarray(cond.reshape(8, 128, 2).transpose(1, 0, 2)).reshape(128, 16)
        per_core.append(m)
    return per_core


def run(inputs, cfg, core_ids=None, trace=False):
    b = Builder(cfg)
    nc = b.build()
    maps = _prep(inputs)
    core_ids = core_ids or list(range(N_CORES))
    maps = [maps[i] for i in core_ids]
    res = run_bass_kernel_spmd(nc, maps, core_ids=list(range(len(core_ids))), trace=trace)
    return res, b


def kernel(**inputs):
    res, b = run(inputs, dict())
    R = res.results
    y_prompt = np.zeros((16, 256, D), np.float32)
    y_sample = np.zeros((2, 1024, D), np.float32)
    new_d = np.zeros((16, 1, 2, 8, 128, 128), np.float32)
    new_c = np.zeros((16, 1, 2, 8, 64, 128), np.float32)
    new_n = np.zeros((16, 1, 2, 8, 64), np.float32)
    new_m = np.zeros((16, 1, 2, 8), np.float32)
    for i in range(N_CORES):
        y = np.asarray(R[i]["yT"]).T
        y_prompt[2 * i] = y[0:256]
        y_prompt[2 * i + 1] = y[256:512]
        if i % 4 == 0:
            y_sample[i // 4] = y[512:]
        new_d[2 * i:2 * i + 2, 0] = np.asarray(R[i]["nsd"])
        new_c[2 * i:2 * i + 2, 0] = np.asarray(R[i]["nsc"])
        new_n[2 * i:2 * i + 2, 0] = np.asarray(R[i]["nsn"])
        new_m[2 * i:2 * i + 2, 0] = np.asarray(R[i]["nsm"])
    return (y_prompt, y_sample, new_d, new_c, new_n, new_m)
```

```python
import numpy as np
from contextlib import ExitStack
import concourse.bass as bass
import concourse.mybir as mybir
from concourse.bass_utils import run_bass_kernel_spmd

F32 = mybir.dt.float32
BF16 = mybir.dt.bfloat16
AF = mybir.ActivationFunctionType
ALU = mybir.AluOpType

ENGS = ("pe", "act", "dve", "pool", "sp")
D = 1024
NT = 1536
DFF = 2816
NCH = 22
TILES = [(0, 512), (512, 1024), (1024, 1536)]
COND = [0, 1, 1]
EPS = 1e-6
N_CORES = 8


def _rect(ap):
    t = ap.tensor
    name = t.name
    pat = ap.ap
    off = ap.offset
    esz = mybir.dt.size(ap.dtype)
    if "dram" in str(type(t)).lower() or "DRam" in str(type(t)):
        ext = 1
        for st, cnt in pat:
            ext += (cnt - 1) * abs(st)
        return (name, 0, 1, off * esz, (off + ext) * esz)
    shape = list(t.shape)
    fsz = 1
    for s in shape[1:]:
        fsz *= s
    pcnt = pat[0][1]
    p_lo = off // fsz
    f_lo = off % fsz
    lo = 0
    hi = 0
    for st, cnt in pat[1:]:
        if st >= 0:
            hi += (cnt - 1) * st
        else:
            lo += (cnt - 1) * st
    return (name, p_lo, p_lo + pcnt, (f_lo + lo) * esz, (f_lo + hi + 1) * esz)


class Sched:
    def __init__(self, nc, n_dma_sems=8):
        self.nc = nc
        self.q = {e: [] for e in ENGS}
        self.cnt = {e: 0 for e in ENGS}
        self.waited = {e: {} for e in ENGS}
        self.recs = {}
        self.n_dma_sems = n_dma_sems
        self.dma_i = {e: 0 for e in ENGS}
        self.dma_cnt = {}
        self.n_ops = 0

    def _deps(self, eng, ap, is_write):
        r = _rect(ap)
        lst = self.recs.setdefault(r[0], [])
        is_psum = r[0].startswith("ps")
        deps = []
        keep = []
        for rec in lst:
            (_, pl, ph, fl, fh), tok, w, e = rec
            overlap = not (ph <= r[1] or r[2] <= pl or fh <= r[3] or r[4] <= fl)
            if is_psum and e != eng:
                deps.append(tok)
                continue
            if overlap:
                if is_write or w:
                    same = (e == eng) and tok[0] == e
                    if same and eng == "pe":
                        pass
                    else:
                        deps.append(tok)
                if is_write and pl >= r[1] and ph <= r[2] and fl >= r[3] and fh <= r[4]:
                    continue
            keep.append(rec)
        self.recs[r[0]] = keep
        return deps, r

    def _emit_waits(self, eng, deps):
        w = self.waited[eng]
        best = {}
        for k, v in deps:
            if w.get(k, 0) >= v:
                continue
            if best.get(k, 0) < v:
                best[k] = v
        for k, v in best.items():
            w[k] = v
            self.q[eng].append(("wait", k, v))

    def _record(self, r, tok, w, eng):
        lst = self.recs[r[0]]
        if not w:
            for rec in lst:
                if (not rec[2]) and rec[3] == eng and rec[0] == r and rec[1][0] == tok[0]:
                    rec[1] = tok
                    return
        lst.append([r, tok, w, eng])

    def op(self, eng, fn, reads=(), writes=()):
        deps = []
        rr = []
        for ap in reads:
            d, r = self._deps(eng, ap, False)
            deps += d
            rr.append((r, False))
        for ap in writes:
            d, r = self._deps(eng, ap, True)
            deps += d
            rr.append((r, True))
        self._emit_waits(eng, deps)
        self.cnt[eng] += 1
        tok = (eng, self.cnt[eng])
        self.q[eng].append(("op", fn, eng, 1))
        for r, w in rr:
            self._record(r, tok, w, eng)
        self.n_ops += 1
        return tok

    def dma(self, eng, out, in_, **kw):
        deps = []
        d, r_in = self._deps(eng, in_, False)
        deps += d
        d, r_out = self._deps(eng, out, True)
        deps += d
        i = self.dma_i[eng] % self.n_dma_sems
        self.dma_i[eng] += 1
        key = ("dma", eng, i)
        prev = self.dma_cnt.get(key, 0)
        if prev:
            deps.append((key, prev))
        self._emit_waits(eng, deps)
        val = prev + 16
        self.dma_cnt[key] = val
        tok = (key, val)
        self.q[eng].append(("op", lambda e: e.dma_start(out=out, in_=in_, **kw), key, 16))
        self._record(r_in, tok, False, eng)
        self._record(r_out, tok, True, eng)
        self.n_ops += 1
        return tok

    def wait_all(self, eng):
        deps = []
        for e in ENGS:
            if self.cnt[e] and e != eng:
                deps.append((e, self.cnt[e]))
        for k, v in self.dma_cnt.items():
            deps.append((k, v))
        self._emit_waits(eng, deps)

    def matmul(self, out, lhsT, rhs, start=True, stop=True):
        return self.op("pe", lambda e: e.matmul(out, lhsT, rhs, start=start, stop=stop), [lhsT, rhs], [out])

    def transpose(self, out, in_, ident):
        return self.op("pe", lambda e: e.transpose(out, in_, ident), [in_, ident], [out])

    def act(self, out, in_, func, bias=None, scale=None, accum_out=None):
        kw = {}
        rd = [in_]
        if bias is not None:
            kw["bias"] = bias
            if not isinstance(bias, (int, float)):
                rd.append(bias)
        if scale is not None:
            kw["scale"] = scale
            if not isinstance(scale, (int, float)):
                rd.append(scale)
        wr = [out]
        if accum_out is not None:
            kw["accum_out"] = accum_out
            wr.append(accum_out)
        return self.op("act", lambda e: e.activation(out, in_, func, **kw), rd, wr)

    def tt(self, eng, out, in0, in1, op):
        return self.op(eng, lambda e: e.tensor_tensor(out, in0, in1, op), [in0, in1], [out])

    def ts(self, eng, out, in0, s1, s2=None, op0=ALU.mult, op1=None):
        rd = [in0]
        for s in (s1, s2):
            if s is not None and not isinstance(s, (int, float)):
                rd.append(s)
        if op1 is None:
            return self.op(eng, lambda e: e.tensor_scalar(out, in0, s1, None, op0), rd, [out])
        return self.op(eng, lambda e: e.tensor_scalar(out, in0, s1, s2, op0, op1), rd, [out])

    def stt(self, out, in0, scalar, in1, op0, op1):
        rd = [in0, in1]
        if not isinstance(scalar, (int, float)):
            rd.append(scalar)
        return self.op("dve", lambda e: e.scalar_tensor_tensor(out, in0, scalar, in1, op0, op1), rd, [out])

    def copy(self, eng, out, in_):
        if eng == "act":
            return self.op(eng, lambda e: e.copy(out, in_), [in_], [out])
        return self.op(eng, lambda e: e.tensor_copy(out, in_), [in_], [out])

    def memset(self, eng, ap, val):
        return self.op(eng, lambda e: e.memset(ap, val), [], [ap])

    def recip(self, out, in_):
        return self.op("dve", lambda e: e.reciprocal(out, in_), [in_], [out])

    def emit(self):
        nc = self.nc
        keys = [e for e in ENGS if self.cnt[e]] + list(self.dma_cnt.keys())
        with ExitStack() as es:
            sems = {}
            for i, k in enumerate(keys):
                sems[k] = es.enter_context(nc.semaphore("s%d" % i))
            block = es.enter_context(nc.Block())
            q = self.q

            def run(engname, eng):
                for it in q[engname]:
                    if it[0] == "wait":
                        eng.wait_ge(sems[it[1]], it[2])
                    else:
                        it[1](eng).then_inc(sems[it[2]], it[3])

            if q["sp"]:
                @block.sync
                def _(e):
                    run("sp", e)
            if q["act"]:
                @block.scalar
                def _(e):
                    run("act", e)
            if q["dve"]:
                @block.vector
                def _(e):
                    run("dve", e)
            if q["pool"]:
                @block.gpsimd
                def _(e):
                    run("pool", e)
            if q["pe"]:
                @block.tensor
                def _(e):
                    run("pe", e)


def _const_tables():
    i = np.arange(128)
    r, c = np.meshgrid(i, i, indexing="ij")
    mats = [
        (r == c), np.ones((128, 128)), -np.ones((128, 128)),
        (r >= c), (r <= c), (r > c), (r < c), (r <= c), (r >= c),
    ]
    consts = np.concatenate([m.astype(np.float32) for m in mats], axis=1)
    m2 = [(r // 8 == c // 8)]
    for sz in (8, 16, 32, 64):
        m2.append((r // (2 * sz) == c // (2 * sz)) & (r // sz != c // sz))
    mask2 = np.concatenate([np.concatenate([m, m], axis=1).astype(np.float32) for m in m2], axis=1)
    k = np.arange(256)
    ang = 2.0 * np.pi * ((k[:, None] * k[None, :]) % 256) / 256.0
    cs3 = np.concatenate([np.cos(ang), np.sin(ang), -np.sin(ang)], axis=1).astype(np.float32)
    t = np.arange(1024)
    ang = 2.0 * np.pi * ((t[:, None] * t[None, :]) % 1024) / 1024.0
    ct = np.cos(ang).astype(np.float32)
    nst = (-np.sin(ang)).astype(np.float32)
    tab = np.zeros((4, 128, 2, 8, 256), np.float32)
    for q in range(4):
        for j, m in enumerate((ct, nst)):
            blk = m[:, q * 256:(q + 1) * 256].reshape(8, 128, 256)
            tab[q, :, j] = blk.transpose(1, 0, 2)
    return consts, cs3, tab, mask2


C_ID, C_ONE, C_NEG, C_LT, C_UT, C_SLT, C_SUT = range(7)


def _fm(v):
    v = np.asarray(v, np.float32)
    lead = v.shape[:-1]
    n = v.shape[-1] // 128
    return np.ascontiguousarray(np.moveaxis(v.reshape(lead + (n, 128)), -1, 0))


class Builder:
    def __init__(self, cfg):
        self.cfg = cfg
        self.nc = bass.Bass("TRN2", target_bir_lowering=False)
        self.S = Sched(self.nc)
        self.es = ExitStack()
        self.steps = []
        self.bank_ctr = {}

    def dram_in(self, name, shape):
        return self.nc.dram_tensor(name, list(shape), F32, kind="ExternalInput").ap()

    def dram_out(self, name, shape):
        return self.nc.dram_tensor(name, list(shape), F32, kind="ExternalOutput").ap()

    def sb(self, name, shape, dt=F32):
        return self.es.enter_context(self.nc.sbuf_tensor(name, list(shape), dt))

    def carve(self, off, shape, dt=F32):
        n = int(np.prod(shape))
        if dt == BF16:
            w = (n + 1) // 2
            v = self.scr[:, off:off + w].bitcast(BF16)[:, 0:n]
        else:
            w = n
            v = self.scr[:, off:off + w]
        assert off + w <= self.scr_words, (off, w, self.scr_words)
        if len(shape) == 2:
            v = v.rearrange("p (a b) -> p a b", a=shape[0])
        elif len(shape) == 3:
            v = v.rearrange("p (a b c) -> p a b c", a=shape[0], b=shape[1])
        elif len(shape) == 4:
            v = v.rearrange("p (a b c d) -> p a b c d", a=shape[0], b=shape[1], c=shape[2])
        return v, w

    def bank(self, role="x", pool=None):
        pool = pool or list(range(8))
        i = self.bank_ctr.get(role, 0)
        self.bank_ctr[role] = i + 1
        return self.ps[pool[i % len(pool)]]

    def step(self, loads, fn):
        self.steps.append((loads, fn))

    def run_steps(self):
        S = self.S
        R = len(self.slots)
        load_steps = [i for i, (l, f) in enumerate(self.steps) if l is not None]
        slot_of = {si: k % R for k, si in enumerate(load_steps)}
        issued = 0

        def issue(upto):
            nonlocal issued
            while issued < len(load_steps) and issued <= upto:
                si = load_steps[issued]
                slot = self.slots[slot_of[si]]
                for (dstf, src) in self.steps[si][0](slot):
                    S.dma("pool", dstf, src)
                issued += 1

        k = 0
        for i, (l, f) in enumerate(self.steps):
            issue(k + R - 1)
            if l is not None:
                f(self.slots[slot_of[i]])
                k += 1
            else:
                f(None)
        self.steps = []

    def norm_mod(self, A, Bv, out_h=True, out_y=None):
        S = self.S
        off = self.scr_tmp
        SQ, w = self.carve(off, [8, 512], BF16); off += w
        RS, w = self.carve(off, [512]); off += w
        TM, w = self.carve(off, [2, 512]); off += w
        for tt, (t0, t1) in enumerate(TILES):
            cd = COND[tt]
            S.act(SQ, self.X[:, :, t0:t1], AF.Square)
            ps = self.bank("n", [6, 7])
            for fc in range(8):
                S.matmul(ps[:], self.ones_bf[:, 0:128], SQ[:, fc, :], start=(fc == 0), stop=(fc == 7))
            S.act(RS, ps[:], AF.Sqrt, bias=self.eps_col[:, 0:1], scale=1.0 / D)
            S.recip(RS, RS)
            for fc in range(8):
                if out_y is not None:
                    S.stt(out_y[:, fc, t0:t1], self.X[:, fc, t0:t1], A[:, fc:fc + 1], RS, ALU.mult, ALU.mult)
                else:
                    tm = TM[:, fc % 2, :]
                    S.tt("dve", tm, self.X[:, fc, t0:t1], RS, ALU.mult)
                    S.act(self.H[:, fc, t0:t1], tm, AF.Identity, bias=Bv[:, fc, cd:cd + 1], scale=A[:, fc, cd:cd + 1])

    def adaln(self, l):
        S = self.S
        for q in range(12):
            def loads(slot, q=q):
                v = slot[:, 0:4096].rearrange("p (k n) -> p k n", k=8)
                return [(v, self.w_ada[l][:, q * 512:(q + 1) * 512].rearrange("(k p) n -> p k n", p=128))]

            def fn(slot, q=q):
                v = slot[:, 0:4096].rearrange("p (k n) -> p k n", k=8)
                ps = self.bank("n", [6, 7])
                for ocl in range(4):
                    for kc in range(8):
                        S.matmul(ps[:, ocl * 2:ocl * 2 + 2], v[:, kc, ocl * 128:(ocl + 1) * 128],
                                 self.SC[:, kc, :], start=(kc == 0), stop=(kc == 7))
                pv = ps[:, 0:8].rearrange("p (o c) -> p o c", c=2)
                for c in range(2):
                    S.tt("dve", self.MOD[:, l, q * 4:(q + 1) * 4, c], pv[:, :, c],
                         self.b_adaT[:, l * 48 + q * 4: l * 48 + (q + 1) * 4], ALU.add)
            self.step(loads, fn)

        def mkfin(sub):
            def fin(slot):
                for c in range(2):
                    S.stt(self.AMOD[:, l, sub, :, c], self.MOD[:, l, (1 + 3 * sub) * 8:(2 + 3 * sub) * 8, c], 1.0,
                          self.nmT[:, (sub * 4 + l) * 8:(sub * 4 + l + 1) * 8], ALU.add, ALU.mult)
            return fin
        st = self.steps
        self.steps = st[:-12] + st[-12:-8] + [(None, mkfin(0))] + st[-8:] + [(None, mkfin(1))]

    def mod(self, l, j):
        return self.MOD[:, l, j * 8:(j + 1) * 8, :]

    def tap(self, k):
        if self.dbg is not None and k < self.cfg.get("ntaps", 0):
            self.S.dma("sp", self.dbg[k].rearrange("(c p) t -> p c t", p=128), self.X[:])

    def ffn(self, l):
        S = self.S
        self.step(None, lambda slot: self.norm_mod(self.AMOD[:, l, 1], self.mod(l, 3)))
        off = self.scr_main
        P, w = self.carve(off, [4, NT], BF16); off += w
        GP = []
        for b in range(2):
            g, w = self.carve(off, [1720], BF16); off += w
            GP.append(g)
        SB = []
        for b in range(2):
            s, w = self.carve(off, [512]); off += w
            SB.append(s)
        DG = []
        for b in range(2):
            d, w = self.carve(off, [9, 128], BF16); off += w
            DG.append(d)
        assert off <= self.scr_tmp

        def zero(slot):
            for g in GP:
                S.memset("dve", g, 0.0)
        self.step(None, zero)

        def chunk(c, slot, j, pj):
            wv = slot[:, 0:4096].rearrange("p (k n) -> p k n", k=8)
            gp = GP[c % 2]
            gpP = gp[:, 0:516].rearrange("p (s t) -> p s t", s=2)
            gpS = gp[:, 516:516 + 18 * 66].rearrange("p (r c) -> p r c", r=18)
            dg = DG[c % 2]
            i0 = (l * NCH + c) * 9
            S.tt("dve", dg, self.ident_bf.unsqueeze(1).broadcast_to([128, 9, 128]),
                 self.cwT[:, i0:i0 + 9].unsqueeze(2).broadcast_to([128, 9, 128]), ALU.mult)
            psG = []
            for tt, (t0, t1) in enumerate(TILES):
                ps = self.bank("g", [0, 1, 2])
                for kc in range(8):
                    S.matmul(ps[:], wv[:, kc, 256 + j * 128:256 + (j + 1) * 128], self.H[:, kc, t0:t1],
                             start=(kc == 0), stop=(kc == 7))
                psG.append(ps)
            S.copy("act", gpP[:, :, 1:257], psG[0][:].rearrange("p (s t) -> p s t", s=2))
            for hf in range(2):
                S.copy("act", gpS[:, 1 + 8 * hf:9 + 8 * hf, 1:65], psG[1 + hf][:].rearrange("p (r c) -> p r c", r=8))
            psA = []
            for tt, (t0, t1) in enumerate(TILES):
                ps = self.bank("a", [3, 4, 5])
                for kc in range(8):
                    S.matmul(ps[:], wv[:, kc, j * 128:(j + 1) * 128], self.H[:, kc, t0:t1],
                             start=(kc == 0), stop=(kc == 7))
                psA.append(ps)
            for tt, (t0, t1) in enumerate(TILES):
                ps = self.bank("c", [6, 7])
                if tt == 0:
                    for dc in range(3):
                        S.matmul(ps[:].rearrange("p (s t) -> p s t", s=2), dg[:, 3 + dc, :], gpP[:, :, dc:dc + 256],
                                 start=(dc == 0), stop=(dc == 2))
                else:
                    hf = tt - 1
                    n = 0
                    for dr in range(3):
                        for dc in range(3):
                            S.matmul(ps[:].rearrange("p (r c) -> p r c", r=8), dg[:, dr * 3 + dc, :],
                                     gpS[:, 8 * hf + dr:8 * hf + dr + 8, dc:dc + 64], start=(n == 0), stop=(n == 8))
                            n += 1
                sb = SB[tt % 2]
                S.act(sb, ps[:], AF.Silu, bias=self.cbT[:, l * NCH + c:l * NCH + c + 1])
                S.tt("dve", P[:, pj, t0:t1], sb, psA[tt][:], ALU.mult)

        c0 = 0
        while c0 < NCH:
            G = min(4, NCH - c0)
            for pr in range(0, G, 2):
                c = c0 + pr

                def loads(slot, c=c):
                    wv = slot[:, 0:4096].rearrange("p (k n) -> p k n", k=8)
                    wu = self.w_up[l]
                    return [(wv[:, :, 0:256], wu[:, c * 128:(c + 2) * 128].rearrange("(k p) n -> p k n", p=128)),
                            (wv[:, :, 256:512], wu[:, DFF + c * 128:DFF + (c + 2) * 128].rearrange("(k p) n -> p k n", p=128))]

                def fn(slot, c=c, pr=pr):
                    chunk(c, slot, 0, pr)
                    chunk(c + 1, slot, 1, pr + 1)
                self.step(loads, fn)

            def dloads(slot, c0=c0, G=G):
                wv = slot[:, 0:G * 1024].rearrange("p (g n) -> p g n", g=G)
                return [(wv, self.w_down[l][c0 * 128:(c0 + G) * 128, :].rearrange("(g p) n -> p g n", p=128))]

            def dfn(slot, c0=c0, G=G):
                wv = slot[:, 0:G * 1024].rearrange("p (g n) -> p g n", g=G)
                g2 = self.mod(l, 5)
                for oc in range(8):
                    for tt, (t0, t1) in enumerate(TILES):
                        ps = self.bank("g", [0, 1, 2])
                        for j in range(G):
                            S.matmul(ps[:], wv[:, j, oc * 128:(oc + 1) * 128], P[:, j, t0:t1], start=(j == 0), stop=(j == G - 1))
                        cd = COND[tt]
                        S.stt(self.X[:, oc, t0:t1], ps[:], g2[:, oc, cd:cd + 1], self.X[:, oc, t0:t1], ALU.mult, ALU.add)
            self.step(dloads, dfn)
            c0 += G

    def fnet(self, l, jf):
        S = self.S
        self.step(None, lambda slot: self.norm_mod(self.AMOD[:, l, 0], self.mod(l, 0)))
        off = self.scr_main
        AB, w = self.carve(off, [12, 4, 512], BF16); off += w
        Fm, w = self.carve(off, [8, NT], BF16); off += w
        assert off <= self.scr_words
        CS3 = self.CS3

        def stage1(slot):
            S.dma("pool", self.fb_row[:], self.fnet_b_d[:, jf * D:(jf + 1) * D])
            n = 0
            for ti in range(12):
                for g in range(4):
                    ps = self.bank("x")
                    for j in range(2):
                        S.matmul(ps[:], self.H[:, 2 * g + j, ti * 128:(ti + 1) * 128], CS3[:, j, 0:512], start=(j == 0), stop=(j == 1))
                    S.copy("act" if n % 2 else "dve", AB[:, ti, g, :], ps[:])
                    n += 1
        self.step(None, stage1)

        def stage2p(slot):
            for cc in range(8):
                g, hf = cc // 2, cc % 2
                ps = self.bank("x")
                for s in range(2):
                    n = 0
                    for tk in range(2):
                        for part in range(2):
                            S.matmul(ps[:, s * 256:(s + 1) * 256], AB[:, 2 * s + tk, g, part * 256 + hf * 128:part * 256 + (hf + 1) * 128],
                                     CS3[:, tk, part * 512:part * 512 + 256], start=(n == 0), stop=(n == 3))
                            n += 1
                S.act(Fm[:, cc, 0:512], ps[:], AF.Copy, scale=1.0 / 256.0)
        self.step(None, stage2p)

        for q in range(4):
            def loads(slot, q=q):
                return [(slot[:, 0:4096].rearrange("p (a b) -> p a b", a=4),
                         self.tab[q].rearrange("p a k u -> p (a k u)").rearrange("p (a b) -> p a b", a=4))]

            def fn(slot, q=q):
                tv = slot[:, 0:4096].rearrange("p (a k u) -> p a k u", a=2, k=8)
                for cc in range(8):
                    g, hf = cc // 2, cc % 2
                    ps = self.bank("x")
                    n = 0
                    for tk in range(8):
                        for part in range(2):
                            S.matmul(ps[:, 0:256], AB[:, 4 + tk, g, part * 256 + hf * 128:part * 256 + (hf + 1) * 128],
                                     tv[:, part, tk, :], start=(n == 0), stop=(n == 15))
                            n += 1
                    S.act(Fm[:, cc, 512 + q * 256:512 + (q + 1) * 256], ps[:, 0:256], AF.Copy, scale=1.0 / 512.0)
            self.step(loads, fn)

        for half in range(2):
            def loads(slot, half=half):
                wv = slot[:, 0:4096].rearrange("p (k n) -> p k n", k=8)
                return [(wv, self.fnet_w[jf][:, half * 512:(half + 1) * 512].rearrange("(k p) n -> p k n", p=128))]

            def fn(slot, half=half):
                wv = slot[:, 0:4096].rearrange("p (k n) -> p k n", k=8)
                g1 = self.mod(l, 2)
                for ocl in range(4):
                    oc = half * 4 + ocl
                    for tt, (t0, t1) in enumerate(TILES):
                        ps = self.bank("x")
                        for kc in range(8):
                            S.matmul(ps[:], wv[:, kc, ocl * 128:(ocl + 1) * 128], Fm[:, kc, t0:t1], start=(kc == 0), stop=False)
                        S.matmul(ps[:], self.fb_row[0:1, oc * 128:(oc + 1) * 128], self.ones_bf[0:1, :],
                                 start=False, stop=True)
                        cd = COND[tt]
                        S.stt(self.X[:, oc, t0:t1], ps[:], g1[:, oc, cd:cd + 1], self.X[:, oc, t0:t1], ALU.mult, ALU.add)
            self.step(loads, fn)

    def gdn(self, l):
        S = self.S
        CF = self.consts
        cf = lambda k: CF[:, k * 128:(k + 1) * 128]
        ident_bf = self.ident_bf
        self.step(None, lambda slot: self.norm_mod(self.AMOD[:, l, 0], self.mod(l, 0)))
        off = 0
        def cv(shape, dt=F32):
            nonlocal off
            v, w = self.carve(off, shape, dt)
            off += w
            return v
        GA = cv([12, 16]); BA = cv([12, 16])
        Z = cv([NT], BF16)
        QKV = cv([3, NT], BF16)
        K_tm = cv([12, 128], BF16); V_tm = cv([12, 128], BF16)
        O_tm = cv([12, 128])
        ST_T = cv([8, 2, 128], BF16); ST_Q = cv([8, 2, 128], BF16); ST_QD = cv([8, 2, 128], BF16); ST_KD = cv([8, 2, 128], BF16)
        CCE = cv([8, 8])
        Sf = [cv([128]) for _ in range(4)]; Sb = [cv([128], BF16) for _ in range(4)]
        RP = [cv([128], BF16) for _ in range(4)]; VN = [cv([128], BF16) for _ in range(4)]
        SSQ = cv([12])
        base = off
        CP = cv([3, 1548], BF16); DG5 = cv([15, 128], BF16); SQ = cv([512], BF16); RS = cv([512])
        T0 = cv([12, 16]); EA = cv([12, 16])
        endA = off
        off = base
        NB = 5
        GM2 = []; EM = []; ER = []; T1 = []; CC = []; LNs = []; PQs = []; MBs = []
        for _ in range(NB):
            o1 = off
            GM2.append(cv([2, 128]))
            ln1, _w = self.carve(o1, [512], BF16)
            o2 = off
            EM.append(cv([512], BF16))
            ln2, _w = self.carve(o2, [512], BF16)
            ER.append(cv([256], BF16)); T1.append(cv([2, 128], BF16)); CC.append(cv([8]))
            LNs.append([cv([512], BF16), ln1, ln2])
            PQs.append(cv([512], BF16))
            MBs.append(cv([512], BF16))
        endB = off
        off = base
        SQ2 = cv([12, 128]); ON = cv([12, 128], BF16); OGh = cv([NT], BF16)
        TMPX = [cv([512]) for _ in range(2)]
        endC = off
        off = max(endA, endB, endC)
        assert off <= self.scr_words, off
        cnt = {"pre": 0, "sc": 0}
        one_col = CF[:, C_ONE * 128:C_ONE * 128 + 1]
        ones_f, neg_f = cf(C_ONE), cf(C_NEG)
        MdT2 = CF[:, 7 * 128:9 * 128].rearrange("p (d c) -> p d c", d=2)
        SM2 = CF[:, 5 * 128:7 * 128].rearrange("p (d c) -> p d c", d=2)
        bc2 = lambda col2: col2.unsqueeze(2).broadcast_to([128, 2, 128])
        v22 = lambda t: t.rearrange("p (a d c) -> p a d c", a=2, d=2)
        v2 = lambda t: t.rearrange("p (d c) -> p d c", d=2)
        MdT2b = self.consts_bf[:, 7 * 128:9 * 128].rearrange("p (d c) -> p d c", d=2)
        SM2b = self.consts_bf[:, 5 * 128:7 * 128].rearrange("p (d c) -> p d c", d=2)

        def gloads(slot):
            return [(slot[:, 0:256].rearrange("p (k n) -> p k n", k=8), self.dn_wg.rearrange("(k p) n -> p k n", p=128))]

        def gfn(slot):
            wg = slot[:, 0:256].rearrange("p (k n) -> p k n", k=8)
            if self.cfg.get("gcut", 9) < 0:
                return
            if self.cfg.get("gcut", 9) < 1:
                return
            ps = self.bank("x")
            for ti in range(12):
                for kc in range(8):
                    S.matmul(ps[:, ti * 32:(ti + 1) * 32], self.H[:, kc, ti * 128:(ti + 1) * 128], wg[:, kc, :],
                             start=(kc == 0), stop=(kc == 7))
            pv = ps[:, 0:384].rearrange("p (t d k h) -> p t d k h", t=12, d=2, k=2)
            gp = self.gpar[:].rearrange("p (a t n) -> p a t n", a=2, t=12)
            for d in range(2):
                S.tt("dve", T0[:, :, d * 8:(d + 1) * 8], pv[:, :, d, 0, :], gp[:, 1, :, d * 8:(d + 1) * 8], ALU.add)
                S.act(BA[:, :, d * 8:(d + 1) * 8], pv[:, :, d, 1, :], AF.Sigmoid)
            cut = self.cfg.get("gcut", 9)
            if cut < 2:
                return
            S.act(T0, T0, AF.Exp)
            if cut < 3:
                return
            S.act(T0, T0, AF.Ln, bias=one_col)
            if cut < 4:
                return
            S.act(EA, gp[:, 0], AF.Exp)
            S.stt(GA, T0, -1.0, EA, ALU.mult, ALU.mult)
        self.step(gloads, gfn)
        stage = self.cfg.get("gdn_stage", 9)
        nheads = self.cfg.get("gdn_heads", 8)

        def pre2(h, ti, n, b):
            LN0b, LN1b, LN2b = LNs[b]
            PQ = [PQs[b], PQs[b]]
            MB = [MBs[b], MBs[b]]
            T1b = T1[b]
            g2 = GA[:, ti, h:h + 9:8]
            b2 = BA[:, ti, h:h + 9:8]
            kT = QKV[:, 1, ti * 128:(ti + 1) * 128]
            qT = QKV[:, 0, ti * 128:(ti + 1) * 128]
            l4 = lambda t: t.rearrange("p (a c) -> p a c", a=4)
            def mask(lv):
                S.tt("pool", l4(MB[lv % 2]), l4(LN0b), self.mask2[:, lv, 0:128].unsqueeze(1).broadcast_to([128, 4, 128]), ALU.mult)
                return v22(MB[lv % 2])
            S.tt("dve", GM2[b], MdT2, bc2(g2), ALU.mult)
            GMf = GM2[b].rearrange("p d c -> p (d c)")
            pD = self.ps[b]
            S.matmul(pD[:, 0:256], neg_f, GMf, start=True, stop=False)
            for d in range(2):
                S.matmul(pD[:, d * 128:(d + 1) * 128], GM2[b][:, d, :], ones_f, start=False, stop=(d == 1))
            S.matmul(pD[:, 256:512], ones_f, GMf, start=True, stop=False)
            for d in range(2):
                S.matmul(pD[:, 256 + d * 128:256 + (d + 1) * 128], GM2[b][:, d, :], neg_f, start=False, stop=(d == 1))
            yield
            S.act(EM[b], pD[:, 0:512], AF.Relu, scale=-1.0)
            S.act(EM[b], EM[b], AF.Exp, scale=-1.0)
            pG = self.ps[b]
            S.matmul(pG[:, 0:256], ones_f, GMf)
            for d in range(2):
                S.matmul(pG[:, 256 + d:257 + d], GM2[b][:, d, :], ones_f[:, 0:1])
            S.matmul(pG[:, 258:260], ones_f, g2)
            yield
            S.act(ER[b], pG[:, 0:256], AF.Exp)
            S.copy("act", CC[b][:, 0:4], pG[:, 256:260])
            S.act(CCE[:, n, 0:4], CC[b][:, 0:4], AF.Exp)
            for d in range(2):
                S.act(CCE[:, n, 4 + d:5 + d], CC[b][:, d:d + 1], AF.Exp, bias=CC[b][:, 2 + d:3 + d], scale=-1.0)
            S.act(CCE[:, n, 6:8], CCE[:, n, 0:2], AF.Copy, scale=-1.0)
            S.tt("pool", T1b, SM2b, bc2(b2), ALU.mult)
            S.tt("pool", EM[b][:, 0:256].rearrange("p (d c) -> p d c", d=2), EM[b][:, 0:256].rearrange("p (d c) -> p d c", d=2), T1b, ALU.mult)
            S.tt("pool", EM[b][:, 256:512].rearrange("p (d c) -> p d c", d=2), EM[b][:, 256:512].rearrange("p (d c) -> p d c", d=2), MdT2b, ALU.mult)
            pB = self.ps[b]
            S.matmul(pB[:, 0:128], kT, kT)
            S.matmul(pB[:, 128:256], kT, qT)
            yield
            E2, ET2 = v2(EM[b][:, 0:256]), v2(EM[b][:, 256:512])
            LN0 = v22(LN0b)
            S.tt("dve", LN0[:, 0], pB[:, 0:128].unsqueeze(1).broadcast_to([128, 2, 128]), E2, ALU.mult)
            yield
            S.tt("dve", ST_Q[:, n], pB[:, 128:256].unsqueeze(1).broadcast_to([128, 2, 128]), ET2, ALU.mult)
            pT = self.ps[b][:].bitcast(BF16)
            for d in range(2):
                S.transpose(pT[:, d * 128:(d + 1) * 128], LN0[:, 0, d, :], ident_bf)
            S.copy("act", LN0b[:, 256:512], pT[:, 0:256])
            S.tt("pool", ST_QD[:, n], qT.unsqueeze(1).broadcast_to([128, 2, 128]), v2(ER[b]), ALU.mult)
            S.tt("pool", ST_KD[:, n], K_tm[:, ti, :].unsqueeze(1).broadcast_to([128, 2, 128]), bc2(CCE[:, n, 4:6]), ALU.mult)
            yield
            LM0 = mask(0)
            S.tt("dve", l4(PQ[0]), ident_bf.unsqueeze(1).broadcast_to([128, 4, 128]), l4(MB[0]), ALU.subtract)
            def blk(ps, a, d):
                return ps[:, (a * 2 + d) * 128:(a * 2 + d + 1) * 128]
            Lc, Nc = LM0[:, 0], LM0[:, 1]
            cur = 0
            for r in range(2):
                pR = self.ps[b]
                for d in range(2):
                    S.matmul(blk(pR, 0, d), Nc[:, d, :], Lc[:, d, :])
                    S.matmul(blk(pR, 1, d), Lc[:, d, :], Nc[:, d, :])
                LNr_b = LN1b if r == 0 else LN2b
                S.copy("act", LNr_b, pR[:, 0:512])
                if r == 1:
                    LMn = mask(1)
                yield
                LNr = v22(LNr_b)
                Lc, Nc = LNr[:, 0], LNr[:, 1]
                PQc = v22(PQ[cur])
                pP = self.ps[b]
                for d in range(2):
                    S.matmul(blk(pP, 0, d), Nc[:, d, :], PQc[:, 0, d, :])
                    S.matmul(blk(pP, 1, d), Lc[:, d, :], PQc[:, 1, d, :])
                S.tt("dve", PQ[1 - cur], pP[:, 0:512], PQ[cur], ALU.add)
                cur = 1 - cur
                yield
            for lv in range(1, 5):
                LMc = LMn
                PQc = v22(PQ[cur])
                YY = v22(LN1b)
                pY = self.ps[b]
                for d in range(2):
                    S.matmul(blk(pY, 1, d), LMc[:, 0, d, :], PQc[:, 1, d, :])
                    if lv < 4:
                        S.matmul(blk(pY, 0, d), LMc[:, 1, d, :], PQc[:, 0, d, :])
                if lv < 4:
                    S.copy("act", LN1b, pY[:, 0:512])
                    LMn = mask(lv + 1)
                else:
                    S.copy("act", LN1b[:, 256:512], pY[:, 256:512])
                yield
                pU = self.ps[b]
                for d in range(2):
                    S.matmul(blk(pU, 1, d), PQc[:, 0, d, :], YY[:, 1, d, :])
                    if lv < 4:
                        S.matmul(blk(pU, 0, d), PQc[:, 1, d, :], YY[:, 0, d, :])
                if lv < 4:
                    S.tt("dve", PQ[1 - cur], PQ[cur], pU[:, 0:512], ALU.subtract)
                else:
                    S.tt("dve", PQ[1 - cur][:, 256:512], PQ[cur][:, 256:512], pU[:, 256:512], ALU.subtract)
                cur = 1 - cur
                yield
            S.tt("pool", ST_T[:, n], v22(PQ[cur])[:, 1], bc2(b2), ALU.mult)

        def scan_step(ti, n, d, sb_i, first):
            b = sb_i
            kT = QKV[:, 1, ti * 128:(ti + 1) * 128]
            pA = self.bank("x")
            S.matmul(pA[:, 0:128], kT, Sb[sb_i])
            S.stt(RP[b], pA[:, 0:128], CCE[:, n, 6 + d:7 + d], V_tm[:, ti, :], ALU.mult, ALU.add)
            yield
            pB = self.bank("x")
            S.matmul(pB[:, 0:128], ST_T[:, n, d, :], RP[b])
            S.copy("act", VN[b], pB[:, 0:128])
            yield
            pC = self.bank("x")
            S.matmul(pC[:, 0:128], ST_QD[:, n, d, :], Sb[sb_i], start=True, stop=False)
            S.matmul(pC[:, 0:128], ST_Q[:, n, d, :], VN[b], start=False, stop=True)
            S.tt("dve", O_tm[:, ti, :], O_tm[:, ti, :], pC[:, 0:128], ALU.add)
            pE = self.bank("x")
            S.matmul(pE[:, 0:128], ST_KD[:, n, d, :], VN[b])
            S.stt(Sf[sb_i], Sf[sb_i], CCE[:, n, 2 + d:3 + d], pE[:, 0:128], ALU.mult, ALU.add)
            S.copy("act", Sb[sb_i], Sf[sb_i])
            yield

        for h in range(nheads if stage >= 2 else 0):
            def loadsA(slot, h=h):
                return [(slot[:, 0:4096].rearrange("p (k n) -> p k n", k=8), self.dn_wh[h].rearrange("(k p) n -> p k n", p=128))]

            def fnA(slot, h=h):
                wv = slot[:, 0:4096].rearrange("p (k n) -> p k n", k=8)
                S.memset("dve", CP, 0.0)
                S.tt("dve", DG5.rearrange("p (j t) c -> p j t c", j=3),
                     ident_bf.unsqueeze(1).unsqueeze(1).broadcast_to([128, 3, 5, 128]),
                     self.dcwT[:].rearrange("p (j h t) -> p j h t", j=3, h=8)[:, :, h, :].unsqueeze(3).broadcast_to([128, 3, 5, 128]),
                     ALU.mult)
                cpP = lambda j: CP[:, j, 0:520].rearrange("p (s t) -> p s t", s=2)
                for j in range(4):
                    for tt, (t0, t1) in enumerate(TILES):
                        ps = self.bank("x")
                        for kc in range(8):
                            S.matmul(ps[:], wv[:, kc, j * 128:(j + 1) * 128], self.H[:, kc, t0:t1], start=(kc == 0), stop=(kc == 7))
                        if j == 3:
                            S.act(Z[:, t0:t1], ps[:], AF.Silu)
                        elif tt == 0:
                            S.copy("act", cpP(j)[:, :, 2:258], ps[:].rearrange("p (s t) -> p s t", s=2))
                        else:
                            S.copy("act", CP[:, j, 522 + (tt - 1) * 512:522 + tt * 512], ps[:])
                for j in range(3):
                    for tt, (t0, t1) in enumerate(TILES):
                        ps = self.bank("x")
                        for tap in range(5):
                            if tt == 0:
                                S.matmul(ps[:].rearrange("p (s t) -> p s t", s=2), DG5[:, j * 5 + tap, :], cpP(j)[:, :, tap:tap + 256],
                                         start=(tap == 0), stop=(tap == 4))
                            else:
                                st = 520 + (tt - 1) * 512 + tap
                                S.matmul(ps[:], DG5[:, j * 5 + tap, :], CP[:, j, st:st + 512], start=(tap == 0), stop=(tap == 4))
                        S.act(QKV[:, j, t0:t1], ps[:], AF.Silu)
                for j in range(2):
                    for tt, (t0, t1) in enumerate(TILES):
                        S.act(SQ, QKV[:, j, t0:t1], AF.Square)
                        ps = self.bank("x")
                        S.matmul(ps[:], self.ones_bf[:, 0:128], SQ)
                        if j == 0:
                            S.act(RS, ps[:], AF.Sqrt, bias=self.eps_col[:, 1:2], scale=128.0)
                        else:
                            S.act(RS, ps[:], AF.Sqrt, bias=self.eps_col[:, 0:1], scale=1.0)
                        S.recip(RS, RS)
                        S.tt("dve", QKV[:, j, t0:t1], QKV[:, j, t0:t1], RS, ALU.mult)
                for (j, dst) in ((1, K_tm), (2, V_tm)):
                    for g4 in range(3):
                        pb = self.bank("x")[:].bitcast(BF16)
                        for i in range(4):
                            ti = g4 * 4 + i
                            S.transpose(pb[:, i * 128:(i + 1) * 128], QKV[:, j, ti * 128:(ti + 1) * 128], ident_bf)
                        S.copy("act", dst[:, g4 * 4:(g4 + 1) * 4, :], pb[:, 0:512].rearrange("p (a b) -> p a b", a=4))
            self.step(loadsA, fnA)

            def loadsB(slot, h=h):
                return [(slot[:, 0:1024], self.dn_wo[h * 128:(h + 1) * 128, :])]

            def fnB(slot, h=h):
                wo = slot[:, 0:1024]
                seqs = [([0, 1], 0), ([2, 3], 1), (list(range(4, 12)), 2)]
                sbi = 0
                def interleave(gens):
                    gens = list(gens)
                    while gens:
                        for g in list(gens):
                            try:
                                next(g)
                            except StopIteration:
                                gens.remove(g)

                def chainx(tiles, sidx, d, n0, sb_i):
                    order = tiles if d == 0 else tiles[::-1]
                    if sidx == 2:
                        S.dma("sp", Sf[sb_i], self.sd[d, h])
                        S.copy("act", Sb[sb_i], Sf[sb_i])
                    else:
                        S.memset("dve", Sf[sb_i], 0.0)
                        S.memset("dve", Sb[sb_i], 0.0)
                    for ti in order:
                        yield from scan_step(ti, n0 + tiles.index(ti), d, sb_i, first=False)
                    if sidx < 2:
                        S.dma("sp", self.nsd[sidx, d, h], Sf[sb_i])

                S.memset("dve", O_tm, 0.0)
                if stage >= 3:
                    for grp in ([0, 1, 2, 3],):
                        interleave([pre2(h, ti, ti, k) for k, ti in enumerate(grp)])
                    if stage >= 4:
                        interleave([chainx([0, 1], 0, d, 0, 2 * 0 + d) for d in range(2)] +
                                   [chainx([2, 3], 1, d, 2, 2 * 1 + d) for d in range(2)])
                    for grp in ([4, 5, 6, 7, 8], [9, 10, 11]):
                        interleave([pre2(h, ti, ti - 4, k) for k, ti in enumerate(grp)])
                    if stage >= 4:
                        interleave([chainx(list(range(4, 12)), 2, d, 0, d) for d in range(2)])
                if stage < 5:
                    return
                S.act(SQ2, O_tm, AF.Square)
                S.op("dve", lambda e: e.reduce_sum(SSQ, SQ2, mybir.AxisListType.X), [SQ2], [SSQ])
                S.act(SSQ, SSQ, AF.Sqrt, bias=self.eps_col[:, 0:1], scale=1.0 / 128.0)
                S.recip(SSQ, SSQ)
                S.tt("dve", ON, O_tm, SSQ.unsqueeze(2).broadcast_to([128, 12, 128]), ALU.mult)
                for g4 in range(3):
                    pb = self.bank("x")[:].bitcast(BF16)
                    for i in range(4):
                        ti = g4 * 4 + i
                        S.transpose(pb[:, i * 128:(i + 1) * 128], ON[:, ti, :], ident_bf)
                    S.stt(OGh[:, g4 * 512:(g4 + 1) * 512], pb[:, 0:512], self.dn_normT[:, 0:1], Z[:, g4 * 512:(g4 + 1) * 512],
                          ALU.mult, ALU.mult)
                g1 = self.mod(l, 2)
                for oc in range(8):
                    for tt, (t0, t1) in enumerate(TILES):
                        ps = self.bank("x")
                        S.matmul(ps[:], wo[:, oc * 128:(oc + 1) * 128], OGh[:, t0:t1])
                        cd = COND[tt]
                        if (oc * 3 + tt) % 2 == 0:
                            S.stt(self.X[:, oc, t0:t1], ps[:], g1[:, oc, cd:cd + 1], self.X[:, oc, t0:t1], ALU.mult, ALU.add)
                        else:
                            tx = TMPX[((oc * 3 + tt) // 2) % 2]
                            S.act(tx, ps[:], AF.Identity, scale=g1[:, oc, cd:cd + 1])
                            S.tt("pool", self.X[:, oc, t0:t1], self.X[:, oc, t0:t1], tx, ALU.add)
            self.step(loadsB, fnB)

    def mlstm(self, l):
        S = self.S
        CF = self.consts
        cf = lambda k: CF[:, k * 128:(k + 1) * 128]
        ident_bf = self.ident_bf
        self.step(None, lambda slot: self.norm_mod(self.AMOD[:, l, 0], self.mod(l, 0)))
        off = 0
        def cv(shape, dt=F32):
            nonlocal off
            v, w = self.carve(off, shape, dt)
            off += w
            return v
        LI = cv([12, 16]); LF = cv([12, 16]); T0 = cv([12, 16])
        LIr = cv([512]); LFr = cv([512]); SCN = cv([2, 2, 256]); NBF = cv([2])
        MFB = cv([2, 2]); EMF = cv([2, 2]); DGE = cv([2, 2, 16]); EMB = cv([2, 2, 16]); EM0 = cv([16])
        qTs = [cv([NT], BF16) for _ in range(2)]; kTs = [cv([NT], BF16) for _ in range(2)]
        vTs = [cv([NT], BF16) for _ in range(2)]; OGts = [cv([NT], BF16) for _ in range(2)]
        V_tms = [cv([12, 129], BF16) for _ in range(2)]; K_tms = [cv([12, 64], BF16) for _ in range(2)]
        Hs = cv([12, 128])
        off_st = off
        ST_S = cv([8, 2, 128], BF16); ST_QB = cv([8, 2, 128], BF16); ST_KW = cv([8, 2, 64], BF16); CCE = cv([8, 4])
        SQ2, _ = self.carve(off_st, [12, 128]); SSQ = cv([12])
        off_tmp = off
        ON, _w = self.carve(off_tmp, [12, 128], BF16); OGh, _w2 = self.carve(off_tmp + 768, [NT], BF16)
        TMPX = [self.carve(off_tmp + 1536 + 512 * i, [512])[0] for i in range(2)]
        NBm = 4
        FM2 = [cv([2, 128]) for _ in range(NBm)]
        EMn = [cv([256]) for _ in range(NBm)]
        ER = [cv([256], BF16) for _ in range(NBm)]; CC = [cv([8]) for _ in range(NBm)]
        MdT2 = CF[:, 7 * 128:9 * 128].rearrange("p (d c) -> p d c", d=2)
        bc2 = lambda col2: col2.unsqueeze(2).broadcast_to([128, 2, 128])
        v2 = lambda t: t.rearrange("p (d c) -> p d c", d=2)
        CA = [cv([129]) for _ in range(4)]; CAb = [cv([130], BF16) for _ in range(4)]; CAo = [cv([129]) for _ in range(4)]
        DN = [cv([2]) for _ in range(4)]
        assert off <= self.scr_words, off
        cnt = {"pre": 0, "sc": 0}
        one_col = CF[:, C_ONE * 128:C_ONE * 128 + 1]
        ones_f, neg_f = cf(C_ONE), cf(C_NEG)
        mp = self.mpar[:].rearrange("p (a t n) -> p a t n", a=2, t=12)

        def gloads(slot):
            return [(slot[:, 0:256].rearrange("p (k n) -> p k n", k=8), self.ml_wg.rearrange("(k p) n -> p k n", p=128)),
                    (slot[:, 256:512].rearrange("p (k n) -> p k n", k=8), self.ml_wgr.rearrange("(k p) n -> p k n", p=128))]

        def gfn(slot):
            wg = slot[:, 0:256].rearrange("p (k n) -> p k n", k=8)
            wr = slot[:, 256:512].rearrange("p (k n) -> p k n", k=8)
            for vv in V_tms:
                S.memset("dve", vv[:, :, 128:129], 1.0)
            ps = self.bank("x")
            for ti in range(12):
                for kc in range(8):
                    S.matmul(ps[:, ti * 32:(ti + 1) * 32], self.H[:, kc, ti * 128:(ti + 1) * 128], wg[:, kc, :],
                             start=(kc == 0), stop=(kc == 7))
            pv = ps[:, 0:384].rearrange("p (t d k h) -> p t d k h", t=12, d=2, k=2)
            for d in range(2):
                S.tt("dve", LI[:, :, d * 8:(d + 1) * 8], pv[:, :, d, 0, :], mp[:, 0, :, d * 8:(d + 1) * 8], ALU.add)
                S.tt("dve", T0[:, :, d * 8:(d + 1) * 8], pv[:, :, d, 1, :], mp[:, 1, :, d * 8:(d + 1) * 8], ALU.add)
            S.act(T0, T0, AF.Exp, scale=-1.0)
            S.act(T0, T0, AF.Ln, bias=one_col)
            S.ts("dve", LF, T0, -1.0)
            pr = self.bank("x")
            for kc in range(8):
                S.matmul(pr[0:16, 0:512], wr[:, kc, 0:16], self.H[:, kc, 0:512], start=(kc == 0), stop=(kc == 7))
            S.act(LIr[0:16, :], pr[0:16, 0:512], AF.Identity, bias=self.mparT[0:16, 0:1])
            pr2 = self.bank("x")
            for kc in range(8):
                S.matmul(pr2[0:16, 0:512], wr[:, kc, 16:32], self.H[:, kc, 0:512], start=(kc == 0), stop=(kc == 7))
            S.ts("dve", NBF[0:16, 0:1], self.mparT[0:16, 1:2], -1.0)
            S.act(LFr[0:16, :], pr2[0:16, 0:512], AF.Exp, bias=NBF[0:16, 0:1], scale=-1.0)
            S.act(LFr[0:16, :], LFr[0:16, :], AF.Ln, bias=one_col[0:16, :])
            S.ts("dve", LFr[0:16, :], LFr[0:16, :], -1.0)
            for s in range(2):
                for fb in range(2):
                    if fb == 0:
                        d0, d1 = LFr[0:16, s * 256:(s + 1) * 256], LIr[0:16, s * 256:(s + 1) * 256]
                    elif s == 0:
                        d0, d1 = LFr[0:16, 255::-1], LIr[0:16, 255::-1]
                    else:
                        d0, d1 = LFr[0:16, 511:255:-1], LIr[0:16, 511:255:-1]
                    o = SCN[0:16, s, fb, :]
                    S.op("dve", lambda e, o=o, d0=d0, d1=d1: e.tensor_tensor_scan(o, d0, d1, 0.0, ALU.add, ALU.max), [d0, d1], [o])
                    S.copy("dve", MFB[0:16, s, fb:fb + 1], SCN[0:16, s, fb, 255:256])
                S.dma("sp", self.nsm[s, 0, :], MFB[0:8, s, 0:1])
                S.dma("sp", self.nsm[s, 1, :], MFB[8:16, s, 1:2])
            S.act(EMF[0:16], MFB[0:16], AF.Exp, scale=-1.0)
            pe = self.bank("x")
            for s in range(2):
                for fb in range(2):
                    S.ts("dve", DGE[0:16, s, fb, :], CF[0:16, 0:16], EMF[0:16, s, fb:fb + 1])
                    c0 = (s * 2 + fb) * 16
                    S.matmul(pe[0:64, c0:c0 + 16], ones_f[0:16, 0:64], DGE[0:16, s, fb, :])
            S.copy("dve", EMB[0:64].rearrange("p a b c -> p (a b c)"), pe[0:64, 0:64])
            S.act(EM0[0:64], self.smm[0:64, :], AF.Exp)
        self.step(gloads, gfn)

        cur = {}

        def pre2(h, ti, n, b):
            qT, kT, K_tm = cur["qT"], cur["kT"], cur["K_tm"]
            lf2 = LF[:, ti, h:h + 9:8]
            li2 = LI[:, ti, h:h + 9:8]
            S.tt("dve", FM2[b], MdT2, bc2(lf2), ALU.mult)
            FMf = FM2[b].rearrange("p d c -> p (d c)")
            pD = self.ps[b]
            S.matmul(pD[:, 0:256], ones_f, FMf, start=True, stop=False)
            for d in range(2):
                S.matmul(pD[:, d * 128:(d + 1) * 128], FM2[b][:, d, :], neg_f, start=False, stop=(d == 1))
            S.matmul(pD[:, 256:512], ones_f, FMf)
            yield
            S.act(EMn[b], pD[:, 0:256], AF.Relu, scale=-1.0)
            for d in range(2):
                S.act(EMn[b][:, d * 128:(d + 1) * 128], EMn[b][:, d * 128:(d + 1) * 128], AF.Exp, bias=li2[:, d:d + 1], scale=-1.0)
            S.tt("pool", v2(EMn[b]), v2(EMn[b]), MdT2, ALU.mult)
            S.act(ER[b][0:64, :], pD[0:64, 256:512], AF.Exp)
            pG = self.ps[b]
            for d in range(2):
                S.matmul(pG[:, d:d + 1], FM2[b][:, d, :], ones_f[:, 0:1])
            S.matmul(pG[:, 2:4], ones_f, lf2)
            yield
            S.copy("act", CC[b][:, 0:4], pG[:, 0:4])
            S.tt("pool", CC[b][:, 4:6], CC[b][:, 2:4], li2, ALU.add)
            S.act(CCE[:, n, 0:2], CC[b][:, 2:4], AF.Exp)
            for d in range(2):
                S.act(CCE[:, n, 2 + d:3 + d], CC[b][:, d:d + 1], AF.Exp, bias=CC[b][:, 4 + d:5 + d], scale=-1.0)
            yield
            pB = self.ps[b]
            S.matmul(pB[:, 0:128], kT[0:64, ti * 128:(ti + 1) * 128], qT[0:64, ti * 128:(ti + 1) * 128])
            S.tt("dve", ST_S[:, n], pB[:, 0:128].unsqueeze(1).broadcast_to([128, 2, 128]), v2(EMn[b]), ALU.mult)
            S.tt("pool", ST_QB[0:64, n], qT[0:64, ti * 128:(ti + 1) * 128].unsqueeze(1).broadcast_to([64, 2, 128]),
                 v2(ER[b])[0:64], ALU.mult)
            S.tt("pool", ST_KW[:, n], K_tm[:, ti, :].unsqueeze(1).broadcast_to([128, 2, 64]),
                 CCE[:, n, 2:4].unsqueeze(2).broadcast_to([128, 2, 64]), ALU.mult)
            yield

        def scan_step(ti, n, d, ci, first):
            b = ci
            V_tm = cur["V_tm"]
            pN = self.bank("sc", [0, 1, 2, 3, 4, 5])
            S.matmul(pN[:, 0:129], ST_QB[0:64, n, d, :], CAb[ci][0:64, 0:129], start=True, stop=False)
            S.matmul(pN[:, 0:129], ST_S[:, n, d, :], V_tm[:, ti, :], start=False, stop=True)
            S.act(DN[b][:, 0:1], pN[:, 128:129], AF.Abs)
            S.ts("dve", DN[b][:, 0:1], DN[b][:, 0:1], 1.0, None, ALU.max)
            S.recip(DN[b][:, 0:1], DN[b][:, 0:1])
            S.stt(Hs[:, ti, :], pN[:, 0:128], DN[b][:, 0:1], Hs[:, ti, :], ALU.mult, ALU.add)
            yield
            pS = self.bank("sc", [0, 1, 2, 3, 4, 5])
            S.matmul(pS[0:64, 0:129], ST_KW[:, n, d, :], V_tm[:, ti, :])
            S.stt(CA[ci][0:64, :], CA[ci][0:64, :], CCE[0:64, n, d:d + 1], pS[0:64, 0:129], ALU.mult, ALU.add)
            S.copy("act", CAb[ci][0:64, 0:129], CA[ci][0:64, :])
            yield

        nheads = self.cfg.get("ml_heads", 8)

        def interleave_g(gens):
            gens = list(gens)
            while gens:
                for g in list(gens):
                    try:
                        next(g)
                    except StopIteration:
                        gens.remove(g)
                yield

        def genA(h, wv):
            p = h % 2
            qT, kT, vT, OGt, V_tm, K_tm = qTs[p], kTs[p], vTs[p], OGts[p], V_tms[p], K_tms[p]
            for j in range(4):
                lo, hi = [(0, 64), (64, 128), (128, 256), (256, 384)][j]
                M = hi - lo
                for tt, (t0, t1) in enumerate(TILES):
                    ps = self.bank("fa", [6, 7])
                    for kc in range(8):
                        S.matmul(ps[0:M, :], wv[:, kc, lo:hi], self.H[:, kc, t0:t1], start=(kc == 0), stop=(kc == 7))
                    if j == 0:
                        S.act(qT[0:64, t0:t1], ps[0:64, :], AF.Copy, scale=0.125)
                    elif j == 1:
                        S.copy("act", kT[0:64, t0:t1], ps[0:64, :])
                    elif j == 2:
                        S.copy("act", vT[:, t0:t1], ps[:])
                    else:
                        S.act(OGt[:, t0:t1], ps[:], AF.Sigmoid)
                    yield
            for g4 in range(3):
                pb = self.bank("fa", [6, 7])[:].bitcast(BF16)
                for i in range(4):
                    ti = g4 * 4 + i
                    S.transpose(pb[:, i * 128:(i + 1) * 128], vT[:, ti * 128:(ti + 1) * 128], ident_bf)
                S.copy("act", V_tm[:, g4 * 4:(g4 + 1) * 4, 0:128], pb[:, 0:512].rearrange("p (a b) -> p a b", a=4))
                yield
            for g4 in range(3):
                pb = self.bank("fa", [6, 7])[:].bitcast(BF16)
                for i in range(4):
                    ti = g4 * 4 + i
                    S.transpose(pb[:, i * 64:(i + 1) * 64], kT[0:64, ti * 128:(ti + 1) * 128], ident_bf[0:64, 0:64])
                S.copy("act", K_tm[:, g4 * 4:(g4 + 1) * 4, :], pb[:, 0:256].rearrange("p (a b) -> p a b", a=4))
                yield

        def genB(h, wo):
            p = h % 2
            cur.update(qT=qTs[p], kT=kTs[p], K_tm=K_tms[p], V_tm=V_tms[p])
            OGt = OGts[p]

            def chainx(tiles, sidx, d, n0, ci):
                order = tiles if d == 0 else tiles[::-1]
                if sidx == 2:
                    S.dma("sp", CA[ci][0:64, :], self.smca[d, h])
                    S.ts("dve", CA[ci][0:64, :], CA[ci][0:64, :], EM0[0:64, d * 8 + h:d * 8 + h + 1])
                    S.copy("act", CAb[ci][0:64, 0:129], CA[ci][0:64, :])
                else:
                    S.memset("dve", CA[ci][0:64, :], 0.0)
                    S.memset("dve", CAb[ci][0:64, :], 0.0)
                for ti in order:
                    yield from scan_step(ti, n0 + tiles.index(ti), d, ci, first=False)
                if sidx < 2:
                    S.ts("dve", CAo[ci][0:64, :], CA[ci][0:64, :], EMB[0:64, sidx, d, d * 8 + h:d * 8 + h + 1])
                    S.dma("sp", self.nsc[sidx, d, h], CAo[ci][0:64, 0:128])
                    S.dma("sp", self.nsn[sidx, d, h, :], CAo[ci][0:64, 128:129])

            S.memset("dve", Hs, 0.0)
            for grp in ([0, 1, 2, 3],):
                yield from interleave_g([pre2(h, ti, ti, k) for k, ti in enumerate(grp)])
            yield from interleave_g([chainx([0, 1], 0, d, 0, d) for d in range(2)] + [chainx([2, 3], 1, d, 2, 2 + d) for d in range(2)])
            for grp in ([4, 5, 6, 7], [8, 9, 10, 11]):
                yield from interleave_g([pre2(h, ti, ti - 4, k) for k, ti in enumerate(grp)])
            yield from interleave_g([chainx(list(range(4, 12)), 2, d, 0, d) for d in range(2)])
            S.act(SQ2, Hs, AF.Square)
            S.op("dve", lambda e: e.reduce_sum(SSQ, SQ2, mybir.AxisListType.X), [SQ2], [SSQ])
            S.act(SSQ, SSQ, AF.Sqrt, bias=self.eps_col[:, 0:1], scale=1.0 / 128.0)
            S.recip(SSQ, SSQ)
            yield
            S.tt("dve", ON, Hs, SSQ.unsqueeze(2).broadcast_to([128, 12, 128]), ALU.mult)
            yield
            for g4 in range(3):
                pb = self.bank("sc", [0, 1, 2, 3, 4, 5])[:].bitcast(BF16)
                for i in range(4):
                    ti = g4 * 4 + i
                    S.transpose(pb[:, i * 128:(i + 1) * 128], ON[:, ti, :], ident_bf)
                S.stt(OGh[:, g4 * 512:(g4 + 1) * 512], pb[:, 0:512], self.ml_normT[:, 0:1], OGt[:, g4 * 512:(g4 + 1) * 512],
                      ALU.mult, ALU.mult)
                yield
            g1 = self.mod(l, 2)
            for oc in range(8):
                for tt, (t0, t1) in enumerate(TILES):
                    ps = self.bank("sc", [0, 1, 2, 3, 4, 5])
                    S.matmul(ps[:], wo[:, oc * 128:(oc + 1) * 128], OGh[:, t0:t1])
                    cd = COND[tt]
                    if (oc * 3 + tt) % 2 == 0:
                        S.stt(self.X[:, oc, t0:t1], ps[:], g1[:, oc, cd:cd + 1], self.X[:, oc, t0:t1], ALU.mult, ALU.add)
                    else:
                        tx = TMPX[((oc * 3 + tt) // 2) % 2]
                        S.act(tx, ps[:], AF.Identity, scale=g1[:, oc, cd:cd + 1])
                        S.tt("pool", self.X[:, oc, t0:t1], self.X[:, oc, t0:t1], tx, ALU.add)
                yield

        def drain(g):
            for _ in g:
                pass

        def loads0(slot):
            return [(slot[:, 0:3072].rearrange("p (k n) -> p k n", k=8), self.ml_wh[0].rearrange("(k p) n -> p k n", p=128))]
        self.step(loads0, lambda slot: drain(genA(0, slot[:, 0:3072].rearrange("p (k n) -> p k n", k=8))))
        for h in range(nheads):
            def loadsH(slot, h=h):
                out = [(slot[:, 3072:4096], self.ml_wo[h * 128:(h + 1) * 128, :])]
                if h + 1 < nheads:
                    out.append((slot[:, 0:3072].rearrange("p (k n) -> p k n", k=8), self.ml_wh[h + 1].rearrange("(k p) n -> p k n", p=128)))
                return out

            def fnH(slot, h=h):
                gens = [genB(h, slot[:, 3072:4096])]
                if h + 1 < nheads:
                    gens.append(genA(h + 1, slot[:, 0:3072].rearrange("p (k n) -> p k n", k=8)))
                drain(interleave_g(gens))
            self.step(loadsH, fnH)

    def build(self):
        cfg = self.cfg
        nc = self.nc
        S = self.S
        self.xT = self.dram_in("xT", [D, NT])
        self.condT = self.dram_in("condT", [128, 16])
        self.w_ada = self.dram_in("w_ada", [4, D, 6 * D])
        b_adaT_d = self.dram_in("b_adaT", [128, 4 * 48])
        nmT_d = self.dram_in("nmT", [128, 72])
        self.w_up = self.dram_in("w_up", [4, D, 2 * DFF])
        cwT_d = self.dram_in("cwT", [128, 4 * NCH * 9])
        cbT_d = self.dram_in("cbT", [128, 4 * NCH])
        self.w_down = self.dram_in("w_down", [4, DFF, D])
        self.fnet_w = self.dram_in("fnet_w", [2, D, D])
        self.fnet_b_d = self.dram_in("fnet_b", [1, 2 * D])
        consts_d = self.dram_in("consts", [128, 1152])
        cs3_d = self.dram_in("cs3", [256, 768])
        self.tab = self.dram_in("tab", [4, 128, 2, 8, 256])
        self.dn_wh = self.dram_in("dn_wh", [8, D, 512])
        self.dn_wg = self.dram_in("dn_wg", [D, 32])
        dcwT_d = self.dram_in("dcwT", [128, 120])
        mask2_d = self.dram_in("mask2", [128, 1280])
        gpar_d = self.dram_in("gpar", [128, 2 * 12 * 16])
        dn_normT_d = self.dram_in("dn_normT", [128, 1])
        self.dn_wo = self.dram_in("dn_wo", [D, D])
        self.sd = self.dram_in("sd", [2, 8, 128, 128])
        self.nsd = self.dram_out("nsd", [2, 2, 8, 128, 128])
        self.ml_wh = self.dram_in("ml_wh", [8, D, 384])
        self.ml_wg = self.dram_in("ml_wg", [D, 32])
        self.ml_wgr = self.dram_in("ml_wgr", [D, 32])
        mpar_d = self.dram_in("mpar", [128, 2 * 12 * 16])
        mparT_d = self.dram_in("mparT", [16, 2])
        ml_normT_d = self.dram_in("ml_normT", [128, 1])
        self.ml_wo = self.dram_in("ml_wo", [D, D])
        self.smca = self.dram_in("smca", [2, 8, 64, 129])
        smm_d = self.dram_in("smm", [64, 16])
        self.nsc = self.dram_out("nsc", [2, 2, 8, 64, 128])
        self.nsn = self.dram_out("nsn", [2, 2, 8, 64])
        self.nsm = self.dram_out("nsm", [2, 2, 8])
        self.yT = self.dram_out("yT", [D, NT])
        ntaps = cfg.get("ntaps", 0)
        self.dbg = self.dram_out("dbg", [ntaps, D, NT]) if ntaps else None

        self.X = self.sb("X", [128, 8, NT])
        self.H = self.sb("H", [128, 8, NT], BF16)
        self.slots = [self.sb("slot%d" % i, [128, 4096], BF16) for i in range(4)]
        self.scr_words = cfg.get("scr_words", 19712)
        self.scr = self.sb("scr", [128, self.scr_words])
        self.scr_tmp = self.scr_words - 3584
        self.scr_main = 0
        self.consts = self.sb("consts_f", [128, 1152])
        self.consts_bf = self.sb("consts_b", [128, 1152], BF16)
        self.ones_bf = self.sb("ones_bf", [128, 512], BF16)
        self.CS3 = self.sb("CS3", [128, 2, 768], BF16)
        self.fb_row = self.sb("fb_row", [1, D], BF16)
        self.b_adaT = self.sb("b_adaT_s", [128, 4 * 48])
        self.nmT = self.sb("nmT_s", [128, 72])
        self.cwT = self.sb("cwT_s", [128, 4 * NCH * 9])
        self.cbT = self.sb("cbT_s", [128, 4 * NCH])
        self.condS = self.sb("condS", [128, 16])
        self.SC = self.sb("SC", [128, 8, 2], BF16)
        self.MOD = self.sb("MOD", [128, 4, 48, 2])
        self.AMOD = self.sb("AMOD", [128, 4, 2, 8, 2])
        self.eps_col = self.sb("eps_col", [128, 2])
        self.mpar = self.sb("mpar_s", [128, 2 * 12 * 16])
        self.mparT = self.sb("mparT_s", [16, 2])
        self.ml_normT = self.sb("ml_normT_s", [128, 1])
        self.smm = self.sb("smm_s", [64, 16])
        self.dcwT = self.sb("dcwT_s", [128, 120])
        self.mask2 = self.sb("mask2_s", [128, 5, 256], BF16)
        self.gpar = self.sb("gpar_s", [128, 2 * 12 * 16])
        self.dn_normT = self.sb("dn_normT_s", [128, 1])
        self.ps = [self.es.enter_context(nc.psum_tensor("ps%d" % i, [128, 512], F32)) for i in range(8)]
        self.ident_bf = self.consts_bf[:, C_ID * 128:(C_ID + 1) * 128]

        S.dma("sp", self.X[:], self.xT.rearrange("(c p) t -> p c t", p=128))
        S.dma("sp", self.condS[:], self.condT)
        S.dma("sp", self.consts[:], consts_d)
        S.dma("pool", self.consts_bf[:], consts_d)
        S.dma("pool", self.CS3[:], cs3_d.rearrange("(j p) n -> p j n", p=128))
        S.dma("sp", self.b_adaT[:], b_adaT_d)
        S.dma("sp", self.nmT[:], nmT_d)
        S.dma("sp", self.cwT[:], cwT_d)
        S.dma("sp", self.cbT[:], cbT_d)
        S.memset("dve", self.ones_bf[:], 1.0)
        S.memset("dve", self.eps_col[:, 0:1], EPS)
        S.memset("dve", self.eps_col[:, 1:2], 128.0 * EPS)
        S.dma("sp", self.mpar[:], mpar_d)
        S.dma("sp", self.mparT[:], mparT_d)
        S.dma("sp", self.ml_normT[:], ml_normT_d)
        S.dma("sp", self.smm[:], smm_d)
        S.dma("sp", self.dcwT[:], dcwT_d)
        S.dma("pool", self.mask2[:].rearrange("p a b -> p (a b)"), mask2_d)
        S.dma("sp", self.gpar[:], gpar_d)
        S.dma("sp", self.dn_normT[:], dn_normT_d)
        S.act(self.SC[:].rearrange("p k c -> p (k c)"), self.condS[:], AF.Silu)

        layers = cfg.get("layers", [0, 1, 2, 3])

        def collect(fn):
            keep = self.steps
            self.steps = []
            fn()
            out = self.steps
            self.steps = keep
            return out

        def merge(a, b):
            out = []
            ia = ib = 0
            while ia < len(a) or ib < len(b):
                if ia < len(a):
                    out.append(a[ia]); ia += 1
                want = (ia * len(b)) // max(1, len(a)) if ia < len(a) else len(b)
                while ib < want:
                    out.append(b[ib]); ib += 1
            return out

        k = 0
        ada0 = collect(lambda: self.adaln(layers[0]))
        self.steps += ada0[:5]
        pending = ada0[5:]
        for li, l in enumerate(layers):
            kind = l % 3
            mix = []
            if kind == 0 and cfg.get("fnet", True):
                mix = collect(lambda: self.fnet(l, l // 3))
            elif kind == 1 and cfg.get("gdn", True):
                mix = collect(lambda: self.gdn(l))
            elif kind == 2 and cfg.get("mlstm", True):
                mix = collect(lambda: self.mlstm(l))
            extra = pending
            if li + 1 < len(layers):
                extra = extra + collect(lambda: self.adaln(layers[li + 1]))
            pending = []
            self.steps += merge(mix, extra)
            self.step(None, lambda slot, k=k: self.tap(k))
            k += 1
            if cfg.get("ffn", True):
                self.ffn(l)
            self.step(None, lambda slot, k=k: self.tap(k))
            k += 1
        self.run_steps()

        Y, w = self.carve(0, [8, NT])
        self.norm_mod(self.nmT[:, 64:72], None, out_y=Y)
        S.dma("sp", self.yT.rearrange("(c p) t -> p c t", p=128), Y)
        S.wait_all("sp")
        S.emit()
        self.es.close()
        return nc


def _prep(inputs):
    consts, cs3, tab, mask2 = _const_tables()
    f = lambda k: np.ascontiguousarray(np.asarray(inputs[k], np.float32))
    shared = {
        "w_ada": f("w_ada"),
        "b_adaT": _fm(f("b_ada")).reshape(128, 4 * 48),
        "nmT": np.concatenate([_fm(f("norm_mix")).reshape(128, 32), _fm(f("norm_ffn")).reshape(128, 32),
                               _fm(f("norm_final")).reshape(128, 8)], axis=1),
        "w_up": f("ffn_w_up"),
        "cwT": np.ascontiguousarray(np.moveaxis(_fm(f("ffn_conv_w").reshape(4, 9, DFF)), 2, 3)).reshape(128, 4 * NCH * 9),
        "cbT": _fm(f("ffn_conv_b")).reshape(128, 4 * NCH),
        "w_down": f("ffn_w_down"),
        "fnet_w": f("fnet_w"),
        "fnet_b": f("fnet_b").reshape(1, 2 * D),
        "consts": consts, "cs3": cs3, "tab": tab, "mask2": mask2,
    }
    wi = f("dn_w_in")[0]
    shared["dn_wh"] = np.ascontiguousarray(np.stack(
        [np.concatenate([wi[:, j * 1024 + h * 128:j * 1024 + (h + 1) * 128] for j in range(4)], axis=1) for h in range(8)]))
    shared["dn_wg"] = np.ascontiguousarray(wi[:, 4096:4128])
    shared["dcwT"] = np.ascontiguousarray(np.moveaxis(_fm(f("dn_conv_w")[0]), 1, 2)).reshape(128, 120)
    gp = np.stack([f("dn_a_log")[0].reshape(16), f("dn_dt_bias")[0].reshape(16)])
    shared["gpar"] = np.ascontiguousarray(np.broadcast_to(gp[None, :, None, :], (128, 2, 12, 16))).reshape(128, 384)
    shared["dn_normT"] = np.ascontiguousarray(f("dn_norm")[0].reshape(128, 1))
    shared["dn_wo"] = f("dn_w_out")[0]
    sdel = f("state_delta")
    mw = f("ml_w_in")[0]
    shared["ml_wh"] = np.ascontiguousarray(np.stack(
        [np.concatenate([mw[:, h * 64:(h + 1) * 64], mw[:, 512 + h * 64:512 + (h + 1) * 64],
                         mw[:, 1024 + h * 128:1024 + (h + 1) * 128], mw[:, 2048 + h * 128:2048 + (h + 1) * 128]], axis=1)
         for h in range(8)]))
    mg = mw[:, 3072:3104]
    shared["ml_wg"] = np.ascontiguousarray(mg)
    mg4 = mg.reshape(1024, 2, 2, 8)
    shared["ml_wgr"] = np.ascontiguousarray(np.concatenate([mg4[:, :, 0, :].reshape(1024, 16), mg4[:, :, 1, :].reshape(1024, 16)], axis=1))
    bp = np.stack([f("ml_b_i")[0].reshape(16), f("ml_b_f")[0].reshape(16)])
    shared["mpar"] = np.ascontiguousarray(np.broadcast_to(bp[None, :, None, :], (128, 2, 12, 16))).reshape(128, 384)
    shared["mparT"] = np.ascontiguousarray(bp.T)
    shared["ml_normT"] = np.ascontiguousarray(f("ml_norm")[0].reshape(128, 1))
    shared["ml_wo"] = f("ml_w_out")[0]
    smc, smn, smmm = f("state_mlstm_c"), f("state_mlstm_n"), f("state_mlstm_m")
    xp = f("x_prompt")
    xs = f("x_sample")
    c = f("c")
    cctx = f("c_ctx")
    per_core = []
    for i in range(N_CORES):
        b = i // 4
        x = np.concatenate([xp[2 * i], xp[2 * i + 1], xs[b]], axis=0)
        cond = np.stack([cctx, c[b]], axis=-1)
        m = dict(shared)
        m["xT"] = np.ascontiguousarray(x.T)
        m["sd"] = np.ascontiguousarray(sdel[b, 0])
        m["smca"] = np.ascontiguousarray(np.concatenate([smc[b, 0], smn[b, 0][..., None]], axis=-1))
        m["smm"] = np.ascontiguousarray(np.broadcast_to(smmm[b, 0].reshape(1, 16), (64, 16)))
        m["condT"] = np.ascontiguousarray(cond.reshape(8, 128, 2).transpose(1, 0, 2)).reshape(128, 16)
        per_core.append(m)
    return per_core


def run(inputs, cfg, core_ids=None, trace=False):
    b = Builder(cfg)
    nc = b.build()
    maps = _prep(inputs)
    core_ids = core_ids or list(range(N_CORES))
    maps = [maps[i] for i in core_ids]
    res = run_bass_kernel_spmd(nc, maps, core_ids=list(range(len(core_ids))), trace=trace)
    return res, b


def kernel(**inputs):
    res, b = run(inputs, dict())
    R = res.results
    y_prompt = np.zeros((16, 256, D), np.float32)
    y_sample = np.zeros((2, 1024, D), np.float32)
    new_d = np.zeros((16, 1, 2, 8, 128, 128), np.float32)
    new_c = np.zeros((16, 1, 2, 8, 64, 128), np.float32)
    new_n = np.zeros((16, 1, 2, 8, 64), np.float32)
    new_m = np.zeros((16, 1, 2, 8), np.float32)
    for i in range(N_CORES):
        y = np.asarray(R[i]["yT"]).T
        y_prompt[2 * i] = y[0:256]
        y_prompt[2 * i + 1] = y[256:512]
        if i % 4 == 0:
            y_sample[i // 4] = y[512:]
        new_d[2 * i:2 * i + 2, 0] = np.asarray(R[i]["nsd"])
        new_c[2 * i:2 * i + 2, 0] = np.asarray(R[i]["nsc"])
        new_n[2 * i:2 * i + 2, 0] = np.asarray(R[i]["nsn"])
        new_m[2 * i:2 * i + 2, 0] = np.asarray(R[i]["nsm"])
    return (y_prompt, y_sample, new_d, new_c, new_n, new_m)
```

```python
import numpy as np
from contextlib import ExitStack
import concourse.bass as bass
import concourse.mybir as mybir
from concourse.bass_utils import run_bass_kernel_spmd

F32 = mybir.dt.float32
BF16 = mybir.dt.bfloat16
AF = mybir.ActivationFunctionType
ALU = mybir.AluOpType

ENGS = ("pe", "act", "dve", "pool", "sp")
D = 1024
NT = 1536
DFF = 2816
NCH = 22
TILES = [(0, 512), (512, 1024), (1024, 1536)]
COND = [0, 1, 1]
EPS = 1e-6
N_CORES = 8


def _rect(ap):
    t = ap.tensor
    name = t.name
    pat = ap.ap
    off = ap.offset
    esz = mybir.dt.size(ap.dtype)
    if "dram" in str(type(t)).lower() or "DRam" in str(type(t)):
        ext = 1
        for st, cnt in pat:
            ext += (cnt - 1) * abs(st)
        return (name, 0, 1, off * esz, (off + ext) * esz)
    shape = list(t.shape)
    fsz = 1
    for s in shape[1:]:
        fsz *= s
    pcnt = pat[0][1]
    p_lo = off // fsz
    f_lo = off % fsz
    lo = 0
    hi = 0
    for st, cnt in pat[1:]:
        if st >= 0:
            hi += (cnt - 1) * st
        else:
            lo += (cnt - 1) * st
    return (name, p_lo, p_lo + pcnt, (f_lo + lo) * esz, (f_lo + hi + 1) * esz)


class Sched:
    def __init__(self, nc, n_dma_sems=8):
        self.nc = nc
        self.q = {e: [] for e in ENGS}
        self.cnt = {e: 0 for e in ENGS}
        self.waited = {e: {} for e in ENGS}
        self.recs = {}
        self.n_dma_sems = n_dma_sems
        self.dma_i = {e: 0 for e in ENGS}
        self.dma_cnt = {}
        self.n_ops = 0

    def _deps(self, eng, ap, is_write):
        r = _rect(ap)
        lst = self.recs.setdefault(r[0], [])
        is_psum = r[0].startswith("ps")
        deps = []
        keep = []
        for rec in lst:
            (_, pl, ph, fl, fh), tok, w, e = rec
            overlap = not (ph <= r[1] or r[2] <= pl or fh <= r[3] or r[4] <= fl)
            if is_psum and e != eng:
                deps.append(tok)
                continue
            if overlap:
                if is_write or w:
                    same = (e == eng) and tok[0] == e
                    if same and eng == "pe":
                        pass
                    else:
                        deps.append(tok)
                if is_write and pl >= r[1] and ph <= r[2] and fl >= r[3] and fh <= r[4]:
                    continue
            keep.append(rec)
        self.recs[r[0]] = keep
        return deps, r

    def _emit_waits(self, eng, deps):
        w = self.waited[eng]
        best = {}
        for k, v in deps:
            if w.get(k, 0) >= v:
                continue
            if best.get(k, 0) < v:
                best[k] = v
        for k, v in best.items():
            w[k] = v
            self.q[eng].append(("wait", k, v))

    def _record(self, r, tok, w, eng):
        lst = self.recs[r[0]]
        if not w:
            for rec in lst:
                if (not rec[2]) and rec[3] == eng and rec[0] == r and rec[1][0] == tok[0]:
                    rec[1] = tok
                    return
        lst.append([r, tok, w, eng])

    def op(self, eng, fn, reads=(), writes=()):
        deps = []
        rr = []
        for ap in reads:
            d, r = self._deps(eng, ap, False)
            deps += d
            rr.append((r, False))
        for ap in writes:
            d, r = self._deps(eng, ap, True)
            deps += d
            rr.append((r, True))
        self._emit_waits(eng, deps)
        self.cnt[eng] += 1
        tok = (eng, self.cnt[eng])
        self.q[eng].append(("op", fn, eng, 1))
        for r, w in rr:
            self._record(r, tok, w, eng)
        self.n_ops += 1
        return tok

    def dma(self, eng, out, in_, **kw):
        deps = []
        d, r_in = self._deps(eng, in_, False)
        deps += d
        d, r_out = self._deps(eng, out, True)
        deps += d
        i = self.dma_i[eng] % self.n_dma_sems
        self.dma_i[eng] += 1
        key = ("dma", eng, i)
        prev = self.dma_cnt.get(key, 0)
        if prev:
            deps.append((key, prev))
        self._emit_waits(eng, deps)
        val = prev + 16
        self.dma_cnt[key] = val
        tok = (key, val)
        self.q[eng].append(("op", lambda e: e.dma_start(out=out, in_=in_, **kw), key, 16))
        self._record(r_in, tok, False, eng)
        self._record(r_out, tok, True, eng)
        self.n_ops += 1
        return tok

    def wait_all(self, eng):
        deps = []
        for e in ENGS:
            if self.cnt[e] and e != eng:
                deps.append((e, self.cnt[e]))
        for k, v in self.dma_cnt.items():
            deps.append((k, v))
        self._emit_waits(eng, deps)

    def matmul(self, out, lhsT, rhs, start=True, stop=True):
        return self.op("pe", lambda e: e.matmul(out, lhsT, rhs, start=start, stop=stop), [lhsT, rhs], [out])

    def transpose(self, out, in_, ident):
        return self.op("pe", lambda e: e.transpose(out, in_, ident), [in_, ident], [out])

    def act(self, out, in_, func, bias=None, scale=None, accum_out=None):
        kw = {}
        rd = [in_]
        if bias is not None:
            kw["bias"] = bias
            if not isinstance(bias, (int, float)):
                rd.append(bias)
        if scale is not None:
            kw["scale"] = scale
            if not isinstance(scale, (int, float)):
                rd.append(scale)
        wr = [out]
        if accum_out is not None:
            kw["accum_out"] = accum_out
            wr.append(accum_out)
        return self.op("act", lambda e: e.activation(out, in_, func, **kw), rd, wr)

    def tt(self, eng, out, in0, in1, op):
        return self.op(eng, lambda e: e.tensor_tensor(out, in0, in1, op), [in0, in1], [out])

    def ts(self, eng, out, in0, s1, s2=None, op0=ALU.mult, op1=None):
        rd = [in0]
        for s in (s1, s2):
            if s is not None and not isinstance(s, (int, float)):
                rd.append(s)
        if op1 is None:
            return self.op(eng, lambda e: e.tensor_scalar(out, in0, s1, None, op0), rd, [out])
        return self.op(eng, lambda e: e.tensor_scalar(out, in0, s1, s2, op0, op1), rd, [out])

    def stt(self, out, in0, scalar, in1, op0, op1):
        rd = [in0, in1]
        if not isinstance(scalar, (int, float)):
            rd.append(scalar)
        return self.op("dve", lambda e: e.scalar_tensor_tensor(out, in0, scalar, in1, op0, op1), rd, [out])

    def copy(self, eng, out, in_):
        if eng == "act":
            return self.op(eng, lambda e: e.copy(out, in_), [in_], [out])
        return self.op(eng, lambda e: e.tensor_copy(out, in_), [in_], [out])

    def memset(self, eng, ap, val):
        return self.op(eng, lambda e: e.memset(ap, val), [], [ap])

    def recip(self, out, in_):
        return self.op("dve", lambda e: e.reciprocal(out, in_), [in_], [out])

    def emit(self):
        nc = self.nc
        keys = [e for e in ENGS if self.cnt[e]] + list(self.dma_cnt.keys())
        with ExitStack() as es:
            sems = {}
            for i, k in enumerate(keys):
                sems[k] = es.enter_context(nc.semaphore("s%d" % i))
            block = es.enter_context(nc.Block())
            q = self.q

            def run(engname, eng):
                for it in q[engname]:
                    if it[0] == "wait":
                        eng.wait_ge(sems[it[1]], it[2])
                    else:
                        it[1](eng).then_inc(sems[it[2]], it[3])

            if q["sp"]:
                @block.sync
                def _(e):
                    run("sp", e)
            if q["act"]:
                @block.scalar
                def _(e):
                    run("act", e)
            if q["dve"]:
                @block.vector
                def _(e):
                    run("dve", e)
            if q["pool"]:
                @block.gpsimd
                def _(e):
                    run("pool", e)
            if q["pe"]:
                @block.tensor
                def _(e):
                    run("pe", e)


def _const_tables():
    i = np.arange(128)
    r, c = np.meshgrid(i, i, indexing="ij")
    mats = [
        (r == c), np.ones((128, 128)), -np.ones((128, 128)),
        (r >= c), (r <= c), (r > c), (r < c), (r <= c), (r >= c),
    ]
    consts = np.concatenate([m.astype(np.float32) for m in mats], axis=1)
    m2 = [(r // 8 == c // 8)]
    for sz in (8, 16, 32, 64):
        m2.append((r // (2 * sz) == c // (2 * sz)) & (r // sz != c // sz))
    mask2 = np.concatenate([np.concatenate([m, m], axis=1).astype(np.float32) for m in m2], axis=1)
    k = np.arange(256)
    ang = 2.0 * np.pi * ((k[:, None] * k[None, :]) % 256) / 256.0
    cs3 = np.concatenate([np.cos(ang), np.sin(ang), -np.sin(ang)], axis=1).astype(np.float32)
    t = np.arange(1024)
    ang = 2.0 * np.pi * ((t[:, None] * t[None, :]) % 1024) / 1024.0
    ct = np.cos(ang).astype(np.float32)
    nst = (-np.sin(ang)).astype(np.float32)
    tab = np.zeros((4, 128, 2, 8, 256), np.float32)
    for q in range(4):
        for j, m in enumerate((ct, nst)):
            blk = m[:, q * 256:(q + 1) * 256].reshape(8, 128, 256)
            tab[q, :, j] = blk.transpose(1, 0, 2)
    return consts, cs3, tab, mask2


C_ID, C_ONE, C_NEG, C_LT, C_UT, C_SLT, C_SUT = range(7)


def _fm(v):
    v = np.asarray(v, np.float32)
    lead = v.shape[:-1]
    n = v.shape[-1] // 128
    return np.ascontiguousarray(np.moveaxis(v.reshape(lead + (n, 128)), -1, 0))


class Builder:
    def __init__(self, cfg):
        self.cfg = cfg
        self.nc = bass.Bass("TRN2", target_bir_lowering=False)
        self.S = Sched(self.nc)
        self.es = ExitStack()
        self.steps = []
        self.bank_ctr = {}

    def dram_in(self, name, shape):
        return self.nc.dram_tensor(name, list(shape), F32, kind="ExternalInput").ap()

    def dram_out(self, name, shape):
        return self.nc.dram_tensor(name, list(shape), F32, kind="ExternalOutput").ap()

    def sb(self, name, shape, dt=F32):
        return self.es.enter_context(self.nc.sbuf_tensor(name, list(shape), dt))

    def carve(self, off, shape, dt=F32):
        n = int(np.prod(shape))
        if dt == BF16:
            w = (n + 1) // 2
            v = self.scr[:, off:off + w].bitcast(BF16)[:, 0:n]
        else:
            w = n
            v = self.scr[:, off:off + w]
        assert off + w <= self.scr_words, (off, w, self.scr_words)
        if len(shape) == 2:
            v = v.rearrange("p (a b) -> p a b", a=shape[0])
        elif len(shape) == 3:
            v = v.rearrange("p (a b c) -> p a b c", a=shape[0], b=shape[1])
        elif len(shape) == 4:
            v = v.rearrange("p (a b c d) -> p a b c d", a=shape[0], b=shape[1], c=shape[2])
        return v, w

    def bank(self, role="x", pool=None):
        pool = pool or list(range(8))
        i = self.bank_ctr.get(role, 0)
        self.bank_ctr[role] = i + 1
        return self.ps[pool[i % len(pool)]]

    def step(self, loads, fn):
        self.steps.append((loads, fn))

    def run_steps(self):
        S = self.S
        R = len(self.slots)
        load_steps = [i for i, (l, f) in enumerate(self.steps) if l is not None]
        slot_of = {si: k % R for k, si in enumerate(load_steps)}
        issued = 0

        def issue(upto):
            nonlocal issued
            while issued < len(load_steps) and issued <= upto:
                si = load_steps[issued]
                slot = self.slots[slot_of[si]]
                for (dstf, src) in self.steps[si][0](slot):
                    S.dma("pool", dstf, src)
                issued += 1

        k = 0
        for i, (l, f) in enumerate(self.steps):
            issue(k + R - 1)
            if l is not None:
                f(self.slots[slot_of[i]])
                k += 1
            else:
                f(None)
        self.steps = []

    def norm_mod(self, A, Bv, out_h=True, out_y=None):
        S = self.S
        off = self.scr_tmp
        SQ, w = self.carve(off, [8, 512], BF16); off += w
        RS, w = self.carve(off, [512]); off += w
        TM, w = self.carve(off, [2, 512]); off += w
        for tt, (t0, t1) in enumerate(TILES):
            cd = COND[tt]
            S.act(SQ, self.X[:, :, t0:t1], AF.Square)
            ps = self.bank("n", [6, 7])
            for fc in range(8):
                S.matmul(ps[:], self.ones_bf[:, 0:128], SQ[:, fc, :], start=(fc == 0), stop=(fc == 7))
            S.act(RS, ps[:], AF.Sqrt, bias=self.eps_col[:, 0:1], scale=1.0 / D)
            S.recip(RS, RS)
            for fc in range(8):
                if out_y is not None:
                    S.stt(out_y[:, fc, t0:t1], self.X[:, fc, t0:t1], A[:, fc:fc + 1], RS, ALU.mult, ALU.mult)
                else:
                    tm = TM[:, fc % 2, :]
                    S.tt("dve", tm, self.X[:, fc, t0:t1], RS, ALU.mult)
                    S.act(self.H[:, fc, t0:t1], tm, AF.Identity, bias=Bv[:, fc, cd:cd + 1], scale=A[:, fc, cd:cd + 1])

    def adaln(self, l):
        S = self.S
        for q in range(12):
            def loads(slot, q=q):
                v = slot[:, 0:4096].rearrange("p (k n) -> p k n", k=8)
                return [(v, self.w_ada[l][:, q * 512:(q + 1) * 512].rearrange("(k p) n -> p k n", p=128))]

            def fn(slot, q=q):
                v = slot[:, 0:4096].rearrange("p (k n) -> p k n", k=8)
                ps = self.bank("n", [6, 7])
                for ocl in range(4):
                    for kc in range(8):
                        S.matmul(ps[:, ocl * 2:ocl * 2 + 2], v[:, kc, ocl * 128:(ocl + 1) * 128],
                                 self.SC[:, kc, :], start=(kc == 0), stop=(kc == 7))
                pv = ps[:, 0:8].rearrange("p (o c) -> p o c", c=2)
                for c in range(2):
                    S.tt("dve", self.MOD[:, l, q * 4:(q + 1) * 4, c], pv[:, :, c],
                         self.b_adaT[:, l * 48 + q * 4: l * 48 + (q + 1) * 4], ALU.add)
            self.step(loads, fn)

        def mkfin(sub):
            def fin(slot):
                for c in range(2):
                    S.stt(self.AMOD[:, l, sub, :, c], self.MOD[:, l, (1 + 3 * sub) * 8:(2 + 3 * sub) * 8, c], 1.0,
                          self.nmT[:, (sub * 4 + l) * 8:(sub * 4 + l + 1) * 8], ALU.add, ALU.mult)
            return fin
        st = self.steps
        self.steps = st[:-12] + st[-12:-8] + [(None, mkfin(0))] + st[-8:] + [(None, mkfin(1))]

    def mod(self, l, j):
        return self.MOD[:, l, j * 8:(j + 1) * 8, :]

    def tap(self, k):
        if self.dbg is not None and k < self.cfg.get("ntaps", 0):
            self.S.dma("sp", self.dbg[k].rearrange("(c p) t -> p c t", p=128), self.X[:])

    def ffn(self, l):
        S = self.S
        self.step(None, lambda slot: self.norm_mod(self.AMOD[:, l, 1], self.mod(l, 3)))
        off = self.scr_main
        P, w = self.carve(off, [4, NT], BF16); off += w
        GP = []
        for b in range(2):
            g, w = self.carve(off, [1720], BF16); off += w
            GP.append(g)
        SB = []
        for b in range(2):
            s, w = self.carve(off, [512]); off += w
            SB.append(s)
        DG = []
        for b in range(2):
            d, w = self.carve(off, [9, 128], BF16); off += w
            DG.append(d)
        assert off <= self.scr_tmp

        def zero(slot):
            for g in GP:
                S.memset("dve", g, 0.0)
        self.step(None, zero)

        def chunk(c, slot, j, pj):
            wv = slot[:, 0:4096].rearrange("p (k n) -> p k n", k=8)
            gp = GP[c % 2]
            gpP = gp[:, 0:516].rearrange("p (s t) -> p s t", s=2)
            gpS = gp[:, 516:516 + 18 * 66].rearrange("p (r c) -> p r c", r=18)
            dg = DG[c % 2]
            i0 = (l * NCH + c) * 9
            S.tt("dve", dg, self.ident_bf.unsqueeze(1).broadcast_to([128, 9, 128]),
                 self.cwT[:, i0:i0 + 9].unsqueeze(2).broadcast_to([128, 9, 128]), ALU.mult)
            psG = []
            for tt, (t0, t1) in enumerate(TILES):
                ps = self.bank("g", [0, 1, 2])
                for kc in range(8):
                    S.matmul(ps[:], wv[:, kc, 256 + j * 128:256 + (j + 1) * 128], self.H[:, kc, t0:t1],
                             start=(kc == 0), stop=(kc == 7))
                psG.append(ps)
            S.copy("act", gpP[:, :, 1:257], psG[0][:].rearrange("p (s t) -> p s t", s=2))
            for hf in range(2):
                S.copy("act", gpS[:, 1 + 8 * hf:9 + 8 * hf, 1:65], psG[1 + hf][:].rearrange("p (r c) -> p r c", r=8))
            psA = []
            for tt, (t0, t1) in enumerate(TILES):
                ps = self.bank("a", [3, 4, 5])
                for kc in range(8):
                    S.matmul(ps[:], wv[:, kc, j * 128:(j + 1) * 128], self.H[:, kc, t0:t1],
                             start=(kc == 0), stop=(kc == 7))
                psA.append(ps)
            for tt, (t0, t1) in enumerate(TILES):
                ps = self.bank("c", [6, 7])
                if tt == 0:
                    for dc in range(3):
                        S.matmul(ps[:].rearrange("p (s t) -> p s t", s=2), dg[:, 3 + dc, :], gpP[:, :, dc:dc + 256],
                                 start=(dc == 0), stop=(dc == 2))
                else:
                    hf = tt - 1
                    n = 0
                    for dr in range(3):
                        for dc in range(3):
                            S.matmul(ps[:].rearrange("p (r c) -> p r c", r=8), dg[:, dr * 3 + dc, :],
                                     gpS[:, 8 * hf + dr:8 * hf + dr + 8, dc:dc + 64], start=(n == 0), stop=(n == 8))
                            n += 1
                sb = SB[tt % 2]
                S.act(sb, ps[:], AF.Silu, bias=self.cbT[:, l * NCH + c:l * NCH + c + 1])
                S.tt("dve", P[:, pj, t0:t1], sb, psA[tt][:], ALU.mult)

        c0 = 0
        while c0 < NCH:
            G = min(4, NCH - c0)
            for pr in range(0, G, 2):
                c = c0 + pr

                def loads(slot, c=c):
                    wv = slot[:, 0:4096].rearrange("p (k n) -> p k n", k=8)
                    wu = self.w_up[l]
                    return [(wv[:, :, 0:256], wu[:, c * 128:(c + 2) * 128].rearrange("(k p) n -> p k n", p=128)),
                            (wv[:, :, 256:512], wu[:, DFF + c * 128:DFF + (c + 2) * 128].rearrange("(k p) n -> p k n", p=128))]

                def fn(slot, c=c, pr=pr):
                    chunk(c, slot, 0, pr)
                    chunk(c + 1, slot, 1, pr + 1)
                self.step(loads, fn)

            def dloads(slot, c0=c0, G=G):
                wv = slot[:, 0:G * 1024].rearrange("p (g n) -> p g n", g=G)
                return [(wv, self.w_down[l][c0 * 128:(c0 + G) * 128, :].rearrange("(g p) n -> p g n", p=128))]

            def dfn(slot, c0=c0, G=G):
                wv = slot[:, 0:G * 1024].rearrange("p (g n) -> p g n", g=G)
                g2 = self.mod(l, 5)
                for oc in range(8):
                    for tt, (t0, t1) in enumerate(TILES):
                        ps = self.bank("g", [0, 1, 2])
                        for j in range(G):
                            S.matmul(ps[:], wv[:, j, oc * 128:(oc + 1) * 128], P[:, j, t0:t1], start=(j == 0), stop=(j == G - 1))
                        cd = COND[tt]
                        S.stt(self.X[:, oc, t0:t1], ps[:], g2[:, oc, cd:cd + 1], self.X[:, oc, t0:t1], ALU.mult, ALU.add)
            self.step(dloads, dfn)
            c0 += G

    def fnet(self, l, jf):
        S = self.S
        self.step(None, lambda slot: self.norm_mod(self.AMOD[:, l, 0], self.mod(l, 0)))
        off = self.scr_main
        AB, w = self.carve(off, [12, 4, 512], BF16); off += w
        Fm, w = self.carve(off, [8, NT], BF16); off += w
        assert off <= self.scr_words
        CS3 = self.CS3

        def stage1(slot):
            S.dma("pool", self.fb_row[:], self.fnet_b_d[:, jf * D:(jf + 1) * D])
            n = 0
            for ti in range(12):
                for g in range(4):
                    ps = self.bank("x")
                    for j in range(2):
                        S.matmul(ps[:], self.H[:, 2 * g + j, ti * 128:(ti + 1) * 128], CS3[:, j, 0:512], start=(j == 0), stop=(j == 1))
                    S.copy("act" if n % 2 else "dve", AB[:, ti, g, :], ps[:])
                    n += 1
        self.step(None, stage1)

        def stage2p(slot):
            for cc in range(8):
                g, hf = cc // 2, cc % 2
                ps = self.bank("x")
                for s in range(2):
                    n = 0
                    for tk in range(2):
                        for part in range(2):
                            S.matmul(ps[:, s * 256:(s + 1) * 256], AB[:, 2 * s + tk, g, part * 256 + hf * 128:part * 256 + (hf + 1) * 128],
                                     CS3[:, tk, part * 512:part * 512 + 256], start=(n == 0), stop=(n == 3))
                            n += 1
                S.act(Fm[:, cc, 0:512], ps[:], AF.Copy, scale=1.0 / 256.0)
        self.step(None, stage2p)

        for q in range(4):
            def loads(slot, q=q):
                return [(slot[:, 0:4096].rearrange("p (a b) -> p a b", a=4),
                         self.tab[q].rearrange("p a k u -> p (a k u)").rearrange("p (a b) -> p a b", a=4))]

            def fn(slot, q=q):
                tv = slot[:, 0:4096].rearrange("p (a k u) -> p a k u", a=2, k=8)
                for cc in range(8):
                    g, hf = cc // 2, cc % 2
                    ps = self.bank("x")
                    n = 0
                    for tk in range(8):
                        for part in range(2):
                            S.matmul(ps[:, 0:256], AB[:, 4 + tk, g, part * 256 + hf * 128:part * 256 + (hf + 1) * 128],
                                     tv[:, part, tk, :], start=(n == 0), stop=(n == 15))
                            n += 1
                    S.act(Fm[:, cc, 512 + q * 256:512 + (q + 1) * 256], ps[:, 0:256], AF.Copy, scale=1.0 / 512.0)
            self.step(loads, fn)

        for half in range(2):
            def loads(slot, half=half):
                wv = slot[:, 0:4096].rearrange("p (k n) -> p k n", k=8)
                return [(wv, self.fnet_w[jf][:, half * 512:(half + 1) * 512].rearrange("(k p) n -> p k n", p=128))]

            def fn(slot, half=half):
                wv = slot[:, 0:4096].rearrange("p (k n) -> p k n", k=8)
                g1 = self.mod(l, 2)
                for ocl in range(4):
                    oc = half * 4 + ocl
                    for tt, (t0, t1) in enumerate(TILES):
                        ps = self.bank("x")
                        for kc in range(8):
                            S.matmul(ps[:], wv[:, kc, ocl * 128:(ocl + 1) * 128], Fm[:, kc, t0:t1], start=(kc == 0), stop=False)
                        S.matmul(ps[:], self.fb_row[0:1, oc * 128:(oc + 1) * 128], self.ones_bf[0:1, :],
                                 start=False, stop=True)
                        cd = COND[tt]
                        S.stt(self.X[:, oc, t0:t1], ps[:], g1[:, oc, cd:cd + 1], self.X[:, oc, t0:t1], ALU.mult, ALU.add)
            self.step(loads, fn)

    def gdn(self, l):
        S = self.S
        CF = self.consts
        cf = lambda k: CF[:, k * 128:(k + 1) * 128]
        ident_bf = self.ident_bf
        self.step(None, lambda slot: self.norm_mod(self.AMOD[:, l, 0], self.mod(l, 0)))
        off = 0
        def cv(shape, dt=F32):
            nonlocal off
            v, w = self.carve(off, shape, dt)
            off += w
            return v
        GA = cv([12, 16]); BA = cv([12, 16])
        Z = cv([NT], BF16)
        QKV = cv([3, NT], BF16)
        K_tm = cv([12, 128], BF16); V_tm = cv([12, 128], BF16)
        O_tm = cv([12, 128])
        ST_T = cv([8, 2, 128], BF16); ST_Q = cv([8, 2, 128], BF16); ST_QD = cv([8, 2, 128], BF16); ST_KD = cv([8, 2, 128], BF16)
        CCE = cv([8, 8])
        Sf = [cv([128]) for _ in range(4)]; Sb = [cv([128], BF16) for _ in range(4)]
        RP = [cv([128], BF16) for _ in range(4)]; VN = [cv([128], BF16) for _ in range(4)]
        SSQ = cv([12])
        base = off
        CP = cv([3, 1548], BF16); DG5 = cv([15, 128], BF16); SQ = cv([512], BF16); RS = cv([512])
        T0 = cv([12, 16]); EA = cv([12, 16])
        endA = off
        off = base
        NB = 5
        GM2 = []; EM = []; ER = []; T1 = []; CC = []; LNs = []; PQs = []; MBs = []
        for _ in range(NB):
            o1 = off
            GM2.append(cv([2, 128]))
            ln1, _w = self.carve(o1, [512], BF16)
            o2 = off
            EM.append(cv([512], BF16))
            ln2, _w = self.carve(o2, [512], BF16)
            ER.append(cv([256], BF16)); T1.append(cv([2, 128], BF16)); CC.append(cv([8]))
            LNs.append([cv([512], BF16), ln1, ln2])
            PQs.append(cv([512], BF16))
            MBs.append(cv([512], BF16))
        endB = off
        off = base
        SQ2 = cv([12, 128]); ON = cv([12, 128], BF16); OGh = cv([NT], BF16)
        TMPX = [cv([512]) for _ in range(2)]
        endC = off
        off = max(endA, endB, endC)
        assert off <= self.scr_words, off
        cnt = {"pre": 0, "sc": 0}
        one_col = CF[:, C_ONE * 128:C_ONE * 128 + 1]
        ones_f, neg_f = cf(C_ONE), cf(C_NEG)
        MdT2 = CF[:, 7 * 128:9 * 128].rearrange("p (d c) -> p d c", d=2)
        SM2 = CF[:, 5 * 128:7 * 128].rearrange("p (d c) -> p d c", d=2)
        bc2 = lambda col2: col2.unsqueeze(2).broadcast_to([128, 2, 128])
        v22 = lambda t: t.rearrange("p (a d c) -> p a d c", a=2, d=2)
        v2 = lambda t: t.rearrange("p (d c) -> p d c", d=2)
        MdT2b = self.consts_bf[:, 7 * 128:9 * 128].rearrange("p (d c) -> p d c", d=2)
        SM2b = self.consts_bf[:, 5 * 128:7 * 128].rearrange("p (d c) -> p d c", d=2)

        def gloads(slot):
            return [(slot[:, 0:256].rearrange("p (k n) -> p k n", k=8), self.dn_wg.rearrange("(k p) n -> p k n", p=128))]

        def gfn(slot):
            wg = slot[:, 0:256].rearrange("p (k n) -> p k n", k=8)
            if self.cfg.get("gcut", 9) < 0:
                return
            if self.cfg.get("gcut", 9) < 1:
                return
            ps = self.bank("x")
            for ti in range(12):
                for kc in range(8):
                    S.matmul(ps[:, ti * 32:(ti + 1) * 32], self.H[:, kc, ti * 128:(ti + 1) * 128], wg[:, kc, :],
                             start=(kc == 0), stop=(kc == 7))
            pv = ps[:, 0:384].rearrange("p (t d k h) -> p t d k h", t=12, d=2, k=2)
            gp = self.gpar[:].rearrange("p (a t n) -> p a t n", a=2, t=12)
            for d in range(2):
                S.tt("dve", T0[:, :, d * 8:(d + 1) * 8], pv[:, :, d, 0, :], gp[:, 1, :, d * 8:(d + 1) * 8], ALU.add)
                S.act(BA[:, :, d * 8:(d + 1) * 8], pv[:, :, d, 1, :], AF.Sigmoid)
            cut = self.cfg.get("gcut", 9)
            if cut < 2:
                return
            S.act(T0, T0, AF.Exp)
            if cut < 3:
                return
            S.act(T0, T0, AF.Ln, bias=one_col)
            if cut < 4:
                return
            S.act(EA, gp[:, 0], AF.Exp)
            S.stt(GA, T0, -1.0, EA, ALU.mult, ALU.mult)
        self.step(gloads, gfn)
        stage = self.cfg.get("gdn_stage", 9)
        nheads = self.cfg.get("gdn_heads", 8)

        def pre2(h, ti, n, b):
            LN0b, LN1b, LN2b = LNs[b]
            PQ = [PQs[b], PQs[b]]
            MB = [MBs[b], MBs[b]]
            T1b = T1[b]
            g2 = GA[:, ti, h:h + 9:8]
            b2 = BA[:, ti, h:h + 9:8]
            kT = QKV[:, 1, ti * 128:(ti + 1) * 128]
            qT = QKV[:, 0, ti * 128:(ti + 1) * 128]
            l4 = lambda t: t.rearrange("p (a c) -> p a c", a=4)
            def mask(lv):
                S.tt("dve" if lv == 0 else "pool", l4(MB[lv % 2]), l4(LN0b), self.mask2[:, lv, 0:128].unsqueeze(1).broadcast_to([128, 4, 128]), ALU.mult)
                return v22(MB[lv % 2])
            S.tt("dve", GM2[b], MdT2, bc2(g2), ALU.mult)
            GMf = GM2[b].rearrange("p d c -> p (d c)")
            pD = self.ps[b]
            S.matmul(pD[:, 0:256], neg_f, GMf, start=True, stop=False)
            for d in range(2):
                S.matmul(pD[:, d * 128:(d + 1) * 128], GM2[b][:, d, :], ones_f, start=False, stop=(d == 1))
            S.matmul(pD[:, 256:512], ones_f, GMf, start=True, stop=False)
            for d in range(2):
                S.matmul(pD[:, 256 + d * 128:256 + (d + 1) * 128], GM2[b][:, d, :], neg_f, start=False, stop=(d == 1))
            yield
            S.act(EM[b], pD[:, 0:512], AF.Relu, scale=-1.0)
            S.act(EM[b], EM[b], AF.Exp, scale=-1.0)
            pG = self.ps[b]
            S.matmul(pG[:, 0:256], ones_f, GMf)
            for d in range(2):
                S.matmul(pG[:, 256 + d:257 + d], GM2[b][:, d, :], ones_f[:, 0:1])
            S.matmul(pG[:, 258:260], ones_f, g2)
            yield
            S.act(ER[b], pG[:, 0:256], AF.Exp)
            S.copy("act", CC[b][:, 0:4], pG[:, 256:260])
            S.act(CCE[:, n, 0:4], CC[b][:, 0:4], AF.Exp)
            for d in range(2):
                S.act(CCE[:, n, 4 + d:5 + d], CC[b][:, d:d + 1], AF.Exp, bias=CC[b][:, 2 + d:3 + d], scale=-1.0)
            S.act(CCE[:, n, 6:8], CCE[:, n, 0:2], AF.Copy, scale=-1.0)
            S.tt("pool", T1b, SM2b, bc2(b2), ALU.mult)
            S.tt("pool", EM[b][:, 0:256].rearrange("p (d c) -> p d c", d=2), EM[b][:, 0:256].rearrange("p (d c) -> p d c", d=2), T1b, ALU.mult)
            S.tt("pool", EM[b][:, 256:512].rearrange("p (d c) -> p d c", d=2), EM[b][:, 256:512].rearrange("p (d c) -> p d c", d=2), MdT2b, ALU.mult)
            pB = self.ps[b]
            S.matmul(pB[:, 0:128], kT, kT)
            S.matmul(pB[:, 128:256], kT, qT)
            yield
            E2, ET2 = v2(EM[b][:, 0:256]), v2(EM[b][:, 256:512])
            LN0 = v22(LN0b)
            S.tt("dve", LN0[:, 0], pB[:, 0:128].unsqueeze(1).broadcast_to([128, 2, 128]), E2, ALU.mult)
            yield
            S.tt("dve", ST_Q[:, n], pB[:, 128:256].unsqueeze(1).broadcast_to([128, 2, 128]), ET2, ALU.mult)
            pT = self.ps[b][:].bitcast(BF16)
            for d in range(2):
                S.transpose(pT[:, d * 128:(d + 1) * 128], LN0[:, 0, d, :], ident_bf)
            S.copy("act", LN0b[:, 256:512], pT[:, 0:256])
            S.tt("pool", ST_QD[:, n], qT.unsqueeze(1).broadcast_to([128, 2, 128]), v2(ER[b]), ALU.mult)
            S.tt("pool", ST_KD[:, n], K_tm[:, ti, :].unsqueeze(1).broadcast_to([128, 2, 128]), bc2(CCE[:, n, 4:6]), ALU.mult)
            yield
            LM0 = mask(0)
            S.tt("dve", l4(PQ[0]), ident_bf.unsqueeze(1).broadcast_to([128, 4, 128]), l4(MB[0]), ALU.subtract)
            def blk(ps, a, d):
                return ps[:, (a * 2 + d) * 128:(a * 2 + d + 1) * 128]
            Lc, Nc = LM0[:, 0], LM0[:, 1]
            cur = 0
            for r in range(2):
                pR = self.ps[b]
                for d in range(2):
                    S.matmul(blk(pR, 0, d), Nc[:, d, :], Lc[:, d, :])
                    S.matmul(blk(pR, 1, d), Lc[:, d, :], Nc[:, d, :])
                LNr_b = LN1b if r == 0 else LN2b
                S.copy("act", LNr_b, pR[:, 0:512])
                if r == 1:
                    LMn = mask(1)
                yield
                LNr = v22(LNr_b)
                Lc, Nc = LNr[:, 0], LNr[:, 1]
                PQc = v22(PQ[cur])
                pP = self.ps[b]
                for d in range(2):
                    S.matmul(blk(pP, 0, d), Nc[:, d, :], PQc[:, 0, d, :])
                    S.matmul(blk(pP, 1, d), Lc[:, d, :], PQc[:, 1, d, :])
                S.tt("dve", PQ[1 - cur], pP[:, 0:512], PQ[cur], ALU.add)
                cur = 1 - cur
                yield
            for lv in range(1, 5):
                LMc = LMn
                PQc = v22(PQ[cur])
                YY = v22(LN1b)
                pY = self.ps[b]
                for d in range(2):
                    S.matmul(blk(pY, 1, d), LMc[:, 0, d, :], PQc[:, 1, d, :])
                    if lv < 4:
                        S.matmul(blk(pY, 0, d), LMc[:, 1, d, :], PQc[:, 0, d, :])
                if lv < 4:
                    S.copy("act", LN1b, pY[:, 0:512])
                    LMn = mask(lv + 1)
                else:
                    S.copy("act", LN1b[:, 256:512], pY[:, 256:512])
                yield
                pU = self.ps[b]
                for d in range(2):
                    S.matmul(blk(pU, 1, d), PQc[:, 0, d, :], YY[:, 1, d, :])
                    if lv < 4:
                        S.matmul(blk(pU, 0, d), PQc[:, 1, d, :], YY[:, 0, d, :])
                if lv < 4:
                    S.tt("dve", PQ[1 - cur], PQ[cur], pU[:, 0:512], ALU.subtract)
                else:
                    S.tt("dve", PQ[1 - cur][:, 256:512], PQ[cur][:, 256:512], pU[:, 256:512], ALU.subtract)
                cur = 1 - cur
                yield
            S.tt("pool", ST_T[:, n], v22(PQ[cur])[:, 1], bc2(b2), ALU.mult)

        def scan_step(ti, n, d, sb_i, first):
            b = sb_i
            kT = QKV[:, 1, ti * 128:(ti + 1) * 128]
            pA = self.bank("x")
            S.matmul(pA[:, 0:128], kT, Sb[sb_i])
            S.stt(RP[b], pA[:, 0:128], CCE[:, n, 6 + d:7 + d], V_tm[:, ti, :], ALU.mult, ALU.add)
            yield
            pB = self.bank("x")
            S.matmul(pB[:, 0:128], ST_T[:, n, d, :], RP[b])
            S.copy("act", VN[b], pB[:, 0:128])
            yield
            pC = self.bank("x")
            S.matmul(pC[:, 0:128], ST_QD[:, n, d, :], Sb[sb_i], start=True, stop=False)
            S.matmul(pC[:, 0:128], ST_Q[:, n, d, :], VN[b], start=False, stop=True)
            S.tt("dve", O_tm[:, ti, :], O_tm[:, ti, :], pC[:, 0:128], ALU.add)
            pE = self.bank("x")
            S.matmul(pE[:, 0:128], ST_KD[:, n, d, :], VN[b])
            S.stt(Sf[sb_i], Sf[sb_i], CCE[:, n, 2 + d:3 + d], pE[:, 0:128], ALU.mult, ALU.add)
            S.copy("act", Sb[sb_i], Sf[sb_i])
            yield

        for h in range(nheads if stage >= 2 else 0):
            def loadsA(slot, h=h):
                return [(slot[:, 0:4096].rearrange("p (k n) -> p k n", k=8), self.dn_wh[h].rearrange("(k p) n -> p k n", p=128))]

            def fnA(slot, h=h):
                wv = slot[:, 0:4096].rearrange("p (k n) -> p k n", k=8)
                S.memset("dve", CP, 0.0)
                S.tt("dve", DG5.rearrange("p (j t) c -> p j t c", j=3),
                     ident_bf.unsqueeze(1).unsqueeze(1).broadcast_to([128, 3, 5, 128]),
                     self.dcwT[:].rearrange("p (j h t) -> p j h t", j=3, h=8)[:, :, h, :].unsqueeze(3).broadcast_to([128, 3, 5, 128]),
                     ALU.mult)
                cpP = lambda j: CP[:, j, 0:520].rearrange("p (s t) -> p s t", s=2)
                for j in range(4):
                    for tt, (t0, t1) in enumerate(TILES):
                        ps = self.bank("x")
                        for kc in range(8):
                            S.matmul(ps[:], wv[:, kc, j * 128:(j + 1) * 128], self.H[:, kc, t0:t1], start=(kc == 0), stop=(kc == 7))
                        if j == 3:
                            S.act(Z[:, t0:t1], ps[:], AF.Silu)
                        elif tt == 0:
                            S.copy("act", cpP(j)[:, :, 2:258], ps[:].rearrange("p (s t) -> p s t", s=2))
                        else:
                            S.copy("act", CP[:, j, 522 + (tt - 1) * 512:522 + tt * 512], ps[:])
                for j in range(3):
                    for tt, (t0, t1) in enumerate(TILES):
                        ps = self.bank("x")
                        for tap in range(5):
                            if tt == 0:
                                S.matmul(ps[:].rearrange("p (s t) -> p s t", s=2), DG5[:, j * 5 + tap, :], cpP(j)[:, :, tap:tap + 256],
                                         start=(tap == 0), stop=(tap == 4))
                            else:
                                st = 520 + (tt - 1) * 512 + tap
                                S.matmul(ps[:], DG5[:, j * 5 + tap, :], CP[:, j, st:st + 512], start=(tap == 0), stop=(tap == 4))
                        S.act(QKV[:, j, t0:t1], ps[:], AF.Silu)
                for j in range(2):
                    for tt, (t0, t1) in enumerate(TILES):
                        S.act(SQ, QKV[:, j, t0:t1], AF.Square)
                        ps = self.bank("x")
                        S.matmul(ps[:], self.ones_bf[:, 0:128], SQ)
                        if j == 0:
                            S.act(RS, ps[:], AF.Sqrt, bias=self.eps_col[:, 1:2], scale=128.0)
                        else:
                            S.act(RS, ps[:], AF.Sqrt, bias=self.eps_col[:, 0:1], scale=1.0)
                        S.recip(RS, RS)
                        S.tt("dve", QKV[:, j, t0:t1], QKV[:, j, t0:t1], RS, ALU.mult)
                for (j, dst) in ((1, K_tm), (2, V_tm)):
                    for g4 in range(3):
                        pb = self.bank("x")[:].bitcast(BF16)
                        for i in range(4):
                            ti = g4 * 4 + i
                            S.transpose(pb[:, i * 128:(i + 1) * 128], QKV[:, j, ti * 128:(ti + 1) * 128], ident_bf)
                        S.copy("act", dst[:, g4 * 4:(g4 + 1) * 4, :], pb[:, 0:512].rearrange("p (a b) -> p a b", a=4))
            self.step(loadsA, fnA)

            def loadsB(slot, h=h):
                return [(slot[:, 0:1024], self.dn_wo[h * 128:(h + 1) * 128, :])]

            def fnB(slot, h=h):
                wo = slot[:, 0:1024]
                seqs = [([0, 1], 0), ([2, 3], 1), (list(range(4, 12)), 2)]
                sbi = 0
                def interleave(gens):
                    gens = list(gens)
                    while gens:
                        for g in list(gens):
                            try:
                                next(g)
                            except StopIteration:
                                gens.remove(g)

                def chainx(tiles, sidx, d, n0, sb_i):
                    order = tiles if d == 0 else tiles[::-1]
                    if sidx == 2:
                        S.dma("sp", Sf[sb_i], self.sd[d, h])
                        S.copy("act", Sb[sb_i], Sf[sb_i])
                    else:
                        S.memset("dve", Sf[sb_i], 0.0)
                        S.memset("dve", Sb[sb_i], 0.0)
                    for ti in order:
                        yield from scan_step(ti, n0 + tiles.index(ti), d, sb_i, first=False)
                    if sidx < 2:
                        S.dma("sp", self.nsd[sidx, d, h], Sf[sb_i])

                S.memset("dve", O_tm, 0.0)
                if stage >= 3:
                    for grp in ([0, 1, 2, 3],):
                        interleave([pre2(h, ti, ti, k) for k, ti in enumerate(grp)])
                    if stage >= 4:
                        interleave([chainx([0, 1], 0, d, 0, 2 * 0 + d) for d in range(2)] +
                                   [chainx([2, 3], 1, d, 2, 2 * 1 + d) for d in range(2)])
                    for grp in ([4, 5, 6, 7, 8], [9, 10, 11]):
                        interleave([pre2(h, ti, ti - 4, k) for k, ti in enumerate(grp)])
                    if stage >= 4:
                        interleave([chainx(list(range(4, 12)), 2, d, 0, d) for d in range(2)])
                if stage < 5:
                    return
                S.act(SQ2, O_tm, AF.Square)
                S.op("dve", lambda e: e.reduce_sum(SSQ, SQ2, mybir.AxisListType.X), [SQ2], [SSQ])
                S.act(SSQ, SSQ, AF.Sqrt, bias=self.eps_col[:, 0:1], scale=1.0 / 128.0)
                S.recip(SSQ, SSQ)
                S.tt("dve", ON, O_tm, SSQ.unsqueeze(2).broadcast_to([128, 12, 128]), ALU.mult)
                for g4 in range(3):
                    pb = self.bank("x")[:].bitcast(BF16)
                    for i in range(4):
                        ti = g4 * 4 + i
                        S.transpose(pb[:, i * 128:(i + 1) * 128], ON[:, ti, :], ident_bf)
                    S.stt(OGh[:, g4 * 512:(g4 + 1) * 512], pb[:, 0:512], self.dn_normT[:, 0:1], Z[:, g4 * 512:(g4 + 1) * 512],
                          ALU.mult, ALU.mult)
                g1 = self.mod(l, 2)
                for oc in range(8):
                    for tt, (t0, t1) in enumerate(TILES):
                        ps = self.bank("x")
                        S.matmul(ps[:], wo[:, oc * 128:(oc + 1) * 128], OGh[:, t0:t1])
                        cd = COND[tt]
                        if (oc * 3 + tt) % 2 == 0:
                            S.stt(self.X[:, oc, t0:t1], ps[:], g1[:, oc, cd:cd + 1], self.X[:, oc, t0:t1], ALU.mult, ALU.add)
                        else:
                            tx = TMPX[((oc * 3 + tt) // 2) % 2]
                            S.act(tx, ps[:], AF.Identity, scale=g1[:, oc, cd:cd + 1])
                            S.tt("pool", self.X[:, oc, t0:t1], self.X[:, oc, t0:t1], tx, ALU.add)
            self.step(loadsB, fnB)

    def mlstm(self, l):
        S = self.S
        CF = self.consts
        cf = lambda k: CF[:, k * 128:(k + 1) * 128]
        ident_bf = self.ident_bf
        self.step(None, lambda slot: self.norm_mod(self.AMOD[:, l, 0], self.mod(l, 0)))
        off = 0
        def cv(shape, dt=F32):
            nonlocal off
            v, w = self.carve(off, shape, dt)
            off += w
            return v
        LI = cv([12, 16]); LF = cv([12, 16]); T0 = cv([12, 16])
        LIr = cv([512]); LFr = cv([512]); SCN = cv([2, 2, 256]); NBF = cv([2])
        MFB = cv([2, 2]); EMF = cv([2, 2]); DGE = cv([2, 2, 16]); EMB = cv([2, 2, 16]); EM0 = cv([16])
        qTs = [cv([NT], BF16) for _ in range(2)]; kTs = [cv([NT], BF16) for _ in range(2)]
        vTs = [cv([NT], BF16) for _ in range(2)]; OGts = [cv([NT], BF16) for _ in range(2)]
        V_tms = [cv([12, 129], BF16) for _ in range(2)]; K_tms = [cv([12, 64], BF16) for _ in range(2)]
        Hs = cv([12, 128])
        off_st = off
        ST_S = cv([8, 2, 128], BF16); ST_QB = cv([8, 2, 128], BF16); ST_KW = cv([8, 2, 64], BF16); CCE = cv([8, 4])
        SQ2, _ = self.carve(off_st, [12, 128]); SSQ = cv([12])
        off_tmp = off
        ON, _w = self.carve(off_tmp, [12, 128], BF16); OGh, _w2 = self.carve(off_tmp + 768, [NT], BF16)
        TMPX = [self.carve(off_tmp + 1536 + 512 * i, [512])[0] for i in range(2)]
        NBm = 4
        FM2 = [cv([2, 128]) for _ in range(NBm)]
        EMn = [cv([256]) for _ in range(NBm)]
        ER = [cv([256], BF16) for _ in range(NBm)]; CC = [cv([8]) for _ in range(NBm)]
        MdT2 = CF[:, 7 * 128:9 * 128].rearrange("p (d c) -> p d c", d=2)
        bc2 = lambda col2: col2.unsqueeze(2).broadcast_to([128, 2, 128])
        v2 = lambda t: t.rearrange("p (d c) -> p d c", d=2)
        CA = [cv([129]) for _ in range(4)]; CAb = [cv([130], BF16) for _ in range(4)]; CAo = [cv([129]) for _ in range(4)]
        DN = [cv([2]) for _ in range(4)]
        assert off <= self.scr_words, off
        cnt = {"pre": 0, "sc": 0}
        one_col = CF[:, C_ONE * 128:C_ONE * 128 + 1]
        ones_f, neg_f = cf(C_ONE), cf(C_NEG)
        mp = self.mpar[:].rearrange("p (a t n) -> p a t n", a=2, t=12)

        def gloads(slot):
            return [(slot[:, 0:256].rearrange("p (k n) -> p k n", k=8), self.ml_wg.rearrange("(k p) n -> p k n", p=128)),
                    (slot[:, 256:512].rearrange("p (k n) -> p k n", k=8), self.ml_wgr.rearrange("(k p) n -> p k n", p=128))]

        def gfn(slot):
            wg = slot[:, 0:256].rearrange("p (k n) -> p k n", k=8)
            wr = slot[:, 256:512].rearrange("p (k n) -> p k n", k=8)
            for vv in V_tms:
                S.memset("dve", vv[:, :, 128:129], 1.0)
            ps = self.bank("x")
            for ti in range(12):
                for kc in range(8):
                    S.matmul(ps[:, ti * 32:(ti + 1) * 32], self.H[:, kc, ti * 128:(ti + 1) * 128], wg[:, kc, :],
                             start=(kc == 0), stop=(kc == 7))
            pv = ps[:, 0:384].rearrange("p (t d k h) -> p t d k h", t=12, d=2, k=2)
            for d in range(2):
                S.tt("dve", LI[:, :, d * 8:(d + 1) * 8], pv[:, :, d, 0, :], mp[:, 0, :, d * 8:(d + 1) * 8], ALU.add)
                S.tt("dve", T0[:, :, d * 8:(d + 1) * 8], pv[:, :, d, 1, :], mp[:, 1, :, d * 8:(d + 1) * 8], ALU.add)
            S.act(T0, T0, AF.Exp, scale=-1.0)
            S.act(T0, T0, AF.Ln, bias=one_col)
            S.ts("dve", LF, T0, -1.0)
            pr = self.bank("x")
            for kc in range(8):
                S.matmul(pr[0:16, 0:512], wr[:, kc, 0:16], self.H[:, kc, 0:512], start=(kc == 0), stop=(kc == 7))
            S.act(LIr[0:16, :], pr[0:16, 0:512], AF.Identity, bias=self.mparT[0:16, 0:1])
            pr2 = self.bank("x")
            for kc in range(8):
                S.matmul(pr2[0:16, 0:512], wr[:, kc, 16:32], self.H[:, kc, 0:512], start=(kc == 0), stop=(kc == 7))
            S.ts("dve", NBF[0:16, 0:1], self.mparT[0:16, 1:2], -1.0)
            S.act(LFr[0:16, :], pr2[0:16, 0:512], AF.Exp, bias=NBF[0:16, 0:1], scale=-1.0)
            S.act(LFr[0:16, :], LFr[0:16, :], AF.Ln, bias=one_col[0:16, :])
            S.ts("dve", LFr[0:16, :], LFr[0:16, :], -1.0)
            for s in range(2):
                for fb in range(2):
                    if fb == 0:
                        d0, d1 = LFr[0:16, s * 256:(s + 1) * 256], LIr[0:16, s * 256:(s + 1) * 256]
                    elif s == 0:
                        d0, d1 = LFr[0:16, 255::-1], LIr[0:16, 255::-1]
                    else:
                        d0, d1 = LFr[0:16, 511:255:-1], LIr[0:16, 511:255:-1]
                    o = SCN[0:16, s, fb, :]
                    S.op("dve", lambda e, o=o, d0=d0, d1=d1: e.tensor_tensor_scan(o, d0, d1, 0.0, ALU.add, ALU.max), [d0, d1], [o])
                    S.copy("dve", MFB[0:16, s, fb:fb + 1], SCN[0:16, s, fb, 255:256])
                S.dma("sp", self.nsm[s, 0, :], MFB[0:8, s, 0:1])
                S.dma("sp", self.nsm[s, 1, :], MFB[8:16, s, 1:2])
            S.act(EMF[0:16], MFB[0:16], AF.Exp, scale=-1.0)
            pe = self.bank("x")
            for s in range(2):
                for fb in range(2):
                    S.ts("dve", DGE[0:16, s, fb, :], CF[0:16, 0:16], EMF[0:16, s, fb:fb + 1])
                    c0 = (s * 2 + fb) * 16
                    S.matmul(pe[0:64, c0:c0 + 16], ones_f[0:16, 0:64], DGE[0:16, s, fb, :])
            S.copy("dve", EMB[0:64].rearrange("p a b c -> p (a b c)"), pe[0:64, 0:64])
            S.act(EM0[0:64], self.smm[0:64, :], AF.Exp)
        self.step(gloads, gfn)

        cur = {}

        def pre2(h, ti, n, b):
            qT, kT, K_tm = cur["qT"], cur["kT"], cur["K_tm"]
            lf2 = LF[:, ti, h:h + 9:8]
            li2 = LI[:, ti, h:h + 9:8]
            S.tt("dve", FM2[b], MdT2, bc2(lf2), ALU.mult)
            FMf = FM2[b].rearrange("p d c -> p (d c)")
            pD = self.ps[b]
            S.matmul(pD[:, 0:256], ones_f, FMf, start=True, stop=False)
            for d in range(2):
                S.matmul(pD[:, d * 128:(d + 1) * 128], FM2[b][:, d, :], neg_f, start=False, stop=(d == 1))
            S.matmul(pD[:, 256:512], ones_f, FMf)
            yield
            S.act(EMn[b], pD[:, 0:256], AF.Relu, scale=-1.0)
            for d in range(2):
                S.act(EMn[b][:, d * 128:(d + 1) * 128], EMn[b][:, d * 128:(d + 1) * 128], AF.Exp, bias=li2[:, d:d + 1], scale=-1.0)
            S.tt("pool", v2(EMn[b]), v2(EMn[b]), MdT2, ALU.mult)
            S.act(ER[b][0:64, :], pD[0:64, 256:512], AF.Exp)
            pG = self.ps[b]
            for d in range(2):
                S.matmul(pG[:, d:d + 1], FM2[b][:, d, :], ones_f[:, 0:1])
            S.matmul(pG[:, 2:4], ones_f, lf2)
            yield
            S.copy("act", CC[b][:, 0:4], pG[:, 0:4])
            S.tt("pool", CC[b][:, 4:6], CC[b][:, 2:4], li2, ALU.add)
            S.act(CCE[:, n, 0:2], CC[b][:, 2:4], AF.Exp)
            for d in range(2):
                S.act(CCE[:, n, 2 + d:3 + d], CC[b][:, d:d + 1], AF.Exp, bias=CC[b][:, 4 + d:5 + d], scale=-1.0)
            yield
            pB = self.ps[b]
            S.matmul(pB[:, 0:128], kT[0:64, ti * 128:(ti + 1) * 128], qT[0:64, ti * 128:(ti + 1) * 128])
            S.tt("dve", ST_S[:, n], pB[:, 0:128].unsqueeze(1).broadcast_to([128, 2, 128]), v2(EMn[b]), ALU.mult)
            S.tt("pool", ST_QB[0:64, n], qT[0:64, ti * 128:(ti + 1) * 128].unsqueeze(1).broadcast_to([64, 2, 128]),
                 v2(ER[b])[0:64], ALU.mult)
            S.tt("pool", ST_KW[:, n], K_tm[:, ti, :].unsqueeze(1).broadcast_to([128, 2, 64]),
                 CCE[:, n, 2:4].unsqueeze(2).broadcast_to([128, 2, 64]), ALU.mult)
            yield

        def scan_step(ti, n, d, ci, first):
            b = ci
            V_tm = cur["V_tm"]
            pN = self.bank("sc", [0, 1, 2, 3, 4, 5])
            S.matmul(pN[:, 0:129], ST_QB[0:64, n, d, :], CAb[ci][0:64, 0:129], start=True, stop=False)
            S.matmul(pN[:, 0:129], ST_S[:, n, d, :], V_tm[:, ti, :], start=False, stop=True)
            S.act(DN[b][:, 0:1], pN[:, 128:129], AF.Abs)
            S.ts("dve", DN[b][:, 0:1], DN[b][:, 0:1], 1.0, None, ALU.max)
            S.recip(DN[b][:, 0:1], DN[b][:, 0:1])
            S.stt(Hs[:, ti, :], pN[:, 0:128], DN[b][:, 0:1], Hs[:, ti, :], ALU.mult, ALU.add)
            yield
            pS = self.bank("sc", [0, 1, 2, 3, 4, 5])
            S.matmul(pS[0:64, 0:129], ST_KW[:, n, d, :], V_tm[:, ti, :])
            S.stt(CA[ci][0:64, :], CA[ci][0:64, :], CCE[0:64, n, d:d + 1], pS[0:64, 0:129], ALU.mult, ALU.add)
            S.copy("act", CAb[ci][0:64, 0:129], CA[ci][0:64, :])
            yield

        nheads = self.cfg.get("ml_heads", 8)

        def interleave_g(gens):
            gens = list(gens)
            while gens:
                for g in list(gens):
                    try:
                        next(g)
                    except StopIteration:
                        gens.remove(g)
                yield

        def genA(h, wv):
            p = h % 2
            qT, kT, vT, OGt, V_tm, K_tm = qTs[p], kTs[p], vTs[p], OGts[p], V_tms[p], K_tms[p]
            for j in range(4):
                lo, hi = [(0, 64), (64, 128), (128, 256), (256, 384)][j]
                M = hi - lo
                for tt, (t0, t1) in enumerate(TILES):
                    ps = self.bank("fa", [6, 7])
                    for kc in range(8):
                        S.matmul(ps[0:M, :], wv[:, kc, lo:hi], self.H[:, kc, t0:t1], start=(kc == 0), stop=(kc == 7))
                    if j == 0:
                        S.act(qT[0:64, t0:t1], ps[0:64, :], AF.Copy, scale=0.125)
                    elif j == 1:
                        S.copy("act", kT[0:64, t0:t1], ps[0:64, :])
                    elif j == 2:
                        S.copy("act", vT[:, t0:t1], ps[:])
                    else:
                        S.act(OGt[:, t0:t1], ps[:], AF.Sigmoid)
                    yield
            for g4 in range(3):
                pb = self.bank("fa", [6, 7])[:].bitcast(BF16)
                for i in range(4):
                    ti = g4 * 4 + i
                    S.transpose(pb[:, i * 128:(i + 1) * 128], vT[:, ti * 128:(ti + 1) * 128], ident_bf)
                S.copy("act", V_tm[:, g4 * 4:(g4 + 1) * 4, 0:128], pb[:, 0:512].rearrange("p (a b) -> p a b", a=4))
                yield
            for g4 in range(3):
                pb = self.bank("fa", [6, 7])[:].bitcast(BF16)
                for i in range(4):
                    ti = g4 * 4 + i
                    S.transpose(pb[:, i * 64:(i + 1) * 64], kT[0:64, ti * 128:(ti + 1) * 128], ident_bf[0:64, 0:64])
                S.copy("act", K_tm[:, g4 * 4:(g4 + 1) * 4, :], pb[:, 0:256].rearrange("p (a b) -> p a b", a=4))
                yield

        def genB(h, wo):
            p = h % 2
            cur.update(qT=qTs[p], kT=kTs[p], K_tm=K_tms[p], V_tm=V_tms[p])
            OGt = OGts[p]

            def chainx(tiles, sidx, d, n0, ci):
                order = tiles if d == 0 else tiles[::-1]
                if sidx == 2:
                    S.dma("sp", CA[ci][0:64, :], self.smca[d, h])
                    S.ts("dve", CA[ci][0:64, :], CA[ci][0:64, :], EM0[0:64, d * 8 + h:d * 8 + h + 1])
                    S.copy("act", CAb[ci][0:64, 0:129], CA[ci][0:64, :])
                else:
                    S.memset("dve", CA[ci][0:64, :], 0.0)
                    S.memset("dve", CAb[ci][0:64, :], 0.0)
                for ti in order:
                    yield from scan_step(ti, n0 + tiles.index(ti), d, ci, first=False)
                if sidx < 2:
                    S.ts("dve", CAo[ci][0:64, :], CA[ci][0:64, :], EMB[0:64, sidx, d, d * 8 + h:d * 8 + h + 1])
                    S.dma("sp", self.nsc[sidx, d, h], CAo[ci][0:64, 0:128])
                    S.dma("sp", self.nsn[sidx, d, h, :], CAo[ci][0:64, 128:129])

            S.memset("dve", Hs, 0.0)
            for grp in ([0, 1, 2, 3],):
                yield from interleave_g([pre2(h, ti, ti, k) for k, ti in enumerate(grp)])
            yield from interleave_g([chainx([0, 1], 0, d, 0, d) for d in range(2)] + [chainx([2, 3], 1, d, 2, 2 + d) for d in range(2)])
            for grp in ([4, 5, 6, 7], [8, 9, 10, 11]):
                yield from interleave_g([pre2(h, ti, ti - 4, k) for k, ti in enumerate(grp)])
            yield from interleave_g([chainx(list(range(4, 12)), 2, d, 0, d) for d in range(2)])
            S.act(SQ2, Hs, AF.Square)
            S.op("dve", lambda e: e.reduce_sum(SSQ, SQ2, mybir.AxisListType.X), [SQ2], [SSQ])
            S.act(SSQ, SSQ, AF.Sqrt, bias=self.eps_col[:, 0:1], scale=1.0 / 128.0)
            S.recip(SSQ, SSQ)
            yield
            S.tt("dve", ON, Hs, SSQ.unsqueeze(2).broadcast_to([128, 12, 128]), ALU.mult)
            yield
            for g4 in range(3):
                pb = self.bank("sc", [0, 1, 2, 3, 4, 5])[:].bitcast(BF16)
                for i in range(4):
                    ti = g4 * 4 + i
                    S.transpose(pb[:, i * 128:(i + 1) * 128], ON[:, ti, :], ident_bf)
                S.stt(OGh[:, g4 * 512:(g4 + 1) * 512], pb[:, 0:512], self.ml_normT[:, 0:1], OGt[:, g4 * 512:(g4 + 1) * 512],
                      ALU.mult, ALU.mult)
                yield
            g1 = self.mod(l, 2)
            for oc in range(8):
                for tt, (t0, t1) in enumerate(TILES):
                    ps = self.bank("sc", [0, 1, 2, 3, 4, 5])
                    S.matmul(ps[:], wo[:, oc * 128:(oc + 1) * 128], OGh[:, t0:t1])
                    cd = COND[tt]
                    if (oc * 3 + tt) % 2 == 0:
                        S.stt(self.X[:, oc, t0:t1], ps[:], g1[:, oc, cd:cd + 1], self.X[:, oc, t0:t1], ALU.mult, ALU.add)
                    else:
                        tx = TMPX[((oc * 3 + tt) // 2) % 2]
                        S.act(tx, ps[:], AF.Identity, scale=g1[:, oc, cd:cd + 1])
                        S.tt("pool", self.X[:, oc, t0:t1], self.X[:, oc, t0:t1], tx, ALU.add)
                yield

        def drain(g):
            for _ in g:
                pass

        def loads0(slot):
            return [(slot[:, 0:3072].rearrange("p (k n) -> p k n", k=8), self.ml_wh[0].rearrange("(k p) n -> p k n", p=128))]
        self.step(loads0, lambda slot: drain(genA(0, slot[:, 0:3072].rearrange("p (k n) -> p k n", k=8))))
        for h in range(nheads):
            def loadsH(slot, h=h):
                out = [(slot[:, 3072:4096], self.ml_wo[h * 128:(h + 1) * 128, :])]
                if h + 1 < nheads:
                    out.append((slot[:, 0:3072].rearrange("p (k n) -> p k n", k=8), self.ml_wh[h + 1].rearrange("(k p) n -> p k n", p=128)))
                return out

            def fnH(slot, h=h):
                gens = [genB(h, slot[:, 3072:4096])]
                if h + 1 < nheads:
                    gens.append(genA(h + 1, slot[:, 0:3072].rearrange("p (k n) -> p k n", k=8)))
                drain(interleave_g(gens))
            self.step(loadsH, fnH)

    def build(self):
        cfg = self.cfg
        nc = self.nc
        S = self.S
        self.xT = self.dram_in("xT", [D, NT])
        self.condT = self.dram_in("condT", [128, 16])
        self.w_ada = self.dram_in("w_ada", [4, D, 6 * D])
        b_adaT_d = self.dram_in("b_adaT", [128, 4 * 48])
        nmT_d = self.dram_in("nmT", [128, 72])
        self.w_up = self.dram_in("w_up", [4, D, 2 * DFF])
        cwT_d = self.dram_in("cwT", [128, 4 * NCH * 9])
        cbT_d = self.dram_in("cbT", [128, 4 * NCH])
        self.w_down = self.dram_in("w_down", [4, DFF, D])
        self.fnet_w = self.dram_in("fnet_w", [2, D, D])
        self.fnet_b_d = self.dram_in("fnet_b", [1, 2 * D])
        consts_d = self.dram_in("consts", [128, 1152])
        cs3_d = self.dram_in("cs3", [256, 768])
        self.tab = self.dram_in("tab", [4, 128, 2, 8, 256])
        self.dn_wh = self.dram_in("dn_wh", [8, D, 512])
        self.dn_wg = self.dram_in("dn_wg", [D, 32])
        dcwT_d = self.dram_in("dcwT", [128, 120])
        mask2_d = self.dram_in("mask2", [128, 1280])
        gpar_d = self.dram_in("gpar", [128, 2 * 12 * 16])
        dn_normT_d = self.dram_in("dn_normT", [128, 1])
        self.dn_wo = self.dram_in("dn_wo", [D, D])
        self.sd = self.dram_in("sd", [2, 8, 128, 128])
        self.nsd = self.dram_out("nsd", [2, 2, 8, 128, 128])
        self.ml_wh = self.dram_in("ml_wh", [8, D, 384])
        self.ml_wg = self.dram_in("ml_wg", [D, 32])
        self.ml_wgr = self.dram_in("ml_wgr", [D, 32])
        mpar_d = self.dram_in("mpar", [128, 2 * 12 * 16])
        mparT_d = self.dram_in("mparT", [16, 2])
        ml_normT_d = self.dram_in("ml_normT", [128, 1])
        self.ml_wo = self.dram_in("ml_wo", [D, D])
        self.smca = self.dram_in("smca", [2, 8, 64, 129])
        smm_d = self.dram_in("smm", [64, 16])
        self.nsc = self.dram_out("nsc", [2, 2, 8, 64, 128])
        self.nsn = self.dram_out("nsn", [2, 2, 8, 64])
        self.nsm = self.dram_out("nsm", [2, 2, 8])
        self.yT = self.dram_out("yT", [D, NT])
        ntaps = cfg.get("ntaps", 0)
        self.dbg = self.dram_out("dbg", [ntaps, D, NT]) if ntaps else None

        self.X = self.sb("X", [128, 8, NT])
        self.H = self.sb("H", [128, 8, NT], BF16)
        self.slots = [self.sb("slot%d" % i, [128, 4096], BF16) for i in range(4)]
        self.scr_words = cfg.get("scr_words", 19712)
        self.scr = self.sb("scr", [128, self.scr_words])
        self.scr_tmp = self.scr_words - 3584
        self.scr_main = 0
        self.consts = self.sb("consts_f", [128, 1152])
        self.consts_bf = self.sb("consts_b", [128, 1152], BF16)
        self.ones_bf = self.sb("ones_bf", [128, 512], BF16)
        self.CS3 = self.sb("CS3", [128, 2, 768], BF16)
        self.fb_row = self.sb("fb_row", [1, D], BF16)
        self.b_adaT = self.sb("b_adaT_s", [128, 4 * 48])
        self.nmT = self.sb("nmT_s", [128, 72])
        self.cwT = self.sb("cwT_s", [128, 4 * NCH * 9])
        self.cbT = self.sb("cbT_s", [128, 4 * NCH])
        self.condS = self.sb("condS", [128, 16])
        self.SC = self.sb("SC", [128, 8, 2], BF16)
        self.MOD = self.sb("MOD", [128, 4, 48, 2])
        self.AMOD = self.sb("AMOD", [128, 4, 2, 8, 2])
        self.eps_col = self.sb("eps_col", [128, 2])
        self.mpar = self.sb("mpar_s", [128, 2 * 12 * 16])
        self.mparT = self.sb("mparT_s", [16, 2])
        self.ml_normT = self.sb("ml_normT_s", [128, 1])
        self.smm = self.sb("smm_s", [64, 16])
        self.dcwT = self.sb("dcwT_s", [128, 120])
        self.mask2 = self.sb("mask2_s", [128, 5, 256], BF16)
        self.gpar = self.sb("gpar_s", [128, 2 * 12 * 16])
        self.dn_normT = self.sb("dn_normT_s", [128, 1])
        self.ps = [self.es.enter_context(nc.psum_tensor("ps%d" % i, [128, 512], F32)) for i in range(8)]
        self.ident_bf = self.consts_bf[:, C_ID * 128:(C_ID + 1) * 128]

        S.dma("sp", self.X[:], self.xT.rearrange("(c p) t -> p c t", p=128))
        S.dma("sp", self.condS[:], self.condT)
        S.dma("sp", self.consts[:], consts_d)
        S.dma("pool", self.consts_bf[:], consts_d)
        S.dma("pool", self.CS3[:], cs3_d.rearrange("(j p) n -> p j n", p=128))
        S.dma("sp", self.b_adaT[:], b_adaT_d)
        S.dma("sp", self.nmT[:], nmT_d)
        S.dma("sp", self.cwT[:], cwT_d)
        S.dma("sp", self.cbT[:], cbT_d)
        S.memset("dve", self.ones_bf[:], 1.0)
        S.memset("dve", self.eps_col[:, 0:1], EPS)
        S.memset("dve", self.eps_col[:, 1:2], 128.0 * EPS)
        S.dma("sp", self.mpar[:], mpar_d)
        S.dma("sp", self.mparT[:], mparT_d)
        S.dma("sp", self.ml_normT[:], ml_normT_d)
        S.dma("sp", self.smm[:], smm_d)
        S.dma("sp", self.dcwT[:], dcwT_d)
        S.dma("pool", self.mask2[:].rearrange("p a b -> p (a b)"), mask2_d)
        S.dma("sp", self.gpar[:], gpar_d)
        S.dma("sp", self.dn_normT[:], dn_normT_d)
        S.act(self.SC[:].rearrange("p k c -> p (k c)"), self.condS[:], AF.Silu)

        layers = cfg.get("layers", [0, 1, 2, 3])

        def collect(fn):
            keep = self.steps
            self.steps = []
            fn()
            out = self.steps
            self.steps = keep
            return out

        def merge(a, b):
            out = []
            ia = ib = 0
            while ia < len(a) or ib < len(b):
                if ia < len(a):
                    out.append(a[ia]); ia += 1
                want = (ia * len(b)) // max(1, len(a)) if ia < len(a) else len(b)
                while ib < want:
                    out.append(b[ib]); ib += 1
            return out

        k = 0
        ada0 = collect(lambda: self.adaln(layers[0]))
        self.steps += ada0[:5]
        pending = ada0[5:]
        for li, l in enumerate(layers):
            kind = l % 3
            mix = []
            if kind == 0 and cfg.get("fnet", True):
                mix = collect(lambda: self.fnet(l, l // 3))
            elif kind == 1 and cfg.get("gdn", True):
                mix = collect(lambda: self.gdn(l))
            elif kind == 2 and cfg.get("mlstm", True):
                mix = collect(lambda: self.mlstm(l))
            extra = pending
            if li + 1 < len(layers):
                extra = extra + collect(lambda: self.adaln(layers[li + 1]))
            pending = []
            self.steps += merge(mix, extra)
            self.step(None, lambda slot, k=k: self.tap(k))
            k += 1
            if cfg.get("ffn", True):
                self.ffn(l)
            self.step(None, lambda slot, k=k: self.tap(k))
            k += 1
        self.run_steps()

        Y, w = self.carve(0, [8, NT])
        self.norm_mod(self.nmT[:, 64:72], None, out_y=Y)
        S.dma("sp", self.yT.rearrange("(c p) t -> p c t", p=128), Y)
        S.wait_all("sp")
        S.emit()
        self.es.close()
        return nc


def _prep(inputs):
    consts, cs3, tab, mask2 = _const_tables()
    f = lambda k: np.ascontiguousarray(np.asarray(inputs[k], np.float32))
    shared = {
        "w_ada": f("w_ada"),
        "b_adaT": _fm(f("b_ada")).reshape(128, 4 * 48),
        "nmT": np.concatenate([_fm(f("norm_mix")).reshape(128, 32), _fm(f("norm_ffn")).reshape(128, 32),
                               _fm(f("norm_final")).reshape(128, 8)], axis=1),
        "w_up": f("ffn_w_up"),
        "cwT": np.ascontiguousarray(np.moveaxis(_fm(f("ffn_conv_w").reshape(4, 9, DFF)), 2, 3)).reshape(128, 4 * NCH * 9),
        "cbT": _fm(f("ffn_conv_b")).reshape(128, 4 * NCH),
        "w_down": f("ffn_w_down"),
        "fnet_w": f("fnet_w"),
        "fnet_b": f("fnet_b").reshape(1, 2 * D),
        "consts": consts, "cs3": cs3, "tab": tab, "mask2": mask2,
    }
    wi = f("dn_w_in")[0]
    shared["dn_wh"] = np.ascontiguousarray(np.stack(
        [np.concatenate([wi[:, j * 1024 + h * 128:j * 1024 + (h + 1) * 128] for j in range(4)], axis=1) for h in range(8)]))
    shared["dn_wg"] = np.ascontiguousarray(wi[:, 4096:4128])
    shared["dcwT"] = np.ascontiguousarray(np.moveaxis(_fm(f("dn_conv_w")[0]), 1, 2)).reshape(128, 120)
    gp = np.stack([f("dn_a_log")[0].reshape(16), f("dn_dt_bias")[0].reshape(16)])
    shared["gpar"] = np.ascontiguousarray(np.broadcast_to(gp[None, :, None, :], (128, 2, 12, 16))).reshape(128, 384)
    shared["dn_normT"] = np.ascontiguousarray(f("dn_norm")[0].reshape(128, 1))
    shared["dn_wo"] = f("dn_w_out")[0]
    sdel = f("state_delta")
    mw = f("ml_w_in")[0]
    shared["ml_wh"] = np.ascontiguousarray(np.stack(
        [np.concatenate([mw[:, h * 64:(h + 1) * 64], mw[:, 512 + h * 64:512 + (h + 1) * 64],
                         mw[:, 1024 + h * 128:1024 + (h + 1) * 128], mw[:, 2048 + h * 128:2048 + (h + 1) * 128]], axis=1)
         for h in range(8)]))
    mg = mw[:, 3072:3104]
    shared["ml_wg"] = np.ascontiguousarray(mg)
    mg4 = mg.reshape(1024, 2, 2, 8)
    shared["ml_wgr"] = np.ascontiguousarray(np.concatenate([mg4[:, :, 0, :].reshape(1024, 16), mg4[:, :, 1, :].reshape(1024, 16)], axis=1))
    bp = np.stack([f("ml_b_i")[0].reshape(16), f("ml_b_f")[0].reshape(16)])
    shared["mpar"] = np.ascontiguousarray(np.broadcast_to(bp[None, :, None, :], (128, 2, 12, 16))).reshape(128, 384)
    shared["mparT"] = np.ascontiguousarray(bp.T)
    shared["ml_normT"] = np.ascontiguousarray(f("ml_norm")[0].reshape(128, 1))
    shared["ml_wo"] = f("ml_w_out")[0]
    smc, smn, smmm = f("state_mlstm_c"), f("state_mlstm_n"), f("state_mlstm_m")
    xp = f("x_prompt")
    xs = f("x_sample")
    c = f("c")
    cctx = f("c_ctx")
    per_core = []
    for i in range(N_CORES):
        b = i // 4
        x = np.concatenate([xp[2 * i], xp[2 * i + 1], xs[b]], axis=0)
        cond = np.stack([cctx, c[b]], axis=-1)
        m = dict(shared)
        m["xT"] = np.ascontiguousarray(x.T)
        m["sd"] = np.ascontiguousarray(sdel[b, 0])
        m["smca"] = np.ascontiguousarray(np.concatenate([smc[b, 0], smn[b, 0][..., None]], axis=-1))
        m["smm"] = np.ascontiguousarray(np.broadcast_to(smmm[b, 0].reshape(1, 16), (64, 16)))
        m["condT"] = np.ascontiguousarray(cond.reshape(8, 128, 2).transpose(1, 0, 2)).reshape(128, 16)
        per_core.append(m)
    return per_core


def run(inputs, cfg, core_ids=None, trace=False):
    b = Builder(cfg)
    nc = b.build()
    maps = _prep(inputs)
    core_ids = core_ids or list(range(N_CORES))
    maps = [maps[i] for i in core_ids]
    res = run_bass_kernel_spmd(nc, maps, core_ids=list(range(len(core_ids))), trace=trace)
    return res, b


def kernel(**inputs):
    res, b = run(inputs, dict())
    R = res.results
    y_prompt = np.zeros((16, 256, D), np.float32)
    y_sample = np.zeros((2, 1024, D), np.float32)
    new_d = np.zeros((16, 1, 2, 8, 128, 128), np.float32)
    new_c = np.zeros((16, 1, 2, 8, 64, 128), np.float32)
    new_n = np.zeros((16, 1, 2, 8, 64), np.float32)
    new_m = np.zeros((16, 1, 2, 8), np.float32)
    for i in range(N_CORES):
        y = np.asarray(R[i]["yT"]).T
        y_prompt[2 * i] = y[0:256]
        y_prompt[2 * i + 1] = y[256:512]
        if i % 4 == 0:
            y_sample[i // 4] = y[512:]
        new_d[2 * i:2 * i + 2, 0] = np.asarray(R[i]["nsd"])
        new_c[2 * i:2 * i + 2, 0] = np.asarray(R[i]["nsc"])
        new_n[2 * i:2 * i + 2, 0] = np.asarray(R[i]["nsn"])
        new_m[2 * i:2 * i + 2, 0] = np.asarray(R[i]["nsm"])
    return (y_prompt, y_sample, new_d, new_c, new_n, new_m)
```

```python
import numpy as np
from contextlib import ExitStack
import concourse.bass as bass
import concourse.mybir as mybir
from concourse.bass_utils import run_bass_kernel_spmd

F32 = mybir.dt.float32
BF16 = mybir.dt.bfloat16
AF = mybir.ActivationFunctionType
ALU = mybir.AluOpType

ENGS = ("pe", "act", "dve", "pool", "sp")
D = 1024
NT = 1536
DFF = 2816
NCH = 22
TILES = [(0, 512), (512, 1024), (1024, 1536)]
COND = [0, 1, 1]
EPS = 1e-6
N_CORES = 8


def _rect(ap):
    t = ap.tensor
    name = t.name
    pat = ap.ap
    off = ap.offset
    esz = mybir.dt.size(ap.dtype)
    if "dram" in str(type(t)).lower() or "DRam" in str(type(t)):
        ext = 1
        for st, cnt in pat:
            ext += (cnt - 1) * abs(st)
        return (name, 0, 1, off * esz, (off + ext) * esz)
    shape = list(t.shape)
    fsz = 1
    for s in shape[1:]:
        fsz *= s
    pcnt = pat[0][1]
    p_lo = off // fsz
    f_lo = off % fsz
    lo = 0
    hi = 0
    for st, cnt in pat[1:]:
        if st >= 0:
            hi += (cnt - 1) * st
        else:
            lo += (cnt - 1) * st
    return (name, p_lo, p_lo + pcnt, (f_lo + lo) * esz, (f_lo + hi + 1) * esz)


class Sched:
    def __init__(self, nc, n_dma_sems=8):
        self.nc = nc
        self.q = {e: [] for e in ENGS}
        self.cnt = {e: 0 for e in ENGS}
        self.waited = {e: {} for e in ENGS}
        self.recs = {}
        self.n_dma_sems = n_dma_sems
        self.dma_i = {e: 0 for e in ENGS}
        self.dma_cnt = {}
        self.n_ops = 0

    def _deps(self, eng, ap, is_write):
        r = _rect(ap)
        lst = self.recs.setdefault(r[0], [])
        is_psum = r[0].startswith("ps")
        deps = []
        keep = []
        for rec in lst:
            (_, pl, ph, fl, fh), tok, w, e = rec
            overlap = not (ph <= r[1] or r[2] <= pl or fh <= r[3] or r[4] <= fl)
            if is_psum and e != eng:
                deps.append(tok)
                continue
            if overlap:
                if is_write or w:
                    same = (e == eng) and tok[0] == e
                    if same and eng == "pe":
                        pass
                    else:
                        deps.append(tok)
                if is_write and pl >= r[1] and ph <= r[2] and fl >= r[3] and fh <= r[4]:
                    continue
            keep.append(rec)
        self.recs[r[0]] = keep
        return deps, r

    def _emit_waits(self, eng, deps):
        w = self.waited[eng]
        best = {}
        for k, v in deps:
            if w.get(k, 0) >= v:
                continue
            if best.get(k, 0) < v:
                best[k] = v
        for k, v in best.items():
            w[k] = v
            self.q[eng].append(("wait", k, v))

    def _record(self, r, tok, w, eng):
        lst = self.recs[r[0]]
        if not w:
            for rec in lst:
                if (not rec[2]) and rec[3] == eng and rec[0] == r and rec[1][0] == tok[0]:
                    rec[1] = tok
                    return
        lst.append([r, tok, w, eng])

    def op(self, eng, fn, reads=(), writes=()):
        deps = []
        rr = []
        for ap in reads:
            d, r = self._deps(eng, ap, False)
            deps += d
            rr.append((r, False))
        for ap in writes:
            d, r = self._deps(eng, ap, True)
            deps += d
            rr.append((r, True))
        self._emit_waits(eng, deps)
        self.cnt[eng] += 1
        tok = (eng, self.cnt[eng])
        self.q[eng].append(("op", fn, eng, 1))
        for r, w in rr:
            self._record(r, tok, w, eng)
        self.n_ops += 1
        return tok

    def dma(self, eng, out, in_, **kw):
        deps = []
        d, r_in = self._deps(eng, in_, False)
        deps += d
        d, r_out = self._deps(eng, out, True)
        deps += d
        i = self.dma_i[eng] % self.n_dma_sems
        self.dma_i[eng] += 1
        key = ("dma", eng, i)
        prev = self.dma_cnt.get(key, 0)
        if prev:
            deps.append((key, prev))
        self._emit_waits(eng, deps)
        val = prev + 16
        self.dma_cnt[key] = val
        tok = (key, val)
        self.q[eng].append(("op", lambda e: e.dma_start(out=out, in_=in_, **kw), key, 16))
        self._record(r_in, tok, False, eng)
        self._record(r_out, tok, True, eng)
        self.n_ops += 1
        return tok

    def wait_all(self, eng):
        deps = []
        for e in ENGS:
            if self.cnt[e] and e != eng:
                deps.append((e, self.cnt[e]))
        for k, v in self.dma_cnt.items():
            deps.append((k, v))
        self._emit_waits(eng, deps)

    def matmul(self, out, lhsT, rhs, start=True, stop=True):
        return self.op("pe", lambda e: e.matmul(out, lhsT, rhs, start=start, stop=stop), [lhsT, rhs], [out])

    def transpose(self, out, in_, ident):
        return self.op("pe", lambda e: e.transpose(out, in_, ident), [in_, ident], [out])

    def act(self, out, in_, func, bias=None, scale=None, accum_out=None):
        kw = {}
        rd = [in_]
        if bias is not None:
            kw["bias"] = bias
            if not isinstance(bias, (int, float)):
                rd.append(bias)
        if scale is not None:
            kw["scale"] = scale
            if not isinstance(scale, (int, float)):
                rd.append(scale)
        wr = [out]
        if accum_out is not None:
            kw["accum_out"] = accum_out
            wr.append(accum_out)
        return self.op("act", lambda e: e.activation(out, in_, func, **kw), rd, wr)

    def tt(self, eng, out, in0, in1, op):
        return self.op(eng, lambda e: e.tensor_tensor(out, in0, in1, op), [in0, in1], [out])

    def ts(self, eng, out, in0, s1, s2=None, op0=ALU.mult, op1=None):
        rd = [in0]
        for s in (s1, s2):
            if s is not None and not isinstance(s, (int, float)):
                rd.append(s)
        if op1 is None:
            return self.op(eng, lambda e: e.tensor_scalar(out, in0, s1, None, op0), rd, [out])
        return self.op(eng, lambda e: e.tensor_scalar(out, in0, s1, s2, op0, op1), rd, [out])

    def stt(self, out, in0, scalar, in1, op0, op1):
        rd = [in0, in1]
        if not isinstance(scalar, (int, float)):
            rd.append(scalar)
        return self.op("dve", lambda e: e.scalar_tensor_tensor(out, in0, scalar, in1, op0, op1), rd, [out])

    def copy(self, eng, out, in_):
        if eng == "act":
            return self.op(eng, lambda e: e.copy(out, in_), [in_], [out])
        return self.op(eng, lambda e: e.tensor_copy(out, in_), [in_], [out])

    def memset(self, eng, ap, val):
        return self.op(eng, lambda e: e.memset(ap, val), [], [ap])

    def recip(self, out, in_):
        return self.op("dve", lambda e: e.reciprocal(out, in_), [in_], [out])

    def emit(self):
        nc = self.nc
        keys = [e for e in ENGS if self.cnt[e]] + list(self.dma_cnt.keys())
        with ExitStack() as es:
            sems = {}
            for i, k in enumerate(keys):
                sems[k] = es.enter_context(nc.semaphore("s%d" % i))
            block = es.enter_context(nc.Block())
            q = self.q

            def run(engname, eng):
                for it in q[engname]:
                    if it[0] == "wait":
                        eng.wait_ge(sems[it[1]], it[2])
                    else:
                        it[1](eng).then_inc(sems[it[2]], it[3])

            if q["sp"]:
                @block.sync
                def _(e):
                    run("sp", e)
            if q["act"]:
                @block.scalar
                def _(e):
                    run("act", e)
            if q["dve"]:
                @block.vector
                def _(e):
                    run("dve", e)
            if q["pool"]:
                @block.gpsimd
                def _(e):
                    run("pool", e)
            if q["pe"]:
                @block.tensor
                def _(e):
                    run("pe", e)


def _const_tables():
    i = np.arange(128)
    r, c = np.meshgrid(i, i, indexing="ij")
    mats = [
        (r == c), np.ones((128, 128)), -np.ones((128, 128)),
        (r >= c), (r <= c), (r > c), (r < c), (r <= c), (r >= c),
    ]
    consts = np.concatenate([m.astype(np.float32) for m in mats], axis=1)
    m2 = [(r // 8 == c // 8)]
    for sz in (8, 16, 32, 64):
        m2.append((r // (2 * sz) == c // (2 * sz)) & (r // sz != c // sz))
    mask2 = np.concatenate([np.concatenate([m, m], axis=1).astype(np.float32) for m in m2], axis=1)
    k = np.arange(256)
    ang = 2.0 * np.pi * ((k[:, None] * k[None, :]) % 256) / 256.0
    cs3 = np.concatenate([np.cos(ang), np.sin(ang), -np.sin(ang)], axis=1).astype(np.float32)
    t = np.arange(1024)
    ang = 2.0 * np.pi * ((t[:, None] * t[None, :]) % 1024) / 1024.0
    ct = np.cos(ang).astype(np.float32)
    nst = (-np.sin(ang)).astype(np.float32)
    tab = np.zeros((4, 128, 2, 8, 256), np.float32)
    for q in range(4):
        for j, m in enumerate((ct, nst)):
            blk = m[:, q * 256:(q + 1) * 256].reshape(8, 128, 256)
            tab[q, :, j] = blk.transpose(1, 0, 2)
    return consts, cs3, tab, mask2


C_ID, C_ONE, C_NEG, C_LT, C_UT, C_SLT, C_SUT = range(7)


def _fm(v):
    v = np.asarray(v, np.float32)
    lead = v.shape[:-1]
    n = v.shape[-1] // 128
    return np.ascontiguousarray(np.moveaxis(v.reshape(lead + (n, 128)), -1, 0))


class Builder:
    def __init__(self, cfg):
        self.cfg = cfg
        self.nc = bass.Bass("TRN2", target_bir_lowering=False)
        self.S = Sched(self.nc)
        self.es = ExitStack()
        self.steps = []
        self.bank_ctr = {}

    def dram_in(self, name, shape):
        return self.nc.dram_tensor(name, list(shape), F32, kind="ExternalInput").ap()

    def dram_out(self, name, shape):
        return self.nc.dram_tensor(name, list(shape), F32, kind="ExternalOutput").ap()

    def sb(self, name, shape, dt=F32):
        return self.es.enter_context(self.nc.sbuf_tensor(name, list(shape), dt))

    def carve(self, off, shape, dt=F32):
        n = int(np.prod(shape))
        if dt == BF16:
            w = (n + 1) // 2
            v = self.scr[:, off:off + w].bitcast(BF16)[:, 0:n]
        else:
            w = n
            v = self.scr[:, off:off + w]
        assert off + w <= self.scr_words, (off, w, self.scr_words)
        if len(shape) == 2:
            v = v.rearrange("p (a b) -> p a b", a=shape[0])
        elif len(shape) == 3:
            v = v.rearrange("p (a b c) -> p a b c", a=shape[0], b=shape[1])
        elif len(shape) == 4:
            v = v.rearrange("p (a b c d) -> p a b c d", a=shape[0], b=shape[1], c=shape[2])
        return v, w

    def bank(self, role="x", pool=None):
        pool = pool or list(range(8))
        i = self.bank_ctr.get(role, 0)
        self.bank_ctr[role] = i + 1
        return self.ps[pool[i % len(pool)]]

    def step(self, loads, fn):
        self.steps.append((loads, fn))

    def run_steps(self):
        S = self.S
        R = len(self.slots)
        load_steps = [i for i, (l, f) in enumerate(self.steps) if l is not None]
        slot_of = {si: k % R for k, si in enumerate(load_steps)}
        issued = 0

        def issue(upto):
            nonlocal issued
            while issued < len(load_steps) and issued <= upto:
                si = load_steps[issued]
                slot = self.slots[slot_of[si]]
                for (dstf, src) in self.steps[si][0](slot):
                    S.dma("pool", dstf, src)
                issued += 1

        k = 0
        for i, (l, f) in enumerate(self.steps):
            issue(k + R - 1)
            if l is not None:
                f(self.slots[slot_of[i]])
                k += 1
            else:
                f(None)
        self.steps = []

    def norm_mod(self, A, Bv, out_h=True, out_y=None):
        S = self.S
        off = self.scr_tmp
        SQ, w = self.carve(off, [8, 512], BF16); off += w
        RS, w = self.carve(off, [512]); off += w
        TM, w = self.carve(off, [2, 512]); off += w
        for tt, (t0, t1) in enumerate(TILES):
            cd = COND[tt]
            S.act(SQ, self.X[:, :, t0:t1], AF.Square)
            ps = self.bank("n", [6, 7])
            for fc in range(8):
                S.matmul(ps[:], self.ones_bf[:, 0:128], SQ[:, fc, :], start=(fc == 0), stop=(fc == 7))
            S.act(RS, ps[:], AF.Sqrt, bias=self.eps_col[:, 0:1], scale=1.0 / D)
            S.recip(RS, RS)
            for fc in range(8):
                if out_y is not None:
                    S.stt(out_y[:, fc, t0:t1], self.X[:, fc, t0:t1], A[:, fc:fc + 1], RS, ALU.mult, ALU.mult)
                else:
                    tm = TM[:, fc % 2, :]
                    S.tt("dve", tm, self.X[:, fc, t0:t1], RS, ALU.mult)
                    S.act(self.H[:, fc, t0:t1], tm, AF.Identity, bias=Bv[:, fc, cd:cd + 1], scale=A[:, fc, cd:cd + 1])

    def adaln(self, l):
        S = self.S
        for q in range(12):
            def loads(slot, q=q):
                v = slot[:, 0:4096].rearrange("p (k n) -> p k n", k=8)
                return [(v, self.w_ada[l][:, q * 512:(q + 1) * 512].rearrange("(k p) n -> p k n", p=128))]

            def fn(slot, q=q):
                v = slot[:, 0:4096].rearrange("p (k n) -> p k n", k=8)
                ps = self.bank("n", [6, 7])
                for ocl in range(4):
                    for kc in range(8):
                        S.matmul(ps[:, ocl * 2:ocl * 2 + 2], v[:, kc, ocl * 128:(ocl + 1) * 128],
                                 self.SC[:, kc, :], start=(kc == 0), stop=(kc == 7))
                pv = ps[:, 0:8].rearrange("p (o c) -> p o c", c=2)
                for c in range(2):
                    S.tt("dve", self.MOD[:, l, q * 4:(q + 1) * 4, c], pv[:, :, c],
                         self.b_adaT[:, l * 48 + q * 4: l * 48 + (q + 1) * 4], ALU.add)
            self.step(loads, fn)

        def mkfin(sub):
            def fin(slot):
                for c in range(2):
                    S.stt(self.AMOD[:, l, sub, :, c], self.MOD[:, l, (1 + 3 * sub) * 8:(2 + 3 * sub) * 8, c], 1.0,
                          self.nmT[:, (sub * 4 + l) * 8:(sub * 4 + l + 1) * 8], ALU.add, ALU.mult)
            return fin
        st = self.steps
        self.steps = st[:-12] + st[-12:-8] + [(None, mkfin(0))] + st[-8:] + [(None, mkfin(1))]

    def mod(self, l, j):
        return self.MOD[:, l, j * 8:(j + 1) * 8, :]

    def tap(self, k):
        if self.dbg is not None and k < self.cfg.get("ntaps", 0):
            self.S.dma("sp", self.dbg[k].rearrange("(c p) t -> p c t", p=128), self.X[:])

    def ffn(self, l):
        S = self.S
        self.step(None, lambda slot: self.norm_mod(self.AMOD[:, l, 1], self.mod(l, 3)))
        off = self.scr_main
        P, w = self.carve(off, [4, NT], BF16); off += w
        GP = []
        for b in range(2):
            g, w = self.carve(off, [1720], BF16); off += w
            GP.append(g)
        SB = []
        for b in range(2):
            s, w = self.carve(off, [512]); off += w
            SB.append(s)
        DG = []
        for b in range(2):
            d, w = self.carve(off, [9, 128], BF16); off += w
            DG.append(d)
        assert off <= self.scr_tmp

        def zero(slot):
            for g in GP:
                S.memset("dve", g, 0.0)
        self.step(None, zero)

        def chunk(c, slot, j, pj):
            wv = slot[:, 0:4096].rearrange("p (k n) -> p k n", k=8)
            gp = GP[c % 2]
            gpP = gp[:, 0:516].rearrange("p (s t) -> p s t", s=2)
            gpS = gp[:, 516:516 + 18 * 66].rearrange("p (r c) -> p r c", r=18)
            dg = DG[c % 2]
            i0 = (l * NCH + c) * 9
            S.tt("dve", dg, self.ident_bf.unsqueeze(1).broadcast_to([128, 9, 128]),
                 self.cwT[:, i0:i0 + 9].unsqueeze(2).broadcast_to([128, 9, 128]), ALU.mult)
            psG = []
            for tt, (t0, t1) in enumerate(TILES):
                ps = self.bank("g", [0, 1, 2])
                for kc in range(8):
                    S.matmul(ps[:], wv[:, kc, 256 + j * 128:256 + (j + 1) * 128], self.H[:, kc, t0:t1],
                             start=(kc == 0), stop=(kc == 7))
                psG.append(ps)
            S.copy("act", gpP[:, :, 1:257], psG[0][:].rearrange("p (s t) -> p s t", s=2))
            for hf in range(2):
                S.copy("act", gpS[:, 1 + 8 * hf:9 + 8 * hf, 1:65], psG[1 + hf][:].rearrange("p (r c) -> p r c", r=8))
            psA = []
            for tt, (t0, t1) in enumerate(TILES):
                ps = self.bank("a", [3, 4, 5])
                for kc in range(8):
                    S.matmul(ps[:], wv[:, kc, j * 128:(j + 1) * 128], self.H[:, kc, t0:t1],
                             start=(kc == 0), stop=(kc == 7))
                psA.append(ps)
            for tt, (t0, t1) in enumerate(TILES):
                ps = self.bank("c", [6, 7])
                if tt == 0:
                    for dc in range(3):
                        S.matmul(ps[:].rearrange("p (s t) -> p s t", s=2), dg[:, 3 + dc, :], gpP[:, :, dc:dc + 256],
                                 start=(dc == 0), stop=(dc == 2))
                else:
                    hf = tt - 1
                    n = 0
                    for dr in range(3):
                        for dc in range(3):
                            S.matmul(ps[:].rearrange("p (r c) -> p r c", r=8), dg[:, dr * 3 + dc, :],
                                     gpS[:, 8 * hf + dr:8 * hf + dr + 8, dc:dc + 64], start=(n == 0), stop=(n == 8))
                            n += 1
                sb = SB[tt % 2]
                S.act(sb, ps[:], AF.Silu, bias=self.cbT[:, l * NCH + c:l * NCH + c + 1])
                S.tt("dve", P[:, pj, t0:t1], sb, psA[tt][:], ALU.mult)

        c0 = 0
        while c0 < NCH:
            G = min(4, NCH - c0)
            for pr in range(0, G, 2):
                c = c0 + pr

                def loads(slot, c=c):
                    wv = slot[:, 0:4096].rearrange("p (k n) -> p k n", k=8)
                    wu = self.w_up[l]
                    return [(wv[:, :, 0:256], wu[:, c * 128:(c + 2) * 128].rearrange("(k p) n -> p k n", p=128)),
                            (wv[:, :, 256:512], wu[:, DFF + c * 128:DFF + (c + 2) * 128].rearrange("(k p) n -> p k n", p=128))]

                def fn(slot, c=c, pr=pr):
                    chunk(c, slot, 0, pr)
                    chunk(c + 1, slot, 1, pr + 1)
                self.step(loads, fn)

            def dloads(slot, c0=c0, G=G):
                wv = slot[:, 0:G * 1024].rearrange("p (g n) -> p g n", g=G)
                return [(wv, self.w_down[l][c0 * 128:(c0 + G) * 128, :].rearrange("(g p) n -> p g n", p=128))]

            def dfn(slot, c0=c0, G=G):
                wv = slot[:, 0:G * 1024].rearrange("p (g n) -> p g n", g=G)
                g2 = self.mod(l, 5)
                for oc in range(8):
                    for tt, (t0, t1) in enumerate(TILES):
                        ps = self.bank("g", [0, 1, 2])
                        for j in range(G):
                            S.matmul(ps[:], wv[:, j, oc * 128:(oc + 1) * 128], P[:, j, t0:t1], start=(j == 0), stop=(j == G - 1))
                        cd = COND[tt]
                        S.stt(self.X[:, oc, t0:t1], ps[:], g2[:, oc, cd:cd + 1], self.X[:, oc, t0:t1], ALU.mult, ALU.add)
            self.step(dloads, dfn)
            c0 += G

    def fnet(self, l, jf):
        S = self.S
        self.step(None, lambda slot: self.norm_mod(self.AMOD[:, l, 0], self.mod(l, 0)))
        off = self.scr_main
        AB, w = self.carve(off, [12, 4, 512], BF16); off += w
        Fm, w = self.carve(off, [8, NT], BF16); off += w
        assert off <= self.scr_words
        CS3 = self.CS3

        def stage1(slot):
            S.dma("pool", self.fb_row[:], self.fnet_b_d[:, jf * D:(jf + 1) * D])
            n = 0
            for ti in range(12):
                for g in range(4):
                    ps = self.bank("x")
                    for j in range(2):
                        S.matmul(ps[:], self.H[:, 2 * g + j, ti * 128:(ti + 1) * 128], CS3[:, j, 0:512], start=(j == 0), stop=(j == 1))
                    S.copy("act" if n % 2 else "dve", AB[:, ti, g, :], ps[:])
                    n += 1
        self.step(None, stage1)

        def stage2p(slot):
            for cc in range(8):
                g, hf = cc // 2, cc % 2
                ps = self.bank("x")
                for s in range(2):
                    n = 0
                    for tk in range(2):
                        for part in range(2):
                            S.matmul(ps[:, s * 256:(s + 1) * 256], AB[:, 2 * s + tk, g, part * 256 + hf * 128:part * 256 + (hf + 1) * 128],
                                     CS3[:, tk, part * 512:part * 512 + 256], start=(n == 0), stop=(n == 3))
                            n += 1
                S.act(Fm[:, cc, 0:512], ps[:], AF.Copy, scale=1.0 / 256.0)
        self.step(None, stage2p)

        for q in range(4):
            def loads(slot, q=q):
                return [(slot[:, 0:4096].rearrange("p (a b) -> p a b", a=4),
                         self.tab[q].rearrange("p a k u -> p (a k u)").rearrange("p (a b) -> p a b", a=4))]

            def fn(slot, q=q):
                tv = slot[:, 0:4096].rearrange("p (a k u) -> p a k u", a=2, k=8)
                for cc in range(8):
                    g, hf = cc // 2, cc % 2
                    ps = self.bank("x")
                    n = 0
                    for tk in range(8):
                        for part in range(2):
                            S.matmul(ps[:, 0:256], AB[:, 4 + tk, g, part * 256 + hf * 128:part * 256 + (hf + 1) * 128],
                                     tv[:, part, tk, :], start=(n == 0), stop=(n == 15))
                            n += 1
                    S.act(Fm[:, cc, 512 + q * 256:512 + (q + 1) * 256], ps[:, 0:256], AF.Copy, scale=1.0 / 512.0)
            self.step(loads, fn)

        for half in range(2):
            def loads(slot, half=half):
                wv = slot[:, 0:4096].rearrange("p (k n) -> p k n", k=8)
                return [(wv, self.fnet_w[jf][:, half * 512:(half + 1) * 512].rearrange("(k p) n -> p k n", p=128))]

            def fn(slot, half=half):
                wv = slot[:, 0:4096].rearrange("p (k n) -> p k n", k=8)
                g1 = self.mod(l, 2)
                for ocl in range(4):
                    oc = half * 4 + ocl
                    for tt, (t0, t1) in enumerate(TILES):
                        ps = self.bank("x")
                        for kc in range(8):
                            S.matmul(ps[:], wv[:, kc, ocl * 128:(ocl + 1) * 128], Fm[:, kc, t0:t1], start=(kc == 0), stop=False)
                        S.matmul(ps[:], self.fb_row[0:1, oc * 128:(oc + 1) * 128], self.ones_bf[0:1, :],
                                 start=False, stop=True)
                        cd = COND[tt]
                        S.stt(self.X[:, oc, t0:t1], ps[:], g1[:, oc, cd:cd + 1], self.X[:, oc, t0:t1], ALU.mult, ALU.add)
            self.step(loads, fn)

    def gdn(self, l):
        S = self.S
        CF = self.consts
        cf = lambda k: CF[:, k * 128:(k + 1) * 128]
        ident_bf = self.ident_bf
        self.step(None, lambda slot: self.norm_mod(self.AMOD[:, l, 0], self.mod(l, 0)))
        off = 0
        def cv(shape, dt=F32):
            nonlocal off
            v, w = self.carve(off, shape, dt)
            off += w
            return v
        GA = cv([12, 16]); BA = cv([12, 16])
        Z = cv([NT], BF16)
        QKV = cv([3, NT], BF16)
        K_tm = cv([12, 128], BF16); V_tm = cv([12, 128], BF16)
        O_tm = cv([12, 128])
        ST_T = cv([8, 2, 128], BF16); ST_Q = cv([8, 2, 128], BF16); ST_QD = cv([8, 2, 128], BF16); ST_KD = cv([8, 2, 128], BF16)
        CCE = cv([8, 8])
        Sf = [cv([128]) for _ in range(4)]; Sb = [cv([128], BF16) for _ in range(4)]
        RP = [cv([128], BF16) for _ in range(4)]; VN = [cv([128], BF16) for _ in range(4)]
        SSQ = cv([12])
        base = off
        CP = cv([3, 1548], BF16); DG5 = cv([15, 128], BF16); SQ = cv([512], BF16); RS = cv([512])
        T0 = cv([12, 16]); EA = cv([12, 16])
        endA = off
        off = base
        NB = 5
        GM2 = []; EM = []; ER = []; T1 = []; CC = []; LNs = []; PQs = []; MBs = []
        for _ in range(NB):
            o1 = off
            GM2.append(cv([2, 128]))
            ln1, _w = self.carve(o1, [512], BF16)
            o2 = off
            EM.append(cv([512], BF16))
            ln2, _w = self.carve(o2, [512], BF16)
            ER.append(cv([256], BF16)); T1.append(cv([2, 128], BF16)); CC.append(cv([8]))
            LNs.append([cv([512], BF16), ln1, ln2])
            PQs.append(cv([512], BF16))
            MBs.append(cv([512], BF16))
        endB = off
        off = base
        SQ2 = cv([12, 128]); ON = cv([12, 128], BF16); OGh = cv([NT], BF16)
        TMPX = [cv([512]) for _ in range(2)]
        endC = off
        off = max(endA, endB, endC)
        assert off <= self.scr_words, off
        cnt = {"pre": 0, "sc": 0}
        one_col = CF[:, C_ONE * 128:C_ONE * 128 + 1]
        ones_f, neg_f = cf(C_ONE), cf(C_NEG)
        MdT2 = CF[:, 7 * 128:9 * 128].rearrange("p (d c) -> p d c", d=2)
        SM2 = CF[:, 5 * 128:7 * 128].rearrange("p (d c) -> p d c", d=2)
        bc2 = lambda col2: col2.unsqueeze(2).broadcast_to([128, 2, 128])
        v22 = lambda t: t.rearrange("p (a d c) -> p a d c", a=2, d=2)
        v2 = lambda t: t.rearrange("p (d c) -> p d c", d=2)
        MdT2b = self.consts_bf[:, 7 * 128:9 * 128].rearrange("p (d c) -> p d c", d=2)
        SM2b = self.consts_bf[:, 5 * 128:7 * 128].rearrange("p (d c) -> p d c", d=2)

        def gloads(slot):
            return [(slot[:, 0:256].rearrange("p (k n) -> p k n", k=8), self.dn_wg.rearrange("(k p) n -> p k n", p=128))]

        def gfn(slot):
            wg = slot[:, 0:256].rearrange("p (k n) -> p k n", k=8)
            if self.cfg.get("gcut", 9) < 0:
                return
            if self.cfg.get("gcut", 9) < 1:
                return
            ps = self.bank("x")
            for ti in range(12):
                for kc in range(8):
                    S.matmul(ps[:, ti * 32:(ti + 1) * 32], self.H[:, kc, ti * 128:(ti + 1) * 128], wg[:, kc, :],
                             start=(kc == 0), stop=(kc == 7))
            pv = ps[:, 0:384].rearrange("p (t d k h) -> p t d k h", t=12, d=2, k=2)
            gp = self.gpar[:].rearrange("p (a t n) -> p a t n", a=2, t=12)
            for d in range(2):
                S.tt("dve", T0[:, :, d * 8:(d + 1) * 8], pv[:, :, d, 0, :], gp[:, 1, :, d * 8:(d + 1) * 8], ALU.add)
                S.act(BA[:, :, d * 8:(d + 1) * 8], pv[:, :, d, 1, :], AF.Sigmoid)
            cut = self.cfg.get("gcut", 9)
            if cut < 2:
                return
            S.act(T0, T0, AF.Exp)
            if cut < 3:
                return
            S.act(T0, T0, AF.Ln, bias=one_col)
            if cut < 4:
                return
            S.act(EA, gp[:, 0], AF.Exp)
            S.stt(GA, T0, -1.0, EA, ALU.mult, ALU.mult)
        self.step(gloads, gfn)
        stage = self.cfg.get("gdn_stage", 9)
        nheads = self.cfg.get("gdn_heads", 8)

        def pre2(h, ti, n, b):
            LN0b, LN1b, LN2b = LNs[b]
            PQ = [PQs[b], PQs[b]]
            MB = [MBs[b], MBs[b]]
            T1b = T1[b]
            g2 = GA[:, ti, h:h + 9:8]
            b2 = BA[:, ti, h:h + 9:8]
            kT = QKV[:, 1, ti * 128:(ti + 1) * 128]
            qT = QKV[:, 0, ti * 128:(ti + 1) * 128]
            l4 = lambda t: t.rearrange("p (a c) -> p a c", a=4)
            def mask(lv):
                S.tt("pool", l4(MB[lv % 2]), l4(LN0b), self.mask2[:, lv, 0:128].unsqueeze(1).broadcast_to([128, 4, 128]), ALU.mult)
                return v22(MB[lv % 2])
            S.tt("dve", GM2[b], MdT2, bc2(g2), ALU.mult)
            GMf = GM2[b].rearrange("p d c -> p (d c)")
            pD = self.ps[b]
            S.matmul(pD[:, 0:256], neg_f, GMf, start=True, stop=False)
            for d in range(2):
                S.matmul(pD[:, d * 128:(d + 1) * 128], GM2[b][:, d, :], ones_f, start=False, stop=(d == 1))
            S.matmul(pD[:, 256:512], ones_f, GMf, start=True, stop=False)
            for d in range(2):
                S.matmul(pD[:, 256 + d * 128:256 + (d + 1) * 128], GM2[b][:, d, :], neg_f, start=False, stop=(d == 1))
            yield
            S.act(EM[b], pD[:, 0:512], AF.Relu, scale=-1.0)
            S.act(EM[b], EM[b], AF.Exp, scale=-1.0)
            pG = self.ps[b]
            S.matmul(pG[:, 0:256], ones_f, GMf)
            for d in range(2):
                S.matmul(pG[:, 256 + d:257 + d], GM2[b][:, d, :], ones_f[:, 0:1])
            S.matmul(pG[:, 258:260], ones_f, g2)
            yield
            S.act(ER[b], pG[:, 0:256], AF.Exp)
            S.copy("act", CC[b][:, 0:4], pG[:, 256:260])
            S.act(CCE[:, n, 0:4], CC[b][:, 0:4], AF.Exp)
            for d in range(2):
                S.act(CCE[:, n, 4 + d:5 + d], CC[b][:, d:d + 1], AF.Exp, bias=CC[b][:, 2 + d:3 + d], scale=-1.0)
            S.act(CCE[:, n, 6:8], CCE[:, n, 0:2], AF.Copy, scale=-1.0)
            S.tt("pool", T1b, SM2b, bc2(b2), ALU.mult)
            S.tt("dve", EM[b][:, 0:256].rearrange("p (d c) -> p d c", d=2), EM[b][:, 0:256].rearrange("p (d c) -> p d c", d=2), T1b, ALU.mult)
            S.tt("pool", EM[b][:, 256:512].rearrange("p (d c) -> p d c", d=2), EM[b][:, 256:512].rearrange("p (d c) -> p d c", d=2), MdT2b, ALU.mult)
            pB = self.ps[b]
            S.matmul(pB[:, 0:128], kT, kT)
            S.matmul(pB[:, 128:256], kT, qT)
            yield
            E2, ET2 = v2(EM[b][:, 0:256]), v2(EM[b][:, 256:512])
            LN0 = v22(LN0b)
            S.tt("dve", LN0[:, 0], pB[:, 0:128].unsqueeze(1).broadcast_to([128, 2, 128]), E2, ALU.mult)
            yield
            S.tt("dve", ST_Q[:, n], pB[:, 128:256].unsqueeze(1).broadcast_to([128, 2, 128]), ET2, ALU.mult)
            pT = self.ps[b][:].bitcast(BF16)
            for d in range(2):
                S.transpose(pT[:, d * 128:(d + 1) * 128], LN0[:, 0, d, :], ident_bf)
            S.copy("act", LN0b[:, 256:512], pT[:, 0:256])
            S.tt("pool", ST_QD[:, n], qT.unsqueeze(1).broadcast_to([128, 2, 128]), v2(ER[b]), ALU.mult)
            S.tt("pool", ST_KD[:, n], K_tm[:, ti, :].unsqueeze(1).broadcast_to([128, 2, 128]), bc2(CCE[:, n, 4:6]), ALU.mult)
            yield
            LM0 = mask(0)
            S.tt("dve", l4(PQ[0]), ident_bf.unsqueeze(1).broadcast_to([128, 4, 128]), l4(MB[0]), ALU.subtract)
            def blk(ps, a, d):
                return ps[:, (a * 2 + d) * 128:(a * 2 + d + 1) * 128]
            Lc, Nc = LM0[:, 0], LM0[:, 1]
            cur = 0
            for r in range(2):
                pR = self.ps[b]
                for d in range(2):
                    S.matmul(blk(pR, 0, d), Nc[:, d, :], Lc[:, d, :])
                    S.matmul(blk(pR, 1, d), Lc[:, d, :], Nc[:, d, :])
                LNr_b = LN1b if r == 0 else LN2b
                S.copy("act", LNr_b, pR[:, 0:512])
                if r == 1:
                    LMn = mask(1)
                yield
                LNr = v22(LNr_b)
                Lc, Nc = LNr[:, 0], LNr[:, 1]
                PQc = v22(PQ[cur])
                pP = self.ps[b]
                for d in range(2):
                    S.matmul(blk(pP, 0, d), Nc[:, d, :], PQc[:, 0, d, :])
                    S.matmul(blk(pP, 1, d), Lc[:, d, :], PQc[:, 1, d, :])
                S.tt("dve", PQ[1 - cur], pP[:, 0:512], PQ[cur], ALU.add)
                cur = 1 - cur
                yield
            for lv in range(1, 5):
                LMc = LMn
                PQc = v22(PQ[cur])
                YY = v22(LN1b)
                pY = self.ps[b]
                for d in range(2):
                    S.matmul(blk(pY, 1, d), LMc[:, 0, d, :], PQc[:, 1, d, :])
                    if lv < 4:
                        S.matmul(blk(pY, 0, d), LMc[:, 1, d, :], PQc[:, 0, d, :])
                if lv < 4:
                    S.copy("act", LN1b, pY[:, 0:512])
                    LMn = mask(lv + 1)
                else:
                    S.copy("act", LN1b[:, 256:512], pY[:, 256:512])
                yield
                pU = self.ps[b]
                for d in range(2):
                    S.matmul(blk(pU, 1, d), PQc[:, 0, d, :], YY[:, 1, d, :])
                    if lv < 4:
                        S.matmul(blk(pU, 0, d), PQc[:, 1, d, :], YY[:, 0, d, :])
                if lv < 4:
                    S.tt("dve", PQ[1 - cur], PQ[cur], pU[:, 0:512], ALU.subtract)
                else:
                    S.tt("dve", PQ[1 - cur][:, 256:512], PQ[cur][:, 256:512], pU[:, 256:512], ALU.subtract)
                cur = 1 - cur
                yield
            S.tt("pool", ST_T[:, n], v22(PQ[cur])[:, 1], bc2(b2), ALU.mult)

        def scan_step(ti, n, d, sb_i, first):
            b = sb_i
            kT = QKV[:, 1, ti * 128:(ti + 1) * 128]
            pA = self.bank("x")
            S.matmul(pA[:, 0:128], kT, Sb[sb_i])
            S.stt(RP[b], pA[:, 0:128], CCE[:, n, 6 + d:7 + d], V_tm[:, ti, :], ALU.mult, ALU.add)
            yield
            pB = self.bank("x")
            S.matmul(pB[:, 0:128], ST_T[:, n, d, :], RP[b])
            S.copy("act", VN[b], pB[:, 0:128])
            yield
            pC = self.bank("x")
            S.matmul(pC[:, 0:128], ST_QD[:, n, d, :], Sb[sb_i], start=True, stop=False)
            S.matmul(pC[:, 0:128], ST_Q[:, n, d, :], VN[b], start=False, stop=True)
            S.tt("dve", O_tm[:, ti, :], O_tm[:, ti, :], pC[:, 0:128], ALU.add)
            pE = self.bank("x")
            S.matmul(pE[:, 0:128], ST_KD[:, n, d, :], VN[b])
            S.stt(Sf[sb_i], Sf[sb_i], CCE[:, n, 2 + d:3 + d], pE[:, 0:128], ALU.mult, ALU.add)
            S.copy("act", Sb[sb_i], Sf[sb_i])
            yield

        for h in range(nheads if stage >= 2 else 0):
            def loadsA(slot, h=h):
                return [(slot[:, 0:4096].rearrange("p (k n) -> p k n", k=8), self.dn_wh[h].rearrange("(k p) n -> p k n", p=128))]

            def fnA(slot, h=h):
                wv = slot[:, 0:4096].rearrange("p (k n) -> p k n", k=8)
                S.memset("dve", CP, 0.0)
                S.tt("dve", DG5.rearrange("p (j t) c -> p j t c", j=3),
                     ident_bf.unsqueeze(1).unsqueeze(1).broadcast_to([128, 3, 5, 128]),
                     self.dcwT[:].rearrange("p (j h t) -> p j h t", j=3, h=8)[:, :, h, :].unsqueeze(3).broadcast_to([128, 3, 5, 128]),
                     ALU.mult)
                cpP = lambda j: CP[:, j, 0:520].rearrange("p (s t) -> p s t", s=2)
                for j in range(4):
                    for tt, (t0, t1) in enumerate(TILES):
                        ps = self.bank("x")
                        for kc in range(8):
                            S.matmul(ps[:], wv[:, kc, j * 128:(j + 1) * 128], self.H[:, kc, t0:t1], start=(kc == 0), stop=(kc == 7))
                        if j == 3:
                            S.act(Z[:, t0:t1], ps[:], AF.Silu)
                        elif tt == 0:
                            S.copy("act", cpP(j)[:, :, 2:258], ps[:].rearrange("p (s t) -> p s t", s=2))
                        else:
                            S.copy("act", CP[:, j, 522 + (tt - 1) * 512:522 + tt * 512], ps[:])
                for j in range(3):
                    for tt, (t0, t1) in enumerate(TILES):
                        ps = self.bank("x")
                        for tap in range(5):
                            if tt == 0:
                                S.matmul(ps[:].rearrange("p (s t) -> p s t", s=2), DG5[:, j * 5 + tap, :], cpP(j)[:, :, tap:tap + 256],
                                         start=(tap == 0), stop=(tap == 4))
                            else:
                                st = 520 + (tt - 1) * 512 + tap
                                S.matmul(ps[:], DG5[:, j * 5 + tap, :], CP[:, j, st:st + 512], start=(tap == 0), stop=(tap == 4))
                        S.act(QKV[:, j, t0:t1], ps[:], AF.Silu)
                for j in range(2):
                    for tt, (t0, t1) in enumerate(TILES):
                        S.act(SQ, QKV[:, j, t0:t1], AF.Square)
                        ps = self.bank("x")
                        S.matmul(ps[:], self.ones_bf[:, 0:128], SQ)
                        if j == 0:
                            S.act(RS, ps[:], AF.Sqrt, bias=self.eps_col[:, 1:2], scale=128.0)
                        else:
                            S.act(RS, ps[:], AF.Sqrt, bias=self.eps_col[:, 0:1], scale=1.0)
                        S.recip(RS, RS)
                        S.tt("dve", QKV[:, j, t0:t1], QKV[:, j, t0:t1], RS, ALU.mult)
                for (j, dst) in ((1, K_tm), (2, V_tm)):
                    for g4 in range(3):
                        pb = self.bank("x")[:].bitcast(BF16)
                        for i in range(4):
                            ti = g4 * 4 + i
                            S.transpose(pb[:, i * 128:(i + 1) * 128], QKV[:, j, ti * 128:(ti + 1) * 128], ident_bf)
                        S.copy("act", dst[:, g4 * 4:(g4 + 1) * 4, :], pb[:, 0:512].rearrange("p (a b) -> p a b", a=4))
            self.step(loadsA, fnA)

            def loadsB(slot, h=h):
                return [(slot[:, 0:1024], self.dn_wo[h * 128:(h + 1) * 128, :])]

            def fnB(slot, h=h):
                wo = slot[:, 0:1024]
                seqs = [([0, 1], 0), ([2, 3], 1), (list(range(4, 12)), 2)]
                sbi = 0
                def interleave(gens):
                    gens = list(gens)
                    while gens:
                        for g in list(gens):
                            try:
                                next(g)
                            except StopIteration:
                                gens.remove(g)

                def chainx(tiles, sidx, d, n0, sb_i):
                    order = tiles if d == 0 else tiles[::-1]
                    if sidx == 2:
                        S.dma("sp", Sf[sb_i], self.sd[d, h])
                        S.copy("act", Sb[sb_i], Sf[sb_i])
                    else:
                        S.memset("dve", Sf[sb_i], 0.0)
                        S.memset("dve", Sb[sb_i], 0.0)
                    for ti in order:
                        yield from scan_step(ti, n0 + tiles.index(ti), d, sb_i, first=False)
                    if sidx < 2:
                        S.dma("sp", self.nsd[sidx, d, h], Sf[sb_i])

                S.memset("dve", O_tm, 0.0)
                if stage >= 3:
                    for grp in ([0, 1, 2, 3],):
                        interleave([pre2(h, ti, ti, k) for k, ti in enumerate(grp)])
                    if stage >= 4:
                        interleave([chainx([0, 1], 0, d, 0, 2 * 0 + d) for d in range(2)] +
                                   [chainx([2, 3], 1, d, 2, 2 * 1 + d) for d in range(2)])
                    for grp in ([4, 5, 6, 7, 8], [9, 10, 11]):
                        interleave([pre2(h, ti, ti - 4, k) for k, ti in enumerate(grp)])
                    if stage >= 4:
                        interleave([chainx(list(range(4, 12)), 2, d, 0, d) for d in range(2)])
                if stage < 5:
                    return
                S.act(SQ2, O_tm, AF.Square)
                S.op("dve", lambda e: e.reduce_sum(SSQ, SQ2, mybir.AxisListType.X), [SQ2], [SSQ])
                S.act(SSQ, SSQ, AF.Sqrt, bias=self.eps_col[:, 0:1], scale=1.0 / 128.0)
                S.recip(SSQ, SSQ)
                S.tt("dve", ON, O_tm, SSQ.unsqueeze(2).broadcast_to([128, 12, 128]), ALU.mult)
                for g4 in range(3):
                    pb = self.bank("x")[:].bitcast(BF16)
                    for i in range(4):
                        ti = g4 * 4 + i
                        S.transpose(pb[:, i * 128:(i + 1) * 128], ON[:, ti, :], ident_bf)
                    S.stt(OGh[:, g4 * 512:(g4 + 1) * 512], pb[:, 0:512], self.dn_normT[:, 0:1], Z[:, g4 * 512:(g4 + 1) * 512],
                          ALU.mult, ALU.mult)
                g1 = self.mod(l, 2)
                for oc in range(8):
                    for tt, (t0, t1) in enumerate(TILES):
                        ps = self.bank("x")
                        S.matmul(ps[:], wo[:, oc * 128:(oc + 1) * 128], OGh[:, t0:t1])
                        cd = COND[tt]
                        if (oc * 3 + tt) % 2 == 0:
                            S.stt(self.X[:, oc, t0:t1], ps[:], g1[:, oc, cd:cd + 1], self.X[:, oc, t0:t1], ALU.mult, ALU.add)
                        else:
                            tx = TMPX[((oc * 3 + tt) // 2) % 2]
                            S.act(tx, ps[:], AF.Identity, scale=g1[:, oc, cd:cd + 1])
                            S.tt("pool", self.X[:, oc, t0:t1], self.X[:, oc, t0:t1], tx, ALU.add)
            self.step(loadsB, fnB)

    def mlstm(self, l):
        S = self.S
        CF = self.consts
        cf = lambda k: CF[:, k * 128:(k + 1) * 128]
        ident_bf = self.ident_bf
        self.step(None, lambda slot: self.norm_mod(self.AMOD[:, l, 0], self.mod(l, 0)))
        off = 0
        def cv(shape, dt=F32):
            nonlocal off
            v, w = self.carve(off, shape, dt)
            off += w
            return v
        LI = cv([12, 16]); LF = cv([12, 16]); T0 = cv([12, 16])
        LIr = cv([512]); LFr = cv([512]); SCN = cv([2, 2, 256]); NBF = cv([2])
        MFB = cv([2, 2]); EMF = cv([2, 2]); DGE = cv([2, 2, 16]); EMB = cv([2, 2, 16]); EM0 = cv([16])
        qTs = [cv([NT], BF16) for _ in range(2)]; kTs = [cv([NT], BF16) for _ in range(2)]
        vTs = [cv([NT], BF16) for _ in range(2)]; OGts = [cv([NT], BF16) for _ in range(2)]
        V_tms = [cv([12, 129], BF16) for _ in range(2)]; K_tms = [cv([12, 64], BF16) for _ in range(2)]
        Hs = cv([12, 128])
        off_st = off
        ST_S = cv([8, 2, 128], BF16); ST_QB = cv([8, 2, 128], BF16); ST_KW = cv([8, 2, 64], BF16); CCE = cv([8, 4])
        SQ2, _ = self.carve(off_st, [12, 128]); SSQ = cv([12])
        off_tmp = off
        ON, _w = self.carve(off_tmp, [12, 128], BF16); OGh, _w2 = self.carve(off_tmp + 768, [NT], BF16)
        TMPX = [self.carve(off_tmp + 1536 + 512 * i, [512])[0] for i in range(2)]
        NBm = 4
        FM2 = [cv([2, 128]) for _ in range(NBm)]
        EMn = [cv([256]) for _ in range(NBm)]
        ER = [cv([256], BF16) for _ in range(NBm)]; CC = [cv([8]) for _ in range(NBm)]
        MdT2 = CF[:, 7 * 128:9 * 128].rearrange("p (d c) -> p d c", d=2)
        bc2 = lambda col2: col2.unsqueeze(2).broadcast_to([128, 2, 128])
        v2 = lambda t: t.rearrange("p (d c) -> p d c", d=2)
        CA = [cv([129]) for _ in range(4)]; CAb = [cv([130], BF16) for _ in range(4)]; CAo = [cv([129]) for _ in range(4)]
        DN = [cv([2]) for _ in range(4)]
        assert off <= self.scr_words, off
        cnt = {"pre": 0, "sc": 0}
        one_col = CF[:, C_ONE * 128:C_ONE * 128 + 1]
        ones_f, neg_f = cf(C_ONE), cf(C_NEG)
        mp = self.mpar[:].rearrange("p (a t n) -> p a t n", a=2, t=12)

        def gloads(slot):
            return [(slot[:, 0:256].rearrange("p (k n) -> p k n", k=8), self.ml_wg.rearrange("(k p) n -> p k n", p=128)),
                    (slot[:, 256:512].rearrange("p (k n) -> p k n", k=8), self.ml_wgr.rearrange("(k p) n -> p k n", p=128))]

        def gfn(slot):
            wg = slot[:, 0:256].rearrange("p (k n) -> p k n", k=8)
            wr = slot[:, 256:512].rearrange("p (k n) -> p k n", k=8)
            for vv in V_tms:
                S.memset("dve", vv[:, :, 128:129], 1.0)
            ps = self.bank("x")
            for ti in range(12):
                for kc in range(8):
                    S.matmul(ps[:, ti * 32:(ti + 1) * 32], self.H[:, kc, ti * 128:(ti + 1) * 128], wg[:, kc, :],
                             start=(kc == 0), stop=(kc == 7))
            pv = ps[:, 0:384].rearrange("p (t d k h) -> p t d k h", t=12, d=2, k=2)
            for d in range(2):
                S.tt("dve", LI[:, :, d * 8:(d + 1) * 8], pv[:, :, d, 0, :], mp[:, 0, :, d * 8:(d + 1) * 8], ALU.add)
                S.tt("dve", T0[:, :, d * 8:(d + 1) * 8], pv[:, :, d, 1, :], mp[:, 1, :, d * 8:(d + 1) * 8], ALU.add)
            S.act(T0, T0, AF.Exp, scale=-1.0)
            S.act(T0, T0, AF.Ln, bias=one_col)
            S.ts("dve", LF, T0, -1.0)
            pr = self.bank("x")
            for kc in range(8):
                S.matmul(pr[0:16, 0:512], wr[:, kc, 0:16], self.H[:, kc, 0:512], start=(kc == 0), stop=(kc == 7))
            S.act(LIr[0:16, :], pr[0:16, 0:512], AF.Identity, bias=self.mparT[0:16, 0:1])
            pr2 = self.bank("x")
            for kc in range(8):
                S.matmul(pr2[0:16, 0:512], wr[:, kc, 16:32], self.H[:, kc, 0:512], start=(kc == 0), stop=(kc == 7))
            S.ts("dve", NBF[0:16, 0:1], self.mparT[0:16, 1:2], -1.0)
            S.act(LFr[0:16, :], pr2[0:16, 0:512], AF.Exp, bias=NBF[0:16, 0:1], scale=-1.0)
            S.act(LFr[0:16, :], LFr[0:16, :], AF.Ln, bias=one_col[0:16, :])
            S.ts("dve", LFr[0:16, :], LFr[0:16, :], -1.0)
            for s in range(2):
                for fb in range(2):
                    if fb == 0:
                        d0, d1 = LFr[0:16, s * 256:(s + 1) * 256], LIr[0:16, s * 256:(s + 1) * 256]
                    elif s == 0:
                        d0, d1 = LFr[0:16, 255::-1], LIr[0:16, 255::-1]
                    else:
                        d0, d1 = LFr[0:16, 511:255:-1], LIr[0:16, 511:255:-1]
                    o = SCN[0:16, s, fb, :]
                    S.op("dve", lambda e, o=o, d0=d0, d1=d1: e.tensor_tensor_scan(o, d0, d1, 0.0, ALU.add, ALU.max), [d0, d1], [o])
                    S.copy("dve", MFB[0:16, s, fb:fb + 1], SCN[0:16, s, fb, 255:256])
                S.dma("sp", self.nsm[s, 0, :], MFB[0:8, s, 0:1])
                S.dma("sp", self.nsm[s, 1, :], MFB[8:16, s, 1:2])
            S.act(EMF[0:16], MFB[0:16], AF.Exp, scale=-1.0)
            pe = self.bank("x")
            for s in range(2):
                for fb in range(2):
                    S.ts("dve", DGE[0:16, s, fb, :], CF[0:16, 0:16], EMF[0:16, s, fb:fb + 1])
                    c0 = (s * 2 + fb) * 16
                    S.matmul(pe[0:64, c0:c0 + 16], ones_f[0:16, 0:64], DGE[0:16, s, fb, :])
            S.copy("dve", EMB[0:64].rearrange("p a b c -> p (a b c)"), pe[0:64, 0:64])
            S.act(EM0[0:64], self.smm[0:64, :], AF.Exp)
        self.step(gloads, gfn)

        cur = {}

        def pre2(h, ti, n, b):
            qT, kT, K_tm = cur["qT"], cur["kT"], cur["K_tm"]
            lf2 = LF[:, ti, h:h + 9:8]
            li2 = LI[:, ti, h:h + 9:8]
            S.tt("dve", FM2[b], MdT2, bc2(lf2), ALU.mult)
            FMf = FM2[b].rearrange("p d c -> p (d c)")
            pD = self.ps[b]
            S.matmul(pD[:, 0:256], ones_f, FMf, start=True, stop=False)
            for d in range(2):
                S.matmul(pD[:, d * 128:(d + 1) * 128], FM2[b][:, d, :], neg_f, start=False, stop=(d == 1))
            S.matmul(pD[:, 256:512], ones_f, FMf)
            yield
            S.act(EMn[b], pD[:, 0:256], AF.Relu, scale=-1.0)
            for d in range(2):
                S.act(EMn[b][:, d * 128:(d + 1) * 128], EMn[b][:, d * 128:(d + 1) * 128], AF.Exp, bias=li2[:, d:d + 1], scale=-1.0)
            S.tt("pool", v2(EMn[b]), v2(EMn[b]), MdT2, ALU.mult)
            S.act(ER[b][0:64, :], pD[0:64, 256:512], AF.Exp)
            pG = self.ps[b]
            for d in range(2):
                S.matmul(pG[:, d:d + 1], FM2[b][:, d, :], ones_f[:, 0:1])
            S.matmul(pG[:, 2:4], ones_f, lf2)
            yield
            S.copy("act", CC[b][:, 0:4], pG[:, 0:4])
            S.tt("pool", CC[b][:, 4:6], CC[b][:, 2:4], li2, ALU.add)
            S.act(CCE[:, n, 0:2], CC[b][:, 2:4], AF.Exp)
            for d in range(2):
                S.act(CCE[:, n, 2 + d:3 + d], CC[b][:, d:d + 1], AF.Exp, bias=CC[b][:, 4 + d:5 + d], scale=-1.0)
            yield
            pB = self.ps[b]
            S.matmul(pB[:, 0:128], kT[0:64, ti * 128:(ti + 1) * 128], qT[0:64, ti * 128:(ti + 1) * 128])
            S.tt("dve", ST_S[:, n], pB[:, 0:128].unsqueeze(1).broadcast_to([128, 2, 128]), v2(EMn[b]), ALU.mult)
            S.tt("pool", ST_QB[0:64, n], qT[0:64, ti * 128:(ti + 1) * 128].unsqueeze(1).broadcast_to([64, 2, 128]),
                 v2(ER[b])[0:64], ALU.mult)
            S.tt("pool", ST_KW[:, n], K_tm[:, ti, :].unsqueeze(1).broadcast_to([128, 2, 64]),
                 CCE[:, n, 2:4].unsqueeze(2).broadcast_to([128, 2, 64]), ALU.mult)
            yield

        def scan_step(ti, n, d, ci, first):
            b = ci
            V_tm = cur["V_tm"]
            pN = self.bank("sc", [0, 1, 2, 3, 4, 5])
            S.matmul(pN[:, 0:129], ST_QB[0:64, n, d, :], CAb[ci][0:64, 0:129], start=True, stop=False)
            S.matmul(pN[:, 0:129], ST_S[:, n, d, :], V_tm[:, ti, :], start=False, stop=True)
            S.act(DN[b][:, 0:1], pN[:, 128:129], AF.Abs)
            S.ts("dve", DN[b][:, 0:1], DN[b][:, 0:1], 1.0, None, ALU.max)
            S.recip(DN[b][:, 0:1], DN[b][:, 0:1])
            S.stt(Hs[:, ti, :], pN[:, 0:128], DN[b][:, 0:1], Hs[:, ti, :], ALU.mult, ALU.add)
            yield
            pS = self.bank("sc", [0, 1, 2, 3, 4, 5])
            S.matmul(pS[0:64, 0:129], ST_KW[:, n, d, :], V_tm[:, ti, :])
            S.stt(CA[ci][0:64, :], CA[ci][0:64, :], CCE[0:64, n, d:d + 1], pS[0:64, 0:129], ALU.mult, ALU.add)
            S.copy("act", CAb[ci][0:64, 0:129], CA[ci][0:64, :])
            yield

        nheads = self.cfg.get("ml_heads", 8)

        def interleave_g(gens):
            gens = list(gens)
            while gens:
                for g in list(gens):
                    try:
                        next(g)
                    except StopIteration:
                        gens.remove(g)
                yield

        def genA(h, wv):
            p = h % 2
            qT, kT, vT, OGt, V_tm, K_tm = qTs[p], kTs[p], vTs[p], OGts[p], V_tms[p], K_tms[p]
            for j in range(4):
                lo, hi = [(0, 64), (64, 128), (128, 256), (256, 384)][j]
                M = hi - lo
                for tt, (t0, t1) in enumerate(TILES):
                    ps = self.bank("fa", [6, 7])
                    for kc in range(8):
                        S.matmul(ps[0:M, :], wv[:, kc, lo:hi], self.H[:, kc, t0:t1], start=(kc == 0), stop=(kc == 7))
                    if j == 0:
                        S.act(qT[0:64, t0:t1], ps[0:64, :], AF.Copy, scale=0.125)
                    elif j == 1:
                        S.copy("act", kT[0:64, t0:t1], ps[0:64, :])
                    elif j == 2:
                        S.copy("act", vT[:, t0:t1], ps[:])
                    else:
                        S.act(OGt[:, t0:t1], ps[:], AF.Sigmoid)
                    yield
            for g4 in range(3):
                pb = self.bank("fa", [6, 7])[:].bitcast(BF16)
                for i in range(4):
                    ti = g4 * 4 + i
                    S.transpose(pb[:, i * 128:(i + 1) * 128], vT[:, ti * 128:(ti + 1) * 128], ident_bf)
                S.copy("act", V_tm[:, g4 * 4:(g4 + 1) * 4, 0:128], pb[:, 0:512].rearrange("p (a b) -> p a b", a=4))
                yield
            for g4 in range(3):
                pb = self.bank("fa", [6, 7])[:].bitcast(BF16)
                for i in range(4):
                    ti = g4 * 4 + i
                    S.transpose(pb[:, i * 64:(i + 1) * 64], kT[0:64, ti * 128:(ti + 1) * 128], ident_bf[0:64, 0:64])
                S.copy("act", K_tm[:, g4 * 4:(g4 + 1) * 4, :], pb[:, 0:256].rearrange("p (a b) -> p a b", a=4))
                yield

        def genB(h, wo):
            p = h % 2
            cur.update(qT=qTs[p], kT=kTs[p], K_tm=K_tms[p], V_tm=V_tms[p])
            OGt = OGts[p]

            def chainx(tiles, sidx, d, n0, ci):
                order = tiles if d == 0 else tiles[::-1]
                if sidx == 2:
                    S.dma("sp", CA[ci][0:64, :], self.smca[d, h])
                    S.ts("dve", CA[ci][0:64, :], CA[ci][0:64, :], EM0[0:64, d * 8 + h:d * 8 + h + 1])
                    S.copy("act", CAb[ci][0:64, 0:129], CA[ci][0:64, :])
                else:
                    S.memset("dve", CA[ci][0:64, :], 0.0)
                    S.memset("dve", CAb[ci][0:64, :], 0.0)
                for ti in order:
                    yield from scan_step(ti, n0 + tiles.index(ti), d, ci, first=False)
                if sidx < 2:
                    S.ts("dve", CAo[ci][0:64, :], CA[ci][0:64, :], EMB[0:64, sidx, d, d * 8 + h:d * 8 + h + 1])
                    S.dma("sp", self.nsc[sidx, d, h], CAo[ci][0:64, 0:128])
                    S.dma("sp", self.nsn[sidx, d, h, :], CAo[ci][0:64, 128:129])

            S.memset("dve", Hs, 0.0)
            for grp in ([0, 1, 2, 3],):
                yield from interleave_g([pre2(h, ti, ti, k) for k, ti in enumerate(grp)])
            yield from interleave_g([chainx([0, 1], 0, d, 0, d) for d in range(2)] + [chainx([2, 3], 1, d, 2, 2 + d) for d in range(2)])
            for grp in ([4, 5, 6, 7], [8, 9, 10, 11]):
                yield from interleave_g([pre2(h, ti, ti - 4, k) for k, ti in enumerate(grp)])
            yield from interleave_g([chainx(list(range(4, 12)), 2, d, 0, d) for d in range(2)])
            S.act(SQ2, Hs, AF.Square)
            S.op("dve", lambda e: e.reduce_sum(SSQ, SQ2, mybir.AxisListType.X), [SQ2], [SSQ])
            S.act(SSQ, SSQ, AF.Sqrt, bias=self.eps_col[:, 0:1], scale=1.0 / 128.0)
            S.recip(SSQ, SSQ)
            yield
            S.tt("dve", ON, Hs, SSQ.unsqueeze(2).broadcast_to([128, 12, 128]), ALU.mult)
            yield
            for g4 in range(3):
                pb = self.bank("sc", [0, 1, 2, 3, 4, 5])[:].bitcast(BF16)
                for i in range(4):
                    ti = g4 * 4 + i
                    S.transpose(pb[:, i * 128:(i + 1) * 128], ON[:, ti, :], ident_bf)
                S.stt(OGh[:, g4 * 512:(g4 + 1) * 512], pb[:, 0:512], self.ml_normT[:, 0:1], OGt[:, g4 * 512:(g4 + 1) * 512],
                      ALU.mult, ALU.mult)
                yield
            g1 = self.mod(l, 2)
            for oc in range(8):
                for tt, (t0, t1) in enumerate(TILES):
                    ps = self.bank("sc", [0, 1, 2, 3, 4, 5])
                    S.matmul(ps[:], wo[:, oc * 128:(oc + 1) * 128], OGh[:, t0:t1])
                    cd = COND[tt]
                    if (oc * 3 + tt) % 2 == 0:
                        S.stt(self.X[:, oc, t0:t1], ps[:], g1[:, oc, cd:cd + 1], self.X[:, oc, t0:t1], ALU.mult, ALU.add)
                    else:
                        tx = TMPX[((oc * 3 + tt) // 2) % 2]
                        S.act(tx, ps[:], AF.Identity, scale=g1[:, oc, cd:cd + 1])
                        S.tt("pool", self.X[:, oc, t0:t1], self.X[:, oc, t0:t1], tx, ALU.add)
                yield

        def drain(g):
            for _ in g:
                pass

        def loads0(slot):
            return [(slot[:, 0:3072].rearrange("p (k n) -> p k n", k=8), self.ml_wh[0].rearrange("(k p) n -> p k n", p=128))]
        self.step(loads0, lambda slot: drain(genA(0, slot[:, 0:3072].rearrange("p (k n) -> p k n", k=8))))
        for h in range(nheads):
            def loadsH(slot, h=h):
                out = [(slot[:, 3072:4096], self.ml_wo[h * 128:(h + 1) * 128, :])]
                if h + 1 < nheads:
                    out.append((slot[:, 0:3072].rearrange("p (k n) -> p k n", k=8), self.ml_wh[h + 1].rearrange("(k p) n -> p k n", p=128)))
                return out

            def fnH(slot, h=h):
                gens = [genB(h, slot[:, 3072:4096])]
                if h + 1 < nheads:
                    gens.append(genA(h + 1, slot[:, 0:3072].rearrange("p (k n) -> p k n", k=8)))
                drain(interleave_g(gens))
            self.step(loadsH, fnH)

    def build(self):
        cfg = self.cfg
        nc = self.nc
        S = self.S
        self.xT = self.dram_in("xT", [D, NT])
        self.condT = self.dram_in("condT", [128, 16])
        self.w_ada = self.dram_in("w_ada", [4, D, 6 * D])
        b_adaT_d = self.dram_in("b_adaT", [128, 4 * 48])
        nmT_d = self.dram_in("nmT", [128, 72])
        self.w_up = self.dram_in("w_up", [4, D, 2 * DFF])
        cwT_d = self.dram_in("cwT", [128, 4 * NCH * 9])
        cbT_d = self.dram_in("cbT", [128, 4 * NCH])
        self.w_down = self.dram_in("w_down", [4, DFF, D])
        self.fnet_w = self.dram_in("fnet_w", [2, D, D])
        self.fnet_b_d = self.dram_in("fnet_b", [1, 2 * D])
        consts_d = self.dram_in("consts", [128, 1152])
        cs3_d = self.dram_in("cs3", [256, 768])
        self.tab = self.dram_in("tab", [4, 128, 2, 8, 256])
        self.dn_wh = self.dram_in("dn_wh", [8, D, 512])
        self.dn_wg = self.dram_in("dn_wg", [D, 32])
        dcwT_d = self.dram_in("dcwT", [128, 120])
        mask2_d = self.dram_in("mask2", [128, 1280])
        gpar_d = self.dram_in("gpar", [128, 2 * 12 * 16])
        dn_normT_d = self.dram_in("dn_normT", [128, 1])
        self.dn_wo = self.dram_in("dn_wo", [D, D])
        self.sd = self.dram_in("sd", [2, 8, 128, 128])
        self.nsd = self.dram_out("nsd", [2, 2, 8, 128, 128])
        self.ml_wh = self.dram_in("ml_wh", [8, D, 384])
        self.ml_wg = self.dram_in("ml_wg", [D, 32])
        self.ml_wgr = self.dram_in("ml_wgr", [D, 32])
        mpar_d = self.dram_in("mpar", [128, 2 * 12 * 16])
        mparT_d = self.dram_in("mparT", [16, 2])
        ml_normT_d = self.dram_in("ml_normT", [128, 1])
        self.ml_wo = self.dram_in("ml_wo", [D, D])
        self.smca = self.dram_in("smca", [2, 8, 64, 129])
        smm_d = self.dram_in("smm", [64, 16])
        self.nsc = self.dram_out("nsc", [2, 2, 8, 64, 128])
        self.nsn = self.dram_out("nsn", [2, 2, 8, 64])
        self.nsm = self.dram_out("nsm", [2, 2, 8])
        self.yT = self.dram_out("yT", [D, NT])
        ntaps = cfg.get("ntaps", 0)
        self.dbg = self.dram_out("dbg", [ntaps, D, NT]) if ntaps else None

        self.X = self.sb("X", [128, 8, NT])
        self.H = self.sb("H", [128, 8, NT], BF16)
        self.slots = [self.sb("slot%d" % i, [128, 4096], BF16) for i in range(4)]
        self.scr_words = cfg.get("scr_words", 19712)
        self.scr = self.sb("scr", [128, self.scr_words])
        self.scr_tmp = self.scr_words - 3584
        self.scr_main = 0
        self.consts = self.sb("consts_f", [128, 1152])
        self.consts_bf = self.sb("consts_b", [128, 1152], BF16)
        self.ones_bf = self.sb("ones_bf", [128, 512], BF16)
        self.CS3 = self.sb("CS3", [128, 2, 768], BF16)
        self.fb_row = self.sb("fb_row", [1, D], BF16)
        self.b_adaT = self.sb("b_adaT_s", [128, 4 * 48])
        self.nmT = self.sb("nmT_s", [128, 72])
        self.cwT = self.sb("cwT_s", [128, 4 * NCH * 9])
        self.cbT = self.sb("cbT_s", [128, 4 * NCH])
        self.condS = self.sb("condS", [128, 16])
        self.SC = self.sb("SC", [128, 8, 2], BF16)
        self.MOD = self.sb("MOD", [128, 4, 48, 2])
        self.AMOD = self.sb("AMOD", [128, 4, 2, 8, 2])
        self.eps_col = self.sb("eps_col", [128, 2])
        self.mpar = self.sb("mpar_s", [128, 2 * 12 * 16])
        self.mparT = self.sb("mparT_s", [16, 2])
        self.ml_normT = self.sb("ml_normT_s", [128, 1])
        self.smm = self.sb("smm_s", [64, 16])
        self.dcwT = self.sb("dcwT_s", [128, 120])
        self.mask2 = self.sb("mask2_s", [128, 5, 256], BF16)
        self.gpar = self.sb("gpar_s", [128, 2 * 12 * 16])
        self.dn_normT = self.sb("dn_normT_s", [128, 1])
        self.ps = [self.es.enter_context(nc.psum_tensor("ps%d" % i, [128, 512], F32)) for i in range(8)]
        self.ident_bf = self.consts_bf[:, C_ID * 128:(C_ID + 1) * 128]

        S.dma("sp", self.X[:], self.xT.rearrange("(c p) t -> p c t", p=128))
        S.dma("sp", self.condS[:], self.condT)
        S.dma("sp", self.consts[:], consts_d)
        S.dma("pool", self.consts_bf[:], consts_d)
        S.dma("pool", self.CS3[:], cs3_d.rearrange("(j p) n -> p j n", p=128))
        S.dma("sp", self.b_adaT[:], b_adaT_d)
        S.dma("sp", self.nmT[:], nmT_d)
        S.dma("sp", self.cwT[:], cwT_d)
        S.dma("sp", self.cbT[:], cbT_d)
        S.memset("dve", self.ones_bf[:], 1.0)
        S.memset("dve", self.eps_col[:, 0:1], EPS)
        S.memset("dve", self.eps_col[:, 1:2], 128.0 * EPS)
        S.dma("sp", self.mpar[:], mpar_d)
        S.dma("sp", self.mparT[:], mparT_d)
        S.dma("sp", self.ml_normT[:], ml_normT_d)
        S.dma("sp", self.smm[:], smm_d)
        S.dma("sp", self.dcwT[:], dcwT_d)
        S.dma("pool", self.mask2[:].rearrange("p a b -> p (a b)"), mask2_d)
        S.dma("sp", self.gpar[:], gpar_d)
        S.dma("sp", self.dn_normT[:], dn_normT_d)
        S.act(self.SC[:].rearrange("p k c -> p (k c)"), self.condS[:], AF.Silu)

        layers = cfg.get("layers", [0, 1, 2, 3])

        def collect(fn):
            keep = self.steps
            self.steps = []
            fn()
            out = self.steps
            self.steps = keep
            return out

        def merge(a, b):
            out = []
            ia = ib = 0
            while ia < len(a) or ib < len(b):
                if ia < len(a):
                    out.append(a[ia]); ia += 1
                want = (ia * len(b)) // max(1, len(a)) if ia < len(a) else len(b)
                while ib < want:
                    out.append(b[ib]); ib += 1
            return out

        k = 0
        ada0 = collect(lambda: self.adaln(layers[0]))
        self.steps += ada0[:5]
        pending = ada0[5:]
        for li, l in enumerate(layers):
            kind = l % 3
            mix = []
            if kind == 0 and cfg.get("fnet", True):
                mix = collect(lambda: self.fnet(l, l // 3))
            elif kind == 1 and cfg.get("gdn", True):
                mix = collect(lambda: self.gdn(l))
            elif kind == 2 and cfg.get("mlstm", True):
                mix = collect(lambda: self.mlstm(l))
            extra = pending
            if li + 1 < len(layers):
                extra = extra + collect(lambda: self.adaln(layers[li + 1]))
            pending = []
            self.steps += merge(mix, extra)
            self.step(None, lambda slot, k=k: self.tap(k))
            k += 1
            if cfg.get("ffn", True):
                self.ffn(l)
            self.step(None, lambda slot, k=k: self.tap(k))
            k += 1
        self.run_steps()

        Y, w = self.carve(0, [8, NT])
        self.norm_mod(self.nmT[:, 64:72], None, out_y=Y)
        S.dma("sp", self.yT.rearrange("(c p) t -> p c t", p=128), Y)
        S.wait_all("sp")
        S.emit()
        self.es.close()
        return nc


def _prep(inputs):
    consts, cs3, tab, mask2 = _const_tables()
    f = lambda k: np.ascontiguousarray(np.asarray(inputs[k], np.float32))
    shared = {
        "w_ada": f("w_ada"),
        "b_adaT": _fm(f("b_ada")).reshape(128, 4 * 48),
        "nmT": np.concatenate([_fm(f("norm_mix")).reshape(128, 32), _fm(f("norm_ffn")).reshape(128, 32),
                               _fm(f("norm_final")).reshape(128, 8)], axis=1),
        "w_up": f("ffn_w_up"),
        "cwT": np.ascontiguousarray(np.moveaxis(_fm(f("ffn_conv_w").reshape(4, 9, DFF)), 2, 3)).reshape(128, 4 * NCH * 9),
        "cbT": _fm(f("ffn_conv_b")).reshape(128, 4 * NCH),
        "w_down": f("ffn_w_down"),
        "fnet_w": f("fnet_w"),
        "fnet_b": f("fnet_b").reshape(1, 2 * D),
        "consts": consts, "cs3": cs3, "tab": tab, "mask2": mask2,
    }
    wi = f("dn_w_in")[0]
    shared["dn_wh"] = np.ascontiguousarray(np.stack(
        [np.concatenate([wi[:, j * 1024 + h * 128:j * 1024 + (h + 1) * 128] for j in range(4)], axis=1) for h in range(8)]))
    shared["dn_wg"] = np.ascontiguousarray(wi[:, 4096:4128])
    shared["dcwT"] = np.ascontiguousarray(np.moveaxis(_fm(f("dn_conv_w")[0]), 1, 2)).reshape(128, 120)
    gp = np.stack([f("dn_a_log")[0].reshape(16), f("dn_dt_bias")[0].reshape(16)])
    shared["gpar"] = np.ascontiguousarray(np.broadcast_to(gp[None, :, None, :], (128, 2, 12, 16))).reshape(128, 384)
    shared["dn_normT"] = np.ascontiguousarray(f("dn_norm")[0].reshape(128, 1))
    shared["dn_wo"] = f("dn_w_out")[0]
    sdel = f("state_delta")
    mw = f("ml_w_in")[0]
    shared["ml_wh"] = np.ascontiguousarray(np.stack(
        [np.concatenate([mw[:, h * 64:(h + 1) * 64], mw[:, 512 + h * 64:512 + (h + 1) * 64],
                         mw[:, 1024 + h * 128:1024 + (h + 1) * 128], mw[:, 2048 + h * 128:2048 + (h + 1) * 128]], axis=1)
         for h in range(8)]))
    mg = mw[:, 3072:3104]
    shared["ml_wg"] = np.ascontiguousarray(mg)
    mg4 = mg.reshape(1024, 2, 2, 8)
    shared["ml_wgr"] = np.ascontiguousarray(np.concatenate([mg4[:, :, 0, :].reshape(1024, 16), mg4[:, :, 1, :].reshape(1024, 16)], axis=1))
    bp = np.stack([f("ml_b_i")[0].reshape(16), f("ml_b_f")[0].reshape(16)])
    shared["mpar"] = np.ascontiguousarray(np.broadcast_to(bp[None, :, None, :], (128, 2, 12, 16))).reshape(128, 384)
    shared["mparT"] = np.ascontiguousarray(bp.T)
    shared["ml_normT"] = np.ascontiguousarray(f("ml_norm")[0].reshape(128, 1))
    shared["ml_wo"] = f("ml_w_out")[0]
    smc, smn, smmm = f("state_mlstm_c"), f("state_mlstm_n"), f("state_mlstm_m")
    xp = f("x_prompt")
    xs = f("x_sample")
    c = f("c")
    cctx = f("c_ctx")
    per_core = []
    for i in range(N_CORES):
        b = i // 4
        x = np.concatenate([xp[2 * i], xp[2 * i + 1], xs[b]], axis=0)
        cond = np.stack([cctx, c[b]], axis=-1)
        m = dict(shared)
        m["xT"] = np.ascontiguousarray(x.T)
        m["sd"] = np.ascontiguousarray(sdel[b, 0])
        m["smca"] = np.ascontiguousarray(np.concatenate([smc[b, 0], smn[b, 0][..., None]], axis=-1))
        m["smm"] = np.ascontiguousarray(np.broadcast_to(smmm[b, 0].reshape(1, 16), (64, 16)))
        m["condT"] = np.ascontiguousarray(cond.reshape(8, 128, 2).transpose(1, 0, 2)).reshape(128, 16)
        per_core.append(m)
    return per_core


def run(inputs, cfg, core_ids=None, trace=False):
    b = Builder(cfg)
    nc = b.build()
    maps = _prep(inputs)
    core_ids = core_ids or list(range(N_CORES))
    maps = [maps[i] for i in core_ids]
    res = run_bass_kernel_spmd(nc, maps, core_ids=list(range(len(core_ids))), trace=trace)
    return res, b


def kernel(**inputs):
    res, b = run(inputs, dict())
    R = res.results
    y_prompt = np.zeros((16, 256, D), np.float32)
    y_sample = np.zeros((2, 1024, D), np.float32)
    new_d = np.zeros((16, 1, 2, 8, 128, 128), np.float32)
    new_c = np.zeros((16, 1, 2, 8, 64, 128), np.float32)
    new_n = np.zeros((16, 1, 2, 8, 64), np.float32)
    new_m = np.zeros((16, 1, 2, 8), np.float32)
    for i in range(N_CORES):
        y = np.asarray(R[i]["yT"]).T
        y_prompt[2 * i] = y[0:256]
        y_prompt[2 * i + 1] = y[256:512]
        if i % 4 == 0:
            y_sample[i // 4] = y[512:]
        new_d[2 * i:2 * i + 2, 0] = np.asarray(R[i]["nsd"])
        new_c[2 * i:2 * i + 2, 0] = np.asarray(R[i]["nsc"])
        new_n[2 * i:2 * i + 2, 0] = np.asarray(R[i]["nsn"])
        new_m[2 * i:2 * i + 2, 0] = np.asarray(R[i]["nsm"])
    return (y_prompt, y_sample, new_d, new_c, new_n, new_m)
```

```python
import numpy as np
from contextlib import ExitStack
import concourse.bass as bass
import concourse.mybir as mybir
from concourse.bass_utils import run_bass_kernel_spmd

F32 = mybir.dt.float32
BF16 = mybir.dt.bfloat16
AF = mybir.ActivationFunctionType
ALU = mybir.AluOpType

ENGS = ("pe", "act", "dve", "pool", "sp")
D = 1024
NT = 1536
DFF = 2816
NCH = 22
TILES = [(0, 512), (512, 1024), (1024, 1536)]
COND = [0, 1, 1]
EPS = 1e-6
N_CORES = 8


def _rect(ap):
    t = ap.tensor
    name = t.name
    pat = ap.ap
    off = ap.offset
    esz = mybir.dt.size(ap.dtype)
    if "dram" in str(type(t)).lower() or "DRam" in str(type(t)):
        ext = 1
        for st, cnt in pat:
            ext += (cnt - 1) * abs(st)
        return (name, 0, 1, off * esz, (off + ext) * esz)
    shape = list(t.shape)
    fsz = 1
    for s in shape[1:]:
        fsz *= s
    pcnt = pat[0][1]
    p_lo = off // fsz
    f_lo = off % fsz
    lo = 0
    hi = 0
    for st, cnt in pat[1:]:
        if st >= 0:
            hi += (cnt - 1) * st
        else:
            lo += (cnt - 1) * st
    return (name, p_lo, p_lo + pcnt, (f_lo + lo) * esz, (f_lo + hi + 1) * esz)


class Sched:
    def __init__(self, nc, n_dma_sems=8):
        self.nc = nc
        self.q = {e: [] for e in ENGS}
        self.cnt = {e: 0 for e in ENGS}
        self.waited = {e: {} for e in ENGS}
        self.recs = {}
        self.n_dma_sems = n_dma_sems
        self.dma_i = {e: 0 for e in ENGS}
        self.dma_cnt = {}
        self.n_ops = 0

    def _deps(self, eng, ap, is_write):
        r = _rect(ap)
        lst = self.recs.setdefault(r[0], [])
        is_psum = r[0].startswith("ps")
        deps = []
        keep = []
        for rec in lst:
            (_, pl, ph, fl, fh), tok, w, e = rec
            overlap = not (ph <= r[1] or r[2] <= pl or fh <= r[3] or r[4] <= fl)
            if is_psum and e != eng:
                deps.append(tok)
                continue
            if overlap:
                if is_write or w:
                    same = (e == eng) and tok[0] == e
                    if same and eng == "pe":
                        pass
                    else:
                        deps.append(tok)
                if is_write and pl >= r[1] and ph <= r[2] and fl >= r[3] and fh <= r[4]:
                    continue
            keep.append(rec)
        self.recs[r[0]] = keep
        return deps, r

    def _emit_waits(self, eng, deps):
        w = self.waited[eng]
        best = {}
        for k, v in deps:
            if w.get(k, 0) >= v:
                continue
            if best.get(k, 0) < v:
                best[k] = v
        for k, v in best.items():
            w[k] = v
            self.q[eng].append(("wait", k, v))

    def _record(self, r, tok, w, eng):
        lst = self.recs[r[0]]
        if not w:
            for rec in lst:
                if (not rec[2]) and rec[3] == eng and rec[0] == r and rec[1][0] == tok[0]:
                    rec[1] = tok
                    return
        lst.append([r, tok, w, eng])

    def op(self, eng, fn, reads=(), writes=()):
        deps = []
        rr = []
        for ap in reads:
            d, r = self._deps(eng, ap, False)
            deps += d
            rr.append((r, False))
        for ap in writes:
            d, r = self._deps(eng, ap, True)
            deps += d
            rr.append((r, True))
        self._emit_waits(eng, deps)
        self.cnt[eng] += 1
        tok = (eng, self.cnt[eng])
        self.q[eng].append(("op", fn, eng, 1))
        for r, w in rr:
            self._record(r, tok, w, eng)
        self.n_ops += 1
        return tok

    def dma(self, eng, out, in_, **kw):
        deps = []
        d, r_in = self._deps(eng, in_, False)
        deps += d
        d, r_out = self._deps(eng, out, True)
        deps += d
        i = self.dma_i[eng] % self.n_dma_sems
        self.dma_i[eng] += 1
        key = ("dma", eng, i)
        prev = self.dma_cnt.get(key, 0)
        if prev:
            deps.append((key, prev))
        self._emit_waits(eng, deps)
        val = prev + 16
        self.dma_cnt[key] = val
        tok = (key, val)
        self.q[eng].append(("op", lambda e: e.dma_start(out=out, in_=in_, **kw), key, 16))
        self._record(r_in, tok, False, eng)
        self._record(r_out, tok, True, eng)
        self.n_ops += 1
        return tok

    def wait_all(self, eng):
        deps = []
        for e in ENGS:
            if self.cnt[e] and e != eng:
                deps.append((e, self.cnt[e]))
        for k, v in self.dma_cnt.items():
            deps.append((k, v))
        self._emit_waits(eng, deps)

    def matmul(self, out, lhsT, rhs, start=True, stop=True):
        return self.op("pe", lambda e: e.matmul(out, lhsT, rhs, start=start, stop=stop), [lhsT, rhs], [out])

    def transpose(self, out, in_, ident):
        return self.op("pe", lambda e: e.transpose(out, in_, ident), [in_, ident], [out])

    def act(self, out, in_, func, bias=None, scale=None, accum_out=None):
        kw = {}
        rd = [in_]
        if bias is not None:
            kw["bias"] = bias
            if not isinstance(bias, (int, float)):
                rd.append(bias)
        if scale is not None:
            kw["scale"] = scale
            if not isinstance(scale, (int, float)):
                rd.append(scale)
        wr = [out]
        if accum_out is not None:
            kw["accum_out"] = accum_out
            wr.append(accum_out)
        return self.op("act", lambda e: e.activation(out, in_, func, **kw), rd, wr)

    def tt(self, eng, out, in0, in1, op):
        return self.op(eng, lambda e: e.tensor_tensor(out, in0, in1, op), [in0, in1], [out])

    def ts(self, eng, out, in0, s1, s2=None, op0=ALU.mult, op1=None):
        rd = [in0]
        for s in (s1, s2):
            if s is not None and not isinstance(s, (int, float)):
                rd.append(s)
        if op1 is None:
            return self.op(eng, lambda e: e.tensor_scalar(out, in0, s1, None, op0), rd, [out])
        return self.op(eng, lambda e: e.tensor_scalar(out, in0, s1, s2, op0, op1), rd, [out])

    def stt(self, out, in0, scalar, in1, op0, op1):
        rd = [in0, in1]
        if not isinstance(scalar, (int, float)):
            rd.append(scalar)
        return self.op("dve", lambda e: e.scalar_tensor_tensor(out, in0, scalar, in1, op0, op1), rd, [out])

    def copy(self, eng, out, in_):
        if eng == "act":
            return self.op(eng, lambda e: e.copy(out, in_), [in_], [out])
        return self.op(eng, lambda e: e.tensor_copy(out, in_), [in_], [out])

    def memset(self, eng, ap, val):
        return self.op(eng, lambda e: e.memset(ap, val), [], [ap])

    def recip(self, out, in_):
        return self.op("dve", lambda e: e.reciprocal(out, in_), [in_], [out])

    def emit(self):
        nc = self.nc
        keys = [e for e in ENGS if self.cnt[e]] + list(self.dma_cnt.keys())
        with ExitStack() as es:
            sems = {}
            for i, k in enumerate(keys):
                sems[k] = es.enter_context(nc.semaphore("s%d" % i))
            block = es.enter_context(nc.Block())
            q = self.q

            def run(engname, eng):
                for it in q[engname]:
                    if it[0] == "wait":
                        eng.wait_ge(sems[it[1]], it[2])
                    else:
                        it[1](eng).then_inc(sems[it[2]], it[3])

            if q["sp"]:
                @block.sync
                def _(e):
                    run("sp", e)
            if q["act"]:
                @block.scalar
                def _(e):
                    run("act", e)
            if q["dve"]:
                @block.vector
                def _(e):
                    run("dve", e)
            if q["pool"]:
                @block.gpsimd
                def _(e):
                    run("pool", e)
            if q["pe"]:
                @block.tensor
                def _(e):
                    run("pe", e)


def _const_tables():
    i = np.arange(128)
    r, c = np.meshgrid(i, i, indexing="ij")
    mats = [
        (r == c), np.ones((128, 128)), -np.ones((128, 128)),
        (r >= c), (r <= c), (r > c), (r < c), (r <= c), (r >= c),
    ]
    consts = np.concatenate([m.astype(np.float32) for m in mats], axis=1)
    m2 = [(r // 8 == c // 8)]
    for sz in (8, 16, 32, 64):
        m2.append((r // (2 * sz) == c // (2 * sz)) & (r // sz != c // sz))
    mask2 = np.concatenate([np.concatenate([m, m], axis=1).astype(np.float32) for m in m2], axis=1)
    k = np.arange(256)
    ang = 2.0 * np.pi * ((k[:, None] * k[None, :]) % 256) / 256.0
    cs3 = np.concatenate([np.cos(ang), np.sin(ang), -np.sin(ang)], axis=1).astype(np.float32)
    t = np.arange(1024)
    ang = 2.0 * np.pi * ((t[:, None] * t[None, :]) % 1024) / 1024.0
    ct = np.cos(ang).astype(np.float32)
    nst = (-np.sin(ang)).astype(np.float32)
    tab = np.zeros((4, 128, 2, 8, 256), np.float32)
    for q in range(4):
        for j, m in enumerate((ct, nst)):
            blk = m[:, q * 256:(q + 1) * 256].reshape(8, 128, 256)
            tab[q, :, j] = blk.transpose(1, 0, 2)
    return consts, cs3, tab, mask2


C_ID, C_ONE, C_NEG, C_LT, C_UT, C_SLT, C_SUT = range(7)


def _fm(v):
    v = np.asarray(v, np.float32)
    lead = v.shape[:-1]
    n = v.shape[-1] // 128
    return np.ascontiguousarray(np.moveaxis(v.reshape(lead + (n, 128)), -1, 0))


class Builder:
    def __init__(self, cfg):
        self.cfg = cfg
        self.nc = bass.Bass("TRN2", target_bir_lowering=False)
        self.S = Sched(self.nc)
        self.es = ExitStack()
        self.steps = []
        self.bank_ctr = {}

    def dram_in(self, name, shape):
        return self.nc.dram_tensor(name, list(shape), F32, kind="ExternalInput").ap()

    def dram_out(self, name, shape):
        return self.nc.dram_tensor(name, list(shape), F32, kind="ExternalOutput").ap()

    def sb(self, name, shape, dt=F32):
        return self.es.enter_context(self.nc.sbuf_tensor(name, list(shape), dt))

    def carve(self, off, shape, dt=F32):
        n = int(np.prod(shape))
        if dt == BF16:
            w = (n + 1) // 2
            v = self.scr[:, off:off + w].bitcast(BF16)[:, 0:n]
        else:
            w = n
            v = self.scr[:, off:off + w]
        assert off + w <= self.scr_words, (off, w, self.scr_words)
        if len(shape) == 2:
            v = v.rearrange("p (a b) -> p a b", a=shape[0])
        elif len(shape) == 3:
            v = v.rearrange("p (a b c) -> p a b c", a=shape[0], b=shape[1])
        elif len(shape) == 4:
            v = v.rearrange("p (a b c d) -> p a b c d", a=shape[0], b=shape[1], c=shape[2])
        return v, w

    def bank(self, role="x", pool=None):
        pool = pool or list(range(8))
        i = self.bank_ctr.get(role, 0)
        self.bank_ctr[role] = i + 1
        return self.ps[pool[i % len(pool)]]

    def step(self, loads, fn):
        self.steps.append((loads, fn))

    def run_steps(self):
        S = self.S
        R = len(self.slots)
        load_steps = [i for i, (l, f) in enumerate(self.steps) if l is not None]
        slot_of = {si: k % R for k, si in enumerate(load_steps)}
        issued = 0

        def issue(upto):
            nonlocal issued
            while issued < len(load_steps) and issued <= upto:
                si = load_steps[issued]
                slot = self.slots[slot_of[si]]
                for (dstf, src) in self.steps[si][0](slot):
                    S.dma("pool", dstf, src)
                issued += 1

        k = 0
        for i, (l, f) in enumerate(self.steps):
            issue(k + R - 1)
            if l is not None:
                f(self.slots[slot_of[i]])
                k += 1
            else:
                f(None)
        self.steps = []

    def norm_mod(self, A, Bv, out_h=True, out_y=None):
        S = self.S
        off = self.scr_tmp
        SQ, w = self.carve(off, [8, 512], BF16); off += w
        RS, w = self.carve(off, [512]); off += w
        TM, w = self.carve(off, [2, 512]); off += w
        for tt, (t0, t1) in enumerate(TILES):
            cd = COND[tt]
            S.act(SQ, self.X[:, :, t0:t1], AF.Square)
            ps = self.bank("n", [6, 7])
            for fc in range(8):
                S.matmul(ps[:], self.ones_bf[:, 0:128], SQ[:, fc, :], start=(fc == 0), stop=(fc == 7))
            S.act(RS, ps[:], AF.Sqrt, bias=self.eps_col[:, 0:1], scale=1.0 / D)
            S.recip(RS, RS)
            for fc in range(8):
                if out_y is not None:
                    S.stt(out_y[:, fc, t0:t1], self.X[:, fc, t0:t1], A[:, fc:fc + 1], RS, ALU.mult, ALU.mult)
                else:
                    tm = TM[:, fc % 2, :]
                    S.tt("dve", tm, self.X[:, fc, t0:t1], RS, ALU.mult)
                    S.act(self.H[:, fc, t0:t1], tm, AF.Identity, bias=Bv[:, fc, cd:cd + 1], scale=A[:, fc, cd:cd + 1])

    def adaln(self, l):
        S = self.S
        for q in range(12):
            def loads(slot, q=q):
                v = slot[:, 0:4096].rearrange("p (k n) -> p k n", k=8)
                return [(v, self.w_ada[l][:, q * 512:(q + 1) * 512].rearrange("(k p) n -> p k n", p=128))]

            def fn(slot, q=q):
                v = slot[:, 0:4096].rearrange("p (k n) -> p k n", k=8)
                ps = self.bank("n", [6, 7])
                for ocl in range(4):
                    for kc in range(8):
                        S.matmul(ps[:, ocl * 2:ocl * 2 + 2], v[:, kc, ocl * 128:(ocl + 1) * 128],
                                 self.SC[:, kc, :], start=(kc == 0), stop=(kc == 7))
                pv = ps[:, 0:8].rearrange("p (o c) -> p o c", c=2)
                for c in range(2):
                    S.tt("dve", self.MOD[:, l, q * 4:(q + 1) * 4, c], pv[:, :, c],
                         self.b_adaT[:, l * 48 + q * 4: l * 48 + (q + 1) * 4], ALU.add)
            self.step(loads, fn)

        def mkfin(sub):
            def fin(slot):
                for c in range(2):
                    S.stt(self.AMOD[:, l, sub, :, c], self.MOD[:, l, (1 + 3 * sub) * 8:(2 + 3 * sub) * 8, c], 1.0,
                          self.nmT[:, (sub * 4 + l) * 8:(sub * 4 + l + 1) * 8], ALU.add, ALU.mult)
            return fin
        st = self.steps
        self.steps = st[:-12] + st[-12:-8] + [(None, mkfin(0))] + st[-8:] + [(None, mkfin(1))]

    def mod(self, l, j):
        return self.MOD[:, l, j * 8:(j + 1) * 8, :]

    def tap(self, k):
        if self.dbg is not None and k < self.cfg.get("ntaps", 0):
            self.S.dma("sp", self.dbg[k].rearrange("(c p) t -> p c t", p=128), self.X[:])

    def ffn(self, l):
        S = self.S
        self.step(None, lambda slot: self.norm_mod(self.AMOD[:, l, 1], self.mod(l, 3)))
        off = self.scr_main
        P, w = self.carve(off, [4, NT], BF16); off += w
        GP = []
        for b in range(2):
            g, w = self.carve(off, [1720], BF16); off += w
            GP.append(g)
        SB = []
        for b in range(2):
            s, w = self.carve(off, [512]); off += w
            SB.append(s)
        DG = []
        for b in range(2):
            d, w = self.carve(off, [9, 128], BF16); off += w
            DG.append(d)
        assert off <= self.scr_tmp

        def zero(slot):
            for g in GP:
                S.memset("dve", g, 0.0)
        self.step(None, zero)

        def chunk(c, slot, j, pj):
            wv = slot[:, 0:4096].rearrange("p (k n) -> p k n", k=8)
            gp = GP[c % 2]
            gpP = gp[:, 0:516].rearrange("p (s t) -> p s t", s=2)
            gpS = gp[:, 516:516 + 18 * 66].rearrange("p (r c) -> p r c", r=18)
            dg = DG[c % 2]
            i0 = (l * NCH + c) * 9
            S.tt("dve", dg, self.ident_bf.unsqueeze(1).broadcast_to([128, 9, 128]),
                 self.cwT[:, i0:i0 + 9].unsqueeze(2).broadcast_to([128, 9, 128]), ALU.mult)
            psG = []
            for tt, (t0, t1) in enumerate(TILES):
                ps = self.bank("g", [0, 1, 2])
                for kc in range(8):
                    S.matmul(ps[:], wv[:, kc, 256 + j * 128:256 + (j + 1) * 128], self.H[:, kc, t0:t1],
                             start=(kc == 0), stop=(kc == 7))
                psG.append(ps)
            S.copy("act", gpP[:, :, 1:257], psG[0][:].rearrange("p (s t) -> p s t", s=2))
            for hf in range(2):
                S.copy("act", gpS[:, 1 + 8 * hf:9 + 8 * hf, 1:65], psG[1 + hf][:].rearrange("p (r c) -> p r c", r=8))
            psA = []
            for tt, (t0, t1) in enumerate(TILES):
                ps = self.bank("a", [3, 4, 5])
                for kc in range(8):
                    S.matmul(ps[:], wv[:, kc, j * 128:(j + 1) * 128], self.H[:, kc, t0:t1],
                             start=(kc == 0), stop=(kc == 7))
                psA.append(ps)
            for tt, (t0, t1) in enumerate(TILES):
                ps = self.bank("c", [6, 7])
                if tt == 0:
                    for dc in range(3):
                        S.matmul(ps[:].rearrange("p (s t) -> p s t", s=2), dg[:, 3 + dc, :], gpP[:, :, dc:dc + 256],
                                 start=(dc == 0), stop=(dc == 2))
                else:
                    hf = tt - 1
                    n = 0
                    for dr in range(3):
                        for dc in range(3):
                            S.matmul(ps[:].rearrange("p (r c) -> p r c", r=8), dg[:, dr * 3 + dc, :],
                                     gpS[:, 8 * hf + dr:8 * hf + dr + 8, dc:dc + 64], start=(n == 0), stop=(n == 8))
                            n += 1
                sb = SB[tt % 2]
                S.act(sb, ps[:], AF.Silu, bias=self.cbT[:, l * NCH + c:l * NCH + c + 1])
                S.tt("dve", P[:, pj, t0:t1], sb, psA[tt][:], ALU.mult)

        c0 = 0
        while c0 < NCH:
            G = min(4, NCH - c0)
            for pr in range(0, G, 2):
                c = c0 + pr

                def loads(slot, c=c):
                    wv = slot[:, 0:4096].rearrange("p (k n) -> p k n", k=8)
                    wu = self.w_up[l]
                    return [(wv[:, :, 0:256], wu[:, c * 128:(c + 2) * 128].rearrange("(k p) n -> p k n", p=128)),
                            (wv[:, :, 256:512], wu[:, DFF + c * 128:DFF + (c + 2) * 128].rearrange("(k p) n -> p k n", p=128))]

                def fn(slot, c=c, pr=pr):
                    chunk(c, slot, 0, pr)
                    chunk(c + 1, slot, 1, pr + 1)
                self.step(loads, fn)

            def dloads(slot, c0=c0, G=G):
                wv = slot[:, 0:G * 1024].rearrange("p (g n) -> p g n", g=G)
                return [(wv, self.w_down[l][c0 * 128:(c0 + G) * 128, :].rearrange("(g p) n -> p g n", p=128))]

            def dfn(slot, c0=c0, G=G):
                wv = slot[:, 0:G * 1024].rearrange("p (g n) -> p g n", g=G)
                g2 = self.mod(l, 5)
                for oc in range(8):
                    for tt, (t0, t1) in enumerate(TILES):
                        ps = self.bank("g", [0, 1, 2])
                        for j in range(G):
                            S.matmul(ps[:], wv[:, j, oc * 128:(oc + 1) * 128], P[:, j, t0:t1], start=(j == 0), stop=(j == G - 1))
                        cd = COND[tt]
                        S.stt(self.X[:, oc, t0:t1], ps[:], g2[:, oc, cd:cd + 1], self.X[:, oc, t0:t1], ALU.mult, ALU.add)
            self.step(dloads, dfn)
            c0 += G

    def fnet(self, l, jf):
        S = self.S
        self.step(None, lambda slot: self.norm_mod(self.AMOD[:, l, 0], self.mod(l, 0)))
        off = self.scr_main
        AB, w = self.carve(off, [12, 4, 512], BF16); off += w
        Fm, w = self.carve(off, [8, NT], BF16); off += w
        assert off <= self.scr_words
        CS3 = self.CS3

        def stage1(slot):
            S.dma("pool", self.fb_row[:], self.fnet_b_d[:, jf * D:(jf + 1) * D])
            n = 0
            for ti in range(12):
                for g in range(4):
                    ps = self.bank("x")
                    for j in range(2):
                        S.matmul(ps[:], self.H[:, 2 * g + j, ti * 128:(ti + 1) * 128], CS3[:, j, 0:512], start=(j == 0), stop=(j == 1))
                    S.copy("act" if n % 2 else "dve", AB[:, ti, g, :], ps[:])
                    n += 1
        self.step(None, stage1)

        def stage2p(slot):
            for cc in range(8):
                g, hf = cc // 2, cc % 2
                ps = self.bank("x")
                for s in range(2):
                    n = 0
                    for tk in range(2):
                        for part in range(2):
                            S.matmul(ps[:, s * 256:(s + 1) * 256], AB[:, 2 * s + tk, g, part * 256 + hf * 128:part * 256 + (hf + 1) * 128],
                                     CS3[:, tk, part * 512:part * 512 + 256], start=(n == 0), stop=(n == 3))
                            n += 1
                S.act(Fm[:, cc, 0:512], ps[:], AF.Copy, scale=1.0 / 256.0)
        self.step(None, stage2p)

        for q in range(4):
            def loads(slot, q=q):
                return [(slot[:, 0:4096].rearrange("p (a b) -> p a b", a=4),
                         self.tab[q].rearrange("p a k u -> p (a k u)").rearrange("p (a b) -> p a b", a=4))]

            def fn(slot, q=q):
                tv = slot[:, 0:4096].rearrange("p (a k u) -> p a k u", a=2, k=8)
                for cc in range(8):
                    g, hf = cc // 2, cc % 2
                    ps = self.bank("x")
                    n = 0
                    for tk in range(8):
                        for part in range(2):
                            S.matmul(ps[:, 0:256], AB[:, 4 + tk, g, part * 256 + hf * 128:part * 256 + (hf + 1) * 128],
                                     tv[:, part, tk, :], start=(n == 0), stop=(n == 15))
                            n += 1
                    S.act(Fm[:, cc, 512 + q * 256:512 + (q + 1) * 256], ps[:, 0:256], AF.Copy, scale=1.0 / 512.0)
            self.step(loads, fn)

        for half in range(2):
            def loads(slot, half=half):
                wv = slot[:, 0:4096].rearrange("p (k n) -> p k n", k=8)
                return [(wv, self.fnet_w[jf][:, half * 512:(half + 1) * 512].rearrange("(k p) n -> p k n", p=128))]

            def fn(slot, half=half):
                wv = slot[:, 0:4096].rearrange("p (k n) -> p k n", k=8)
                g1 = self.mod(l, 2)
                for ocl in range(4):
                    oc = half * 4 + ocl
                    for tt, (t0, t1) in enumerate(TILES):
                        ps = self.bank("x")
                        for kc in range(8):
                            S.matmul(ps[:], wv[:, kc, ocl * 128:(ocl + 1) * 128], Fm[:, kc, t0:t1], start=(kc == 0), stop=False)
                        S.matmul(ps[:], self.fb_row[0:1, oc * 128:(oc + 1) * 128], self.ones_bf[0:1, :],
                                 start=False, stop=True)
                        cd = COND[tt]
                        S.stt(self.X[:, oc, t0:t1], ps[:], g1[:, oc, cd:cd + 1], self.X[:, oc, t0:t1], ALU.mult, ALU.add)
            self.step(loads, fn)

    def gdn(self, l):
        S = self.S
        CF = self.consts
        cf = lambda k: CF[:, k * 128:(k + 1) * 128]
        ident_bf = self.ident_bf
        self.step(None, lambda slot: self.norm_mod(self.AMOD[:, l, 0], self.mod(l, 0)))
        off = 0
        def cv(shape, dt=F32):
            nonlocal off
            v, w = self.carve(off, shape, dt)
            off += w
            return v
        GA = cv([12, 16]); BA = cv([12, 16])
        Z = cv([NT], BF16)
        QKV = cv([3, NT], BF16)
        K_tm = cv([12, 128], BF16); V_tm = cv([12, 128], BF16)
        O_tm = cv([12, 128])
        ST_T = cv([8, 2, 128], BF16); ST_Q = cv([8, 2, 128], BF16); ST_QD = cv([8, 2, 128], BF16); ST_KD = cv([8, 2, 128], BF16)
        CCE = cv([8, 8])
        Sf = [cv([128]) for _ in range(4)]; Sb = [cv([128], BF16) for _ in range(4)]
        RP = [cv([128], BF16) for _ in range(4)]; VN = [cv([128], BF16) for _ in range(4)]
        SSQ = cv([12])
        base = off
        CP = cv([3, 1548], BF16); DG5 = cv([15, 128], BF16); SQ = cv([512], BF16); RS = cv([512])
        T0 = cv([12, 16]); EA = cv([12, 16])
        endA = off
        off = base
        NB = 5
        GM2 = []; EM = []; ER = []; T1 = []; CC = []; LNs = []; PQs = []; MBs = []
        for _ in range(NB):
            o1 = off
            GM2.append(cv([2, 128]))
            ln1, _w = self.carve(o1, [512], BF16)
            o2 = off
            EM.append(cv([512], BF16))
            ln2, _w = self.carve(o2, [512], BF16)
            ER.append(cv([256], BF16)); T1.append(cv([2, 128], BF16)); CC.append(cv([8]))
            LNs.append([cv([512], BF16), ln1, ln2])
            PQs.append(cv([512], BF16))
            MBs.append(cv([512], BF16))
        endB = off
        off = base
        SQ2 = cv([12, 128]); ON = cv([12, 128], BF16); OGh = cv([NT], BF16)
        TMPX = [cv([512]) for _ in range(2)]
        endC = off
        off = max(endA, endB, endC)
        assert off <= self.scr_words, off
        cnt = {"pre": 0, "sc": 0}
        one_col = CF[:, C_ONE * 128:C_ONE * 128 + 1]
        ones_f, neg_f = cf(C_ONE), cf(C_NEG)
        MdT2 = CF[:, 7 * 128:9 * 128].rearrange("p (d c) -> p d c", d=2)
        SM2 = CF[:, 5 * 128:7 * 128].rearrange("p (d c) -> p d c", d=2)
        bc2 = lambda col2: col2.unsqueeze(2).broadcast_to([128, 2, 128])
        v22 = lambda t: t.rearrange("p (a d c) -> p a d c", a=2, d=2)
        v2 = lambda t: t.rearrange("p (d c) -> p d c", d=2)
        MdT2b = self.consts_bf[:, 7 * 128:9 * 128].rearrange("p (d c) -> p d c", d=2)
        SM2b = self.consts_bf[:, 5 * 128:7 * 128].rearrange("p (d c) -> p d c", d=2)

        def gloads(slot):
            return [(slot[:, 0:256].rearrange("p (k n) -> p k n", k=8), self.dn_wg.rearrange("(k p) n -> p k n", p=128))]

        def gfn(slot):
            wg = slot[:, 0:256].rearrange("p (k n) -> p k n", k=8)
            if self.cfg.get("gcut", 9) < 0:
                return
            if self.cfg.get("gcut", 9) < 1:
                return
            ps = self.bank("x")
            for ti in range(12):
                for kc in range(8):
                    S.matmul(ps[:, ti * 32:(ti + 1) * 32], self.H[:, kc, ti * 128:(ti + 1) * 128], wg[:, kc, :],
                             start=(kc == 0), stop=(kc == 7))
            pv = ps[:, 0:384].rearrange("p (t d k h) -> p t d k h", t=12, d=2, k=2)
            gp = self.gpar[:].rearrange("p (a t n) -> p a t n", a=2, t=12)
            for d in range(2):
                S.tt("dve", T0[:, :, d * 8:(d + 1) * 8], pv[:, :, d, 0, :], gp[:, 1, :, d * 8:(d + 1) * 8], ALU.add)
                S.act(BA[:, :, d * 8:(d + 1) * 8], pv[:, :, d, 1, :], AF.Sigmoid)
            cut = self.cfg.get("gcut", 9)
            if cut < 2:
                return
            S.act(T0, T0, AF.Exp)
            if cut < 3:
                return
            S.act(T0, T0, AF.Ln, bias=one_col)
            if cut < 4:
                return
            S.act(EA, gp[:, 0], AF.Exp)
            S.stt(GA, T0, -1.0, EA, ALU.mult, ALU.mult)
        self.step(gloads, gfn)
        stage = self.cfg.get("gdn_stage", 9)
        nheads = self.cfg.get("gdn_heads", 8)

        def pre2(h, ti, n, b):
            LN0b, LN1b, LN2b = LNs[b]
            PQ = [PQs[b], PQs[b]]
            MB = [MBs[b], MBs[b]]
            T1b = T1[b]
            g2 = GA[:, ti, h:h + 9:8]
            b2 = BA[:, ti, h:h + 9:8]
            kT = QKV[:, 1, ti * 128:(ti + 1) * 128]
            qT = QKV[:, 0, ti * 128:(ti + 1) * 128]
            l4 = lambda t: t.rearrange("p (a c) -> p a c", a=4)
            def mask(lv):
                S.tt("pool", l4(MB[lv % 2]), l4(LN0b), self.mask2[:, lv, 0:128].unsqueeze(1).broadcast_to([128, 4, 128]), ALU.mult)
                return v22(MB[lv % 2])
            S.tt("dve", GM2[b], MdT2, bc2(g2), ALU.mult)
            GMf = GM2[b].rearrange("p d c -> p (d c)")
            pD = self.ps[b]
            S.matmul(pD[:, 0:256], neg_f, GMf, start=True, stop=False)
            for d in range(2):
                S.matmul(pD[:, d * 128:(d + 1) * 128], GM2[b][:, d, :], ones_f, start=False, stop=(d == 1))
            S.matmul(pD[:, 256:512], ones_f, GMf, start=True, stop=False)
            for d in range(2):
                S.matmul(pD[:, 256 + d * 128:256 + (d + 1) * 128], GM2[b][:, d, :], neg_f, start=False, stop=(d == 1))
            yield
            S.act(EM[b], pD[:, 0:512], AF.Relu, scale=-1.0)
            S.act(EM[b], EM[b], AF.Exp, scale=-1.0)
            pG = self.ps[b]
            S.matmul(pG[:, 0:256], ones_f, GMf)
            for d in range(2):
                S.matmul(pG[:, 256 + d:257 + d], GM2[b][:, d, :], ones_f[:, 0:1])
            S.matmul(pG[:, 258:260], ones_f, g2)
            yield
            S.act(ER[b], pG[:, 0:256], AF.Exp)
            S.copy("act", CC[b][:, 0:4], pG[:, 256:260])
            S.act(CCE[:, n, 0:4], CC[b][:, 0:4], AF.Exp)
            for d in range(2):
                S.act(CCE[:, n, 4 + d:5 + d], CC[b][:, d:d + 1], AF.Exp, bias=CC[b][:, 2 + d:3 + d], scale=-1.0)
            S.act(CCE[:, n, 6:8], CCE[:, n, 0:2], AF.Copy, scale=-1.0)
            S.tt("pool", T1b, SM2b, bc2(b2), ALU.mult)
            S.tt("dve", EM[b][:, 0:256].rearrange("p (d c) -> p d c", d=2), EM[b][:, 0:256].rearrange("p (d c) -> p d c", d=2), T1b, ALU.mult)
            S.tt("pool", EM[b][:, 256:512].rearrange("p (d c) -> p d c", d=2), EM[b][:, 256:512].rearrange("p (d c) -> p d c", d=2), MdT2b, ALU.mult)
            pB = self.ps[b]
            S.matmul(pB[:, 0:128], kT, kT)
            S.matmul(pB[:, 128:256], kT, qT)
            yield
            E2, ET2 = v2(EM[b][:, 0:256]), v2(EM[b][:, 256:512])
            LN0 = v22(LN0b)
            S.tt("dve", LN0[:, 0], pB[:, 0:128].unsqueeze(1).broadcast_to([128, 2, 128]), E2, ALU.mult)
            yield
            S.tt("dve", ST_Q[:, n], pB[:, 128:256].unsqueeze(1).broadcast_to([128, 2, 128]), ET2, ALU.mult)
            pT = self.ps[b][:].bitcast(BF16)
            for d in range(2):
                S.transpose(pT[:, d * 128:(d + 1) * 128], LN0[:, 0, d, :], ident_bf)
            S.copy("act", LN0b[:, 256:512], pT[:, 0:256])
            S.tt("pool", ST_QD[:, n], qT.unsqueeze(1).broadcast_to([128, 2, 128]), v2(ER[b]), ALU.mult)
            S.tt("pool", ST_KD[:, n], K_tm[:, ti, :].unsqueeze(1).broadcast_to([128, 2, 128]), bc2(CCE[:, n, 4:6]), ALU.mult)
            yield
            LM0 = mask(0)
            S.tt("dve", l4(PQ[0]), ident_bf.unsqueeze(1).broadcast_to([128, 4, 128]), l4(MB[0]), ALU.subtract)
            def blk(ps, a, d):
                return ps[:, (a * 2 + d) * 128:(a * 2 + d + 1) * 128]
            Lc, Nc = LM0[:, 0], LM0[:, 1]
            cur = 0
            for r in range(2):
                pR = self.ps[b]
                for d in range(2):
                    S.matmul(blk(pR, 0, d), Nc[:, d, :], Lc[:, d, :])
                    S.matmul(blk(pR, 1, d), Lc[:, d, :], Nc[:, d, :])
                LNr_b = LN1b if r == 0 else LN2b
                S.copy("act", LNr_b, pR[:, 0:512])
                if r == 1:
                    LMn = mask(1)
                yield
                LNr = v22(LNr_b)
                Lc, Nc = LNr[:, 0], LNr[:, 1]
                PQc = v22(PQ[cur])
                pP = self.ps[b]
                for d in range(2):
                    S.matmul(blk(pP, 0, d), Nc[:, d, :], PQc[:, 0, d, :])
                    S.matmul(blk(pP, 1, d), Lc[:, d, :], PQc[:, 1, d, :])
                S.tt("dve", PQ[1 - cur], pP[:, 0:512], PQ[cur], ALU.add)
                cur = 1 - cur
                yield
            for lv in range(1, 5):
                LMc = LMn
                PQc = v22(PQ[cur])
                YY = v22(LN1b)
                pY = self.ps[b]
                for d in range(2):
                    S.matmul(blk(pY, 1, d), LMc[:, 0, d, :], PQc[:, 1, d, :])
                    if lv < 4:
                        S.matmul(blk(pY, 0, d), LMc[:, 1, d, :], PQc[:, 0, d, :])
                if lv < 4:
                    S.copy("act", LN1b, pY[:, 0:512])
                    LMn = mask(lv + 1)
                else:
                    S.copy("act", LN1b[:, 256:512], pY[:, 256:512])
                yield
                pU = self.ps[b]
                for d in range(2):
                    S.matmul(blk(pU, 1, d), PQc[:, 0, d, :], YY[:, 1, d, :])
                    if lv < 4:
                        S.matmul(blk(pU, 0, d), PQc[:, 1, d, :], YY[:, 0, d, :])
                if lv < 4:
                    S.tt("dve", PQ[1 - cur], PQ[cur], pU[:, 0:512], ALU.subtract)
                else:
                    S.tt("dve", PQ[1 - cur][:, 256:512], PQ[cur][:, 256:512], pU[:, 256:512], ALU.subtract)
                cur = 1 - cur
                yield
            S.tt("pool", ST_T[:, n], v22(PQ[cur])[:, 1], bc2(b2), ALU.mult)

        def scan_step(ti, n, d, sb_i, first):
            b = sb_i
            kT = QKV[:, 1, ti * 128:(ti + 1) * 128]
            pA = self.bank("x")
            S.matmul(pA[:, 0:128], kT, Sb[sb_i])
            S.stt(RP[b], pA[:, 0:128], CCE[:, n, 6 + d:7 + d], V_tm[:, ti, :], ALU.mult, ALU.add)
            yield
            pB = self.bank("x")
            S.matmul(pB[:, 0:128], ST_T[:, n, d, :], RP[b])
            S.copy("act", VN[b], pB[:, 0:128])
            yield
            pC = self.bank("x")
            S.matmul(pC[:, 0:128], ST_QD[:, n, d, :], Sb[sb_i], start=True, stop=False)
            S.matmul(pC[:, 0:128], ST_Q[:, n, d, :], VN[b], start=False, stop=True)
            S.tt("dve", O_tm[:, ti, :], O_tm[:, ti, :], pC[:, 0:128], ALU.add)
            pE = self.bank("x")
            S.matmul(pE[:, 0:128], ST_KD[:, n, d, :], VN[b])
            S.stt(Sf[sb_i], Sf[sb_i], CCE[:, n, 2 + d:3 + d], pE[:, 0:128], ALU.mult, ALU.add)
            S.copy("act", Sb[sb_i], Sf[sb_i])
            yield

        for h in range(nheads if stage >= 2 else 0):
            def loadsA(slot, h=h):
                return [(slot[:, 0:4096].rearrange("p (k n) -> p k n", k=8), self.dn_wh[h].rearrange("(k p) n -> p k n", p=128))]

            def fnA(slot, h=h):
                wv = slot[:, 0:4096].rearrange("p (k n) -> p k n", k=8)
                S.memset("dve", CP, 0.0)
                S.tt("dve", DG5.rearrange("p (j t) c -> p j t c", j=3),
                     ident_bf.unsqueeze(1).unsqueeze(1).broadcast_to([128, 3, 5, 128]),
                     self.dcwT[:].rearrange("p (j h t) -> p j h t", j=3, h=8)[:, :, h, :].unsqueeze(3).broadcast_to([128, 3, 5, 128]),
                     ALU.mult)
                cpP = lambda j: CP[:, j, 0:520].rearrange("p (s t) -> p s t", s=2)
                for j in range(4):
                    for tt, (t0, t1) in enumerate(TILES):
                        ps = self.bank("x")
                        for kc in range(8):
                            S.matmul(ps[:], wv[:, kc, j * 128:(j + 1) * 128], self.H[:, kc, t0:t1], start=(kc == 0), stop=(kc == 7))
                        if j == 3:
                            S.act(Z[:, t0:t1], ps[:], AF.Silu)
                        elif tt == 0:
                            S.copy("act", cpP(j)[:, :, 2:258], ps[:].rearrange("p (s t) -> p s t", s=2))
                        else:
                            S.copy("act", CP[:, j, 522 + (tt - 1) * 512:522 + tt * 512], ps[:])
                for j in range(3):
                    for tt, (t0, t1) in enumerate(TILES):
                        ps = self.bank("x")
                        for tap in range(5):
                            if tt == 0:
                                S.matmul(ps[:].rearrange("p (s t) -> p s t", s=2), DG5[:, j * 5 + tap, :], cpP(j)[:, :, tap:tap + 256],
                                         start=(tap == 0), stop=(tap == 4))
                            else:
                                st = 520 + (tt - 1) * 512 + tap
                                S.matmul(ps[:], DG5[:, j * 5 + tap, :], CP[:, j, st:st + 512], start=(tap == 0), stop=(tap == 4))
                        S.act(QKV[:, j, t0:t1], ps[:], AF.Silu)
                for j in range(2):
                    for tt, (t0, t1) in enumerate(TILES):
                        S.act(SQ, QKV[:, j, t0:t1], AF.Square)
                        ps = self.bank("x")
                        S.matmul(ps[:], self.ones_bf[:, 0:128], SQ)
                        if j == 0:
                            S.act(RS, ps[:], AF.Sqrt, bias=self.eps_col[:, 1:2], scale=128.0)
                        else:
                            S.act(RS, ps[:], AF.Sqrt, bias=self.eps_col[:, 0:1], scale=1.0)
                        S.recip(RS, RS)
                        S.tt("dve", QKV[:, j, t0:t1], QKV[:, j, t0:t1], RS, ALU.mult)
                for (j, dst) in ((1, K_tm), (2, V_tm)):
                    for g4 in range(3):
                        pb = self.bank("x")[:].bitcast(BF16)
                        for i in range(4):
                            ti = g4 * 4 + i
                            S.transpose(pb[:, i * 128:(i + 1) * 128], QKV[:, j, ti * 128:(ti + 1) * 128], ident_bf)
                        S.copy("act", dst[:, g4 * 4:(g4 + 1) * 4, :], pb[:, 0:512].rearrange("p (a b) -> p a b", a=4))
            self.step(loadsA, fnA)

            def loadsB(slot, h=h):
                return [(slot[:, 0:1024], self.dn_wo[h * 128:(h + 1) * 128, :])]

            def fnB(slot, h=h):
                wo = slot[:, 0:1024]
                seqs = [([0, 1], 0), ([2, 3], 1), (list(range(4, 12)), 2)]
                sbi = 0
                def interleave(gens):
                    gens = list(gens)
                    while gens:
                        for g in list(gens):
                            try:
                                next(g)
                            except StopIteration:
                                gens.remove(g)

                def chainx(tiles, sidx, d, n0, sb_i):
                    order = tiles if d == 0 else tiles[::-1]
                    if sidx == 2:
                        S.dma("sp", Sf[sb_i], self.sd[d, h])
                        S.copy("act", Sb[sb_i], Sf[sb_i])
                    else:
                        S.memset("dve", Sf[sb_i], 0.0)
                        S.memset("dve", Sb[sb_i], 0.0)
                    for ti in order:
                        yield from scan_step(ti, n0 + tiles.index(ti), d, sb_i, first=False)
                    if sidx < 2:
                        S.dma("sp", self.nsd[sidx, d, h], Sf[sb_i])

                S.memset("dve", O_tm, 0.0)
                if stage >= 3:
                    for grp in ([0, 1, 2, 3],):
                        interleave([pre2(h, ti, ti, k) for k, ti in enumerate(grp)])
                    if stage >= 4:
                        interleave([chainx([0, 1], 0, d, 0, 2 * 0 + d) for d in range(2)] +
                                   [chainx([2, 3], 1, d, 2, 2 * 1 + d) for d in range(2)])
                    for grp in ([4, 5, 6, 7, 8], [9, 10, 11]):
                        interleave([pre2(h, ti, ti - 4, k) for k, ti in enumerate(grp)])
                    if stage >= 4:
                        interleave([chainx(list(range(4, 12)), 2, d, 0, d) for d in range(2)])
                if stage < 5:
                    return
                S.act(SQ2, O_tm, AF.Square)
                S.op("dve", lambda e: e.reduce_sum(SSQ, SQ2, mybir.AxisListType.X), [SQ2], [SSQ])
                S.act(SSQ, SSQ, AF.Sqrt, bias=self.eps_col[:, 0:1], scale=1.0 / 128.0)
                S.recip(SSQ, SSQ)
                S.tt("dve", ON, O_tm, SSQ.unsqueeze(2).broadcast_to([128, 12, 128]), ALU.mult)
                for g4 in range(3):
                    pb = self.bank("x")[:].bitcast(BF16)
                    for i in range(4):
                        ti = g4 * 4 + i
                        S.transpose(pb[:, i * 128:(i + 1) * 128], ON[:, ti, :], ident_bf)
                    S.stt(OGh[:, g4 * 512:(g4 + 1) * 512], pb[:, 0:512], self.dn_normT[:, 0:1], Z[:, g4 * 512:(g4 + 1) * 512],
                          ALU.mult, ALU.mult)
                g1 = self.mod(l, 2)
                for oc in range(8):
                    for tt, (t0, t1) in enumerate(TILES):
                        ps = self.bank("x")
                        S.matmul(ps[:], wo[:, oc * 128:(oc + 1) * 128], OGh[:, t0:t1])
                        cd = COND[tt]
                        if (oc * 3 + tt) % 2 == 0:
                            S.stt(self.X[:, oc, t0:t1], ps[:], g1[:, oc, cd:cd + 1], self.X[:, oc, t0:t1], ALU.mult, ALU.add)
                        else:
                            tx = TMPX[((oc * 3 + tt) // 2) % 2]
                            S.act(tx, ps[:], AF.Identity, scale=g1[:, oc, cd:cd + 1])
                            S.tt("pool", self.X[:, oc, t0:t1], self.X[:, oc, t0:t1], tx, ALU.add)
            self.step(loadsB, fnB)

    def mlstm(self, l):
        S = self.S
        CF = self.consts
        cf = lambda k: CF[:, k * 128:(k + 1) * 128]
        ident_bf = self.ident_bf
        self.step(None, lambda slot: self.norm_mod(self.AMOD[:, l, 0], self.mod(l, 0)))
        off = 0
        def cv(shape, dt=F32):
            nonlocal off
            v, w = self.carve(off, shape, dt)
            off += w
            return v
        LI = cv([12, 16]); LF = cv([12, 16]); T0 = cv([12, 16])
        LIr = cv([512]); LFr = cv([512]); SCN = cv([2, 2, 256]); NBF = cv([2])
        MFB = cv([2, 2]); EMF = cv([2, 2]); DGE = cv([2, 2, 16]); EMB = cv([2, 2, 16]); EM0 = cv([16])
        qTs = [cv([NT], BF16) for _ in range(2)]; kTs = [cv([NT], BF16) for _ in range(2)]
        vTs = [cv([NT], BF16) for _ in range(2)]; OGts = [cv([NT], BF16) for _ in range(2)]
        V_tms = [cv([12, 129], BF16) for _ in range(2)]; K_tms = [cv([12, 64], BF16) for _ in range(2)]
        Hs = cv([12, 128])
        off_st = off
        ST_S = cv([8, 2, 128], BF16); ST_QB = cv([8, 2, 128], BF16); ST_KW = cv([8, 2, 64], BF16); CCE = cv([8, 4])
        SQ2, _ = self.carve(off_st, [12, 128]); SSQ = cv([12])
        off_tmp = off
        ON, _w = self.carve(off_tmp, [12, 128], BF16); OGh, _w2 = self.carve(off_tmp + 768, [NT], BF16)
        TMPX = [self.carve(off_tmp + 1536 + 512 * i, [512])[0] for i in range(2)]
        NBm = 4
        FM2 = [cv([2, 128]) for _ in range(NBm)]
        EMn = [cv([256]) for _ in range(NBm)]
        ER = [cv([256], BF16) for _ in range(NBm)]; CC = [cv([8]) for _ in range(NBm)]
        MdT2 = CF[:, 7 * 128:9 * 128].rearrange("p (d c) -> p d c", d=2)
        bc2 = lambda col2: col2.unsqueeze(2).broadcast_to([128, 2, 128])
        v2 = lambda t: t.rearrange("p (d c) -> p d c", d=2)
        CA = [cv([129]) for _ in range(4)]; CAb = [cv([130], BF16) for _ in range(4)]; CAo = [cv([129]) for _ in range(4)]
        DN = [cv([2]) for _ in range(4)]
        assert off <= self.scr_words, off
        cnt = {"pre": 0, "sc": 0}
        one_col = CF[:, C_ONE * 128:C_ONE * 128 + 1]
        ones_f, neg_f = cf(C_ONE), cf(C_NEG)
        mp = self.mpar[:].rearrange("p (a t n) -> p a t n", a=2, t=12)

        def gloads(slot):
            return [(slot[:, 0:256].rearrange("p (k n) -> p k n", k=8), self.ml_wg.rearrange("(k p) n -> p k n", p=128)),
                    (slot[:, 256:512].rearrange("p (k n) -> p k n", k=8), self.ml_wgr.rearrange("(k p) n -> p k n", p=128))]

        def gfn(slot):
            wg = slot[:, 0:256].rearrange("p (k n) -> p k n", k=8)
            wr = slot[:, 256:512].rearrange("p (k n) -> p k n", k=8)
            for vv in V_tms:
                S.memset("dve", vv[:, :, 128:129], 1.0)
            ps = self.bank("x")
            for ti in range(12):
                for kc in range(8):
                    S.matmul(ps[:, ti * 32:(ti + 1) * 32], self.H[:, kc, ti * 128:(ti + 1) * 128], wg[:, kc, :],
                             start=(kc == 0), stop=(kc == 7))
            pv = ps[:, 0:384].rearrange("p (t d k h) -> p t d k h", t=12, d=2, k=2)
            for d in range(2):
                S.tt("dve", LI[:, :, d * 8:(d + 1) * 8], pv[:, :, d, 0, :], mp[:, 0, :, d * 8:(d + 1) * 8], ALU.add)
                S.tt("dve", T0[:, :, d * 8:(d + 1) * 8], pv[:, :, d, 1, :], mp[:, 1, :, d * 8:(d + 1) * 8], ALU.add)
            S.act(T0, T0, AF.Exp, scale=-1.0)
            S.act(T0, T0, AF.Ln, bias=one_col)
            S.ts("dve", LF, T0, -1.0)
            pr = self.bank("x")
            for kc in range(8):
                S.matmul(pr[0:16, 0:512], wr[:, kc, 0:16], self.H[:, kc, 0:512], start=(kc == 0), stop=(kc == 7))
            S.act(LIr[0:16, :], pr[0:16, 0:512], AF.Identity, bias=self.mparT[0:16, 0:1])
            pr2 = self.bank("x")
            for kc in range(8):
                S.matmul(pr2[0:16, 0:512], wr[:, kc, 16:32], self.H[:, kc, 0:512], start=(kc == 0), stop=(kc == 7))
            S.ts("dve", NBF[0:16, 0:1], self.mparT[0:16, 1:2], -1.0)
            S.act(LFr[0:16, :], pr2[0:16, 0:512], AF.Exp, bias=NBF[0:16, 0:1], scale=-1.0)
            S.act(LFr[0:16, :], LFr[0:16, :], AF.Ln, bias=one_col[0:16, :])
            S.ts("dve", LFr[0:16, :], LFr[0:16, :], -1.0)
            for s in range(2):
                for fb in range(2):
                    if fb == 0:
                        d0, d1 = LFr[0:16, s * 256:(s + 1) * 256], LIr[0:16, s * 256:(s + 1) * 256]
                    elif s == 0:
                        d0, d1 = LFr[0:16, 255::-1], LIr[0:16, 255::-1]
                    else:
                        d0, d1 = LFr[0:16, 511:255:-1], LIr[0:16, 511:255:-1]
                    o = SCN[0:16, s, fb, :]
                    S.op("dve", lambda e, o=o, d0=d0, d1=d1: e.tensor_tensor_scan(o, d0, d1, 0.0, ALU.add, ALU.max), [d0, d1], [o])
                    S.copy("dve", MFB[0:16, s, fb:fb + 1], SCN[0:16, s, fb, 255:256])
                S.dma("sp", self.nsm[s, 0, :], MFB[0:8, s, 0:1])
                S.dma("sp", self.nsm[s, 1, :], MFB[8:16, s, 1:2])
            S.act(EMF[0:16], MFB[0:16], AF.Exp, scale=-1.0)
            pe = self.bank("x")
            for s in range(2):
                for fb in range(2):
                    S.ts("dve", DGE[0:16, s, fb, :], CF[0:16, 0:16], EMF[0:16, s, fb:fb + 1])
                    c0 = (s * 2 + fb) * 16
                    S.matmul(pe[0:64, c0:c0 + 16], ones_f[0:16, 0:64], DGE[0:16, s, fb, :])
            S.copy("dve", EMB[0:64].rearrange("p a b c -> p (a b c)"), pe[0:64, 0:64])
            S.act(EM0[0:64], self.smm[0:64, :], AF.Exp)
        self.step(gloads, gfn)

        cur = {}

        def pre2(h, ti, n, b):
            qT, kT, K_tm = cur["qT"], cur["kT"], cur["K_tm"]
            lf2 = LF[:, ti, h:h + 9:8]
            li2 = LI[:, ti, h:h + 9:8]
            S.tt("dve", FM2[b], MdT2, bc2(lf2), ALU.mult)
            FMf = FM2[b].rearrange("p d c -> p (d c)")
            pD = self.ps[b]
            S.matmul(pD[:, 0:256], ones_f, FMf, start=True, stop=False)
            for d in range(2):
                S.matmul(pD[:, d * 128:(d + 1) * 128], FM2[b][:, d, :], neg_f, start=False, stop=(d == 1))
            S.matmul(pD[:, 256:512], ones_f, FMf)
            yield
            S.act(EMn[b], pD[:, 0:256], AF.Relu, scale=-1.0)
            for d in range(2):
                S.act(EMn[b][:, d * 128:(d + 1) * 128], EMn[b][:, d * 128:(d + 1) * 128], AF.Exp, bias=li2[:, d:d + 1], scale=-1.0)
            S.tt("dve", v2(EMn[b]), v2(EMn[b]), MdT2, ALU.mult)
            S.act(ER[b][0:64, :], pD[0:64, 256:512], AF.Exp)
            pG = self.ps[b]
            for d in range(2):
                S.matmul(pG[:, d:d + 1], FM2[b][:, d, :], ones_f[:, 0:1])
            S.matmul(pG[:, 2:4], ones_f, lf2)
            yield
            S.copy("act", CC[b][:, 0:4], pG[:, 0:4])
            S.tt("pool", CC[b][:, 4:6], CC[b][:, 2:4], li2, ALU.add)
            S.act(CCE[:, n, 0:2], CC[b][:, 2:4], AF.Exp)
            for d in range(2):
                S.act(CCE[:, n, 2 + d:3 + d], CC[b][:, d:d + 1], AF.Exp, bias=CC[b][:, 4 + d:5 + d], scale=-1.0)
            yield
            pB = self.ps[b]
            S.matmul(pB[:, 0:128], kT[0:64, ti * 128:(ti + 1) * 128], qT[0:64, ti * 128:(ti + 1) * 128])
            S.tt("dve", ST_S[:, n], pB[:, 0:128].unsqueeze(1).broadcast_to([128, 2, 128]), v2(EMn[b]), ALU.mult)
            S.tt("pool", ST_QB[0:64, n], qT[0:64, ti * 128:(ti + 1) * 128].unsqueeze(1).broadcast_to([64, 2, 128]),
                 v2(ER[b])[0:64], ALU.mult)
            S.tt("pool", ST_KW[:, n], K_tm[:, ti, :].unsqueeze(1).broadcast_to([128, 2, 64]),
                 CCE[:, n, 2:4].unsqueeze(2).broadcast_to([128, 2, 64]), ALU.mult)
            yield

        def scan_step(ti, n, d, ci, first):
            b = ci
            V_tm = cur["V_tm"]
            pN = self.bank("sc", [0, 1, 2, 3, 4, 5])
            S.matmul(pN[:, 0:129], ST_QB[0:64, n, d, :], CAb[ci][0:64, 0:129], start=True, stop=False)
            S.matmul(pN[:, 0:129], ST_S[:, n, d, :], V_tm[:, ti, :], start=False, stop=True)
            S.act(DN[b][:, 0:1], pN[:, 128:129], AF.Abs)
            S.ts("dve", DN[b][:, 0:1], DN[b][:, 0:1], 1.0, None, ALU.max)
            S.recip(DN[b][:, 0:1], DN[b][:, 0:1])
            S.stt(Hs[:, ti, :], pN[:, 0:128], DN[b][:, 0:1], Hs[:, ti, :], ALU.mult, ALU.add)
            yield
            pS = self.bank("sc", [0, 1, 2, 3, 4, 5])
            S.matmul(pS[0:64, 0:129], ST_KW[:, n, d, :], V_tm[:, ti, :])
            S.stt(CA[ci][0:64, :], CA[ci][0:64, :], CCE[0:64, n, d:d + 1], pS[0:64, 0:129], ALU.mult, ALU.add)
            S.copy("act", CAb[ci][0:64, 0:129], CA[ci][0:64, :])
            yield

        nheads = self.cfg.get("ml_heads", 8)

        def interleave_g(gens):
            gens = list(gens)
            while gens:
                for g in list(gens):
                    try:
                        next(g)
                    except StopIteration:
                        gens.remove(g)
                yield

        def genA(h, wv):
            p = h % 2
            qT, kT, vT, OGt, V_tm, K_tm = qTs[p], kTs[p], vTs[p], OGts[p], V_tms[p], K_tms[p]
            for j in range(4):
                lo, hi = [(0, 64), (64, 128), (128, 256), (256, 384)][j]
                M = hi - lo
                for tt, (t0, t1) in enumerate(TILES):
                    ps = self.bank("fa", [6, 7])
                    for kc in range(8):
                        S.matmul(ps[0:M, :], wv[:, kc, lo:hi], self.H[:, kc, t0:t1], start=(kc == 0), stop=(kc == 7))
                    if j == 0:
                        S.act(qT[0:64, t0:t1], ps[0:64, :], AF.Copy, scale=0.125)
                    elif j == 1:
                        S.copy("act", kT[0:64, t0:t1], ps[0:64, :])
                    elif j == 2:
                        S.copy("act", vT[:, t0:t1], ps[:])
                    else:
                        S.act(OGt[:, t0:t1], ps[:], AF.Sigmoid)
                    yield
            for g4 in range(3):
                pb = self.bank("fa", [6, 7])[:].bitcast(BF16)
                for i in range(4):
                    ti = g4 * 4 + i
                    S.transpose(pb[:, i * 128:(i + 1) * 128], vT[:, ti * 128:(ti + 1) * 128], ident_bf)
                S.copy("act", V_tm[:, g4 * 4:(g4 + 1) * 4, 0:128], pb[:, 0:512].rearrange("p (a b) -> p a b", a=4))
                yield
            for g4 in range(3):
                pb = self.bank("fa", [6, 7])[:].bitcast(BF16)
                for i in range(4):
                    ti = g4 * 4 + i
                    S.transpose(pb[:, i * 64:(i + 1) * 64], kT[0:64, ti * 128:(ti + 1) * 128], ident_bf[0:64, 0:64])
                S.copy("act", K_tm[:, g4 * 4:(g4 + 1) * 4, :], pb[:, 0:256].rearrange("p (a b) -> p a b", a=4))
                yield

        def genB(h, wo):
            p = h % 2
            cur.update(qT=qTs[p], kT=kTs[p], K_tm=K_tms[p], V_tm=V_tms[p])
            OGt = OGts[p]

            def chainx(tiles, sidx, d, n0, ci):
                order = tiles if d == 0 else tiles[::-1]
                if sidx == 2:
                    S.dma("sp", CA[ci][0:64, :], self.smca[d, h])
                    S.ts("dve", CA[ci][0:64, :], CA[ci][0:64, :], EM0[0:64, d * 8 + h:d * 8 + h + 1])
                    S.copy("act", CAb[ci][0:64, 0:129], CA[ci][0:64, :])
                else:
                    S.memset("dve", CA[ci][0:64, :], 0.0)
                    S.memset("dve", CAb[ci][0:64, :], 0.0)
                for ti in order:
                    yield from scan_step(ti, n0 + tiles.index(ti), d, ci, first=False)
                if sidx < 2:
                    S.ts("dve", CAo[ci][0:64, :], CA[ci][0:64, :], EMB[0:64, sidx, d, d * 8 + h:d * 8 + h + 1])
                    S.dma("sp", self.nsc[sidx, d, h], CAo[ci][0:64, 0:128])
                    S.dma("sp", self.nsn[sidx, d, h, :], CAo[ci][0:64, 128:129])

            S.memset("dve", Hs, 0.0)
            for grp in ([0, 1, 2, 3],):
                yield from interleave_g([pre2(h, ti, ti, k) for k, ti in enumerate(grp)])
            yield from interleave_g([chainx([0, 1], 0, d, 0, d) for d in range(2)] + [chainx([2, 3], 1, d, 2, 2 + d) for d in range(2)])
            for grp in ([4, 5, 6, 7], [8, 9, 10, 11]):
                yield from interleave_g([pre2(h, ti, ti - 4, k) for k, ti in enumerate(grp)])
            yield from interleave_g([chainx(list(range(4, 12)), 2, d, 0, d) for d in range(2)])
            S.act(SQ2, Hs, AF.Square)
            S.op("dve", lambda e: e.reduce_sum(SSQ, SQ2, mybir.AxisListType.X), [SQ2], [SSQ])
            S.act(SSQ, SSQ, AF.Sqrt, bias=self.eps_col[:, 0:1], scale=1.0 / 128.0)
            S.recip(SSQ, SSQ)
            yield
            S.tt("dve", ON, Hs, SSQ.unsqueeze(2).broadcast_to([128, 12, 128]), ALU.mult)
            yield
            for g4 in range(3):
                pb = self.bank("sc", [0, 1, 2, 3, 4, 5])[:].bitcast(BF16)
                for i in range(4):
                    ti = g4 * 4 + i
                    S.transpose(pb[:, i * 128:(i + 1) * 128], ON[:, ti, :], ident_bf)
                S.stt(OGh[:, g4 * 512:(g4 + 1) * 512], pb[:, 0:512], self.ml_normT[:, 0:1], OGt[:, g4 * 512:(g4 + 1) * 512],
                      ALU.mult, ALU.mult)
                yield
            g1 = self.mod(l, 2)
            for oc in range(8):
                for tt, (t0, t1) in enumerate(TILES):
                    ps = self.bank("sc", [0, 1, 2, 3, 4, 5])
                    S.matmul(ps[:], wo[:, oc * 128:(oc + 1) * 128], OGh[:, t0:t1])
                    cd = COND[tt]
                    if (oc * 3 + tt) % 2 == 0:
                        S.stt(self.X[:, oc, t0:t1], ps[:], g1[:, oc, cd:cd + 1], self.X[:, oc, t0:t1], ALU.mult, ALU.add)
                    else:
                        tx = TMPX[((oc * 3 + tt) // 2) % 2]
                        S.act(tx, ps[:], AF.Identity, scale=g1[:, oc, cd:cd + 1])
                        S.tt("pool", self.X[:, oc, t0:t1], self.X[:, oc, t0:t1], tx, ALU.add)
                yield

        def drain(g):
            for _ in g:
                pass

        def loads0(slot):
            return [(slot[:, 0:3072].rearrange("p (k n) -> p k n", k=8), self.ml_wh[0].rearrange("(k p) n -> p k n", p=128))]
        self.step(loads0, lambda slot: drain(genA(0, slot[:, 0:3072].rearrange("p (k n) -> p k n", k=8))))
        for h in range(nheads):
            def loadsH(slot, h=h):
                out = [(slot[:, 3072:4096], self.ml_wo[h * 128:(h + 1) * 128, :])]
                if h + 1 < nheads:
                    out.append((slot[:, 0:3072].rearrange("p (k n) -> p k n", k=8), self.ml_wh[h + 1].rearrange("(k p) n -> p k n", p=128)))
                return out

            def fnH(slot, h=h):
                gens = [genB(h, slot[:, 3072:4096])]
                if h + 1 < nheads:
                    gens.append(genA(h + 1, slot[:, 0:3072].rearrange("p (k n) -> p k n", k=8)))
                drain(interleave_g(gens))
            self.step(loadsH, fnH)

    def build(self):
        cfg = self.cfg
        nc = self.nc
        S = self.S
        self.xT = self.dram_in("xT", [D, NT])
        self.condT = self.dram_in("condT", [128, 16])
        self.w_ada = self.dram_in("w_ada", [4, D, 6 * D])
        b_adaT_d = self.dram_in("b_adaT", [128, 4 * 48])
        nmT_d = self.dram_in("nmT", [128, 72])
        self.w_up = self.dram_in("w_up", [4, D, 2 * DFF])
        cwT_d = self.dram_in("cwT", [128, 4 * NCH * 9])
        cbT_d = self.dram_in("cbT", [128, 4 * NCH])
        self.w_down = self.dram_in("w_down", [4, DFF, D])
        self.fnet_w = self.dram_in("fnet_w", [2, D, D])
        self.fnet_b_d = self.dram_in("fnet_b", [1, 2 * D])
        consts_d = self.dram_in("consts", [128, 1152])
        cs3_d = self.dram_in("cs3", [256, 768])
        self.tab = self.dram_in("tab", [4, 128, 2, 8, 256])
        self.dn_wh = self.dram_in("dn_wh", [8, D, 512])
        self.dn_wg = self.dram_in("dn_wg", [D, 32])
        dcwT_d = self.dram_in("dcwT", [128, 120])
        mask2_d = self.dram_in("mask2", [128, 1280])
        gpar_d = self.dram_in("gpar", [128, 2 * 12 * 16])
        dn_normT_d = self.dram_in("dn_normT", [128, 1])
        self.dn_wo = self.dram_in("dn_wo", [D, D])
        self.sd = self.dram_in("sd", [2, 8, 128, 128])
        self.nsd = self.dram_out("nsd", [2, 2, 8, 128, 128])
        self.ml_wh = self.dram_in("ml_wh", [8, D, 384])
        self.ml_wg = self.dram_in("ml_wg", [D, 32])
        self.ml_wgr = self.dram_in("ml_wgr", [D, 32])
        mpar_d = self.dram_in("mpar", [128, 2 * 12 * 16])
        mparT_d = self.dram_in("mparT", [16, 2])
        ml_normT_d = self.dram_in("ml_normT", [128, 1])
        self.ml_wo = self.dram_in("ml_wo", [D, D])
        self.smca = self.dram_in("smca", [2, 8, 64, 129])
        smm_d = self.dram_in("smm", [64, 16])
        self.nsc = self.dram_out("nsc", [2, 2, 8, 64, 128])
        self.nsn = self.dram_out("nsn", [2, 2, 8, 64])
        self.nsm = self.dram_out("nsm", [2, 2, 8])
        self.yT = self.dram_out("yT", [D, NT])
        ntaps = cfg.get("ntaps", 0)
        self.dbg = self.dram_out("dbg", [ntaps, D, NT]) if ntaps else None

        self.X = self.sb("X", [128, 8, NT])
        self.H = self.sb("H", [128, 8, NT], BF16)
        self.slots = [self.sb("slot%d" % i, [128, 4096], BF16) for i in range(4)]
        self.scr_words = cfg.get("scr_words", 19712)
        self.scr = self.sb("scr", [128, self.scr_words])
        self.scr_tmp = self.scr_words - 3584
        self.scr_main = 0
        self.consts = self.sb("consts_f", [128, 1152])
        self.consts_bf = self.sb("consts_b", [128, 1152], BF16)
        self.ones_bf = self.sb("ones_bf", [128, 512], BF16)
        self.CS3 = self.sb("CS3", [128, 2, 768], BF16)
        self.fb_row = self.sb("fb_row", [1, D], BF16)
        self.b_adaT = self.sb("b_adaT_s", [128, 4 * 48])
        self.nmT = self.sb("nmT_s", [128, 72])
        self.cwT = self.sb("cwT_s", [128, 4 * NCH * 9])
        self.cbT = self.sb("cbT_s", [128, 4 * NCH])
        self.condS = self.sb("condS", [128, 16])
        self.SC = self.sb("SC", [128, 8, 2], BF16)
        self.MOD = self.sb("MOD", [128, 4, 48, 2])
        self.AMOD = self.sb("AMOD", [128, 4, 2, 8, 2])
        self.eps_col = self.sb("eps_col", [128, 2])
        self.mpar = self.sb("mpar_s", [128, 2 * 12 * 16])
        self.mparT = self.sb("mparT_s", [16, 2])
        self.ml_normT = self.sb("ml_normT_s", [128, 1])
        self.smm = self.sb("smm_s", [64, 16])
        self.dcwT = self.sb("dcwT_s", [128, 120])
        self.mask2 = self.sb("mask2_s", [128, 5, 256], BF16)
        self.gpar = self.sb("gpar_s", [128, 2 * 12 * 16])
        self.dn_normT = self.sb("dn_normT_s", [128, 1])
        self.ps = [self.es.enter_context(nc.psum_tensor("ps%d" % i, [128, 512], F32)) for i in range(8)]
        self.ident_bf = self.consts_bf[:, C_ID * 128:(C_ID + 1) * 128]

        S.dma("sp", self.X[:], self.xT.rearrange("(c p) t -> p c t", p=128))
        S.dma("sp", self.condS[:], self.condT)
        S.dma("sp", self.consts[:], consts_d)
        S.dma("pool", self.consts_bf[:], consts_d)
        S.dma("pool", self.CS3[:], cs3_d.rearrange("(j p) n -> p j n", p=128))
        S.dma("sp", self.b_adaT[:], b_adaT_d)
        S.dma("sp", self.nmT[:], nmT_d)
        S.dma("sp", self.cwT[:], cwT_d)
        S.dma("sp", self.cbT[:], cbT_d)
        S.memset("dve", self.ones_bf[:], 1.0)
        S.memset("dve", self.eps_col[:, 0:1], EPS)
        S.memset("dve", self.eps_col[:, 1:2], 128.0 * EPS)
        S.dma("sp", self.mpar[:], mpar_d)
        S.dma("sp", self.mparT[:], mparT_d)
        S.dma("sp", self.ml_normT[:], ml_normT_d)
        S.dma("sp", self.smm[:], smm_d)
        S.dma("sp", self.dcwT[:], dcwT_d)
        S.dma("pool", self.mask2[:].rearrange("p a b -> p (a b)"), mask2_d)
        S.dma("sp", self.gpar[:], gpar_d)
        S.dma("sp", self.dn_normT[:], dn_normT_d)
        S.act(self.SC[:].rearrange("p k c -> p (k c)"), self.condS[:], AF.Silu)

        layers = cfg.get("layers", [0, 1, 2, 3])

        def collect(fn):
            keep = self.steps
            self.steps = []
            fn()
            out = self.steps
            self.steps = keep
            return out

        def merge(a, b):
            out = []
            ia = ib = 0
            while ia < len(a) or ib < len(b):
                if ia < len(a):
                    out.append(a[ia]); ia += 1
                want = (ia * len(b)) // max(1, len(a)) if ia < len(a) else len(b)
                while ib < want:
                    out.append(b[ib]); ib += 1
            return out

        k = 0
        ada0 = collect(lambda: self.adaln(layers[0]))
        self.steps += ada0[:5]
        pending = ada0[5:]
        for li, l in enumerate(layers):
            kind = l % 3
            mix = []
            if kind == 0 and cfg.get("fnet", True):
                mix = collect(lambda: self.fnet(l, l // 3))
            elif kind == 1 and cfg.get("gdn", True):
                mix = collect(lambda: self.gdn(l))
            elif kind == 2 and cfg.get("mlstm", True):
                mix = collect(lambda: self.mlstm(l))
            extra = pending
            if li + 1 < len(layers):
                extra = extra + collect(lambda: self.adaln(layers[li + 1]))
            pending = []
            self.steps += merge(mix, extra)
            self.step(None, lambda slot, k=k: self.tap(k))
            k += 1
            if cfg.get("ffn", True):
                self.ffn(l)
            self.step(None, lambda slot, k=k: self.tap(k))
            k += 1
        self.run_steps()

        Y, w = self.carve(0, [8, NT])
        self.norm_mod(self.nmT[:, 64:72], None, out_y=Y)
        S.dma("sp", self.yT.rearrange("(c p) t -> p c t", p=128), Y)
        S.wait_all("sp")
        S.emit()
        self.es.close()
        return nc


def _prep(inputs):
    consts, cs3, tab, mask2 = _const_tables()
    f = lambda k: np.ascontiguousarray(np.asarray(inputs[k], np.float32))
    shared = {
        "w_ada": f("w_ada"),
        "b_adaT": _fm(f("b_ada")).reshape(128, 4 * 48),
        "nmT": np.concatenate([_fm(f("norm_mix")).reshape(128, 32), _fm(f("norm_ffn")).reshape(128, 32),
                               _fm(f("norm_final")).reshape(128, 8)], axis=1),
        "w_up": f("ffn_w_up"),
        "cwT": np.ascontiguousarray(np.moveaxis(_fm(f("ffn_conv_w").reshape(4, 9, DFF)), 2, 3)).reshape(128, 4 * NCH * 9),
        "cbT": _fm(f("ffn_conv_b")).reshape(128, 4 * NCH),
        "w_down": f("ffn_w_down"),
        "fnet_w": f("fnet_w"),
        "fnet_b": f("fnet_b").reshape(1, 2 * D),
        "consts": consts, "cs3": cs3, "tab": tab, "mask2": mask2,
    }
    wi = f("dn_w_in")[0]
    shared["dn_wh"] = np.ascontiguousarray(np.stack(
        [np.concatenate([wi[:, j * 1024 + h * 128:j * 1024 + (h + 1) * 128] for j in range(4)], axis=1) for h in range(8)]))
    shared["dn_wg"] = np.ascontiguousarray(wi[:, 4096:4128])
    shared["dcwT"] = np.ascontiguousarray(np.moveaxis(_fm(f("dn_conv_w")[0]), 1, 2)).reshape(128, 120)
    gp = np.stack([f("dn_a_log")[0].reshape(16), f("dn_dt_bias")[0].reshape(16)])
    shared["gpar"] = np.ascontiguousarray(np.broadcast_to(gp[None, :, None, :], (128, 2, 12, 16))).reshape(128, 384)
    shared["dn_normT"] = np.ascontiguousarray(f("dn_norm")[0].reshape(128, 1))
    shared["dn_wo"] = f("dn_w_out")[0]
    sdel = f("state_delta")
    mw = f("ml_w_in")[0]
    shared["ml_wh"] = np.ascontiguousarray(np.stack(
        [np.concatenate([mw[:, h * 64:(h + 1) * 64], mw[:, 512 + h * 64:512 + (h + 1) * 64],
                         mw[:, 1024 + h * 128:1024 + (h + 1) * 128], mw[:, 2048 + h * 128:2048 + (h + 1) * 128]], axis=1)
         for h in range(8)]))
    mg = mw[:, 3072:3104]
    shared["ml_wg"] = np.ascontiguousarray(mg)
    mg4 = mg.reshape(1024, 2, 2, 8)
    shared["ml_wgr"] = np.ascontiguousarray(np.concatenate([mg4[:, :, 0, :].reshape(1024, 16), mg4[:, :, 1, :].reshape(1024, 16)], axis=1))
    bp = np.stack([f("ml_b_i")[0].reshape(16), f("ml_b_f")[0].reshape(16)])
    shared["mpar"] = np.ascontiguousarray(np.broadcast_to(bp[None, :, None, :], (128, 2, 12, 16))).reshape(128, 384)
    shared["mparT"] = np.ascontiguousarray(bp.T)
    shared["ml_normT"] = np.ascontiguousarray(f("ml_norm")[0].reshape(128, 1))
    shared["ml_wo"] = f("ml_w_out")[0]
    smc, smn, smmm = f("state_mlstm_c"), f("state_mlstm_n"), f("state_mlstm_m")
    xp = f("x_prompt")
    xs = f("x_sample")
    c = f("c")
    cctx = f("c_ctx")
    per_core = []
    for i in range(N_CORES):
        b = i // 4
        x = np.concatenate([xp[2 * i], xp[2 * i + 1], xs[b]], axis=0)
        cond = np.stack([cctx, c[b]], axis=-1)
        m = dict(shared)
        m["xT"] = np.ascontiguousarray(x.T)
        m["sd"] = np.ascontiguousarray(sdel[b, 0])
        m["smca"] = np.ascontiguousarray(np.concatenate([smc[b, 0], smn[b, 0][..., None]], axis=-1))
        m["smm"] = np.ascontiguousarray(np.broadcast_to(smmm[b, 0].reshape(1, 16), (64, 16)))
        m["condT"] = np.ascontiguousarray(cond.reshape(8, 128, 2).transpose(1, 0, 2)).reshape(128, 16)
        per_core.append(m)
    return per_core


def run(inputs, cfg, core_ids=None, trace=False):
    b = Builder(cfg)
    nc = b.build()
    maps = _prep(inputs)
    core_ids = core_ids or list(range(N_CORES))
    maps = [maps[i] for i in core_ids]
    res = run_bass_kernel_spmd(nc, maps, core_ids=list(range(len(core_ids))), trace=trace)
    return res, b


def kernel(**inputs):
    res, b = run(inputs, dict())
    R = res.results
    y_prompt = np.zeros((16, 256, D), np.float32)
    y_sample = np.zeros((2, 1024, D), np.float32)
    new_d = np.zeros((16, 1, 2, 8, 128, 128), np.float32)
    new_c = np.zeros((16, 1, 2, 8, 64, 128), np.float32)
    new_n = np.zeros((16, 1, 2, 8, 64), np.float32)
    new_m = np.zeros((16, 1, 2, 8), np.float32)
    for i in range(N_CORES):
        y = np.asarray(R[i]["yT"]).T
        y_prompt[2 * i] = y[0:256]
        y_prompt[2 * i + 1] = y[256:512]
        if i % 4 == 0:
            y_sample[i // 4] = y[512:]
        new_d[2 * i:2 * i + 2, 0] = np.asarray(R[i]["nsd"])
        new_c[2 * i:2 * i + 2, 0] = np.asarray(R[i]["nsc"])
        new_n[2 * i:2 * i + 2, 0] = np.asarray(R[i]["nsn"])
        new_m[2 * i:2 * i + 2, 0] = np.asarray(R[i]["nsm"])
    return (y_prompt, y_sample, new_d, new_c, new_n, new_m)
```
